# Optimizing a Trainium2 kernel written in Bass

```python
import jax, jax.numpy as jnp
from jax import lax
import numpy as np

D_MODEL = 2048
BATCH = 4
SEQ = 4096
DEPTH = 4

N_A = DEPTH // 2
N_B = DEPTH - N_A
D_PLE = 256
D_FF = 5632
HEAD_DIM = 64
N_HEADS_A = D_MODEL // HEAD_DIM
LORA_DECAY = 96
LORA_A = 96
LORA_V = 64
LORA_G = 256
N_Q_HEADS = D_MODEL // HEAD_DIM
N_KV_HEADS = 8
GROUP = N_Q_HEADS // N_KV_HEADS
WINDOW = 128
BLOCK = 128
ROPE_THETA = 10000.0
RMS_EPS = 1e-6
GN_EPS = 64e-5

kernel_name = "yoco_rwkv7_swa_sink_macaron_ple"


def rms_norm(x, g):
    xf = x.astype(jnp.float32)
    y = xf * lax.rsqrt(jnp.mean(xf * xf, axis=-1, keepdims=True) + RMS_EPS)
    return (y * g.astype(jnp.float32)).astype(x.dtype)


def swiglu(h, w_gate, w_up, w_down):
    return (jax.nn.silu(h @ w_gate) * (h @ w_up)) @ w_down


def rope_tables(positions):
    inv_freq = ROPE_THETA ** (-jnp.arange(0, HEAD_DIM, 2, dtype=jnp.float32) / HEAD_DIM)
    ang = positions.astype(jnp.float32)[..., None] * inv_freq
    return jnp.cos(ang)[:, :, None, :], jnp.sin(ang)[:, :, None, :]


def apply_rope(t, cos, sin):
    tf = t.astype(jnp.float32)
    t1, t2 = tf[..., : HEAD_DIM // 2], tf[..., HEAD_DIM // 2:]
    out = jnp.concatenate([t1 * cos - t2 * sin, t2 * cos + t1 * sin], axis=-1)
    return out.astype(t.dtype)


def wkv7_scan(r, w, k, v, a, b):
    def step(state, inp):
        r_t, w_t, k_t, v_t, a_t, b_t = inp
        sa = jnp.einsum('bhij,bhj->bhi', state, a_t)
        state = (state * w_t[:, :, None, :] + sa[..., None] * b_t[:, :, None, :]
                 + v_t[..., None] * k_t[:, :, None, :])
        return state, jnp.einsum('bhij,bhj->bhi', state, r_t)
    xs = tuple(jnp.swapaxes(t, 0, 1) for t in (r, w, k, v, a, b))
    state0 = jnp.zeros((r.shape[0], r.shape[2], HEAD_DIM, HEAD_DIM), jnp.float32)
    _, y = lax.scan(step, state0, xs)
    return jnp.swapaxes(y, 0, 1)


def rwkv7_time_mix(h, v_first, mu, w_r, w_k, w_v, w_o, w0, w1, w2, a0, a1, a2,
                   g1, g2, k_k, k_a, r_k, gn_g, gn_b, vmix):
    bsz, seq, _ = h.shape
    f32 = jnp.float32
    h_prev = jnp.pad(h, ((0, 0), (1, 0), (0, 0)))[:, :-1]
    dx = h_prev - h
    xr, xw, xk, xv, xa, xg = (h + dx * mu[c] for c in range(6))
    r = xr @ w_r
    k = xk @ w_k
    v = xv @ w_v
    w_log = -jax.nn.softplus(-(w0 + jnp.tanh(xw @ w1) @ w2).astype(f32)) - 0.5
    decay = jnp.exp(-jnp.exp(w_log))
    a = jax.nn.sigmoid((a0 + (xa @ a1) @ a2).astype(f32))
    g = jax.nn.sigmoid(xg @ g1) @ g2
    if vmix is None:
        v_first = v
    else:
        v0, v1, v2 = vmix
        v = v + (v_first - v) * jax.nn.sigmoid(v0 + (xv @ v1) @ v2)

    def heads(t):
        return t.astype(f32).reshape(bsz, seq, N_HEADS_A, HEAD_DIM)

    kk = heads(k * k_k)
    kk = kk / jnp.maximum(jnp.sqrt(jnp.sum(kk * kk, axis=-1, keepdims=True)), 1e-12)
    a_h = heads(a)
    k_h = heads(k.astype(f32) * (1.0 + (a - 1.0) * k_a.astype(f32)))
    r_h = heads(r)
    v_h = heads(v)
    y = wkv7_scan(r_h, heads(decay), k_h, v_h, -kk, kk * a_h)
    mean = jnp.mean(y, axis=-1, keepdims=True)
    var = jnp.mean(jnp.square(y - mean), axis=-1, keepdims=True)
    y = ((y - mean) * lax.rsqrt(var + GN_EPS)).reshape(bsz, seq, D_MODEL)
    y = y * gn_g.astype(f32) + gn_b.astype(f32)
    bonus = jnp.sum(r_h * k_h * r_k.astype(f32), axis=-1, keepdims=True) * v_h
    y = y + bonus.reshape(bsz, seq, D_MODEL)
    out = (y.astype(h.dtype) * g) @ w_o
    return out, v_first


def to_band(t):
    bsz, seq = t.shape[0], t.shape[1]
    tb = t.reshape(bsz, seq // BLOCK, BLOCK, N_KV_HEADS, HEAD_DIM)
    prev = jnp.pad(tb, ((0, 0), (1, 0), (0, 0), (0, 0), (0, 0)))[:, :-1]
    return jnp.concatenate([prev, tb], axis=2)


def band_mask(n_blk):
    qi = jnp.arange(BLOCK)[None, :, None]
    ki = jnp.arange(2 * BLOCK)[None, None, :]
    blk = jnp.arange(n_blk)[:, None, None]
    rel = BLOCK + qi - ki
    return (rel >= 0) & (rel < WINDOW) & (blk * BLOCK + ki - BLOCK >= 0)


def swa_sink_attention(q, k_band, v_band, sinks, mask):
    bsz, seq = q.shape[0], q.shape[1]
    n_blk = seq // BLOCK
    qb = q.reshape(bsz, n_blk, BLOCK, N_KV_HEADS, GROUP, HEAD_DIM)
    scores = jnp.einsum('bnqhgd,bnshd->bnhgqs', qb, k_band).astype(jnp.float32) * (HEAD_DIM ** -0.5)
    scores = jnp.where(mask[None, :, None, None], scores, -jnp.inf)
    sink = sinks.astype(jnp.float32).reshape(N_KV_HEADS, GROUP)[None, None, :, :, None, None]
    m = jnp.maximum(jnp.max(scores, axis=-1, keepdims=True), sink)
    e = jnp.exp(scores - m)
    probs = e / (jnp.sum(e, axis=-1, keepdims=True) + jnp.exp(sink - m))
    out = jnp.einsum('bnhgqs,bnshd->bnqhgd', probs.astype(v_band.dtype), v_band)
    return out.reshape(bsz, seq, N_Q_HEADS * HEAD_DIM)


def setup_inputs(seed: int = 0) -> dict:
    key = jax.random.key(seed)
    ks = iter(jax.random.split(key, 48))
    nrm = lambda shape, s: jax.random.normal(next(ks), shape, jnp.float32) * s
    D, F, H, N = D_MODEL, D_FF, N_HEADS_A, HEAD_DIM
    start = jax.random.randint(next(ks), (BATCH, 1), 0, 1024, dtype=jnp.int32)
    positions = (start + jnp.arange(SEQ, dtype=jnp.int32)[None, :]).astype(jnp.int32)
    return {
        "x": nrm((BATCH, SEQ, D), 1.0),
        "p": nrm((DEPTH, BATCH, SEQ, D_PLE), 1.0),
        "positions": positions,
        "norm_g": 1.0 + nrm((DEPTH, 4, D), 0.1),
        "ffn_w_gate": nrm((DEPTH, 2, D, F), D ** -0.5),
        "ffn_w_up": nrm((DEPTH, 2, D, F), D ** -0.5),
        "ffn_w_down": nrm((DEPTH, 2, F, D), F ** -0.5),
        "ple_w_up": nrm((DEPTH, D_PLE, D), D_PLE ** -0.5),
        "ple_w_gate": nrm((DEPTH, D, D), D ** -0.5),
        "rwkv_mu": jax.random.uniform(next(ks), (N_A, 6, D), jnp.float32),
        "rwkv_w_rkv": nrm((N_A, 3, D, D), D ** -0.5),
        "rwkv_w_o": nrm((N_A, D, D), D ** -0.5),
        "rwkv_w0": jax.random.uniform(next(ks), (N_A, D), jnp.float32, -6.0, -1.0),
        "rwkv_w1": nrm((N_A, D, LORA_DECAY), D ** -0.5),
        "rwkv_w2": nrm((N_A, LORA_DECAY, D), 0.5 * LORA_DECAY ** -0.5),
        "rwkv_a0": nrm((N_A, D), 0.1),
        "rwkv_a1": nrm((N_A, D, LORA_A), D ** -0.5),
        "rwkv_a2": nrm((N_A, LORA_A, D), 0.5 * LORA_A ** -0.5),
        "rwkv_v0": nrm((N_A - 1, D), 0.1),
        "rwkv_v1": nrm((N_A - 1, D, LORA_V), D ** -0.5),
        "rwkv_v2": nrm((N_A - 1, LORA_V, D), 0.5 * LORA_V ** -0.5),
        "rwkv_g1": nrm((N_A, D, LORA_G), D ** -0.5),
        "rwkv_g2": nrm((N_A, LORA_G, D), LORA_G ** -0.5),
        "rwkv_k_k": 0.85 + nrm((N_A, D), 0.1),
        "rwkv_k_a": 1.0 + nrm((N_A, D), 0.1),
        "rwkv_r_k": nrm((N_A, H, N), 0.1),
        "rwkv_gn_g": 1.0 + nrm((N_A, D), 0.1),
        "rwkv_gn_b": nrm((N_A, D), 0.02),
        "kv_norm_g": 1.0 + nrm((D,), 0.1),
        "w_kv": nrm((D, 2 * N_KV_HEADS * HEAD_DIM), D ** -0.5),
        "attn_w_q": nrm((N_B, D, N_Q_HEADS * HEAD_DIM), D ** -0.5),
        "attn_w_o": nrm((N_B, N_Q_HEADS * HEAD_DIM, D), D ** -0.5),
        "attn_sinks": nrm((N_B, N_Q_HEADS), 1.0),
        "final_norm_g": 1.0 + nrm((D,), 0.1),
    }


def reference(x, p, positions, norm_g, ffn_w_gate, ffn_w_up, ffn_w_down, ple_w_up, ple_w_gate,
              rwkv_mu, rwkv_w_rkv, rwkv_w_o, rwkv_w0, rwkv_w1, rwkv_w2, rwkv_a0, rwkv_a1, rwkv_a2,
              rwkv_v0, rwkv_v1, rwkv_v2, rwkv_g1, rwkv_g2, rwkv_k_k, rwkv_k_a, rwkv_r_k,
              rwkv_gn_g, rwkv_gn_b, kv_norm_g, w_kv, attn_w_q, attn_w_o, attn_sinks, final_norm_g):
    bsz, seq, _ = x.shape
    cos, sin = rope_tables(positions)
    mask = band_mask(seq // BLOCK)
    v_first = None
    k_band = None
    v_band = None
    for i in range(DEPTH):
        if i == N_A:
            kv = rms_norm(x, kv_norm_g) @ w_kv
            k_sh, v_sh = jnp.split(kv.reshape(bsz, seq, 2 * N_KV_HEADS, HEAD_DIM), 2, axis=2)
            k_band = to_band(apply_rope(k_sh, cos, sin))
            v_band = to_band(v_sh)
        x = x + 0.5 * swiglu(rms_norm(x, norm_g[i, 0]), ffn_w_gate[i, 0], ffn_w_up[i, 0], ffn_w_down[i, 0])
        h = rms_norm(x, norm_g[i, 1])
        if i < N_A:
            j = i
            vmix = None if j == 0 else (rwkv_v0[j - 1], rwkv_v1[j - 1], rwkv_v2[j - 1])
            mix, v_first = rwkv7_time_mix(
                h, v_first, rwkv_mu[j], rwkv_w_rkv[j, 0], rwkv_w_rkv[j, 1], rwkv_w_rkv[j, 2], rwkv_w_o[j],
                rwkv_w0[j], rwkv_w1[j], rwkv_w2[j], rwkv_a0[j], rwkv_a1[j], rwkv_a2[j],
                rwkv_g1[j], rwkv_g2[j], rwkv_k_k[j], rwkv_k_a[j], rwkv_r_k[j],
                rwkv_gn_g[j], rwkv_gn_b[j], vmix)
        else:
            j = i - N_A
            q = apply_rope((h @ attn_w_q[j]).reshape(bsz, seq, N_Q_HEADS, HEAD_DIM), cos, sin)
            mix = swa_sink_attention(q, k_band, v_band, attn_sinks[j], mask) @ attn_w_o[j]
        x = x + mix
        x = x + 0.5 * swiglu(rms_norm(x, norm_g[i, 2]), ffn_w_gate[i, 1], ffn_w_up[i, 1], ffn_w_down[i, 1])
        gate = jax.nn.sigmoid(rms_norm(x, norm_g[i, 3]) @ ple_w_gate[i])
        x = x + gate * (p[i] @ ple_w_up[i])
    return rms_norm(x, final_norm_g)
```

```python
import numpy as np
import os
DUMP = int(os.environ.get('DUMP', '0'))
INV_SUB = int(os.environ.get('INV_SUB', '9'))
SCAN_DBG = float(os.environ.get('SCAN_DBG', '99'))
from contextlib import ExitStack
import concourse.bass as bass
import concourse.mybir as mybir
from concourse.bass_utils import run_bass_kernel_spmd

F32 = mybir.dt.float32
BF16 = mybir.dt.bfloat16
I32 = mybir.dt.int32
ALU = mybir.AluOpType
AF = mybir.ActivationFunctionType
AX = mybir.AxisListType

D = 2048
KC = D // 128
NH = 32
HD = 64
RMS_EPS = 1e-6
GN_EPS = 64e-5

ENGS = ("act", "dve", "pool", "pe", "sync")
NS_DMA = 8


class Sched:
    def __init__(self):
        self.ops = {e: [] for e in ENGS}
        self.lastw = {}
        self.readers = {}

    def emit(self, eng, fn, reads=(), writes=(), dma=False):
        idx = len(self.ops[eng])
        me = (eng, idx)
        deps = set()
        for k in tuple(reads) + tuple(writes):
            w = self.lastw.get(k)
            if w is not None:
                deps.add(w)
        for k in writes:
            rd = self.readers.get(k)
            if rd:
                deps.update(rd)
            self.readers[k] = []
        for k in reads:
            lst = self.readers.setdefault(k, [])
            if not dma:
                lst[:] = [r for r in lst if not (r[0] == eng and not self.ops[eng][r[1]][2])]
            lst.append(me)
        for k in writes:
            self.lastw[k] = me
        deps.discard(me)
        if eng == "pe":
            deps = {d for d in deps if d[0] != "pe"}
        self.ops[eng].append((fn, deps, dma))

    def setup(self, nc, es):
        self.nc = nc
        self.sems = {e: es.enter_context(nc.semaphore("s_" + e)) for e in ENGS}
        self.dsems = {e: [es.enter_context(nc.semaphore("d_%s%d" % (e, i))) for i in range(NS_DMA)]
                      for e in ("sync", "pool", "act")}
        self.cnt = {e: 0 for e in ENGS}
        self.dcnt = {e: 0 for e in ENGS}
        self.waited = {e: {} for e in ENGS}
        self.dlast = {e: [] for e in ENGS}

    def flush(self):
        nc = self.nc
        sems, dsems = self.sems, self.dsems
        signaled = {e: set() for e in ENGS}
        for e, lst in self.ops.items():
            for fn, deps, dma in lst:
                for (e2, i2) in deps:
                    signaled[e2].add(i2)
        val = {}
        for e in ENGS:
            c = self.cnt[e]
            j = self.dcnt[e]
            for i, (fn, deps, dma) in enumerate(self.ops[e]):
                if dma:
                    val[(e, i)] = (dsems[e][j % NS_DMA], 16 * (j // NS_DMA + 1))
                    j += 1
                elif i in signaled[e]:
                    c += 1
                    val[(e, i)] = (sems[e], c)
            self.cnt[e] = c
            self.dcnt[e] = j

        def run(e, engobj):
            waited = self.waited[e]
            dlast = self.dlast[e]
            for i, (fn, deps, dma) in enumerate(self.ops[e]):
                need = {}

                def want(s, v):
                    key = id(s)
                    if waited.get(key, 0) >= v:
                        return
                    if key not in need or need[key][1] < v:
                        need[key] = (s, v)
                for d in deps:
                    want(*val[d])
                if dma and len(dlast) >= NS_DMA:
                    want(*dlast[-NS_DMA])
                for key, (s, v) in need.items():
                    engobj.wait_ge(s, v)
                    waited[key] = v
                    if DUMP:
                        print("W", e, i, getattr(s, "name", s), v)
                ins = fn(engobj)
                if DUMP:
                    print("I", e, i, "dma" if dma else "", "SIG" if (dma or i in signaled[e]) else "", getattr(val.get((e, i), ("", ""))[0], "name", "-"), val.get((e, i), ("", ""))[1], str(ins)[:150].replace("\n", " "))
                if dma:
                    ins.then_inc(val[(e, i)][0], 16)
                    dlast.append(val[(e, i)])
                    del dlast[:-NS_DMA]
                elif i in signaled[e]:
                    ins.then_inc(sems[e], 1)
            for (s, v) in dlast:
                if waited.get(id(s), 0) < v:
                    engobj.wait_ge(s, v)
                    waited[id(s)] = v

        if DUMP:
            print("F flush")
        with nc.Block() as block:
            @block.sync
            def _(eng):
                run("sync", eng)

            @block.gpsimd
            def _(eng):
                run("pool", eng)

            @block.scalar
            def _(eng):
                run("act", eng)

            @block.vector
            def _(eng):
                run("dve", eng)

            @block.tensor
            def _(eng):
                run("pe", eng)
        nc.all_engine_barrier()
        self.ops = {e: [] for e in ENGS}
        self.lastw = {}
        self.readers = {}


class Ctx:
    def __init__(self, nc, es, S):
        self.nc = nc
        self.es = es
        self.S = S
        self.NT = S // 128
        self.sch = Sched()
        self.sch.setup(nc, es)
        self.ps = [es.enter_context(nc.psum_tensor("ps%d" % i, [128, 512], F32)) for i in range(8)]
        self.ps_i = 0
        self.uid = 0

    def sb(self, name, shape, dt=F32, es=None):
        self.uid += 1
        return (es or self.es).enter_context(self.nc.sbuf_tensor("%s_%d" % (name, self.uid), shape, dt))

    def flush(self):
        self.sch.flush()

    def psum(self):
        i = self.ps_i
        self.ps_i = (i + 1) % 8
        return self.ps[i], [("ps", i, q) for q in range(4)]

    @staticmethod
    def q(keys, lo, hi):
        return list(keys)

    def wslot(self, name):
        d = self.__dict__.setdefault("_ws", {})
        d[name] = d.get(name, -1) + 1
        return d[name] % 2

    def emit(self, eng, fn, reads=(), writes=()):
        self.sch.emit(eng, fn, reads, writes)

    def dma(self, eng, out, in_, reads=(), writes=()):
        self.sch.emit(eng, lambda e: e.dma_start(out=out, in_=in_), reads, writes, dma=True)


def make_consts(c):
    ident = c.sb("ident", [128, 128])
    c.emit("pool", lambda e: e.memset(ident[:], 0.0), writes=["const"])
    c.emit("pool", lambda e: e.affine_select(out=ident[:], in_=ident[:], compare_op=ALU.not_equal, fill=1.0,
                                             base=0, pattern=[[-1, 128]], channel_multiplier=1),
           writes=["const"])
    c.ident = ident


def transpose_tile(c, src, skey, dstT, dkeyf, tcol, nchunks=KC, evac=("act", "dve")):
    ident = c.ident
    for g in range(0, nchunks, 4):
        n = min(4, nchunks - g)
        ps, pk = c.psum()
        for j in range(n):
            kc = g + j
            c.emit("pe", lambda e, ps=ps, j=j, kc=kc: e.transpose(ps[:, j * 128:(j + 1) * 128],
                                                                     src[:, kc * 128:(kc + 1) * 128], ident[:]),
                   reads=[skey, "const"], writes=pk)
        eng = evac[(g // 4) % len(evac)]
        dst = dstT[:, g:g + n, tcol:tcol + 128]
        srcp = ps[:, 0:n * 128].rearrange("p (a b) -> p a b", a=n)
        if eng == "act":
            c.emit("act", lambda e, dst=dst, srcp=srcp: e.copy(out=dst, in_=srcp), reads=pk, writes=[dkeyf(g // 4)])
        else:
            c.emit("dve", lambda e, dst=dst, srcp=srcp: e.tensor_copy(out=dst, in_=srcp), reads=pk,
                   writes=[dkeyf(g // 4)])


def norm_to_hT(c, xt, xkey, gam, gkey, hT, hkeyf, tcol, B):
    junk, stat, hbuf = B["junk"], B["stat"], B["hbuf"]
    c.emit("act", lambda e: e.activation(out=junk[:], in_=xt, func=AF.Square, accum_out=stat[:, 0:1]),
           reads=[xkey], writes=["junk", "stat"])
    c.emit("dve", lambda e: e.tensor_scalar(out=stat[:, 1:2], in0=stat[:, 0:1], scalar1=1.0 / D, scalar2=RMS_EPS,
                                            op0=ALU.mult, op1=ALU.add), reads=["stat"], writes=["stat"])
    c.emit("act", lambda e: e.activation(out=stat[:, 3:4], in_=stat[:, 1:2], func=AF.Sqrt), reads=["stat"], writes=["stat"])
    c.emit("dve", lambda e: e.reciprocal(out=stat[:, 2:3], in_=stat[:, 3:4]), reads=["stat"], writes=["stat"])
    c.emit("dve", lambda e: e.scalar_tensor_tensor(out=hbuf[:], in0=xt, scalar=stat[:, 2:3], in1=gam,
                                                   op0=ALU.mult, op1=ALU.mult),
           reads=[xkey, "stat", gkey], writes=["hbuf"])
    transpose_tile(c, hbuf, "hbuf", hT, hkeyf, tcol)


def ffn_phase(c, x_d, gam_row, wg, wu, wd, FF, B):
    TB = 256
    NTT = TB // 128
    HC = FF // 128
    xt, gam, hT, act, wgs, wus, wds, sg = (B[k] for k in ("xt", "gam", "hT", "act", "wgs", "wus", "wds", "sg"))
    c.dma("sync", gam[:], gam_row.partition_broadcast(128), writes=["gam"])
    wg_v = wg.rearrange("(kc p) f -> p kc f", p=128)
    wu_v = wu.rearrange("(kc p) f -> p kc f", p=128)
    wd_v = wd.rearrange("(hc p) d -> p hc d", p=128)
    GH = 4
    assert HC % GH == 0
    hkeyf = lambda g: ("hT", g)
    for tb in range(c.S // TB):
        for tt in range(NTT):
            r0 = tb * TB + tt * 128
            c.dma("sync", xt[tt][:], x_d[r0:r0 + 128, :], reads=[("xd", r0 // 128)], writes=[("xt", tt)])
            norm_to_hT(c, xt[tt][:], ("xt", tt), gam[:], "gam", hT, hkeyf, tt * 128, B)
        for hc in range(HC):
            s = hc % 2
            c.dma("sync", wgs[s][:], wg_v[:, :, hc * 128:(hc + 1) * 128], writes=[("wg", s)])
            c.dma("pool", wus[s][:], wu_v[:, :, hc * 128:(hc + 1) * 128], writes=[("wu", s)])
            psg, pgk = c.psum()
            psu, puk = c.psum()
            for kc in range(KC):
                c.emit("pe", lambda e, psg=psg, s=s, kc=kc: e.matmul(psg[:, 0:TB], lhsT=wgs[s][:, kc, :], rhs=hT[:, kc, 0:TB],
                                                                      start=(kc == 0), stop=(kc == KC - 1)),
                       reads=[("wg", s), ("hT", kc // 4)], writes=pgk)
            for kc in range(KC):
                c.emit("pe", lambda e, psu=psu, s=s, kc=kc: e.matmul(psu[:, 0:TB], lhsT=wus[s][:, kc, :], rhs=hT[:, kc, 0:TB],
                                                                      start=(kc == 0), stop=(kc == KC - 1)),
                       reads=[("wu", s), ("hT", kc // 4)], writes=puk)
            c.emit("act", lambda e, psg=psg, s=s: e.activation(out=sg[s][:], in_=psg[:, 0:TB], func=AF.Silu),
                   reads=pgk, writes=[("sg", s)])
            c.emit("dve", lambda e, psu=psu, s=s, hc=hc: e.tensor_tensor(out=act[:, hc, :], in0=sg[s][:], in1=psu[:, 0:TB],
                                                                          op=ALU.mult),
                   reads=puk + [("sg", s)], writes=[("act", hc)])
        for db in range(D // 512):
            pds = [c.psum() for _ in range(NTT)]
            for hg in range(HC // GH):
                s = (db * (HC // GH) + hg) % 2
                q = "sync" if hg % 2 == 0 else "pool"
                c.dma(q, wds[s][:], wd_v[:, hg * GH:(hg + 1) * GH, db * 512:(db + 1) * 512], writes=[("wd", s)])
                for tt in range(NTT):
                    pd, pdk = pds[tt]
                    for j in range(GH):
                        hc = hg * GH + j
                        c.emit("pe", lambda e, pd=pd, s=s, j=j, hc=hc, tt=tt: e.matmul(
                            pd[:, :], lhsT=act[:, hc, tt * 128:(tt + 1) * 128], rhs=wds[s][:, j, :],
                            start=(hc == 0), stop=(hc == HC - 1)),
                            reads=[("act", hc), ("wd", s)], writes=pdk)
            for tt in range(NTT):
                pd, pdk = pds[tt]
                c.emit("dve", lambda e, pd=pd, tt=tt, db=db: e.scalar_tensor_tensor(
                    out=xt[tt][:, db * 512:(db + 1) * 512], in0=pd[:, :], scalar=0.5,
                    in1=xt[tt][:, db * 512:(db + 1) * 512], op0=ALU.mult, op1=ALU.add),
                    reads=pdk + [("xt", tt)], writes=[("xt", tt)])
        for tt in range(NTT):
            r0 = tb * TB + tt * 128
            c.dma("sync", x_d[r0:r0 + 128, :], xt[tt][:], reads=[("xt", tt)], writes=[("xd", r0 // 128)])


def alloc_common(c, es):
    B = {}
    B["xt"] = [c.sb("xt%d" % i, [128, D], es=es) for i in range(2)]
    B["gam"] = c.sb("gam", [128, D], es=es)
    B["junk"] = c.sb("junk", [128, D], es=es)
    B["hbuf"] = c.sb("hbuf", [128, D], es=es)
    B["stat"] = c.sb("stat", [128, 8], es=es)
    B["hT"] = c.sb("hT", [128, KC, 256], es=es)
    return B


def run_ffn(c, x_d, gam_row, wg, wu, wd, FF):
    with ExitStack() as pes:
        B = alloc_common(c, pes)
        B["act"] = c.sb("act", [128, FF // 128, 256], es=pes)
        B["wgs"] = [c.sb("wgs%d" % i, [128, KC, 128], es=pes) for i in range(2)]
        B["wus"] = [c.sb("wus%d" % i, [128, KC, 128], es=pes) for i in range(2)]
        B["wds"] = [c.sb("wds%d" % i, [128, 4, 512], es=pes) for i in range(2)]
        B["sg"] = [c.sb("sg%d" % i, [128, 256], es=pes) for i in range(2)]
        ffn_phase(c, x_d, gam_row, wg, wu, wd, FF, B)
        c.flush()


def copy_rows(c, dst, src, S):
    for t in range(S // 128):
        c.dma("sync", dst[t * 128:(t + 1) * 128, :], src[t * 128:(t + 1) * 128, :])
    c.flush()


def test_ffn_program(S, FF):
    nc = bass.Bass("TRN2", target_bir_lowering=False)
    x_in = nc.dram_tensor("x", [S, D], F32, kind="ExternalInput").ap()
    g_in = nc.dram_tensor("g", [1, D], F32, kind="ExternalInput").ap()
    wg = nc.dram_tensor("wg", [D, FF], F32, kind="ExternalInput").ap()
    wu = nc.dram_tensor("wu", [D, FF], F32, kind="ExternalInput").ap()
    wd = nc.dram_tensor("wd", [FF, D], F32, kind="ExternalInput").ap()
    y = nc.dram_tensor("y", [S, D], F32, kind="ExternalOutput").ap()
    xs = nc.dram_tensor("xs", [S, D], F32).ap()
    with ExitStack() as es:
        c = Ctx(nc, es, S)
        make_consts(c)
        copy_rows(c, xs, x_in, S)
        run_ffn(c, xs, g_in, wg, wu, wd, FF)
        run_ffn(c, xs, g_in, wg, wu, wd, FF)
        copy_rows(c, y, xs, S)
    return nc


def proj(c, xT, xkeyf, ntt, W, N, wbufs, wname, epi, kcn=KC, CW=256, qi=[0]):
    Wv = W.rearrange("(kc p) n -> p kc n", p=128)
    for cb in range(N // CW):
        s = c.wslot(wname)
        qi[0] += 1
        c.dma("sync" if qi[0] % 2 else "pool", wbufs[s][:, 0:kcn, 0:CW], Wv[:, :, cb * CW:(cb + 1) * CW],
              writes=[(wname, s)])
        ps, pk = c.psum()
        for tt in range(ntt):
            for kc in range(kcn):
                c.emit("pe", lambda e, ps=ps, tt=tt, kc=kc, s=s: e.matmul(
                    ps[:, tt * CW:(tt + 1) * CW], lhsT=xT[:, kc, tt * 128:(tt + 1) * 128], rhs=wbufs[s][:, kc, 0:CW],
                    start=(kc == 0), stop=(kc == kcn - 1)),
                    reads=[(wname, s)] + xkeyf(kc), writes=c.q(pk, tt * CW, (tt + 1) * CW))
        epi(cb, ps, pk)


def store_epi(c, out_d, r0, ntt, ost, func, CW=256):
    def epi(cb, ps, pk):
        s = c.wslot("ost")
        c.emit("act", lambda e: e.activation(out=ost[s][:, 0:ntt * CW], in_=ps[:, 0:ntt * CW], func=func),
               reads=c.q(pk, 0, ntt * CW), writes=[("ost", s)])
        dst = out_d[r0:r0 + ntt * 128, cb * CW:(cb + 1) * CW].rearrange("(tt p) n -> p tt n", p=128)
        c.dma("sync", dst, ost[s][:, 0:ntt * CW].rearrange("p (tt n) -> p tt n", tt=ntt),
              reads=[("ost", s)], writes=[("od", id(out_d), r0, cb)])
    return epi


def load_rows_T(c, rows_ap, R, dst, pes):
    tmp = c.sb("rowsT", [128, D], es=pes)
    c.dma("sync", tmp[0:R, :], rows_ap, writes=["rowsT"])
    for g in range(0, KC, 4):
        ps, pk = c.psum()
        for j in range(4):
            kc = g + j
            c.emit("pe", lambda e, ps=ps, j=j, kc=kc: e.transpose(ps[:, j * R:(j + 1) * R],
                                                                     tmp[0:R, kc * 128:(kc + 1) * 128], c.ident[0:R, 0:R]),
                   reads=["rowsT", "const"], writes=pk)
        c.emit("dve", lambda e, ps=ps, g=g: e.tensor_copy(out=dst[:, g:g + 4, 0:R],
                                                         in_=ps[:, 0:4 * R].rearrange("p (a b) -> p a b", a=4)),
               reads=pk, writes=["rowsTd"])


C0 = -float(np.exp(-0.5))


def rwkv_proj_phase(c, x_d, P, O, has_vmix):
    TB, NTT = 256, 2
    with ExitStack() as pes:
        xt = c.sb("xt", [128, D], es=pes)
        gam = c.sb("gam", [128, D], es=pes)
        stat = c.sb("stat", [128, 8], es=pes)
        hTw = c.sb("hTw", [128, KC, 257], es=pes)
        dxT = c.sb("dxT", [128, KC, 256], es=pes)
        xcT = [c.sb("xcT", [128, KC, 256], es=pes) for _ in range(2)]
        wb = [c.sb("wb", [128, KC, 256], es=pes) for _ in range(2)]
        muT = c.sb("muT", [128, KC, 6], es=pes)
        w2a = c.sb("w2a", [128, D], es=pes)
        a2a = c.sb("a2a", [128, D], es=pes)
        v2a = c.sb("v2a", [128, D], es=pes)
        g2s = c.sb("g2s", [128, 2, D], es=pes)
        t1 = {k: c.sb("t1" + k, [128, 256], es=pes) for k in "wav"}
        t1g = c.sb("t1g", [128, 2, 256], es=pes)
        ost = [c.sb("ost", [128, 512], es=pes) for _ in range(2)]
        B = dict(junk=dxT[:, 0:8, :].rearrange("p a b -> p (a b)"), stat=stat,
                 hbuf=xcT[1][:, 0:8, :].rearrange("p a b -> p (a b)"))
        c.dma("sync", gam[:], P["g"].partition_broadcast(128), writes=["gam"])
        load_rows_T(c, P["mu"], 6, muT, pes)
        c.dma("sync", w2a[0:96, :], P["w2"], writes=["w2a"])
        c.dma("sync", w2a[96:97, :], P["w0"], writes=["w2a"])
        c.dma("sync", a2a[0:96, :], P["a2"], writes=["a2a"])
        c.dma("sync", a2a[96:97, :], P["a0"], writes=["a2a"])
        if has_vmix:
            c.dma("sync", v2a[0:64, :], P["v2"], writes=["v2a"])
            c.dma("sync", v2a[64:65, :], P["v0"], writes=["v2a"])
        c.dma("sync", g2s[:], P["g2"].rearrange("(ch p) n -> p ch n", p=128), writes=["g2s"])
        for k in "wav":
            c.emit("dve", lambda e, k=k: e.memset(t1[k][:], 1.0), writes=[("t1", k)])
        c.emit("dve", lambda e: e.memset(hTw[:, :, 0:1], 0.0), writes=[("hT", g) for g in range(4)])

        hkeys = [("hT", g) for g in range(4)]
        for tb in range(c.S // TB):
            r0 = tb * TB
            for tt in range(NTT):
                c.dma("sync", xt[:], x_d[r0 + tt * 128:r0 + (tt + 1) * 128, :], writes=["xt"])
                junk = B["junk"]
                c.emit("act", lambda e: e.activation(out=junk, in_=xt[:], func=AF.Square, accum_out=stat[:, 0:1]),
                       reads=["xt"], writes=["dxT", "stat"])
                c.emit("dve", lambda e: e.tensor_scalar(out=stat[:, 1:2], in0=stat[:, 0:1], scalar1=1.0 / D,
                                                        scalar2=RMS_EPS, op0=ALU.mult, op1=ALU.add),
                       reads=["stat"], writes=["stat"])
                c.emit("act", lambda e: e.activation(out=stat[:, 3:4], in_=stat[:, 1:2], func=AF.Sqrt),
                       reads=["stat"], writes=["stat"])
                c.emit("dve", lambda e: e.reciprocal(out=stat[:, 2:3], in_=stat[:, 3:4]), reads=["stat"], writes=["stat"])
                hb = B["hbuf"]
                c.emit("dve", lambda e: e.scalar_tensor_tensor(out=hb, in0=xt[:], scalar=stat[:, 2:3], in1=gam[:],
                                                               op0=ALU.mult, op1=ALU.mult),
                       reads=["xt", "stat", "gam"], writes=[("xc", 1, g) for g in range(4)])
                ident = c.ident
                for g in range(0, KC, 4):
                    ps, pk = c.psum()
                    for j in range(4):
                        kc = g + j
                        c.emit("pe", lambda e, ps=ps, j=j, kc=kc: e.transpose(
                            ps[:, j * 128:(j + 1) * 128], hb[:, kc * 128:(kc + 1) * 128], ident[:]),
                            reads=[("xc", 1, q) for q in range(4)] + ["const"], writes=pk)
                    dst = hTw[:, g:g + 4, 1 + tt * 128:1 + (tt + 1) * 128]
                    srcp = ps[:, :].rearrange("p (a b) -> p a b", a=4)
                    if (g // 4) % 2 == 0:
                        c.emit("act", lambda e, dst=dst, srcp=srcp: e.copy(out=dst, in_=srcp), reads=pk,
                               writes=[("hT", g // 4)])
                    else:
                        c.emit("dve", lambda e, dst=dst, srcp=srcp: e.tensor_copy(out=dst, in_=srcp), reads=pk,
                               writes=[("hT", g // 4)])
            c.emit("dve", lambda e: e.tensor_tensor(out=dxT[:], in0=hTw[:, :, 0:256], in1=hTw[:, :, 1:257],
                                                    op=ALU.subtract), reads=hkeys, writes=["dxT"])

            def mix(m):
                s = c.wslot("xc")
                for kc in range(KC):
                    c.emit("dve", lambda e, s=s, kc=kc, m=m: e.scalar_tensor_tensor(
                        out=xcT[s][:, kc, :], in0=dxT[:, kc, :], scalar=muT[:, kc, m:m + 1], in1=hTw[:, kc, 1:257],
                        op0=ALU.mult, op1=ALU.add),
                        reads=["dxT", "rowsTd", ("hT", kc // 4)], writes=[("xc", s, kc // 4)])
                return xcT[s], (lambda kc, s=s: [("xc", s, kc // 4)])

            def lora(xT, xkf, w1, R, func, t1tile, t1key, nch=1):
                s = c.wslot("wb")
                c.dma("pool", wb[s][:, :, 0:R * nch], w1.rearrange("(kc p) r -> p kc r", p=128), writes=[("wb", s)])
                for ch in range(nch):
                    ps, pk = c.psum()
                    for kc in range(KC):
                        c.emit("pe", lambda e, ps=ps, kc=kc, s=s, ch=ch: e.matmul(
                            ps[0:R, 0:256], lhsT=wb[s][:, kc, ch * R:(ch + 1) * R], rhs=xT[:, kc, 0:256],
                            start=(kc == 0), stop=(kc == KC - 1)),
                            reads=[("wb", s)] + xkf(kc), writes=c.q(pk, 0, 256))
                    dst = t1tile[0:R, :] if nch == 1 else t1tile[0:R, ch, :]
                    c.emit("act", lambda e, ps=ps, dst=dst: e.activation(out=dst, in_=ps[0:R, 0:256], func=func),
                           reads=c.q(pk, 0, 256), writes=[t1key])

            def lora2(t1tile, t1key, K, w2tile, w2key, out_d, func, nch=1):
                epi = store_epi(c, out_d, r0, NTT, ost, func)
                for cb in range(D // 256):
                    ps, pk = c.psum()
                    for tt in range(NTT):
                        for ch in range(nch):
                            lhsT = t1tile[0:K, tt * 128:(tt + 1) * 128] if nch == 1 else t1tile[0:K, ch, tt * 128:(tt + 1) * 128]
                            rhs = w2tile[0:K, cb * 256:(cb + 1) * 256] if nch == 1 else w2tile[0:K, ch, cb * 256:(cb + 1) * 256]
                            c.emit("pe", lambda e, ps=ps, tt=tt, lhsT=lhsT, rhs=rhs, ch=ch: e.matmul(
                                ps[:, tt * 256:(tt + 1) * 256], lhsT=lhsT, rhs=rhs, start=(ch == 0), stop=(ch == nch - 1)),
                                reads=[t1key, w2key], writes=c.q(pk, tt * 256, (tt + 1) * 256))
                    epi(cb, ps, pk)

            xT, xkf = mix(0)
            proj(c, xT, xkf, NTT, P["wr"], D, wb, "wb", store_epi(c, O["r"], r0, NTT, ost, AF.Copy))
            xT, xkf = mix(2)
            proj(c, xT, xkf, NTT, P["wk"], D, wb, "wb", store_epi(c, O["k"], r0, NTT, ost, AF.Copy))
            xT, xkf = mix(3)
            proj(c, xT, xkf, NTT, P["wv"], D, wb, "wb", store_epi(c, O["v"], r0, NTT, ost, AF.Copy))
            if has_vmix:
                lora(xT, xkf, P["v1"], 64, AF.Copy, t1["v"], ("t1", "v"))
                lora2(t1["v"], ("t1", "v"), 65, v2a, "v2a", O["sv"], AF.Sigmoid)
            xT, xkf = mix(1)
            lora(xT, xkf, P["w1"], 96, AF.Tanh, t1["w"], ("t1", "w"))
            lora2(t1["w"], ("t1", "w"), 97, w2a, "w2a", O["sw"], AF.Sigmoid)
            xT, xkf = mix(4)
            lora(xT, xkf, P["a1"], 96, AF.Copy, t1["a"], ("t1", "a"))
            lora2(t1["a"], ("t1", "a"), 97, a2a, "a2a", O["a"], AF.Sigmoid)
            xT, xkf = mix(5)
            lora(xT, xkf, P["g1"], 128, AF.Sigmoid, t1g, ("t1", "g"), nch=2)
            lora2(t1g, ("t1", "g"), 128, g2s, "g2s", O["g"], AF.Copy, nch=2)
            c.emit("dve", lambda e: e.tensor_copy(out=hTw[:, :, 0:1], in_=hTw[:, :, 256:257]), reads=hkeys, writes=hkeys)
        c.flush()


def rwkv_param_aps(nc_inputs, j, has_vmix):
    I = nc_inputs
    P = dict(
        mu=I["rwkv_mu"][j], wr=I["rwkv_w_rkv"][j, 0], wk=I["rwkv_w_rkv"][j, 1], wv=I["rwkv_w_rkv"][j, 2],
        wo=I["rwkv_w_o"][j], w0=I["rwkv_w0"][j:j + 1, :], w1=I["rwkv_w1"][j], w2=I["rwkv_w2"][j],
        a0=I["rwkv_a0"][j:j + 1, :], a1=I["rwkv_a1"][j], a2=I["rwkv_a2"][j], g1=I["rwkv_g1"][j], g2=I["rwkv_g2"][j],
        k_k=I["rwkv_k_k"][j:j + 1, :], k_a=I["rwkv_k_a"][j:j + 1, :],
        r_k=I["rwkv_r_k"][j:j + 1].rearrange("o h n -> o (h n)"),
        gn_g=I["rwkv_gn_g"][j:j + 1, :], gn_b=I["rwkv_gn_b"][j:j + 1, :])
    if has_vmix:
        P.update(v0=I["rwkv_v0"][j - 1:j, :], v1=I["rwkv_v1"][j - 1], v2=I["rwkv_v2"][j - 1])
    return P


RWKV_SHAPES = dict(
    rwkv_mu=(2, 6, D), rwkv_w_rkv=(2, 3, D, D), rwkv_w_o=(2, D, D), rwkv_w0=(2, D), rwkv_w1=(2, D, 96),
    rwkv_w2=(2, 96, D), rwkv_a0=(2, D), rwkv_a1=(2, D, 96), rwkv_a2=(2, 96, D), rwkv_v0=(1, D), rwkv_v1=(1, D, 64),
    rwkv_v2=(1, 64, D), rwkv_g1=(2, D, 256), rwkv_g2=(2, 256, D), rwkv_k_k=(2, D), rwkv_k_a=(2, D),
    rwkv_r_k=(2, 32, 64), rwkv_gn_g=(2, D), rwkv_gn_b=(2, D))


def test_rwkv_program(S, stage):
    nc = bass.Bass("TRN2", target_bir_lowering=False)
    I = {k: nc.dram_tensor(k, list(v), F32, kind="ExternalInput").ap() for k, v in RWKV_SHAPES.items()}
    x_in = nc.dram_tensor("x", [S, D], F32, kind="ExternalInput").ap()
    g_in = nc.dram_tensor("g", [1, D], F32, kind="ExternalInput").ap()
    vf_in = nc.dram_tensor("vf", [S, D], F32, kind="ExternalInput").ap()
    names = ["r", "k", "v", "sw", "a", "g", "sv", "At", "Rt", "Bh", "Kh", "Bt", "Kt", "V", "bonus", "y", "xo"]
    O = {k: nc.dram_tensor("o_" + k, [S, D], F32, kind="ExternalOutput").ap() for k in names}
    O["el"] = nc.dram_tensor("o_el", [S // 128, D], F32, kind="ExternalOutput").ap()
    O["vf"] = vf_in
    with ExitStack() as es:
        c = Ctx(nc, es, S)
        make_consts(c)
        make_masks(c)
        P = rwkv_param_aps(I, 1, True)
        P["g"] = g_in
        copy_rows(c, O["xo"], x_in, S)
        rwkv_proj_phase(c, O["xo"], P, O, True)
        if stage >= 2:
            rwkv_prep_phase(c, P, O, True)
        if stage >= 3:
            rwkv_scan_phase(c, O)
        if stage >= 4:
            rwkv_post_phase(c, O["xo"], P, O)
    return nc


def make_masks(c):
    tri = c.sb("tri", [128, 128])
    ones = c.sb("ones", [128, 128])
    mask4 = c.sb("mask4", [128, 512])
    msl = c.sb("msl", [128, 128])
    c.emit("pool", lambda e: e.memset(ones[:], 1.0), writes=["const"])
    c.emit("pool", lambda e: e.memset(tri[:], 1.0), writes=["const"])
    c.emit("pool", lambda e: e.affine_select(out=tri[:], in_=tri[:], compare_op=ALU.is_ge, fill=0.0, base=0,
                                             pattern=[[1, 128]], channel_multiplier=-1), writes=["const"])
    c.emit("pool", lambda e: e.memset(mask4[:], 1.0), writes=["const"])
    for q in range(4):
        op = ALU.is_gt if q % 2 == 0 else ALU.is_ge
        c.emit("pool", lambda e, q=q, op=op: e.affine_select(
            out=mask4[:, q * 128:(q + 1) * 128], in_=mask4[:, q * 128:(q + 1) * 128], compare_op=op, fill=0.0, base=0,
            pattern=[[1, 128]], channel_multiplier=-1), writes=["const"])
    c.emit("pool", lambda e: e.memset(msl[:], 1.0), writes=["const"])
    c.emit("pool", lambda e: e.affine_select(out=msl[:], in_=msl[:], compare_op=ALU.is_gt, fill=0.0, base=0,
                                             pattern=[[-1, 128]], channel_multiplier=1), writes=["const"])
    c.tri, c.ones, c.mask4, c.msl = tri, ones, mask4, msl


def rwkv_prep_phase(c, P, O, has_vmix):
    CB = 512
    with ExitStack() as pes:
        names_in = ["r", "k", "v", "sw", "a"] + (["sv", "vf"] if has_vmix else [])
        ld = {n: [c.sb("ld" + n, [128, CB], es=pes) for _ in range(2)] for n in names_in}
        names_out = ["At", "Rt", "Bh", "Kh", "Bt", "Kt", "V", "bonus"]
        ob = {n: [c.sb("ob" + n, [128, CB], es=pes) for _ in range(2)] for n in names_out}
        tm = {n: c.sb("tm" + n, [128, CB], es=pes) for n in ["lw", "epos", "eneg", "eprev", "EL", "kk", "sq", "b", "t", "kh", "t2", "d"]}
        st = c.sb("st", [128, 64], es=pes)
        kkp = c.sb("kkp", [128, D], es=pes)
        kap = c.sb("kap", [128, D], es=pes)
        rkp = c.sb("rkp", [128, D], es=pes)
        c.dma("sync", kkp[:], P["k_k"].partition_broadcast(128), writes=["par"])
        c.dma("sync", kap[:], P["k_a"].partition_broadcast(128), writes=["par"])
        c.dma("sync", rkp[:], P["r_k"].partition_broadcast(128), writes=["par"])
        it = 0
        for ch in range(c.S // 128):
            rs = slice(ch * 128, (ch + 1) * 128)
            for cb in range(D // CB):
                cs = slice(cb * CB, (cb + 1) * CB)
                s = it % 2
                it += 1
                L = {}
                for i, n in enumerate(names_in):
                    c.dma("sync" if i % 2 == 0 else "pool", ld[n][s][:], O[n][rs, cs], writes=[("ld", n, s)])
                    L[n] = ld[n][s]
                lk = lambda *ns: [("ld", n, s) for n in ns]
                T = tm
                ok = lambda *ns: [("ob", n, s) for n in ns]
                OB = {n: ob[n][s] for n in names_out}
                c.emit("dve", lambda e, L=L: e.tensor_scalar(out=T["lw"][:], in0=L["sw"][:], scalar1=C0, scalar2=0.0,
                                                             op0=ALU.mult, op1=ALU.add), reads=lk("sw"), writes=["lw"])
                psc, pck = c.psum()
                pst, ptk = c.psum()
                c.emit("pe", lambda e, psc=psc: e.matmul(psc[:, :], lhsT=c.tri[:], rhs=T["lw"][:], start=True, stop=True),
                       reads=["lw", "const"], writes=pck)
                c.emit("pe", lambda e, pst=pst: e.matmul(pst[:, :], lhsT=c.ones[:], rhs=T["lw"][:], start=True, stop=True),
                       reads=["lw", "const"], writes=ptk)
                c.emit("act", lambda e, psc=psc: e.activation(out=T["epos"][:], in_=psc[:, :], func=AF.Exp), reads=pck, writes=["epos"])
                c.emit("act", lambda e, psc=psc: e.activation(out=T["eneg"][:], in_=psc[:, :], func=AF.Exp, scale=-1.0),
                       reads=pck, writes=["eneg"])
                c.emit("dve", lambda e, psc=psc: e.tensor_tensor(out=T["eprev"][:], in0=psc[:, :], in1=T["lw"][:], op=ALU.subtract),
                       reads=pck + ["lw"], writes=["eprev"])
                c.emit("act", lambda e: e.activation(out=T["eprev"][:], in_=T["eprev"][:], func=AF.Exp), reads=["eprev"], writes=["eprev"])
                c.emit("act", lambda e, pst=pst: e.activation(out=T["EL"][:], in_=pst[:, :], func=AF.Exp), reads=ptk, writes=["EL"])
                c.emit("pool", lambda e, L=L, cs=cs: e.tensor_tensor(out=T["kk"][:], in0=L["k"][:], in1=kkp[:, cs], op=ALU.mult),
                       reads=lk("k") + ["par"], writes=["kk"])
                c.emit("pool", lambda e: e.tensor_tensor(out=T["sq"][:], in0=T["kk"][:], in1=T["kk"][:], op=ALU.mult),
                       reads=["kk"], writes=["sq"])
                c.emit("dve", lambda e: e.tensor_reduce(out=st[:, 0:8], in_=T["sq"][:].rearrange("p (h j) -> p h j", j=64),
                                                        axis=AX.X, op=ALU.add), reads=["sq"], writes=["st"])
                c.emit("act", lambda e: e.activation(out=st[:, 8:16], in_=st[:, 0:8], func=AF.Sqrt), reads=["st"], writes=["st"])
                c.emit("dve", lambda e: e.tensor_scalar(out=st[:, 8:16], in0=st[:, 8:16], scalar1=1e-12, scalar2=0.0,
                                                        op0=ALU.max, op1=ALU.add), reads=["st"], writes=["st"])
                c.emit("dve", lambda e: e.reciprocal(out=st[:, 16:24], in_=st[:, 8:16]), reads=["st"], writes=["st"])
                c.emit("dve", lambda e: e.tensor_tensor(
                    out=T["kk"][:].rearrange("p (h j) -> p h j", j=64), in0=T["kk"][:].rearrange("p (h j) -> p h j", j=64),
                    in1=st[:, 16:24].unsqueeze(2).to_broadcast([128, 8, 64]), op=ALU.mult), reads=["kk", "st"], writes=["kk"])
                c.emit("dve", lambda e, OB=OB: e.scalar_tensor_tensor(out=OB["At"][:], in0=T["kk"][:], scalar=-1.0, in1=T["eprev"][:],
                                                                     op0=ALU.mult, op1=ALU.mult),
                       reads=["kk", "eprev"], writes=ok("At"))
                c.emit("pool", lambda e, L=L: e.tensor_tensor(out=T["b"][:], in0=T["kk"][:], in1=L["a"][:], op=ALU.mult),
                       reads=["kk"] + lk("a"), writes=["b"])
                c.emit("pool", lambda e, OB=OB: e.tensor_tensor(out=OB["Bh"][:], in0=T["b"][:], in1=T["eneg"][:], op=ALU.mult),
                       reads=["b", "eneg"], writes=ok("Bh"))
                c.emit("pool", lambda e, OB=OB: e.tensor_tensor(out=OB["Bt"][:], in0=OB["Bh"][:], in1=T["EL"][:], op=ALU.mult),
                       reads=ok("Bh") + ["EL"], writes=ok("Bt"))
                c.emit("dve", lambda e, L=L, cs=cs: e.scalar_tensor_tensor(out=T["t"][:], in0=L["a"][:], scalar=-1.0, in1=kap[:, cs],
                                                                          op0=ALU.add, op1=ALU.mult),
                       reads=lk("a") + ["par"], writes=["t"])
                c.emit("dve", lambda e, L=L: e.scalar_tensor_tensor(out=T["kh"][:], in0=T["t"][:], scalar=1.0, in1=L["k"][:],
                                                                   op0=ALU.add, op1=ALU.mult),
                       reads=["t"] + lk("k"), writes=["kh"])
                c.emit("pool", lambda e, OB=OB: e.tensor_tensor(out=OB["Kh"][:], in0=T["kh"][:], in1=T["eneg"][:], op=ALU.mult),
                       reads=["kh", "eneg"], writes=ok("Kh"))
                c.emit("pool", lambda e, OB=OB: e.tensor_tensor(out=OB["Kt"][:], in0=OB["Kh"][:], in1=T["EL"][:], op=ALU.mult),
                       reads=ok("Kh") + ["EL"], writes=ok("Kt"))
                c.emit("pool", lambda e, OB=OB, L=L: e.tensor_tensor(out=OB["Rt"][:], in0=L["r"][:], in1=T["epos"][:], op=ALU.mult),
                       reads=lk("r") + ["epos"], writes=ok("Rt"))
                if has_vmix:
                    c.emit("pool", lambda e, L=L: e.tensor_tensor(out=T["d"][:], in0=L["vf"][:], in1=L["v"][:], op=ALU.subtract),
                           reads=lk("vf", "v"), writes=["d"])
                    c.emit("pool", lambda e, L=L: e.tensor_tensor(out=T["d"][:], in0=T["d"][:], in1=L["sv"][:], op=ALU.mult),
                           reads=lk("sv") + ["d"], writes=["d"])
                    c.emit("pool", lambda e, L=L, OB=OB: e.tensor_tensor(out=OB["V"][:], in0=T["d"][:], in1=L["v"][:], op=ALU.add),
                           reads=lk("v") + ["d"], writes=ok("V"))
                else:
                    c.emit("pool", lambda e, L=L, OB=OB: e.tensor_copy(out=OB["V"][:], in_=L["v"][:]), reads=lk("v"), writes=ok("V"))
                c.emit("pool", lambda e, L=L: e.tensor_tensor(out=T["t2"][:], in0=L["r"][:], in1=T["kh"][:], op=ALU.mult),
                       reads=lk("r") + ["kh"], writes=["t2"])
                c.emit("pool", lambda e, cs=cs: e.tensor_tensor(out=T["t2"][:], in0=T["t2"][:], in1=rkp[:, cs], op=ALU.mult),
                       reads=["t2", "par"], writes=["t2"])
                c.emit("dve", lambda e: e.tensor_reduce(out=st[:, 32:40], in_=T["t2"][:].rearrange("p (h j) -> p h j", j=64),
                                                        axis=AX.X, op=ALU.add), reads=["t2"], writes=["st2"])
                c.emit("dve", lambda e, OB=OB: e.tensor_tensor(
                    out=OB["bonus"][:].rearrange("p (h j) -> p h j", j=64), in0=OB["V"][:].rearrange("p (h j) -> p h j", j=64),
                    in1=st[:, 32:40].unsqueeze(2).to_broadcast([128, 8, 64]), op=ALU.mult), reads=ok("V") + ["st2"], writes=ok("bonus"))
                for i, n in enumerate(names_out):
                    c.dma("sync" if i % 2 == 0 else "pool", O[n][rs, cs], OB[n][:], reads=ok(n), writes=[("od", n, ch, cb)])
                c.dma("sync", O["el"][ch:ch + 1, cs], T["EL"][0:1, :], reads=["EL"], writes=[("od", "el", ch, cb)])
        c.flush()


def rwkv_scan_phase(c, O):
    CB, G = 512, 4
    ident = c.ident
    with ExitStack() as pes:
        names_in = ["At", "Rt", "Bh", "Kh", "Bt", "Kt", "V"]
        ld = {n: [c.sb("sl" + n, [128, CB], es=pes) for _ in range(2)] for n in names_in}
        elrow = [c.sb("elrow", [1, CB], es=pes) for _ in range(2)]
        gl = [c.sb("gl", [64, 8], es=pes) for _ in range(2)]
        Tst = c.sb("Tst", [64, NH, 64], es=pes)
        XT = [c.sb("XT", [64, 512], es=pes) for _ in range(G)]
        ABK = [c.sb("ABK", [128, 512], es=pes) for _ in range(G)]
        Xb = [[c.sb("Xb", [128, 128], es=pes) for _ in range(2)] for _ in range(G)]
        XTb = [[c.sb("XTb", [128, 128], es=pes) for _ in range(3)] for _ in range(G)]
        PTb = [[c.sb("PTb", [128, 128], es=pes) for _ in range(2)] for _ in range(G)]
        Wb = [c.sb("Wb", [128, 64], es=pes) for _ in range(G)]
        Ub = [c.sb("Ub", [128, 64], es=pes) for _ in range(G)]
        Yst = [c.sb("Yst", [128, CB], es=pes) for _ in range(2)]
        print("scan sbuf remaining", c.nc.sbuf_bytes_remaining)
        c.emit("dve", lambda e: e.memset(Tst[:], 0.0), writes=[("T", h) for h in range(NH)])
        it = 0
        for ch in range(c.S // 128):
            rs = slice(ch * 128, (ch + 1) * 128)
            for cb in range(D // CB):
                cs = slice(cb * CB, (cb + 1) * CB)
                s = it % 2
                it += 1
                L = {}
                for i, n in enumerate(names_in):
                    c.dma("sync" if i % 2 == 0 else "pool", ld[n][s][:], O[n][rs, cs], writes=[("sl", n, s)])
                    L[n] = ld[n][s]
                lk = lambda *ns: [("sl", n, s) for n in ns]
                c.dma("sync", elrow[s][:], O["el"][ch:ch + 1, cs], writes=[("elrow", s)])
                psg, pgk = c.psum()
                for hl in range(8):
                    c.emit("pe", lambda e, psg=psg, hl=hl, s=s: e.matmul(
                        psg[0:64, hl:hl + 1], lhsT=elrow[s][0:1, hl * 64:(hl + 1) * 64], rhs=c.ones[0:1, 0:1], start=True, stop=True),
                        reads=[("elrow", s), "const"], writes=c.q(pgk, 0, 8))
                c.emit("dve", lambda e, psg=psg, s=s: e.tensor_copy(out=gl[s][:], in_=psg[0:64, 0:8]),
                       reads=c.q(pgk, 0, 8), writes=[("gl", s)])
                for grp in range(8 // G):
                    if SCAN_DBG < 1:
                        continue
                    heads = [grp * G + i for i in range(G)]
                    for i, hl in enumerate(heads):
                        hc = slice(hl * 64, (hl + 1) * 64)
                        ps, pk = c.psum()
                        for qn, n in enumerate(["At", "Rt", "Bh", "Kh"]):
                            c.emit("pe", lambda e, ps=ps, qn=qn, n=n, hc=hc, L=L: e.transpose(
                                ps[0:64, qn * 128:(qn + 1) * 128], L[n][:, hc], ident[:]),
                                reads=lk(n) + ["const"], writes=c.q(pk, qn * 128, (qn + 1) * 128))
                        if i % 2 == 0:
                            c.emit("act", lambda e, ps=ps, i=i: e.copy(out=XT[i][:], in_=ps[0:64, :]), reads=pk, writes=[("XT", i)])
                        else:
                            c.emit("dve", lambda e, ps=ps, i=i: e.tensor_copy(out=XT[i][:], in_=ps[0:64, :]), reads=pk,
                                   writes=[("XT", i)])
                    if SCAN_DBG < 2:
                        continue
                    psC, pCk = c.psum()
                    for i, hl in enumerate(heads):
                        c.emit("pe", lambda e, psC=psC, i=i: e.matmul(psC[:, i * 128:(i + 1) * 128], lhsT=XT[i][:, 0:128],
                                                                      rhs=XT[i][:, 256:384], start=True, stop=True),
                               reads=[("XT", i)], writes=pCk)
                    for i, hl in enumerate(heads):
                        c.emit("dve", lambda e, psC=psC, i=i: e.tensor_tensor(out=XTb[i][2][:], in0=psC[:, i * 128:(i + 1) * 128],
                                                                              in1=c.msl[:], op=ALU.mult),
                               reads=pCk + ["const"], writes=[("XTb", i, 2)])
                    for i, hl in enumerate(heads):
                        ps, pk = c.psum()
                        c.emit("pe", lambda e, ps=ps, i=i: e.matmul(ps[:, 0:256], lhsT=XT[i][:, 256:384], rhs=XT[i][:, 0:256],
                                                                    start=True, stop=True), reads=[("XT", i)], writes=pk)
                        c.emit("pe", lambda e, ps=ps, i=i: e.matmul(ps[:, 256:512], lhsT=XT[i][:, 384:512], rhs=XT[i][:, 0:256],
                                                                    start=True, stop=True), reads=[("XT", i)], writes=pk)
                        c.emit("dve", lambda e, ps=ps, i=i: e.tensor_tensor(out=ABK[i][:], in0=ps[:, :], in1=c.mask4[:], op=ALU.mult),
                               reads=pk + ["const"], writes=[("ABK", i)])
                        c.emit("dve", lambda e, i=i: e.tensor_tensor(out=PTb[i][0][:], in0=ABK[i][:, 0:128], in1=ident[:], op=ALU.add),
                               reads=[("ABK", i), "const"], writes=[("PTb", i, 0)])
                    if SCAN_DBG < 3:
                        continue
                    Xc = [(ABK[i][:, 0:128], ("ABK", i)) for i in range(G)]
                    XTc = [(XTb[i][2][:], ("XTb", i, 2)) for i in range(G)]
                    PTc = [(PTb[i][0][:], ("PTb", i, 0)) for i in range(G)]
                    for lev in range(int(os.environ.get("NLEV", "6"))):
                        pX, pXk = c.psum()
                        pXT, pXTk = c.psum()
                        newX, newXT = [], []
                        for i in range(G):
                            qs = c.q(pXk, i * 128, (i + 1) * 128)
                            qt = c.q(pXTk, i * 128, (i + 1) * 128)
                            xa, xk_ = Xc[i]
                            xta, xtk = XTc[i]
                            if lev < 5:
                                c.emit("pe", lambda e, pX=pX, i=i, xa=xa, xta=xta: e.matmul(
                                    pX[:, i * 128:(i + 1) * 128], lhsT=xta, rhs=xa, start=True, stop=True),
                                    reads=[xk_, xtk], writes=qs)
                            c.emit("pe", lambda e, pXT=pXT, i=i, xa=xa, xta=xta: e.matmul(
                                pXT[:, i * 128:(i + 1) * 128], lhsT=xa, rhs=xta, start=True, stop=True),
                                reads=[xk_, xtk], writes=qt)
                        if INV_SUB < 2:
                            continue
                        for i in range(G):
                            qs = c.q(pXk, i * 128, (i + 1) * 128)
                            qt = c.q(pXTk, i * 128, (i + 1) * 128)
                            b = lev % 2
                            if lev < 5:
                                c.emit("act", lambda e, pX=pX, i=i, b=b: e.copy(out=Xb[i][b][:], in_=pX[:, i * 128:(i + 1) * 128]),
                                       reads=pXk, writes=[("Xb", i, b)])
                                newX.append((Xb[i][b][:], ("Xb", i, b)))
                            else:
                                newX.append(None)
                            c.emit("act", lambda e, pXT=pXT, i=i, b=b: e.copy(out=XTb[i][b][:], in_=pXT[:, i * 128:(i + 1) * 128]),
                                   reads=pXTk, writes=[("XTb", i, b)])
                            newXT.append((XTb[i][b][:], ("XTb", i, b)))
                        if INV_SUB < 3:
                            continue
                        pP, pPk = c.psum()
                        for i in range(G):
                            qp = c.q(pPk, i * 128, (i + 1) * 128)
                            pa, pkk = PTc[i]
                            c.emit("pe", lambda e, pP=pP, i=i, pa=pa, l=newXT[i][0]: e.matmul(
                                pP[:, i * 128:(i + 1) * 128], lhsT=l, rhs=pa, start=True, stop=True),
                                reads=[newXT[i][1], pkk], writes=qp)
                        for i in range(G):
                            qp = c.q(pPk, i * 128, (i + 1) * 128)
                            pa, pkk = PTc[i]
                            nb = (lev + 1) % 2
                            c.emit("dve", lambda e, pP=pP, i=i, pa=pa, nb=nb: e.tensor_tensor(
                                out=PTb[i][nb][:], in0=pP[:, i * 128:(i + 1) * 128], in1=pa, op=ALU.add),
                                reads=qp + [pkk], writes=[("PTb", i, nb)])
                            PTc[i] = (PTb[i][nb][:], ("PTb", i, nb))
                        Xc, XTc = newX, newXT
                    if SCAN_DBG < 4:
                        continue
                    pW, pWk = c.psum()
                    pU, pUk = c.psum()
                    pY, pYk = c.psum()
                    pT, pTk = c.psum()
                    H = [(i, hl, cb * 8 + hl, slice(hl * 64, (hl + 1) * 64), slice(i * 128, i * 128 + 64)) for i, hl in enumerate(heads)]
                    for i, hl, h, hc, q0 in H:
                        c.emit("pe", lambda e, pW=pW, pU=pU, pY=pY, pT=pT, i=i, h=h, q0=q0: e.matmul(pW[:, q0], lhsT=XT[i][:, 0:128], rhs=Tst[:, h, :],
                                                                      start=True, stop=False),
                               reads=[("XT", i), ("T", h)], writes=pWk)
                        c.emit("pe", lambda e, pW=pW, pU=pU, pY=pY, pT=pT, i=i, hc=hc, q0=q0, L=L: e.matmul(pW[:, q0], lhsT=ABK[i][:, 256:384], rhs=L["V"][:, hc],
                                                                             start=False, stop=True),
                               reads=[("ABK", i)] + lk("V"), writes=pWk)
                    for i, hl, h, hc, q0 in H:
                        c.emit("act", lambda e, pW=pW, pU=pU, pY=pY, pT=pT, i=i, q0=q0: e.copy(out=Wb[i][:], in_=pW[:, q0]), reads=pWk, writes=[("Wb", i)])
                    for i, hl, h, hc, q0 in H:
                        pa, pkk = PTc[i]
                        c.emit("pe", lambda e, pW=pW, pU=pU, pY=pY, pT=pT, i=i, pa=pa, q0=q0: e.matmul(pU[:, q0], lhsT=pa, rhs=Wb[i][:], start=True, stop=True),
                               reads=[pkk, ("Wb", i)], writes=pUk)
                    for i, hl, h, hc, q0 in H:
                        c.emit("act", lambda e, pW=pW, pU=pU, pY=pY, pT=pT, i=i, q0=q0: e.copy(out=Ub[i][:], in_=pU[:, q0]), reads=pUk, writes=[("Ub", i)])
                    for i, hl, h, hc, q0 in H:
                        c.emit("pe", lambda e, pW=pW, pU=pU, pY=pY, pT=pT, i=i, h=h, q0=q0: e.matmul(pY[:, q0], lhsT=XT[i][:, 128:256], rhs=Tst[:, h, :],
                                                                      start=True, stop=False),
                               reads=[("XT", i), ("T", h)], writes=pYk)
                        c.emit("pe", lambda e, pW=pW, pU=pU, pY=pY, pT=pT, i=i, q0=q0: e.matmul(pY[:, q0], lhsT=ABK[i][:, 128:256], rhs=Ub[i][:],
                                                                 start=False, stop=False),
                               reads=[("ABK", i), ("Ub", i)], writes=pYk)
                        c.emit("pe", lambda e, pW=pW, pU=pU, pY=pY, pT=pT, i=i, hc=hc, q0=q0, L=L: e.matmul(pY[:, q0], lhsT=ABK[i][:, 384:512], rhs=L["V"][:, hc],
                                                                             start=False, stop=True),
                               reads=[("ABK", i)] + lk("V"), writes=pYk)
                        c.emit("pe", lambda e, pW=pW, pU=pU, pY=pY, pT=pT, i=i, hc=hc, q0=q0, L=L: e.matmul(pT[0:64, q0], lhsT=L["Bt"][:, hc], rhs=Ub[i][:],
                                                                             start=True, stop=False),
                               reads=lk("Bt") + [("Ub", i)], writes=pTk)
                        c.emit("pe", lambda e, pW=pW, pU=pU, pY=pY, pT=pT, i=i, hc=hc, q0=q0, L=L: e.matmul(pT[0:64, q0], lhsT=L["Kt"][:, hc], rhs=L["V"][:, hc],
                                                                             start=False, stop=True),
                               reads=lk("Kt", "V"), writes=pTk)
                    for i, hl, h, hc, q0 in H:
                        c.emit("act", lambda e, pW=pW, pU=pU, pY=pY, pT=pT, hc=hc, q0=q0, s=s: e.copy(out=Yst[s][:, hc], in_=pY[:, q0]),
                               reads=pYk, writes=[("Yst", s)])
                        c.emit("dve", lambda e, pW=pW, pU=pU, pY=pY, pT=pT, h=h, hl=hl, q0=q0, s=s: e.scalar_tensor_tensor(
                            out=Tst[:, h, :], in0=Tst[:, h, :], scalar=gl[s][:, hl:hl + 1], in1=pT[0:64, q0],
                            op0=ALU.mult, op1=ALU.add), reads=pTk + [("T", h), ("gl", s)], writes=[("T", h)])
                c.dma("sync", O["y"][rs, cs], Yst[s][:], reads=[("Yst", s)], writes=[("od", "y", ch, cb)])
        c.flush()


def rwkv_post_phase(c, x_d, P, O):
    with ExitStack() as pes:
        yb = c.sb("yb", [128, D], es=pes)
        bb = c.sb("bb", [128, D], es=pes)
        gb = c.sb("gb", [128, D], es=pes)
        xt = c.sb("xt", [128, D], es=pes)
        sq = c.sb("sq", [128, D], es=pes)
        gng = c.sb("gng", [128, D], es=pes)
        gnb = c.sb("gnb", [128, D], es=pes)
        st = c.sb("st", [128, 160], es=pes)
        zT = c.sb("zT", [128, KC, 128], es=pes)
        wb = [c.sb("wb", [128, KC, 256], es=pes) for _ in range(2)]
        c.dma("sync", gng[:], P["gn_g"].partition_broadcast(128), writes=["par"])
        c.dma("sync", gnb[:], P["gn_b"].partition_broadcast(128), writes=["par"])
        y3 = yb[:].rearrange("p (h j) -> p h j", j=64)
        s3 = sq[:].rearrange("p (h j) -> p h j", j=64)
        bc = lambda a: a.unsqueeze(2).to_broadcast([128, NH, 64])
        for t in range(c.S // 128):
            rs = slice(t * 128, (t + 1) * 128)
            c.dma("sync", yb[:], O["y"][rs, :], writes=["yb"])
            c.dma("pool", bb[:], O["bonus"][rs, :], writes=["bb"])
            c.dma("sync", gb[:], O["g"][rs, :], writes=["gb"])
            c.dma("pool", xt[:], x_d[rs, :], writes=["xt"])
            c.emit("dve", lambda e: e.tensor_reduce(out=st[:, 0:32], in_=y3, axis=AX.X, op=ALU.add), reads=["yb"], writes=["st"])
            c.emit("dve", lambda e: e.tensor_scalar(out=st[:, 32:64], in0=st[:, 0:32], scalar1=1.0 / 64, scalar2=0.0,
                                                    op0=ALU.mult, op1=ALU.add), reads=["st"], writes=["st"])
            c.emit("dve", lambda e: e.tensor_tensor(out=y3, in0=y3, in1=bc(st[:, 32:64]), op=ALU.subtract),
                   reads=["yb", "st"], writes=["yb"])
            c.emit("pool", lambda e: e.tensor_tensor(out=sq[:], in0=yb[:], in1=yb[:], op=ALU.mult), reads=["yb"], writes=["sq"])
            c.emit("dve", lambda e: e.tensor_reduce(out=st[:, 64:96], in_=s3, axis=AX.X, op=ALU.add), reads=["sq"], writes=["st"])
            c.emit("dve", lambda e: e.tensor_scalar(out=st[:, 96:128], in0=st[:, 64:96], scalar1=1.0 / 64, scalar2=GN_EPS,
                                                    op0=ALU.mult, op1=ALU.add), reads=["st"], writes=["st"])
            c.emit("act", lambda e: e.activation(out=st[:, 96:128], in_=st[:, 96:128], func=AF.Sqrt), reads=["st"], writes=["st"])
            c.emit("dve", lambda e: e.reciprocal(out=st[:, 128:160], in_=st[:, 96:128]), reads=["st"], writes=["st"])
            c.emit("dve", lambda e: e.tensor_tensor(out=y3, in0=y3, in1=bc(st[:, 128:160]), op=ALU.mult),
                   reads=["yb", "st"], writes=["yb"])
            c.emit("pool", lambda e: e.tensor_tensor(out=yb[:], in0=yb[:], in1=gng[:], op=ALU.mult), reads=["yb", "par"], writes=["yb"])
            c.emit("pool", lambda e: e.tensor_tensor(out=yb[:], in0=yb[:], in1=gnb[:], op=ALU.add), reads=["yb", "par"], writes=["yb"])
            c.emit("pool", lambda e: e.tensor_tensor(out=yb[:], in0=yb[:], in1=bb[:], op=ALU.add), reads=["yb", "bb"], writes=["yb"])
            c.emit("pool", lambda e: e.tensor_tensor(out=sq[:], in0=yb[:], in1=gb[:], op=ALU.mult), reads=["yb", "gb"], writes=["sq"])
            transpose_tile(c, sq, "sq", zT, lambda g: ("zT", g), 0)

            def epi(cb, ps, pk):
                c.emit("dve", lambda e, cb=cb, ps=ps: e.tensor_tensor(out=xt[:, cb * 256:(cb + 1) * 256], in0=ps[:, 0:256],
                                                                       in1=xt[:, cb * 256:(cb + 1) * 256], op=ALU.add),
                       reads=pk + ["xt"], writes=["xt"])
            proj(c, zT, lambda kc: [("zT", kc // 4)], 1, P["wo"], D, wb, "wb", epi)
            c.dma("sync", x_d[rs, :], xt[:], reads=["xt"], writes=[("xd", t)])
        c.flush()


import math


def rope_tables_phase(c, pos_ap, cos_d, sin_d):
    NT = c.S // 128
    with ExitStack() as pes:
        pi_ = c.sb("pos_i", [NT, 128], I32, es=pes)
        pf = c.sb("pos_f", [NT, 128], es=pes)
        posT = c.sb("posT", [128, NT], es=pes)
        io_i = c.sb("io_i", [128, 32], I32, es=pes)
        invf = c.sb("invf", [128, 32], es=pes)
        ang = c.sb("ang", [128, 32], es=pes)
        ob = [c.sb("ropeo", [128, 64], es=pes) for _ in range(2)]
        nb = c.sb("negpi", [128, 1], es=pes)
        ni = c.sb("ni", [128, 64], I32, es=pes)
        nf = c.sb("nf", [128, 64], es=pes)
        c.emit("pool", lambda e: e.memset(nb[:], -math.pi), writes=["nb"])
        c.dma("sync", pi_[:], pos_ap.rearrange("(t p) -> t p", p=128), writes=["pi"])
        c.emit("dve", lambda e: e.tensor_copy(out=pf[:], in_=pi_[:]), reads=["pi"], writes=["pf"])
        ps, pk = c.psum()
        c.emit("pe", lambda e: e.transpose(ps[:, 0:NT], pf[:], c.ident[0:NT, 0:NT]), reads=["pf", "const"], writes=pk)
        c.emit("dve", lambda e: e.tensor_copy(out=posT[:], in_=ps[:, 0:NT]), reads=pk, writes=["posT"])
        c.emit("pool", lambda e: e.iota(io_i[:], pattern=[[1, 32]], base=0, channel_multiplier=0), writes=["io"])
        c.emit("dve", lambda e: e.tensor_copy(out=invf[:], in_=io_i[:]), reads=["io"], writes=["invf"])
        c.emit("act", lambda e: e.activation(out=invf[:], in_=invf[:], func=AF.Exp, scale=-math.log(10000.0) / 32.0),
               reads=["invf"], writes=["invf"])
        for t in range(NT):
            s = t % 2
            c.emit("dve", lambda e, t=t: e.tensor_scalar(out=ang[:], in0=invf[:], scalar1=posT[:, t:t + 1], scalar2=0.0,
                                                         op0=ALU.mult, op1=ALU.add), reads=["invf", "posT"], writes=["ang"])
            c.emit("dve", lambda e, s=s: e.tensor_scalar(out=ob[s][:, 32:64], in0=ang[:], scalar1=1.0 / (2 * math.pi), scalar2=0.5,
                                                         op0=ALU.mult, op1=ALU.add), reads=["ang"], writes=[("ob", s)])
            c.emit("dve", lambda e, s=s: e.tensor_scalar(out=ob[s][:, 0:32], in0=ang[:], scalar1=1.0 / (2 * math.pi), scalar2=0.75,
                                                         op0=ALU.mult, op1=ALU.add), reads=["ang"], writes=[("ob", s)])
            c.emit("dve", lambda e, s=s: e.tensor_copy(out=ni[:], in_=ob[s][:]), reads=[("ob", s)], writes=["ni"])
            c.emit("dve", lambda e: e.tensor_copy(out=nf[:], in_=ni[:]), reads=["ni"], writes=["nf"])
            c.emit("dve", lambda e, s=s: e.tensor_tensor(out=ob[s][:], in0=ob[s][:], in1=nf[:], op=ALU.subtract),
                   reads=[("ob", s), "nf"], writes=[("ob", s)])
            c.emit("dve", lambda e, s=s: e.tensor_scalar(out=nf[:], in0=ob[s][:], scalar1=0.0, scalar2=0.0,
                                                         op0=ALU.is_lt, op1=ALU.add), reads=[("ob", s)], writes=["nf"])
            c.emit("dve", lambda e, s=s: e.tensor_tensor(out=ob[s][:], in0=ob[s][:], in1=nf[:], op=ALU.add),
                   reads=[("ob", s), "nf"], writes=[("ob", s)])
            c.emit("act", lambda e, s=s: e.activation(out=ob[s][:], in_=ob[s][:], func=AF.Sin, bias=nb[:, 0:1], scale=2 * math.pi),
                   reads=[("ob", s), "nb"], writes=[("ob", s)])
            c.dma("sync", cos_d[t * 128:(t + 1) * 128, :], ob[s][:, 0:32], reads=[("ob", s)], writes=[("cd", t)])
            c.dma("sync", sin_d[t * 128:(t + 1) * 128, :], ob[s][:, 32:64], reads=[("ob", s)], writes=[("sd", t)])
        c.flush()


def rope_epi_ops(c, ps, pk, nh, cs_tile, cskey, dst, dkey, tmp, tkey):
    p3 = ps[:, 0:nh * 64].rearrange("p (h d) -> p h d", d=64)
    cosb = cs_tile[:, 0:32].unsqueeze(1).to_broadcast([128, nh, 32])
    sinb = cs_tile[:, 32:64].unsqueeze(1).to_broadcast([128, nh, 32])
    d3 = dst.rearrange("p (h d) -> p h d", d=64)
    t3 = tmp[:, 0:nh * 64].rearrange("p (h d) -> p h d", d=64)
    R = pk + [cskey]
    c.emit("dve", lambda e: e.tensor_tensor(out=d3[:, :, 0:32], in0=p3[:, :, 0:32], in1=cosb, op=ALU.mult), reads=R, writes=[dkey])
    c.emit("dve", lambda e: e.tensor_tensor(out=t3[:, :, 0:32], in0=p3[:, :, 32:64], in1=sinb, op=ALU.mult), reads=R, writes=[tkey])
    c.emit("dve", lambda e: e.tensor_tensor(out=d3[:, :, 32:64], in0=p3[:, :, 32:64], in1=cosb, op=ALU.mult), reads=R, writes=[dkey])
    c.emit("dve", lambda e: e.tensor_tensor(out=t3[:, :, 32:64], in0=p3[:, :, 0:32], in1=sinb, op=ALU.mult), reads=R, writes=[tkey])
    c.emit("dve", lambda e: e.tensor_tensor(out=d3[:, :, 0:32], in0=d3[:, :, 0:32], in1=t3[:, :, 0:32], op=ALU.subtract),
           reads=[dkey, tkey], writes=[dkey])
    c.emit("dve", lambda e: e.tensor_tensor(out=d3[:, :, 32:64], in0=d3[:, :, 32:64], in1=t3[:, :, 32:64], op=ALU.add),
           reads=[dkey, tkey], writes=[dkey])


def norm_tile(c, xt, gam, junk, hbuf, stat, jkeys, hkeys):
    c.emit("act", lambda e: e.activation(out=junk, in_=xt[:], func=AF.Square, accum_out=stat[:, 0:1]),
           reads=["xt"], writes=jkeys + ["stat"])
    c.emit("dve", lambda e: e.tensor_scalar(out=stat[:, 1:2], in0=stat[:, 0:1], scalar1=1.0 / D, scalar2=RMS_EPS,
                                            op0=ALU.mult, op1=ALU.add), reads=["stat"], writes=["stat"])
    c.emit("act", lambda e: e.activation(out=stat[:, 3:4], in_=stat[:, 1:2], func=AF.Sqrt), reads=["stat"], writes=["stat"])
    c.emit("dve", lambda e: e.reciprocal(out=stat[:, 2:3], in_=stat[:, 3:4]), reads=["stat"], writes=["stat"])
    c.emit("dve", lambda e: e.scalar_tensor_tensor(out=hbuf, in0=xt[:], scalar=stat[:, 2:3], in1=gam[:],
                                                   op0=ALU.mult, op1=ALU.mult), reads=["xt", "stat", "gam"], writes=hkeys)


def kv_phase(c, x_d, g_row, w_kv, cos_d, sin_d, kT_d, v_d):
    with ExitStack() as pes:
        xt = c.sb("xt", [128, D], es=pes)
        gam = c.sb("gam", [128, D], es=pes)
        junk = c.sb("junk", [128, D], es=pes)
        hbuf = c.sb("hbuf", [128, D], es=pes)
        stat = c.sb("stat", [128, 8], es=pes)
        hT = c.sb("hT", [128, KC, 128], es=pes)
        wb = [c.sb("wb", [128, KC, 256], es=pes) for _ in range(2)]
        cs = [c.sb("cs", [128, 64], es=pes) for _ in range(2)]
        krot = c.sb("krot", [128, 512], es=pes)
        tmp = c.sb("tmp", [128, 256], es=pes)
        vb = [c.sb("vb", [128, 256], es=pes) for _ in range(2)]
        kTt = c.sb("kTt", [64, 8, 128], es=pes)
        c.dma("sync", gam[:], g_row.partition_broadcast(128), writes=["gam"])
        for t in range(c.S // 128):
            rs = slice(t * 128, (t + 1) * 128)
            s = t % 2
            c.dma("sync", xt[:], x_d[rs, :], writes=["xt"])
            c.dma("pool", cs[s][:, 0:32], cos_d[rs, :], writes=[("cs", s)])
            c.dma("pool", cs[s][:, 32:64], sin_d[rs, :], writes=[("cs", s)])
            norm_tile(c, xt, gam, junk[:], hbuf[:], stat, ["junk"], ["hbuf"])
            transpose_tile(c, hbuf, "hbuf", hT, lambda g: ("hT", g), 0)

            def epi(cb, ps, pk, s=s, rs=rs):
                if cb < 2:
                    rope_epi_ops(c, ps, pk, 4, cs[s], ("cs", s), krot[:, cb * 256:(cb + 1) * 256], ("krot", cb), tmp, "tmp")
                else:
                    vs = c.wslot("vb")
                    c.emit("act", lambda e, vs=vs, ps=ps: e.copy(out=vb[vs][:], in_=ps[:, 0:256]), reads=pk, writes=[("vb", vs)])
                    c.dma("sync", v_d[rs, (cb - 2) * 256:(cb - 1) * 256], vb[vs][:], reads=[("vb", vs)], writes=[("vd", rs.start, cb)])
            proj(c, hT, lambda kc: [("hT", kc // 4)], 1, w_kv, 1024, wb, "wb", epi)
            for hg in range(2):
                ps, pk = c.psum()
                for j in range(4):
                    h = hg * 4 + j
                    c.emit("pe", lambda e, ps=ps, j=j, h=h: e.transpose(ps[0:64, j * 128:(j + 1) * 128], krot[:, h * 64:(h + 1) * 64], c.ident[:]),
                           reads=[("krot", h // 4), "const"], writes=pk)
                c.emit("act", lambda e, ps=ps, hg=hg: e.copy(out=kTt[:, hg * 4:(hg + 1) * 4, :],
                                                             in_=ps[0:64, :].rearrange("p (a b) -> p a b", a=4)),
                       reads=pk, writes=["kTt"])
            c.dma("sync", kT_d[:, :, rs].rearrange("g d s -> d g s"), kTt[:], reads=["kTt"], writes=[("kTd", t)])
        c.flush()


def attn_phase(c, x_d, g_row, w_q, w_o, sinks_row, cos_d, sin_d, kT_d, v_d, dbg=None):
    with ExitStack() as pes:
        xt = c.sb("xt", [128, D], es=pes)
        gam = c.sb("gam", [128, D], es=pes)
        junk = c.sb("junk", [128, D], es=pes)
        hbuf = c.sb("hbuf", [128, D], es=pes)
        stat = c.sb("stat", [128, 8], es=pes)
        hT = c.sb("hT", [128, KC, 128], es=pes)
        wb = [c.sb("wb", [128, KC, 256], es=pes) for _ in range(2)]
        cs = [c.sb("cs", [128, 64], es=pes) for _ in range(2)]
        tmp = c.sb("tmp", [128, 256], es=pes)
        qT = c.sb("qT", [64, NH, 128], es=pes)
        kTs = [c.sb("kTs", [64, 8, 256], es=pes) for _ in range(2)]
        v1 = [c.sb("v1", [128, 2, 8, 65], es=pes) for _ in range(2)]
        E = [[c.sb("E", [128, 512], es=pes) for _ in range(2)] for _ in range(2)]
        mC = c.sb("mC", [128, 512], es=pes)
        mP = c.sb("mP", [128, 512], es=pes)
        sk = c.sb("sk", [128, NH], es=pes)
        dn = c.sb("dn", [128, 8], es=pes)
        attn = junk
        c.dma("sync", gam[:], g_row.partition_broadcast(128), writes=["gam"])
        c.dma("sync", sk[:], sinks_row.partition_broadcast(128), writes=["sk"])
        c.emit("act", lambda e: e.activation(out=sk[:], in_=sk[:], func=AF.Exp), reads=["sk"], writes=["sk"])
        for q in range(4):
            c.emit("pool", lambda e, q=q: e.tensor_copy(out=mC[:, q * 128:(q + 1) * 128], in_=c.mask4[:, 128:256]),
                   reads=["const"], writes=["mC"])
            c.emit("pool", lambda e, q=q: e.tensor_copy(out=mP[:, q * 128:(q + 1) * 128], in_=c.msl[:]),
                   reads=["const"], writes=["mP"])
        for s in range(2):
            c.emit("pool", lambda e, s=s: e.memset(v1[s][:], 1.0), writes=[("v1", s)])
        for t in range(c.S // 128):
            rs = slice(t * 128, (t + 1) * 128)
            s = t % 2
            nkb = 1 if t == 0 else 2
            k0 = (t - 1) * 128 if t > 0 else 0
            c.dma("sync", xt[:], x_d[rs, :], writes=["xt"])
            c.dma("pool", cs[s][:, 0:32], cos_d[rs, :], writes=[("cs", s)])
            c.dma("pool", cs[s][:, 32:64], sin_d[rs, :], writes=[("cs", s)])
            kb0 = 2 - nkb
            c.dma("pool", kTs[s][:, :, kb0 * 128:256], kT_d[:, :, k0:(t + 1) * 128].rearrange("g d s -> d g s"),
                  writes=[("kTs", s)])
            for kb in range(kb0, 2):
                r1 = (t - 1 + kb) * 128
                c.dma("sync", v1[s][:, kb, :, 0:64], v_d[r1:r1 + 128, :].rearrange("p (g d) -> p g d", d=64), writes=[("v1", s)])
            norm_tile(c, xt, gam, junk[:], hbuf[:], stat, ["junk"], ["hbuf"])
            transpose_tile(c, hbuf, "hbuf", hT, lambda g: ("hT", g), 0)

            def epi(cb, ps, pk, s=s):
                rope_epi_ops(c, ps, pk, 4, cs[s], ("cs", s), hbuf[:, cb * 256:(cb + 1) * 256], "hbuf", tmp, "tmp")
            proj(c, hT, lambda kc: [("hT", kc // 4)], 1, w_q, D, wb, "wb", epi)
            for hg in range(8):
                ps, pk = c.psum()
                for j in range(4):
                    h = hg * 4 + j
                    c.emit("pe", lambda e, ps=ps, j=j, h=h: e.transpose(ps[0:64, j * 128:(j + 1) * 128], hbuf[:, h * 64:(h + 1) * 64], c.ident[:]),
                           reads=["hbuf", "const"], writes=pk)
                c.emit("act", lambda e, ps=ps, hg=hg: e.copy(out=qT[:, hg * 4:(hg + 1) * 4, :],
                                                             in_=ps[0:64, :].rearrange("p (a b) -> p a b", a=4)),
                       reads=pk, writes=[("qT", hg)])
            for g in range(8):
                es_ = g % 2
                for kb in range(kb0, 2):
                    ps, pk = c.psum()
                    for j in range(4):
                        c.emit("pe", lambda e, ps=ps, j=j, g=g, kb=kb, s=s: e.matmul(
                            ps[:, j * 128:(j + 1) * 128], lhsT=kTs[s][:, g, kb * 128:(kb + 1) * 128], rhs=qT[:, 4 * g + j, :],
                            start=True, stop=True), reads=[("kTs", s), ("qT", g)], writes=pk)
                    c.emit("act", lambda e, ps=ps, kb=kb, es_=es_: e.activation(out=E[es_][kb][:], in_=ps[:, :], func=AF.Exp, scale=0.125),
                           reads=pk, writes=[("E", es_, kb)])
                    m = mC if kb == 1 else mP
                    c.emit("pool", lambda e, kb=kb, es_=es_, m=m: e.tensor_tensor(out=E[es_][kb][:], in0=E[es_][kb][:], in1=m[:], op=ALU.mult),
                           reads=[("E", es_, kb), "mC", "mP"], writes=[("E", es_, kb)])
                po, pok = c.psum()
                for j in range(4):
                    for kb in range(kb0, 2):
                        c.emit("pe", lambda e, po=po, j=j, kb=kb, g=g, s=s, es_=es_, kb0=kb0: e.matmul(
                            po[:, j * 65:(j + 1) * 65], lhsT=E[es_][kb][:, j * 128:(j + 1) * 128], rhs=v1[s][:, kb, g, :],
                            start=(kb == kb0), stop=(kb == 1)), reads=[("E", es_, kb), ("v1", s)], writes=pok)
                po3 = po[:, 0:260].rearrange("p (h d) -> p h d", d=65)
                c.emit("dve", lambda e, po3=po3, g=g: e.tensor_tensor(out=dn[:, 0:4], in0=po3[:, :, 64], in1=sk[:, 4 * g:4 * g + 4], op=ALU.add),
                       reads=pok + ["sk"], writes=["dn"])
                c.emit("dve", lambda e: e.reciprocal(out=dn[:, 4:8], in_=dn[:, 0:4]), reads=["dn"], writes=["dn"])
                c.emit("dve", lambda e, po3=po3, g=g: e.tensor_tensor(
                    out=attn[:, g * 256:(g + 1) * 256].rearrange("p (h d) -> p h d", d=64), in0=po3[:, :, 0:64],
                    in1=dn[:, 4:8].unsqueeze(2).to_broadcast([128, 4, 64]), op=ALU.mult), reads=pok + ["dn"], writes=["junk"])
            if dbg is not None:
                c.dma("sync", dbg[rs, :], attn[:], reads=["junk"], writes=[("dbg", t)])
            transpose_tile(c, attn, "junk", hT, lambda g: ("hT", g), 0)

            def epi2(cb, ps, pk):
                c.emit("dve", lambda e, cb=cb, ps=ps: e.tensor_tensor(out=xt[:, cb * 256:(cb + 1) * 256], in0=ps[:, 0:256],
                                                                       in1=xt[:, cb * 256:(cb + 1) * 256], op=ALU.add),
                       reads=pk + ["xt"], writes=["xt"])
            proj(c, hT, lambda kc: [("hT", kc // 4)], 1, w_o, D, wb, "wb", epi2)
            c.dma("sync", x_d[rs, :], xt[:], reads=["xt"], writes=[("xd", t)])
        c.flush()


def test_attn_program(S):
    nc = bass.Bass("TRN2", target_bir_lowering=False)
    x_in = nc.dram_tensor("x", [S, D], F32, kind="ExternalInput").ap()
    pos = nc.dram_tensor("pos", [S], I32, kind="ExternalInput").ap()
    gk = nc.dram_tensor("gk", [1, D], F32, kind="ExternalInput").ap()
    g1 = nc.dram_tensor("g1", [1, D], F32, kind="ExternalInput").ap()
    wkv = nc.dram_tensor("wkv", [D, 1024], F32, kind="ExternalInput").ap()
    wq = nc.dram_tensor("wq", [D, D], F32, kind="ExternalInput").ap()
    wo = nc.dram_tensor("wo", [D, D], F32, kind="ExternalInput").ap()
    sinks = nc.dram_tensor("sinks", [1, NH], F32, kind="ExternalInput").ap()
    xo = nc.dram_tensor("xo", [S, D], F32, kind="ExternalOutput").ap()
    cos_d = nc.dram_tensor("cos_d", [S, 32], F32, kind="ExternalOutput").ap()
    sin_d = nc.dram_tensor("sin_d", [S, 32], F32, kind="ExternalOutput").ap()
    kT_d = nc.dram_tensor("kT_d", [8, 64, S], F32, kind="ExternalOutput").ap()
    v_d = nc.dram_tensor("v_d", [S, 512], F32, kind="ExternalOutput").ap()
    with ExitStack() as es:
        c = Ctx(nc, es, S)
        make_consts(c)
        make_masks(c)
        copy_rows(c, xo, x_in, S)
        rope_tables_phase(c, pos, cos_d, sin_d)
        kv_phase(c, xo, gk, wkv, cos_d, sin_d, kT_d, v_d)
        dbg = nc.dram_tensor("dbg", [S, D], F32, kind="ExternalOutput").ap()
        attn_phase(c, xo, g1, wq, wo, sinks, cos_d, sin_d, kT_d, v_d, dbg=dbg)
    return nc


def ple_phase(c, x_d, g_row, w_gate, w_up, p_d):
    with ExitStack() as pes:
        xt = c.sb("xt", [128, D], es=pes)
        gam = c.sb("gam", [128, D], es=pes)
        junk = c.sb("junk", [128, D], es=pes)
        hbuf = c.sb("hbuf", [128, D], es=pes)
        stat = c.sb("stat", [128, 8], es=pes)
        hT = c.sb("hT", [128, KC, 128], es=pes)
        wb = [c.sb("wb", [128, KC, 256], es=pes) for _ in range(2)]
        wup = c.sb("wup", [128, 2, D], es=pes)
        pt = c.sb("pt", [128, 256], es=pes)
        pT = c.sb("pT", [128, 2, 128], es=pes)
        sg = [c.sb("sg", [128, 256], es=pes) for _ in range(2)]
        c.dma("sync", gam[:], g_row.partition_broadcast(128), writes=["gam"])
        c.dma("sync", wup[:], w_up.rearrange("(ch p) n -> p ch n", p=128), writes=["wup"])
        for t in range(c.S // 128):
            rs = slice(t * 128, (t + 1) * 128)
            c.dma("sync", xt[:], x_d[rs, :], writes=["xt"])
            c.dma("pool", pt[:], p_d[rs, :], writes=["pt"])
            norm_tile(c, xt, gam, junk[:], hbuf[:], stat, ["junk"], ["hbuf"])
            transpose_tile(c, hbuf, "hbuf", hT, lambda g: ("hT", g), 0)
            transpose_tile(c, pt, "pt", pT, lambda g: "pT", 0, nchunks=2)

            def epi(cb, ps, pk):
                s = c.wslot("sg")
                c.emit("act", lambda e, s=s, ps=ps: e.activation(out=sg[s][:], in_=ps[:, 0:256], func=AF.Sigmoid),
                       reads=pk, writes=[("sg", s)])
                pu, puk = c.psum()
                for ch in range(2):
                    c.emit("pe", lambda e, pu=pu, ch=ch, cb=cb: e.matmul(pu[:, 0:256], lhsT=pT[:, ch, :],
                                                                         rhs=wup[:, ch, cb * 256:(cb + 1) * 256],
                                                                         start=(ch == 0), stop=(ch == 1)),
                           reads=["pT", "wup"], writes=puk)
                c.emit("dve", lambda e, s=s, pu=pu: e.tensor_tensor(out=sg[s][:], in0=sg[s][:], in1=pu[:, 0:256], op=ALU.mult),
                       reads=puk + [("sg", s)], writes=[("sg", s)])
                c.emit("dve", lambda e, s=s, cb=cb: e.tensor_tensor(out=xt[:, cb * 256:(cb + 1) * 256], in0=sg[s][:],
                                                                    in1=xt[:, cb * 256:(cb + 1) * 256], op=ALU.add),
                       reads=[("sg", s), "xt"], writes=["xt"])
            proj(c, hT, lambda kc: [("hT", kc // 4)], 1, w_gate, D, wb, "wb", epi)
            c.dma("sync", x_d[rs, :], xt[:], reads=["xt"], writes=[("xd", t)])
        c.flush()


def final_norm_phase(c, x_d, g_row, y_d):
    with ExitStack() as pes:
        xt = c.sb("xt", [128, D], es=pes)
        gam = c.sb("gam", [128, D], es=pes)
        junk = c.sb("junk", [128, D], es=pes)
        hb = [c.sb("hbuf", [128, D], es=pes) for _ in range(2)]
        stat = c.sb("stat", [128, 8], es=pes)
        c.dma("sync", gam[:], g_row.partition_broadcast(128), writes=["gam"])
        for t in range(c.S // 128):
            rs = slice(t * 128, (t + 1) * 128)
            s = t % 2
            c.dma("sync", xt[:], x_d[rs, :], writes=["xt"])
            norm_tile(c, xt, gam, junk[:], hb[s][:], stat, ["junk"], [("hb", s)])
            c.dma("pool", y_d[rs, :], hb[s][:], reads=[("hb", s)], writes=[("yd", t)])
        c.flush()


def convert_ffn_weights_phase(c, wg, wu, wd, wg_b, wu_b, wd_b, ff):
    HC = ff // 128
    CH = 2816 if ff % 2816 == 0 else ff
    with ExitStack() as pes:
        fb = [c.sb("cvf", [128, 2816], es=pes) for _ in range(3)]
        bb = [c.sb("cvb", [128, 2816], BF16, es=pes) for _ in range(3)]
        it = 0
        engs = ["act", "dve", "pool"]

        def cast(s, n, it):
            e_ = engs[it % 3]
            if e_ == "act":
                c.emit("act", lambda e: e.copy(out=bb[s][:, 0:n], in_=fb[s][:, 0:n]), reads=[("cvf", s)], writes=[("cvb", s)])
            else:
                c.emit(e_, lambda e: e.tensor_copy(out=bb[s][:, 0:n], in_=fb[s][:, 0:n]), reads=[("cvf", s)], writes=[("cvb", s)])
        for (w, wb_) in ((wg, wg_b), (wu, wu_b)):
            for kc in range(KC):
                for c0 in range(0, ff, CH):
                    s = it % 3
                    c.dma("sync", fb[s][:, 0:CH], w[kc * 128:(kc + 1) * 128, c0:c0 + CH], writes=[("cvf", s)])
                    cast(s, CH, it)
                    h0, nh = c0 // 128, CH // 128
                    c.dma("pool", wb_[h0:h0 + nh, :, kc, :].rearrange("h p c -> p h c"),
                          bb[s][:, 0:CH].rearrange("p (h c) -> p h c", c=128), reads=[("cvb", s)], writes=[("cvo", it)])
                    it += 1
        for hc in range(HC):
            s = it % 3
            c.dma("sync", fb[s][:, 0:D], wd[hc * 128:(hc + 1) * 128, :], writes=[("cvf", s)])
            cast(s, D, it)
            c.dma("pool", wd_b[hc // 4, :, :, hc % 4, :].rearrange("db p c -> p db c"),
                  bb[s][:, 0:D].rearrange("p (db c) -> p db c", c=512), reads=[("cvb", s)], writes=[("cvo", it)])
            it += 1
        c.flush()


def run_ffn_bf16(c, x_d, gam_row, wg_b, wu_b, wd_b, ff):
    TB, NTT = 512, 4
    HC = ff // 128
    HG = HC // 4
    with ExitStack() as pes:
        xt = [c.sb("xt", [128, D], es=pes) for _ in range(NTT)]
        gam = c.sb("gam", [128, D], es=pes)
        junk = c.sb("junk", [128, D], es=pes)
        hbuf = c.sb("hbuf", [128, D], es=pes)
        stat = c.sb("stat", [128, 8], es=pes)
        hT = c.sb("hT", [128, KC, TB], BF16, es=pes)
        act = c.sb("act", [128, HC, TB], BF16, es=pes)
        wgs = [c.sb("wgs", [128, KC, 128], BF16, es=pes) for _ in range(3)]
        wus = [c.sb("wus", [128, KC, 128], BF16, es=pes) for _ in range(3)]
        wds = [c.sb("wds", [128, 4, 512], BF16, es=pes) for _ in range(3)]
        sg = [c.sb("sg", [128, TB], es=pes) for _ in range(2)]
        c.dma("sync", gam[:], gam_row.partition_broadcast(128), writes=["gam"])
        B = dict(junk=junk, stat=stat, hbuf=hbuf)
        wi = 0
        di = 0
        for tb in range(c.S // TB):
            for tt in range(NTT):
                r0 = tb * TB + tt * 128
                c.dma("sync", xt[tt][:], x_d[r0:r0 + 128, :], writes=[("xt", tt)])
                norm_to_hT(c, xt[tt][:], ("xt", tt), gam[:], "gam", hT, lambda g: ("hT", g), tt * 128, B)
            for hc in range(HC):
                s = wi % 3
                wi += 1
                c.dma("sync", wgs[s][:], wg_b[hc], writes=[("wg", s)])
                c.dma("pool", wus[s][:], wu_b[hc], writes=[("wu", s)])
                psg, pgk = c.psum()
                psu, puk = c.psum()
                for kc in range(KC):
                    c.emit("pe", lambda e, psg=psg, s=s, kc=kc: e.matmul(psg[:, :], lhsT=wgs[s][:, kc, :], rhs=hT[:, kc, :],
                                                                          start=(kc == 0), stop=(kc == KC - 1)),
                           reads=[("wg", s), ("hT", kc // 4)], writes=pgk)
                for kc in range(KC):
                    c.emit("pe", lambda e, psu=psu, s=s, kc=kc: e.matmul(psu[:, :], lhsT=wus[s][:, kc, :], rhs=hT[:, kc, :],
                                                                          start=(kc == 0), stop=(kc == KC - 1)),
                           reads=[("wu", s), ("hT", kc // 4)], writes=puk)
                ss = hc % 2
                c.emit("act", lambda e, psg=psg, ss=ss: e.activation(out=sg[ss][:], in_=psg[:, :], func=AF.Silu),
                       reads=pgk, writes=[("sg", ss)])
                c.emit("dve", lambda e, psu=psu, ss=ss, hc=hc: e.tensor_tensor(out=act[:, hc, :], in0=sg[ss][:], in1=psu[:, :],
                                                                               op=ALU.mult),
                       reads=puk + [("sg", ss)], writes=[("act", hc)])
            for db in range(D // 512):
                pds = [c.psum() for _ in range(NTT)]
                for hg in range(HG):
                    s = di % 3
                    di += 1
                    c.dma("sync" if di % 2 else "pool", wds[s][:], wd_b[hg, db], writes=[("wd", s)])
                    for tt in range(NTT):
                        pd, pdk = pds[tt]
                        for j in range(4):
                            hc = hg * 4 + j
                            c.emit("pe", lambda e, pd=pd, s=s, j=j, hc=hc, tt=tt: e.matmul(
                                pd[:, :], lhsT=act[:, hc, tt * 128:(tt + 1) * 128], rhs=wds[s][:, j, :],
                                start=(hc == 0), stop=(hc == HC - 1)),
                                reads=[("act", hc), ("wd", s)], writes=pdk)
                for tt in range(NTT):
                    pd, pdk = pds[tt]
                    c.emit("dve", lambda e, pd=pd, tt=tt, db=db: e.scalar_tensor_tensor(
                        out=xt[tt][:, db * 512:(db + 1) * 512], in0=pd[:, :], scalar=0.5,
                        in1=xt[tt][:, db * 512:(db + 1) * 512], op0=ALU.mult, op1=ALU.add),
                        reads=pdk + [("xt", tt)], writes=[("xt", tt)])
            for tt in range(NTT):
                r0 = tb * TB + tt * 128
                c.dma("pool", x_d[r0:r0 + 128, :], xt[tt][:], reads=[("xt", tt)], writes=[("xd", r0 // 128)])
        c.flush()


FF = 5632
DEPTH = 4
FULL_SHAPES = dict(
    norm_g=(4, 4, D), ffn_w_gate=(4, 2, D, FF), ffn_w_up=(4, 2, D, FF), ffn_w_down=(4, 2, FF, D),
    ple_w_up=(4, 256, D), ple_w_gate=(4, D, D), kv_norm_g=(1, D), w_kv=(D, 1024), attn_w_q=(2, D, D),
    attn_w_o=(2, D, D), attn_sinks=(2, NH), final_norm_g=(1, D))
FULL_SHAPES.update(RWKV_SHAPES)


def build_full(S, ff=FF, layers=(0, 1, 2, 3)):
    nc = bass.Bass("TRN2", target_bir_lowering=False)
    shapes = dict(FULL_SHAPES)
    shapes["ffn_w_gate"] = (4, 2, D, ff)
    shapes["ffn_w_up"] = (4, 2, D, ff)
    shapes["ffn_w_down"] = (4, 2, ff, D)
    I = {k: nc.dram_tensor(k, list(v), F32, kind="ExternalInput").ap() for k, v in shapes.items()}
    x_in = nc.dram_tensor("x", [S, D], F32, kind="ExternalInput").ap()
    p_in = nc.dram_tensor("p", [DEPTH, S, 256], F32, kind="ExternalInput").ap()
    pos = nc.dram_tensor("positions", [S], I32, kind="ExternalInput").ap()
    y = nc.dram_tensor("y", [S, D], F32, kind="ExternalOutput").ap()
    scr = lambda n, shp=None: nc.dram_tensor("scr_" + n, list(shp or [S, D]), F32).ap()
    xs = scr("x")
    O = {k: scr(k) for k in ["r", "k", "v", "sw", "a", "g", "sv", "At", "Rt", "Bh", "Kh", "Bt", "Kt", "bonus", "y"]}
    O["el"] = scr("el", [S // 128, D])
    V0, V1 = scr("V0"), scr("V1")
    cos_d, sin_d = scr("cos", [S, 32]), scr("sin", [S, 32])
    kT_d, v_d = scr("kT", [8, 64, S]), scr("vkv", [S, 512])
    HC = ff // 128
    WB = {}
    for i in layers:
        for hf in range(2):
            WB[(i, hf)] = (nc.dram_tensor("wgb_%d_%d" % (i, hf), [HC, 128, KC, 128], BF16).ap(),
                           nc.dram_tensor("wub_%d_%d" % (i, hf), [HC, 128, KC, 128], BF16).ap(),
                           nc.dram_tensor("wdb_%d_%d" % (i, hf), [HC // 4, 4, 128, 4, 512], BF16).ap())
    with ExitStack() as es:
        c = Ctx(nc, es, S)
        make_consts(c)
        make_masks(c)
        copy_rows(c, xs, x_in, S)
        rope_tables_phase(c, pos, cos_d, sin_d)
        for i in layers:
            for hf in range(2):
                convert_ffn_weights_phase(c, I["ffn_w_gate"][i, hf], I["ffn_w_up"][i, hf], I["ffn_w_down"][i, hf],
                                          WB[(i, hf)][0], WB[(i, hf)][1], WB[(i, hf)][2], ff)
        for i in layers:
            if i == 2:
                kv_phase(c, xs, I["kv_norm_g"], I["w_kv"], cos_d, sin_d, kT_d, v_d)
            run_ffn_bf16(c, xs, I["norm_g"][i, 0:1, :], WB[(i, 0)][0], WB[(i, 0)][1], WB[(i, 0)][2], ff)
            if i < 2:
                P = rwkv_param_aps(I, i, i == 1)
                P["g"] = I["norm_g"][i, 1:2, :]
                Oi = dict(O)
                Oi["V"] = V0 if i == 0 else V1
                Oi["vf"] = V0
                rwkv_proj_phase(c, xs, P, Oi, i == 1)
                rwkv_prep_phase(c, P, Oi, i == 1)
                rwkv_scan_phase(c, Oi)
                rwkv_post_phase(c, xs, P, Oi)
            else:
                j = i - 2
                attn_phase(c, xs, I["norm_g"][i, 1:2, :], I["attn_w_q"][j], I["attn_w_o"][j], I["attn_sinks"][j:j + 1, :],
                           cos_d, sin_d, kT_d, v_d)
            run_ffn_bf16(c, xs, I["norm_g"][i, 2:3, :], WB[(i, 1)][0], WB[(i, 1)][1], WB[(i, 1)][2], ff)
            ple_phase(c, xs, I["norm_g"][i, 3:4, :], I["ple_w_gate"][i], I["ple_w_up"][i], p_in[i])
        final_norm_phase(c, xs, I["final_norm_g"], y)
    return nc


N_CORES = 4
_NC_CACHE = {}


def kernel(**inputs):
    x = np.ascontiguousarray(inputs["x"], dtype=np.float32)
    B, S, _ = x.shape
    if S not in _NC_CACHE:
        _NC_CACHE[S] = build_full(S)
    nc = _NC_CACHE[S]
    shared = {}
    for k, shp in FULL_SHAPES.items():
        shared[k] = np.ascontiguousarray(np.asarray(inputs[k], dtype=np.float32).reshape(shp))
    p = np.asarray(inputs["p"], dtype=np.float32)
    pos = np.asarray(inputs["positions"], dtype=np.int32)
    in_maps = []
    for b in range(B):
        m = dict(shared)
        m["x"] = x[b]
        m["p"] = np.ascontiguousarray(p[:, b])
        m["positions"] = np.ascontiguousarray(pos[b])
        in_maps.append(m)
    res = run_bass_kernel_spmd(nc, in_maps, core_ids=list(range(B)))
    return np.stack([np.asarray(r["y"], dtype=np.float32) for r in res.results], axis=0)
```

```python
import numpy as np
import os
DUMP = int(os.environ.get('DUMP', '0'))
INV_SUB = int(os.environ.get('INV_SUB', '9'))
SCAN_DBG = float(os.environ.get('SCAN_DBG', '99'))
from contextlib import ExitStack
import concourse.bass as bass
import concourse.mybir as mybir
from concourse.bass_utils import run_bass_kernel_spmd

F32 = mybir.dt.float32
BF16 = mybir.dt.bfloat16
I32 = mybir.dt.int32
ALU = mybir.AluOpType
AF = mybir.ActivationFunctionType
AX = mybir.AxisListType

D = 2048
KC = D // 128
NH = 32
HD = 64
RMS_EPS = 1e-6
GN_EPS = 64e-5

ENGS = ("act", "dve", "pool", "pe", "sync")
NS_DMA = 8


class Sched:
    def __init__(self):
        self.ops = {e: [] for e in ENGS}
        self.lastw = {}
        self.readers = {}

    def emit(self, eng, fn, reads=(), writes=(), dma=False):
        idx = len(self.ops[eng])
        me = (eng, idx)
        deps = set()
        for k in tuple(reads) + tuple(writes):
            w = self.lastw.get(k)
            if w is not None:
                deps.add(w)
        for k in writes:
            rd = self.readers.get(k)
            if rd:
                deps.update(rd)
            self.readers[k] = []
        for k in reads:
            lst = self.readers.setdefault(k, [])
            if not dma:
                lst[:] = [r for r in lst if not (r[0] == eng and not self.ops[eng][r[1]][2])]
            lst.append(me)
        for k in writes:
            self.lastw[k] = me
        deps.discard(me)
        if eng == "pe":
            deps = {d for d in deps if d[0] != "pe"}
        self.ops[eng].append((fn, deps, dma))

    def setup(self, nc, es):
        self.nc = nc
        self.sems = {e: es.enter_context(nc.semaphore("s_" + e)) for e in ENGS}
        self.dsems = {e: [es.enter_context(nc.semaphore("d_%s%d" % (e, i))) for i in range(NS_DMA)]
                      for e in ("sync", "pool", "act")}
        self.cnt = {e: 0 for e in ENGS}
        self.dcnt = {e: 0 for e in ENGS}
        self.waited = {e: {} for e in ENGS}
        self.dlast = {e: [] for e in ENGS}

    def flush(self):
        nc = self.nc
        sems, dsems = self.sems, self.dsems
        signaled = {e: set() for e in ENGS}
        for e, lst in self.ops.items():
            for fn, deps, dma in lst:
                for (e2, i2) in deps:
                    signaled[e2].add(i2)
        val = {}
        for e in ENGS:
            c = self.cnt[e]
            j = self.dcnt[e]
            for i, (fn, deps, dma) in enumerate(self.ops[e]):
                if dma:
                    val[(e, i)] = (dsems[e][j % NS_DMA], 16 * (j // NS_DMA + 1))
                    j += 1
                elif i in signaled[e]:
                    c += 1
                    val[(e, i)] = (sems[e], c)
            self.cnt[e] = c
            self.dcnt[e] = j

        def run(e, engobj):
            waited = self.waited[e]
            dlast = self.dlast[e]
            for i, (fn, deps, dma) in enumerate(self.ops[e]):
                need = {}

                def want(s, v):
                    key = id(s)
                    if waited.get(key, 0) >= v:
                        return
                    if key not in need or need[key][1] < v:
                        need[key] = (s, v)
                for d in deps:
                    want(*val[d])
                if dma and len(dlast) >= NS_DMA:
                    want(*dlast[-NS_DMA])
                for key, (s, v) in need.items():
                    engobj.wait_ge(s, v)
                    waited[key] = v
                    if DUMP:
                        print("W", e, i, getattr(s, "name", s), v)
                ins = fn(engobj)
                if DUMP:
                    print("I", e, i, "dma" if dma else "", "SIG" if (dma or i in signaled[e]) else "", getattr(val.get((e, i), ("", ""))[0], "name", "-"), val.get((e, i), ("", ""))[1], str(ins)[:150].replace("\n", " "))
                if dma:
                    ins.then_inc(val[(e, i)][0], 16)
                    dlast.append(val[(e, i)])
                    del dlast[:-NS_DMA]
                elif i in signaled[e]:
                    ins.then_inc(sems[e], 1)
            for (s, v) in dlast:
                if waited.get(id(s), 0) < v:
                    engobj.wait_ge(s, v)
                    waited[id(s)] = v

        if DUMP:
            print("F flush")
        with nc.Block() as block:
            @block.sync
            def _(eng):
                run("sync", eng)

            @block.gpsimd
            def _(eng):
                run("pool", eng)

            @block.scalar
            def _(eng):
                run("act", eng)

            @block.vector
            def _(eng):
                run("dve", eng)

            @block.tensor
            def _(eng):
                run("pe", eng)
        nc.all_engine_barrier()
        self.ops = {e: [] for e in ENGS}
        self.lastw = {}
        self.readers = {}


class Ctx:
    def __init__(self, nc, es, S):
        self.nc = nc
        self.es = es
        self.S = S
        self.NT = S // 128
        self.sch = Sched()
        self.sch.setup(nc, es)
        self.ps = [es.enter_context(nc.psum_tensor("ps%d" % i, [128, 512], F32)) for i in range(8)]
        self.ps_i = 0
        self.uid = 0

    def sb(self, name, shape, dt=F32, es=None):
        self.uid += 1
        return (es or self.es).enter_context(self.nc.sbuf_tensor("%s_%d" % (name, self.uid), shape, dt))

    def flush(self):
        self.sch.flush()

    def psum(self):
        i = self.ps_i
        self.ps_i = (i + 1) % 8
        return self.ps[i], [("ps", i, q) for q in range(4)]

    @staticmethod
    def q(keys, lo, hi):
        return list(keys)

    def wslot(self, name):
        d = self.__dict__.setdefault("_ws", {})
        d[name] = d.get(name, -1) + 1
        return d[name] % 2

    def emit(self, eng, fn, reads=(), writes=()):
        self.sch.emit(eng, fn, reads, writes)

    def dma(self, eng, out, in_, reads=(), writes=()):
        self.sch.emit(eng, lambda e: e.dma_start(out=out, in_=in_), reads, writes, dma=True)


def make_consts(c):
    ident = c.sb("ident", [128, 128])
    c.emit("pool", lambda e: e.memset(ident[:], 0.0), writes=["const"])
    c.emit("pool", lambda e: e.affine_select(out=ident[:], in_=ident[:], compare_op=ALU.not_equal, fill=1.0,
                                             base=0, pattern=[[-1, 128]], channel_multiplier=1),
           writes=["const"])
    c.ident = ident


def transpose_tile(c, src, skey, dstT, dkeyf, tcol, nchunks=KC, evac=("act", "dve")):
    ident = c.ident
    for g in range(0, nchunks, 4):
        n = min(4, nchunks - g)
        ps, pk = c.psum()
        for j in range(n):
            kc = g + j
            c.emit("pe", lambda e, ps=ps, j=j, kc=kc: e.transpose(ps[:, j * 128:(j + 1) * 128],
                                                                     src[:, kc * 128:(kc + 1) * 128], ident[:]),
                   reads=[skey, "const"], writes=pk)
        eng = evac[(g // 4) % len(evac)]
        dst = dstT[:, g:g + n, tcol:tcol + 128]
        srcp = ps[:, 0:n * 128].rearrange("p (a b) -> p a b", a=n)
        if eng == "act":
            c.emit("act", lambda e, dst=dst, srcp=srcp: e.copy(out=dst, in_=srcp), reads=pk, writes=[dkeyf(g // 4)])
        else:
            c.emit("dve", lambda e, dst=dst, srcp=srcp: e.tensor_copy(out=dst, in_=srcp), reads=pk,
                   writes=[dkeyf(g // 4)])


def norm_to_hT(c, xt, xkey, gam, gkey, hT, hkeyf, tcol, B):
    junk, stat, hbuf = B["junk"], B["stat"], B["hbuf"]
    c.emit("act", lambda e: e.activation(out=junk[:], in_=xt, func=AF.Square, accum_out=stat[:, 0:1]),
           reads=[xkey], writes=["junk", "stat"])
    c.emit("dve", lambda e: e.tensor_scalar(out=stat[:, 1:2], in0=stat[:, 0:1], scalar1=1.0 / D, scalar2=RMS_EPS,
                                            op0=ALU.mult, op1=ALU.add), reads=["stat"], writes=["stat"])
    c.emit("act", lambda e: e.activation(out=stat[:, 3:4], in_=stat[:, 1:2], func=AF.Sqrt), reads=["stat"], writes=["stat"])
    c.emit("dve", lambda e: e.reciprocal(out=stat[:, 2:3], in_=stat[:, 3:4]), reads=["stat"], writes=["stat"])
    c.emit("dve", lambda e: e.scalar_tensor_tensor(out=hbuf[:], in0=xt, scalar=stat[:, 2:3], in1=gam,
                                                   op0=ALU.mult, op1=ALU.mult),
           reads=[xkey, "stat", gkey], writes=["hbuf"])
    transpose_tile(c, hbuf, "hbuf", hT, hkeyf, tcol)


def ffn_phase(c, x_d, gam_row, wg, wu, wd, FF, B):
    TB = 256
    NTT = TB // 128
    HC = FF // 128
    xt, gam, hT, act, wgs, wus, wds, sg = (B[k] for k in ("xt", "gam", "hT", "act", "wgs", "wus", "wds", "sg"))
    c.dma("sync", gam[:], gam_row.partition_broadcast(128), writes=["gam"])
    wg_v = wg.rearrange("(kc p) f -> p kc f", p=128)
    wu_v = wu.rearrange("(kc p) f -> p kc f", p=128)
    wd_v = wd.rearrange("(hc p) d -> p hc d", p=128)
    GH = 4
    assert HC % GH == 0
    hkeyf = lambda g: ("hT", g)
    for tb in range(c.S // TB):
        for tt in range(NTT):
            r0 = tb * TB + tt * 128
            c.dma("sync", xt[tt][:], x_d[r0:r0 + 128, :], reads=[("xd", r0 // 128)], writes=[("xt", tt)])
            norm_to_hT(c, xt[tt][:], ("xt", tt), gam[:], "gam", hT, hkeyf, tt * 128, B)
        for hc in range(HC):
            s = hc % 2
            c.dma("sync", wgs[s][:], wg_v[:, :, hc * 128:(hc + 1) * 128], writes=[("wg", s)])
            c.dma("pool", wus[s][:], wu_v[:, :, hc * 128:(hc + 1) * 128], writes=[("wu", s)])
            psg, pgk = c.psum()
            psu, puk = c.psum()
            for kc in range(KC):
                c.emit("pe", lambda e, psg=psg, s=s, kc=kc: e.matmul(psg[:, 0:TB], lhsT=wgs[s][:, kc, :], rhs=hT[:, kc, 0:TB],
                                                                      start=(kc == 0), stop=(kc == KC - 1)),
                       reads=[("wg", s), ("hT", kc // 4)], writes=pgk)
            for kc in range(KC):
                c.emit("pe", lambda e, psu=psu, s=s, kc=kc: e.matmul(psu[:, 0:TB], lhsT=wus[s][:, kc, :], rhs=hT[:, kc, 0:TB],
                                                                      start=(kc == 0), stop=(kc == KC - 1)),
                       reads=[("wu", s), ("hT", kc // 4)], writes=puk)
            c.emit("act", lambda e, psg=psg, s=s: e.activation(out=sg[s][:], in_=psg[:, 0:TB], func=AF.Silu),
                   reads=pgk, writes=[("sg", s)])
            c.emit("dve", lambda e, psu=psu, s=s, hc=hc: e.tensor_tensor(out=act[:, hc, :], in0=sg[s][:], in1=psu[:, 0:TB],
                                                                          op=ALU.mult),
                   reads=puk + [("sg", s)], writes=[("act", hc)])
        for db in range(D // 512):
            pds = [c.psum() for _ in range(NTT)]
            for hg in range(HC // GH):
                s = (db * (HC // GH) + hg) % 2
                q = "sync" if hg % 2 == 0 else "pool"
                c.dma(q, wds[s][:], wd_v[:, hg * GH:(hg + 1) * GH, db * 512:(db + 1) * 512], writes=[("wd", s)])
                for tt in range(NTT):
                    pd, pdk = pds[tt]
                    for j in range(GH):
                        hc = hg * GH + j
                        c.emit("pe", lambda e, pd=pd, s=s, j=j, hc=hc, tt=tt: e.matmul(
                            pd[:, :], lhsT=act[:, hc, tt * 128:(tt + 1) * 128], rhs=wds[s][:, j, :],
                            start=(hc == 0), stop=(hc == HC - 1)),
                            reads=[("act", hc), ("wd", s)], writes=pdk)
            for tt in range(NTT):
                pd, pdk = pds[tt]
                c.emit("dve", lambda e, pd=pd, tt=tt, db=db: e.scalar_tensor_tensor(
                    out=xt[tt][:, db * 512:(db + 1) * 512], in0=pd[:, :], scalar=0.5,
                    in1=xt[tt][:, db * 512:(db + 1) * 512], op0=ALU.mult, op1=ALU.add),
                    reads=pdk + [("xt", tt)], writes=[("xt", tt)])
        for tt in range(NTT):
            r0 = tb * TB + tt * 128
            c.dma("sync", x_d[r0:r0 + 128, :], xt[tt][:], reads=[("xt", tt)], writes=[("xd", r0 // 128)])


def alloc_common(c, es):
    B = {}
    B["xt"] = [c.sb("xt%d" % i, [128, D], es=es) for i in range(2)]
    B["gam"] = c.sb("gam", [128, D], es=es)
    B["junk"] = c.sb("junk", [128, D], es=es)
    B["hbuf"] = c.sb("hbuf", [128, D], es=es)
    B["stat"] = c.sb("stat", [128, 8], es=es)
    B["hT"] = c.sb("hT", [128, KC, 256], es=es)
    return B


def run_ffn(c, x_d, gam_row, wg, wu, wd, FF):
    with ExitStack() as pes:
        B = alloc_common(c, pes)
        B["act"] = c.sb("act", [128, FF // 128, 256], es=pes)
        B["wgs"] = [c.sb("wgs%d" % i, [128, KC, 128], es=pes) for i in range(2)]
        B["wus"] = [c.sb("wus%d" % i, [128, KC, 128], es=pes) for i in range(2)]
        B["wds"] = [c.sb("wds%d" % i, [128, 4, 512], es=pes) for i in range(2)]
        B["sg"] = [c.sb("sg%d" % i, [128, 256], es=pes) for i in range(2)]
        ffn_phase(c, x_d, gam_row, wg, wu, wd, FF, B)
        c.flush()


def copy_rows(c, dst, src, S):
    for t in range(S // 128):
        c.dma("sync", dst[t * 128:(t + 1) * 128, :], src[t * 128:(t + 1) * 128, :])
    c.flush()


def test_ffn_program(S, FF):
    nc = bass.Bass("TRN2", target_bir_lowering=False)
    x_in = nc.dram_tensor("x", [S, D], F32, kind="ExternalInput").ap()
    g_in = nc.dram_tensor("g", [1, D], F32, kind="ExternalInput").ap()
    wg = nc.dram_tensor("wg", [D, FF], F32, kind="ExternalInput").ap()
    wu = nc.dram_tensor("wu", [D, FF], F32, kind="ExternalInput").ap()
    wd = nc.dram_tensor("wd", [FF, D], F32, kind="ExternalInput").ap()
    y = nc.dram_tensor("y", [S, D], F32, kind="ExternalOutput").ap()
    xs = nc.dram_tensor("xs", [S, D], F32).ap()
    with ExitStack() as es:
        c = Ctx(nc, es, S)
        make_consts(c)
        copy_rows(c, xs, x_in, S)
        run_ffn(c, xs, g_in, wg, wu, wd, FF)
        run_ffn(c, xs, g_in, wg, wu, wd, FF)
        copy_rows(c, y, xs, S)
    return nc


def proj(c, xT, xkeyf, ntt, W, N, wbufs, wname, epi, kcn=KC, CW=256, qi=[0]):
    Wv = W.rearrange("(kc p) n -> p kc n", p=128)
    for cb in range(N // CW):
        s = c.wslot(wname)
        qi[0] += 1
        c.dma("sync" if qi[0] % 2 else "pool", wbufs[s][:, 0:kcn, 0:CW], Wv[:, :, cb * CW:(cb + 1) * CW],
              writes=[(wname, s)])
        ps, pk = c.psum()
        for tt in range(ntt):
            for kc in range(kcn):
                c.emit("pe", lambda e, ps=ps, tt=tt, kc=kc, s=s: e.matmul(
                    ps[:, tt * CW:(tt + 1) * CW], lhsT=xT[:, kc, tt * 128:(tt + 1) * 128], rhs=wbufs[s][:, kc, 0:CW],
                    start=(kc == 0), stop=(kc == kcn - 1)),
                    reads=[(wname, s)] + xkeyf(kc), writes=c.q(pk, tt * CW, (tt + 1) * CW))
        epi(cb, ps, pk)


def store_epi(c, out_d, r0, ntt, ost, func, CW=256):
    def epi(cb, ps, pk):
        s = c.wslot("ost")
        c.emit("act", lambda e: e.activation(out=ost[s][:, 0:ntt * CW], in_=ps[:, 0:ntt * CW], func=func),
               reads=c.q(pk, 0, ntt * CW), writes=[("ost", s)])
        dst = out_d[r0:r0 + ntt * 128, cb * CW:(cb + 1) * CW].rearrange("(tt p) n -> p tt n", p=128)
        c.dma("sync", dst, ost[s][:, 0:ntt * CW].rearrange("p (tt n) -> p tt n", tt=ntt),
              reads=[("ost", s)], writes=[("od", id(out_d), r0, cb)])
    return epi


def load_rows_T(c, rows_ap, R, dst, pes):
    tmp = c.sb("rowsT", [128, D], es=pes)
    c.dma("sync", tmp[0:R, :], rows_ap, writes=["rowsT"])
    for g in range(0, KC, 4):
        ps, pk = c.psum()
        for j in range(4):
            kc = g + j
            c.emit("pe", lambda e, ps=ps, j=j, kc=kc: e.transpose(ps[:, j * R:(j + 1) * R],
                                                                     tmp[0:R, kc * 128:(kc + 1) * 128], c.ident[0:R, 0:R]),
                   reads=["rowsT", "const"], writes=pk)
        c.emit("dve", lambda e, ps=ps, g=g: e.tensor_copy(out=dst[:, g:g + 4, 0:R],
                                                         in_=ps[:, 0:4 * R].rearrange("p (a b) -> p a b", a=4)),
               reads=pk, writes=["rowsTd"])


C0 = -float(np.exp(-0.5))


def rwkv_proj_phase(c, x_d, P, O, has_vmix):
    TB, NTT = 256, 2
    with ExitStack() as pes:
        xt = c.sb("xt", [128, D], es=pes)
        gam = c.sb("gam", [128, D], es=pes)
        stat = c.sb("stat", [128, 8], es=pes)
        hTw = c.sb("hTw", [128, KC, 257], es=pes)
        dxT = c.sb("dxT", [128, KC, 256], es=pes)
        xcT = [c.sb("xcT", [128, KC, 256], es=pes) for _ in range(2)]
        wb = [c.sb("wb", [128, KC, 256], es=pes) for _ in range(2)]
        muT = c.sb("muT", [128, KC, 6], es=pes)
        w2a = c.sb("w2a", [128, D], es=pes)
        a2a = c.sb("a2a", [128, D], es=pes)
        v2a = c.sb("v2a", [128, D], es=pes)
        g2s = c.sb("g2s", [128, 2, D], es=pes)
        t1 = {k: c.sb("t1" + k, [128, 256], es=pes) for k in "wav"}
        t1g = c.sb("t1g", [128, 2, 256], es=pes)
        ost = [c.sb("ost", [128, 512], es=pes) for _ in range(2)]
        B = dict(junk=dxT[:, 0:8, :].rearrange("p a b -> p (a b)"), stat=stat,
                 hbuf=xcT[1][:, 0:8, :].rearrange("p a b -> p (a b)"))
        c.dma("sync", gam[:], P["g"].partition_broadcast(128), writes=["gam"])
        load_rows_T(c, P["mu"], 6, muT, pes)
        c.dma("sync", w2a[0:96, :], P["w2"], writes=["w2a"])
        c.dma("sync", w2a[96:97, :], P["w0"], writes=["w2a"])
        c.dma("sync", a2a[0:96, :], P["a2"], writes=["a2a"])
        c.dma("sync", a2a[96:97, :], P["a0"], writes=["a2a"])
        if has_vmix:
            c.dma("sync", v2a[0:64, :], P["v2"], writes=["v2a"])
            c.dma("sync", v2a[64:65, :], P["v0"], writes=["v2a"])
        c.dma("sync", g2s[:], P["g2"].rearrange("(ch p) n -> p ch n", p=128), writes=["g2s"])
        for k in "wav":
            c.emit("dve", lambda e, k=k: e.memset(t1[k][:], 1.0), writes=[("t1", k)])
        c.emit("dve", lambda e: e.memset(hTw[:, :, 0:1], 0.0), writes=[("hT", g) for g in range(4)])

        hkeys = [("hT", g) for g in range(4)]
        for tb in range(c.S // TB):
            r0 = tb * TB
            for tt in range(NTT):
                c.dma("sync", xt[:], x_d[r0 + tt * 128:r0 + (tt + 1) * 128, :], writes=["xt"])
                junk = B["junk"]
                c.emit("act", lambda e: e.activation(out=junk, in_=xt[:], func=AF.Square, accum_out=stat[:, 0:1]),
                       reads=["xt"], writes=["dxT", "stat"])
                c.emit("dve", lambda e: e.tensor_scalar(out=stat[:, 1:2], in0=stat[:, 0:1], scalar1=1.0 / D,
                                                        scalar2=RMS_EPS, op0=ALU.mult, op1=ALU.add),
                       reads=["stat"], writes=["stat"])
                c.emit("act", lambda e: e.activation(out=stat[:, 3:4], in_=stat[:, 1:2], func=AF.Sqrt),
                       reads=["stat"], writes=["stat"])
                c.emit("dve", lambda e: e.reciprocal(out=stat[:, 2:3], in_=stat[:, 3:4]), reads=["stat"], writes=["stat"])
                hb = B["hbuf"]
                c.emit("dve", lambda e: e.scalar_tensor_tensor(out=hb, in0=xt[:], scalar=stat[:, 2:3], in1=gam[:],
                                                               op0=ALU.mult, op1=ALU.mult),
                       reads=["xt", "stat", "gam"], writes=[("xc", 1, g) for g in range(4)])
                ident = c.ident
                for g in range(0, KC, 4):
                    ps, pk = c.psum()
                    for j in range(4):
                        kc = g + j
                        c.emit("pe", lambda e, ps=ps, j=j, kc=kc: e.transpose(
                            ps[:, j * 128:(j + 1) * 128], hb[:, kc * 128:(kc + 1) * 128], ident[:]),
                            reads=[("xc", 1, q) for q in range(4)] + ["const"], writes=pk)
                    dst = hTw[:, g:g + 4, 1 + tt * 128:1 + (tt + 1) * 128]
                    srcp = ps[:, :].rearrange("p (a b) -> p a b", a=4)
                    if (g // 4) % 2 == 0:
                        c.emit("act", lambda e, dst=dst, srcp=srcp: e.copy(out=dst, in_=srcp), reads=pk,
                               writes=[("hT", g // 4)])
                    else:
                        c.emit("dve", lambda e, dst=dst, srcp=srcp: e.tensor_copy(out=dst, in_=srcp), reads=pk,
                               writes=[("hT", g // 4)])
            c.emit("dve", lambda e: e.tensor_tensor(out=dxT[:], in0=hTw[:, :, 0:256], in1=hTw[:, :, 1:257],
                                                    op=ALU.subtract), reads=hkeys, writes=["dxT"])

            def mix(m):
                s = c.wslot("xc")
                for kc in range(KC):
                    c.emit("dve", lambda e, s=s, kc=kc, m=m: e.scalar_tensor_tensor(
                        out=xcT[s][:, kc, :], in0=dxT[:, kc, :], scalar=muT[:, kc, m:m + 1], in1=hTw[:, kc, 1:257],
                        op0=ALU.mult, op1=ALU.add),
                        reads=["dxT", "rowsTd", ("hT", kc // 4)], writes=[("xc", s, kc // 4)])
                return xcT[s], (lambda kc, s=s: [("xc", s, kc // 4)])

            def lora(xT, xkf, w1, R, func, t1tile, t1key, nch=1):
                s = c.wslot("wb")
                c.dma("pool", wb[s][:, :, 0:R * nch], w1.rearrange("(kc p) r -> p kc r", p=128), writes=[("wb", s)])
                for ch in range(nch):
                    ps, pk = c.psum()
                    for kc in range(KC):
                        c.emit("pe", lambda e, ps=ps, kc=kc, s=s, ch=ch: e.matmul(
                            ps[0:R, 0:256], lhsT=wb[s][:, kc, ch * R:(ch + 1) * R], rhs=xT[:, kc, 0:256],
                            start=(kc == 0), stop=(kc == KC - 1)),
                            reads=[("wb", s)] + xkf(kc), writes=c.q(pk, 0, 256))
                    dst = t1tile[0:R, :] if nch == 1 else t1tile[0:R, ch, :]
                    c.emit("act", lambda e, ps=ps, dst=dst: e.activation(out=dst, in_=ps[0:R, 0:256], func=func),
                           reads=c.q(pk, 0, 256), writes=[t1key])

            def lora2(t1tile, t1key, K, w2tile, w2key, out_d, func, nch=1):
                epi = store_epi(c, out_d, r0, NTT, ost, func)
                for cb in range(D // 256):
                    ps, pk = c.psum()
                    for tt in range(NTT):
                        for ch in range(nch):
                            lhsT = t1tile[0:K, tt * 128:(tt + 1) * 128] if nch == 1 else t1tile[0:K, ch, tt * 128:(tt + 1) * 128]
                            rhs = w2tile[0:K, cb * 256:(cb + 1) * 256] if nch == 1 else w2tile[0:K, ch, cb * 256:(cb + 1) * 256]
                            c.emit("pe", lambda e, ps=ps, tt=tt, lhsT=lhsT, rhs=rhs, ch=ch: e.matmul(
                                ps[:, tt * 256:(tt + 1) * 256], lhsT=lhsT, rhs=rhs, start=(ch == 0), stop=(ch == nch - 1)),
                                reads=[t1key, w2key], writes=c.q(pk, tt * 256, (tt + 1) * 256))
                    epi(cb, ps, pk)

            xT, xkf = mix(0)
            proj(c, xT, xkf, NTT, P["wr"], D, wb, "wb", store_epi(c, O["r"], r0, NTT, ost, AF.Copy))
            xT, xkf = mix(2)
            proj(c, xT, xkf, NTT, P["wk"], D, wb, "wb", store_epi(c, O["k"], r0, NTT, ost, AF.Copy))
            xT, xkf = mix(3)
            proj(c, xT, xkf, NTT, P["wv"], D, wb, "wb", store_epi(c, O["v"], r0, NTT, ost, AF.Copy))
            if has_vmix:
                lora(xT, xkf, P["v1"], 64, AF.Copy, t1["v"], ("t1", "v"))
                lora2(t1["v"], ("t1", "v"), 65, v2a, "v2a", O["sv"], AF.Sigmoid)
            xT, xkf = mix(1)
            lora(xT, xkf, P["w1"], 96, AF.Tanh, t1["w"], ("t1", "w"))
            lora2(t1["w"], ("t1", "w"), 97, w2a, "w2a", O["sw"], AF.Sigmoid)
            xT, xkf = mix(4)
            lora(xT, xkf, P["a1"], 96, AF.Copy, t1["a"], ("t1", "a"))
            lora2(t1["a"], ("t1", "a"), 97, a2a, "a2a", O["a"], AF.Sigmoid)
            xT, xkf = mix(5)
            lora(xT, xkf, P["g1"], 128, AF.Sigmoid, t1g, ("t1", "g"), nch=2)
            lora2(t1g, ("t1", "g"), 128, g2s, "g2s", O["g"], AF.Copy, nch=2)
            c.emit("dve", lambda e: e.tensor_copy(out=hTw[:, :, 0:1], in_=hTw[:, :, 256:257]), reads=hkeys, writes=hkeys)
        c.flush()


def rwkv_param_aps(nc_inputs, j, has_vmix):
    I = nc_inputs
    P = dict(
        mu=I["rwkv_mu"][j], wr=I["rwkv_w_rkv"][j, 0], wk=I["rwkv_w_rkv"][j, 1], wv=I["rwkv_w_rkv"][j, 2],
        wo=I["rwkv_w_o"][j], w0=I["rwkv_w0"][j:j + 1, :], w1=I["rwkv_w1"][j], w2=I["rwkv_w2"][j],
        a0=I["rwkv_a0"][j:j + 1, :], a1=I["rwkv_a1"][j], a2=I["rwkv_a2"][j], g1=I["rwkv_g1"][j], g2=I["rwkv_g2"][j],
        k_k=I["rwkv_k_k"][j:j + 1, :], k_a=I["rwkv_k_a"][j:j + 1, :],
        r_k=I["rwkv_r_k"][j:j + 1].rearrange("o h n -> o (h n)"),
        gn_g=I["rwkv_gn_g"][j:j + 1, :], gn_b=I["rwkv_gn_b"][j:j + 1, :])
    if has_vmix:
        P.update(v0=I["rwkv_v0"][j - 1:j, :], v1=I["rwkv_v1"][j - 1], v2=I["rwkv_v2"][j - 1])
    return P


RWKV_SHAPES = dict(
    rwkv_mu=(2, 6, D), rwkv_w_rkv=(2, 3, D, D), rwkv_w_o=(2, D, D), rwkv_w0=(2, D), rwkv_w1=(2, D, 96),
    rwkv_w2=(2, 96, D), rwkv_a0=(2, D), rwkv_a1=(2, D, 96), rwkv_a2=(2, 96, D), rwkv_v0=(1, D), rwkv_v1=(1, D, 64),
    rwkv_v2=(1, 64, D), rwkv_g1=(2, D, 256), rwkv_g2=(2, 256, D), rwkv_k_k=(2, D), rwkv_k_a=(2, D),
    rwkv_r_k=(2, 32, 64), rwkv_gn_g=(2, D), rwkv_gn_b=(2, D))


def test_rwkv_program(S, stage):
    nc = bass.Bass("TRN2", target_bir_lowering=False)
    I = {k: nc.dram_tensor(k, list(v), F32, kind="ExternalInput").ap() for k, v in RWKV_SHAPES.items()}
    x_in = nc.dram_tensor("x", [S, D], F32, kind="ExternalInput").ap()
    g_in = nc.dram_tensor("g", [1, D], F32, kind="ExternalInput").ap()
    vf_in = nc.dram_tensor("vf", [S, D], F32, kind="ExternalInput").ap()
    names = ["r", "k", "v", "sw", "a", "g", "sv", "At", "Rt", "Bh", "Kh", "Bt", "Kt", "V", "bonus", "y", "xo"]
    O = {k: nc.dram_tensor("o_" + k, [S, D], F32, kind="ExternalOutput").ap() for k in names}
    O["el"] = nc.dram_tensor("o_el", [S // 128, D], F32, kind="ExternalOutput").ap()
    O["vf"] = vf_in
    with ExitStack() as es:
        c = Ctx(nc, es, S)
        make_consts(c)
        make_masks(c)
        P = rwkv_param_aps(I, 1, True)
        P["g"] = g_in
        copy_rows(c, O["xo"], x_in, S)
        rwkv_proj_phase(c, O["xo"], P, O, True)
        if stage >= 2:
            rwkv_prep_phase(c, P, O, True)
        if stage >= 3:
            rwkv_scan_phase(c, O)
        if stage >= 4:
            rwkv_post_phase(c, O["xo"], P, O)
    return nc


def make_masks(c):
    tri = c.sb("tri", [128, 128])
    ones = c.sb("ones", [128, 128])
    mask4 = c.sb("mask4", [128, 512])
    msl = c.sb("msl", [128, 128])
    c.emit("pool", lambda e: e.memset(ones[:], 1.0), writes=["const"])
    c.emit("pool", lambda e: e.memset(tri[:], 1.0), writes=["const"])
    c.emit("pool", lambda e: e.affine_select(out=tri[:], in_=tri[:], compare_op=ALU.is_ge, fill=0.0, base=0,
                                             pattern=[[1, 128]], channel_multiplier=-1), writes=["const"])
    c.emit("pool", lambda e: e.memset(mask4[:], 1.0), writes=["const"])
    for q in range(4):
        op = ALU.is_gt if q % 2 == 0 else ALU.is_ge
        c.emit("pool", lambda e, q=q, op=op: e.affine_select(
            out=mask4[:, q * 128:(q + 1) * 128], in_=mask4[:, q * 128:(q + 1) * 128], compare_op=op, fill=0.0, base=0,
            pattern=[[1, 128]], channel_multiplier=-1), writes=["const"])
    c.emit("pool", lambda e: e.memset(msl[:], 1.0), writes=["const"])
    c.emit("pool", lambda e: e.affine_select(out=msl[:], in_=msl[:], compare_op=ALU.is_gt, fill=0.0, base=0,
                                             pattern=[[-1, 128]], channel_multiplier=1), writes=["const"])
    c.tri, c.ones, c.mask4, c.msl = tri, ones, mask4, msl


def rwkv_prep_phase(c, P, O, has_vmix):
    CB = 512
    with ExitStack() as pes:
        names_in = ["r", "k", "v", "sw", "a"] + (["sv", "vf"] if has_vmix else [])
        ld = {n: [c.sb("ld" + n, [128, CB], es=pes) for _ in range(2)] for n in names_in}
        names_out = ["At", "Rt", "Bh", "Kh", "Bt", "Kt", "V", "bonus"]
        ob = {n: [c.sb("ob" + n, [128, CB], es=pes) for _ in range(2)] for n in names_out}
        tm = {n: c.sb("tm" + n, [128, CB], es=pes) for n in ["lw", "epos", "eneg", "eprev", "EL", "kk", "sq", "b", "t", "kh", "t2", "d"]}
        st = c.sb("st", [128, 64], es=pes)
        kkp = c.sb("kkp", [128, D], es=pes)
        kap = c.sb("kap", [128, D], es=pes)
        rkp = c.sb("rkp", [128, D], es=pes)
        c.dma("sync", kkp[:], P["k_k"].partition_broadcast(128), writes=["par"])
        c.dma("sync", kap[:], P["k_a"].partition_broadcast(128), writes=["par"])
        c.dma("sync", rkp[:], P["r_k"].partition_broadcast(128), writes=["par"])
        it = 0
        for ch in range(c.S // 128):
            rs = slice(ch * 128, (ch + 1) * 128)
            for cb in range(D // CB):
                cs = slice(cb * CB, (cb + 1) * CB)
                s = it % 2
                it += 1
                L = {}
                for i, n in enumerate(names_in):
                    c.dma("sync" if i % 2 == 0 else "pool", ld[n][s][:], O[n][rs, cs], writes=[("ld", n, s)])
                    L[n] = ld[n][s]
                lk = lambda *ns: [("ld", n, s) for n in ns]
                T = tm
                ok = lambda *ns: [("ob", n, s) for n in ns]
                OB = {n: ob[n][s] for n in names_out}
                c.emit("dve", lambda e, L=L: e.tensor_scalar(out=T["lw"][:], in0=L["sw"][:], scalar1=C0, scalar2=0.0,
                                                             op0=ALU.mult, op1=ALU.add), reads=lk("sw"), writes=["lw"])
                psc, pck = c.psum()
                pst, ptk = c.psum()
                c.emit("pe", lambda e, psc=psc: e.matmul(psc[:, :], lhsT=c.tri[:], rhs=T["lw"][:], start=True, stop=True),
                       reads=["lw", "const"], writes=pck)
                c.emit("pe", lambda e, pst=pst: e.matmul(pst[:, :], lhsT=c.ones[:], rhs=T["lw"][:], start=True, stop=True),
                       reads=["lw", "const"], writes=ptk)
                c.emit("act", lambda e, psc=psc: e.activation(out=T["epos"][:], in_=psc[:, :], func=AF.Exp), reads=pck, writes=["epos"])
                c.emit("act", lambda e, psc=psc: e.activation(out=T["eneg"][:], in_=psc[:, :], func=AF.Exp, scale=-1.0),
                       reads=pck, writes=["eneg"])
                c.emit("dve", lambda e, psc=psc: e.tensor_tensor(out=T["eprev"][:], in0=psc[:, :], in1=T["lw"][:], op=ALU.subtract),
                       reads=pck + ["lw"], writes=["eprev"])
                c.emit("act", lambda e: e.activation(out=T["eprev"][:], in_=T["eprev"][:], func=AF.Exp), reads=["eprev"], writes=["eprev"])
                c.emit("act", lambda e, pst=pst: e.activation(out=T["EL"][:], in_=pst[:, :], func=AF.Exp), reads=ptk, writes=["EL"])
                c.emit("pool", lambda e, L=L, cs=cs: e.tensor_tensor(out=T["kk"][:], in0=L["k"][:], in1=kkp[:, cs], op=ALU.mult),
                       reads=lk("k") + ["par"], writes=["kk"])
                c.emit("pool", lambda e: e.tensor_tensor(out=T["sq"][:], in0=T["kk"][:], in1=T["kk"][:], op=ALU.mult),
                       reads=["kk"], writes=["sq"])
                c.emit("dve", lambda e: e.tensor_reduce(out=st[:, 0:8], in_=T["sq"][:].rearrange("p (h j) -> p h j", j=64),
                                                        axis=AX.X, op=ALU.add), reads=["sq"], writes=["st"])
                c.emit("act", lambda e: e.activation(out=st[:, 8:16], in_=st[:, 0:8], func=AF.Sqrt), reads=["st"], writes=["st"])
                c.emit("dve", lambda e: e.tensor_scalar(out=st[:, 8:16], in0=st[:, 8:16], scalar1=1e-12, scalar2=0.0,
                                                        op0=ALU.max, op1=ALU.add), reads=["st"], writes=["st"])
                c.emit("dve", lambda e: e.reciprocal(out=st[:, 16:24], in_=st[:, 8:16]), reads=["st"], writes=["st"])
                c.emit("dve", lambda e: e.tensor_tensor(
                    out=T["kk"][:].rearrange("p (h j) -> p h j", j=64), in0=T["kk"][:].rearrange("p (h j) -> p h j", j=64),
                    in1=st[:, 16:24].unsqueeze(2).to_broadcast([128, 8, 64]), op=ALU.mult), reads=["kk", "st"], writes=["kk"])
                c.emit("dve", lambda e, OB=OB: e.scalar_tensor_tensor(out=OB["At"][:], in0=T["kk"][:], scalar=-1.0, in1=T["eprev"][:],
                                                                     op0=ALU.mult, op1=ALU.mult),
                       reads=["kk", "eprev"], writes=ok("At"))
                c.emit("pool", lambda e, L=L: e.tensor_tensor(out=T["b"][:], in0=T["kk"][:], in1=L["a"][:], op=ALU.mult),
                       reads=["kk"] + lk("a"), writes=["b"])
                c.emit("pool", lambda e, OB=OB: e.tensor_tensor(out=OB["Bh"][:], in0=T["b"][:], in1=T["eneg"][:], op=ALU.mult),
                       reads=["b", "eneg"], writes=ok("Bh"))
                c.emit("pool", lambda e, OB=OB: e.tensor_tensor(out=OB["Bt"][:], in0=OB["Bh"][:], in1=T["EL"][:], op=ALU.mult),
                       reads=ok("Bh") + ["EL"], writes=ok("Bt"))
                c.emit("dve", lambda e, L=L, cs=cs: e.scalar_tensor_tensor(out=T["t"][:], in0=L["a"][:], scalar=-1.0, in1=kap[:, cs],
                                                                          op0=ALU.add, op1=ALU.mult),
                       reads=lk("a") + ["par"], writes=["t"])
                c.emit("dve", lambda e, L=L: e.scalar_tensor_tensor(out=T["kh"][:], in0=T["t"][:], scalar=1.0, in1=L["k"][:],
                                                                   op0=ALU.add, op1=ALU.mult),
                       reads=["t"] + lk("k"), writes=["kh"])
                c.emit("pool", lambda e, OB=OB: e.tensor_tensor(out=OB["Kh"][:], in0=T["kh"][:], in1=T["eneg"][:], op=ALU.mult),
                       reads=["kh", "eneg"], writes=ok("Kh"))
                c.emit("pool", lambda e, OB=OB: e.tensor_tensor(out=OB["Kt"][:], in0=OB["Kh"][:], in1=T["EL"][:], op=ALU.mult),
                       reads=ok("Kh") + ["EL"], writes=ok("Kt"))
                c.emit("pool", lambda e, OB=OB, L=L: e.tensor_tensor(out=OB["Rt"][:], in0=L["r"][:], in1=T["epos"][:], op=ALU.mult),
                       reads=lk("r") + ["epos"], writes=ok("Rt"))
                if has_vmix:
                    c.emit("pool", lambda e, L=L: e.tensor_tensor(out=T["d"][:], in0=L["vf"][:], in1=L["v"][:], op=ALU.subtract),
                           reads=lk("vf", "v"), writes=["d"])
                    c.emit("pool", lambda e, L=L: e.tensor_tensor(out=T["d"][:], in0=T["d"][:], in1=L["sv"][:], op=ALU.mult),
                           reads=lk("sv") + ["d"], writes=["d"])
                    c.emit("pool", lambda e, L=L, OB=OB: e.tensor_tensor(out=OB["V"][:], in0=T["d"][:], in1=L["v"][:], op=ALU.add),
                           reads=lk("v") + ["d"], writes=ok("V"))
                else:
                    c.emit("pool", lambda e, L=L, OB=OB: e.tensor_copy(out=OB["V"][:], in_=L["v"][:]), reads=lk("v"), writes=ok("V"))
                c.emit("pool", lambda e, L=L: e.tensor_tensor(out=T["t2"][:], in0=L["r"][:], in1=T["kh"][:], op=ALU.mult),
                       reads=lk("r") + ["kh"], writes=["t2"])
                c.emit("pool", lambda e, cs=cs: e.tensor_tensor(out=T["t2"][:], in0=T["t2"][:], in1=rkp[:, cs], op=ALU.mult),
                       reads=["t2", "par"], writes=["t2"])
                c.emit("dve", lambda e: e.tensor_reduce(out=st[:, 32:40], in_=T["t2"][:].rearrange("p (h j) -> p h j", j=64),
                                                        axis=AX.X, op=ALU.add), reads=["t2"], writes=["st2"])
                c.emit("dve", lambda e, OB=OB: e.tensor_tensor(
                    out=OB["bonus"][:].rearrange("p (h j) -> p h j", j=64), in0=OB["V"][:].rearrange("p (h j) -> p h j", j=64),
                    in1=st[:, 32:40].unsqueeze(2).to_broadcast([128, 8, 64]), op=ALU.mult), reads=ok("V") + ["st2"], writes=ok("bonus"))
                for i, n in enumerate(names_out):
                    c.dma("sync" if i % 2 == 0 else "pool", O[n][rs, cs], OB[n][:], reads=ok(n), writes=[("od", n, ch, cb)])
                c.dma("sync", O["el"][ch:ch + 1, cs], T["EL"][0:1, :], reads=["EL"], writes=[("od", "el", ch, cb)])
        c.flush()


def rwkv_scan_phase(c, O):
    CB, G = 512, 4
    ident = c.ident
    with ExitStack() as pes:
        names_in = ["At", "Rt", "Bh", "Kh", "Bt", "Kt", "V"]
        ld = {n: [c.sb("sl" + n, [128, CB], es=pes) for _ in range(2)] for n in names_in}
        elrow = [c.sb("elrow", [1, CB], es=pes) for _ in range(2)]
        gl = [c.sb("gl", [64, 8], es=pes) for _ in range(2)]
        Tst = c.sb("Tst", [64, NH, 64], es=pes)
        XT = [c.sb("XT", [64, 512], es=pes) for _ in range(G)]
        ABK = [c.sb("ABK", [128, 512], es=pes) for _ in range(G)]
        Xb = [[c.sb("Xb", [128, 128], es=pes) for _ in range(2)] for _ in range(G)]
        XTb = [[c.sb("XTb", [128, 128], es=pes) for _ in range(3)] for _ in range(G)]
        PTb = [[c.sb("PTb", [128, 128], es=pes) for _ in range(2)] for _ in range(G)]
        Wb = [c.sb("Wb", [128, 64], es=pes) for _ in range(G)]
        Ub = [c.sb("Ub", [128, 64], es=pes) for _ in range(G)]
        Yst = [c.sb("Yst", [128, CB], es=pes) for _ in range(2)]
        print("scan sbuf remaining", c.nc.sbuf_bytes_remaining)
        c.emit("dve", lambda e: e.memset(Tst[:], 0.0), writes=[("T", h) for h in range(NH)])
        it = 0
        for ch in range(c.S // 128):
            rs = slice(ch * 128, (ch + 1) * 128)
            for cb in range(D // CB):
                cs = slice(cb * CB, (cb + 1) * CB)
                s = it % 2
                it += 1
                L = {}
                for i, n in enumerate(names_in):
                    c.dma("sync" if i % 2 == 0 else "pool", ld[n][s][:], O[n][rs, cs], writes=[("sl", n, s)])
                    L[n] = ld[n][s]
                lk = lambda *ns: [("sl", n, s) for n in ns]
                c.dma("sync", elrow[s][:], O["el"][ch:ch + 1, cs], writes=[("elrow", s)])
                psg, pgk = c.psum()
                for hl in range(8):
                    c.emit("pe", lambda e, psg=psg, hl=hl, s=s: e.matmul(
                        psg[0:64, hl:hl + 1], lhsT=elrow[s][0:1, hl * 64:(hl + 1) * 64], rhs=c.ones[0:1, 0:1], start=True, stop=True),
                        reads=[("elrow", s), "const"], writes=c.q(pgk, 0, 8))
                c.emit("dve", lambda e, psg=psg, s=s: e.tensor_copy(out=gl[s][:], in_=psg[0:64, 0:8]),
                       reads=c.q(pgk, 0, 8), writes=[("gl", s)])
                for grp in range(8 // G):
                    if SCAN_DBG < 1:
                        continue
                    heads = [grp * G + i for i in range(G)]
                    for i, hl in enumerate(heads):
                        hc = slice(hl * 64, (hl + 1) * 64)
                        ps, pk = c.psum()
                        for qn, n in enumerate(["At", "Rt", "Bh", "Kh"]):
                            c.emit("pe", lambda e, ps=ps, qn=qn, n=n, hc=hc, L=L: e.transpose(
                                ps[0:64, qn * 128:(qn + 1) * 128], L[n][:, hc], ident[:]),
                                reads=lk(n) + ["const"], writes=c.q(pk, qn * 128, (qn + 1) * 128))
                        if i % 2 == 0:
                            c.emit("act", lambda e, ps=ps, i=i: e.copy(out=XT[i][:], in_=ps[0:64, :]), reads=pk, writes=[("XT", i)])
                        else:
                            c.emit("dve", lambda e, ps=ps, i=i: e.tensor_copy(out=XT[i][:], in_=ps[0:64, :]), reads=pk,
                                   writes=[("XT", i)])
                    if SCAN_DBG < 2:
                        continue
                    psC, pCk = c.psum()
                    for i, hl in enumerate(heads):
                        c.emit("pe", lambda e, psC=psC, i=i: e.matmul(psC[:, i * 128:(i + 1) * 128], lhsT=XT[i][:, 0:128],
                                                                      rhs=XT[i][:, 256:384], start=True, stop=True),
                               reads=[("XT", i)], writes=pCk)
                    for i, hl in enumerate(heads):
                        c.emit("dve", lambda e, psC=psC, i=i: e.tensor_tensor(out=XTb[i][2][:], in0=psC[:, i * 128:(i + 1) * 128],
                                                                              in1=c.msl[:], op=ALU.mult),
                               reads=pCk + ["const"], writes=[("XTb", i, 2)])
                    for i, hl in enumerate(heads):
                        ps, pk = c.psum()
                        c.emit("pe", lambda e, ps=ps, i=i: e.matmul(ps[:, 0:256], lhsT=XT[i][:, 256:384], rhs=XT[i][:, 0:256],
                                                                    start=True, stop=True), reads=[("XT", i)], writes=pk)
                        c.emit("pe", lambda e, ps=ps, i=i: e.matmul(ps[:, 256:512], lhsT=XT[i][:, 384:512], rhs=XT[i][:, 0:256],
                                                                    start=True, stop=True), reads=[("XT", i)], writes=pk)
                        c.emit("dve", lambda e, ps=ps, i=i: e.tensor_tensor(out=ABK[i][:], in0=ps[:, :], in1=c.mask4[:], op=ALU.mult),
                               reads=pk + ["const"], writes=[("ABK", i)])
                        c.emit("dve", lambda e, i=i: e.tensor_tensor(out=PTb[i][0][:], in0=ABK[i][:, 0:128], in1=ident[:], op=ALU.add),
                               reads=[("ABK", i), "const"], writes=[("PTb", i, 0)])
                    if SCAN_DBG < 3:
                        continue
                    Xc = [(ABK[i][:, 0:128], ("ABK", i)) for i in range(G)]
                    XTc = [(XTb[i][2][:], ("XTb", i, 2)) for i in range(G)]
                    PTc = [(PTb[i][0][:], ("PTb", i, 0)) for i in range(G)]
                    for lev in range(int(os.environ.get("NLEV", "6"))):
                        pX, pXk = c.psum()
                        pXT, pXTk = c.psum()
                        newX, newXT = [], []
                        for i in range(G):
                            qs = c.q(pXk, i * 128, (i + 1) * 128)
                            qt = c.q(pXTk, i * 128, (i + 1) * 128)
                            xa, xk_ = Xc[i]
                            xta, xtk = XTc[i]
                            if lev < 5:
                                c.emit("pe", lambda e, pX=pX, i=i, xa=xa, xta=xta: e.matmul(
                                    pX[:, i * 128:(i + 1) * 128], lhsT=xta, rhs=xa, start=True, stop=True),
                                    reads=[xk_, xtk], writes=qs)
                            c.emit("pe", lambda e, pXT=pXT, i=i, xa=xa, xta=xta: e.matmul(
                                pXT[:, i * 128:(i + 1) * 128], lhsT=xa, rhs=xta, start=True, stop=True),
                                reads=[xk_, xtk], writes=qt)
                        if INV_SUB < 2:
                            continue
                        for i in range(G):
                            qs = c.q(pXk, i * 128, (i + 1) * 128)
                            qt = c.q(pXTk, i * 128, (i + 1) * 128)
                            b = lev % 2
                            if lev < 5:
                                c.emit("act", lambda e, pX=pX, i=i, b=b: e.copy(out=Xb[i][b][:], in_=pX[:, i * 128:(i + 1) * 128]),
                                       reads=pXk, writes=[("Xb", i, b)])
                                newX.append((Xb[i][b][:], ("Xb", i, b)))
                            else:
                                newX.append(None)
                            c.emit("act", lambda e, pXT=pXT, i=i, b=b: e.copy(out=XTb[i][b][:], in_=pXT[:, i * 128:(i + 1) * 128]),
                                   reads=pXTk, writes=[("XTb", i, b)])
                            newXT.append((XTb[i][b][:], ("XTb", i, b)))
                        if INV_SUB < 3:
                            continue
                        pP, pPk = c.psum()
                        for i in range(G):
                            qp = c.q(pPk, i * 128, (i + 1) * 128)
                            pa, pkk = PTc[i]
                            c.emit("pe", lambda e, pP=pP, i=i, pa=pa, l=newXT[i][0]: e.matmul(
                                pP[:, i * 128:(i + 1) * 128], lhsT=l, rhs=pa, start=True, stop=True),
                                reads=[newXT[i][1], pkk], writes=qp)
                        for i in range(G):
                            qp = c.q(pPk, i * 128, (i + 1) * 128)
                            pa, pkk = PTc[i]
                            nb = (lev + 1) % 2
                            c.emit("dve", lambda e, pP=pP, i=i, pa=pa, nb=nb: e.tensor_tensor(
                                out=PTb[i][nb][:], in0=pP[:, i * 128:(i + 1) * 128], in1=pa, op=ALU.add),
                                reads=qp + [pkk], writes=[("PTb", i, nb)])
                            PTc[i] = (PTb[i][nb][:], ("PTb", i, nb))
                        Xc, XTc = newX, newXT
                    if SCAN_DBG < 4:
                        continue
                    pW, pWk = c.psum()
                    pU, pUk = c.psum()
                    pY, pYk = c.psum()
                    pT, pTk = c.psum()
                    H = [(i, hl, cb * 8 + hl, slice(hl * 64, (hl + 1) * 64), slice(i * 128, i * 128 + 64)) for i, hl in enumerate(heads)]
                    for i, hl, h, hc, q0 in H:
                        c.emit("pe", lambda e, pW=pW, pU=pU, pY=pY, pT=pT, i=i, h=h, q0=q0: e.matmul(pW[:, q0], lhsT=XT[i][:, 0:128], rhs=Tst[:, h, :],
                                                                      start=True, stop=False),
                               reads=[("XT", i), ("T", h)], writes=pWk)
                        c.emit("pe", lambda e, pW=pW, pU=pU, pY=pY, pT=pT, i=i, hc=hc, q0=q0, L=L: e.matmul(pW[:, q0], lhsT=ABK[i][:, 256:384], rhs=L["V"][:, hc],
                                                                             start=False, stop=True),
                               reads=[("ABK", i)] + lk("V"), writes=pWk)
                    for i, hl, h, hc, q0 in H:
                        c.emit("act", lambda e, pW=pW, pU=pU, pY=pY, pT=pT, i=i, q0=q0: e.copy(out=Wb[i][:], in_=pW[:, q0]), reads=pWk, writes=[("Wb", i)])
                    for i, hl, h, hc, q0 in H:
                        pa, pkk = PTc[i]
                        c.emit("pe", lambda e, pW=pW, pU=pU, pY=pY, pT=pT, i=i, pa=pa, q0=q0: e.matmul(pU[:, q0], lhsT=pa, rhs=Wb[i][:], start=True, stop=True),
                               reads=[pkk, ("Wb", i)], writes=pUk)
                    for i, hl, h, hc, q0 in H:
                        c.emit("act", lambda e, pW=pW, pU=pU, pY=pY, pT=pT, i=i, q0=q0: e.copy(out=Ub[i][:], in_=pU[:, q0]), reads=pUk, writes=[("Ub", i)])
                    for i, hl, h, hc, q0 in H:
                        c.emit("pe", lambda e, pW=pW, pU=pU, pY=pY, pT=pT, i=i, h=h, q0=q0: e.matmul(pY[:, q0], lhsT=XT[i][:, 128:256], rhs=Tst[:, h, :],
                                                                      start=True, stop=False),
                               reads=[("XT", i), ("T", h)], writes=pYk)
                        c.emit("pe", lambda e, pW=pW, pU=pU, pY=pY, pT=pT, i=i, q0=q0: e.matmul(pY[:, q0], lhsT=ABK[i][:, 128:256], rhs=Ub[i][:],
                                                                 start=False, stop=False),
                               reads=[("ABK", i), ("Ub", i)], writes=pYk)
                        c.emit("pe", lambda e, pW=pW, pU=pU, pY=pY, pT=pT, i=i, hc=hc, q0=q0, L=L: e.matmul(pY[:, q0], lhsT=ABK[i][:, 384:512], rhs=L["V"][:, hc],
                                                                             start=False, stop=True),
                               reads=[("ABK", i)] + lk("V"), writes=pYk)
                        c.emit("pe", lambda e, pW=pW, pU=pU, pY=pY, pT=pT, i=i, hc=hc, q0=q0, L=L: e.matmul(pT[0:64, q0], lhsT=L["Bt"][:, hc], rhs=Ub[i][:],
                                                                             start=True, stop=False),
                               reads=lk("Bt") + [("Ub", i)], writes=pTk)
                        c.emit("pe", lambda e, pW=pW, pU=pU, pY=pY, pT=pT, i=i, hc=hc, q0=q0, L=L: e.matmul(pT[0:64, q0], lhsT=L["Kt"][:, hc], rhs=L["V"][:, hc],
                                                                             start=False, stop=True),
                               reads=lk("Kt", "V"), writes=pTk)
                    for i, hl, h, hc, q0 in H:
                        c.emit("act", lambda e, pW=pW, pU=pU, pY=pY, pT=pT, hc=hc, q0=q0, s=s: e.copy(out=Yst[s][:, hc], in_=pY[:, q0]),
                               reads=pYk, writes=[("Yst", s)])
                        c.emit("dve", lambda e, pW=pW, pU=pU, pY=pY, pT=pT, h=h, hl=hl, q0=q0, s=s: e.scalar_tensor_tensor(
                            out=Tst[:, h, :], in0=Tst[:, h, :], scalar=gl[s][:, hl:hl + 1], in1=pT[0:64, q0],
                            op0=ALU.mult, op1=ALU.add), reads=pTk + [("T", h), ("gl", s)], writes=[("T", h)])
                c.dma("sync", O["y"][rs, cs], Yst[s][:], reads=[("Yst", s)], writes=[("od", "y", ch, cb)])
        c.flush()


def rwkv_post_phase(c, x_d, P, O, wdt=F32):
    with ExitStack() as pes:
        yb = c.sb("yb", [128, D], es=pes)
        bb = c.sb("bb", [128, D], es=pes)
        gb = c.sb("gb", [128, D], es=pes)
        xt = c.sb("xt", [128, D], es=pes)
        sq = c.sb("sq", [128, D], es=pes)
        gng = c.sb("gng", [128, D], es=pes)
        gnb = c.sb("gnb", [128, D], es=pes)
        st = c.sb("st", [128, 160], es=pes)
        zT = c.sb("zT", [128, KC, 128], wdt, es=pes)
        wb = [c.sb("wb", [128, KC, 256], wdt, es=pes) for _ in range(2)]
        c.dma("sync", gng[:], P["gn_g"].partition_broadcast(128), writes=["par"])
        c.dma("sync", gnb[:], P["gn_b"].partition_broadcast(128), writes=["par"])
        y3 = yb[:].rearrange("p (h j) -> p h j", j=64)
        s3 = sq[:].rearrange("p (h j) -> p h j", j=64)
        bc = lambda a: a.unsqueeze(2).to_broadcast([128, NH, 64])
        for t in range(c.S // 128):
            rs = slice(t * 128, (t + 1) * 128)
            c.dma("sync", yb[:], O["y"][rs, :], writes=["yb"])
            c.dma("pool", bb[:], O["bonus"][rs, :], writes=["bb"])
            c.dma("sync", gb[:], O["g"][rs, :], writes=["gb"])
            c.dma("pool", xt[:], x_d[rs, :], writes=["xt"])
            c.emit("dve", lambda e: e.tensor_reduce(out=st[:, 0:32], in_=y3, axis=AX.X, op=ALU.add), reads=["yb"], writes=["st"])
            c.emit("dve", lambda e: e.tensor_scalar(out=st[:, 32:64], in0=st[:, 0:32], scalar1=1.0 / 64, scalar2=0.0,
                                                    op0=ALU.mult, op1=ALU.add), reads=["st"], writes=["st"])
            c.emit("dve", lambda e: e.tensor_tensor(out=y3, in0=y3, in1=bc(st[:, 32:64]), op=ALU.subtract),
                   reads=["yb", "st"], writes=["yb"])
            c.emit("pool", lambda e: e.tensor_tensor(out=sq[:], in0=yb[:], in1=yb[:], op=ALU.mult), reads=["yb"], writes=["sq"])
            c.emit("dve", lambda e: e.tensor_reduce(out=st[:, 64:96], in_=s3, axis=AX.X, op=ALU.add), reads=["sq"], writes=["st"])
            c.emit("dve", lambda e: e.tensor_scalar(out=st[:, 96:128], in0=st[:, 64:96], scalar1=1.0 / 64, scalar2=GN_EPS,
                                                    op0=ALU.mult, op1=ALU.add), reads=["st"], writes=["st"])
            c.emit("act", lambda e: e.activation(out=st[:, 96:128], in_=st[:, 96:128], func=AF.Sqrt), reads=["st"], writes=["st"])
            c.emit("dve", lambda e: e.reciprocal(out=st[:, 128:160], in_=st[:, 96:128]), reads=["st"], writes=["st"])
            c.emit("dve", lambda e: e.tensor_tensor(out=y3, in0=y3, in1=bc(st[:, 128:160]), op=ALU.mult),
                   reads=["yb", "st"], writes=["yb"])
            c.emit("pool", lambda e: e.tensor_tensor(out=yb[:], in0=yb[:], in1=gng[:], op=ALU.mult), reads=["yb", "par"], writes=["yb"])
            c.emit("pool", lambda e: e.tensor_tensor(out=yb[:], in0=yb[:], in1=gnb[:], op=ALU.add), reads=["yb", "par"], writes=["yb"])
            c.emit("pool", lambda e: e.tensor_tensor(out=yb[:], in0=yb[:], in1=bb[:], op=ALU.add), reads=["yb", "bb"], writes=["yb"])
            c.emit("pool", lambda e: e.tensor_tensor(out=sq[:], in0=yb[:], in1=gb[:], op=ALU.mult), reads=["yb", "gb"], writes=["sq"])
            transpose_tile(c, sq, "sq", zT, lambda g: ("zT", g), 0)

            def epi(cb, ps, pk):
                c.emit("dve", lambda e, cb=cb, ps=ps: e.tensor_tensor(out=xt[:, cb * 256:(cb + 1) * 256], in0=ps[:, 0:256],
                                                                       in1=xt[:, cb * 256:(cb + 1) * 256], op=ALU.add),
                       reads=pk + ["xt"], writes=["xt"])
            proj(c, zT, lambda kc: [("zT", kc // 4)], 1, P["wo"], D, wb, "wb", epi)
            c.dma("sync", x_d[rs, :], xt[:], reads=["xt"], writes=[("xd", t)])
        c.flush()


import math


def rope_tables_phase(c, pos_ap, cos_d, sin_d):
    NT = c.S // 128
    with ExitStack() as pes:
        pi_ = c.sb("pos_i", [NT, 128], I32, es=pes)
        pf = c.sb("pos_f", [NT, 128], es=pes)
        posT = c.sb("posT", [128, NT], es=pes)
        io_i = c.sb("io_i", [128, 32], I32, es=pes)
        invf = c.sb("invf", [128, 32], es=pes)
        ang = c.sb("ang", [128, 32], es=pes)
        ob = [c.sb("ropeo", [128, 64], es=pes) for _ in range(2)]
        nb = c.sb("negpi", [128, 1], es=pes)
        ni = c.sb("ni", [128, 64], I32, es=pes)
        nf = c.sb("nf", [128, 64], es=pes)
        c.emit("pool", lambda e: e.memset(nb[:], -math.pi), writes=["nb"])
        c.dma("sync", pi_[:], pos_ap.rearrange("(t p) -> t p", p=128), writes=["pi"])
        c.emit("dve", lambda e: e.tensor_copy(out=pf[:], in_=pi_[:]), reads=["pi"], writes=["pf"])
        ps, pk = c.psum()
        c.emit("pe", lambda e: e.transpose(ps[:, 0:NT], pf[:], c.ident[0:NT, 0:NT]), reads=["pf", "const"], writes=pk)
        c.emit("dve", lambda e: e.tensor_copy(out=posT[:], in_=ps[:, 0:NT]), reads=pk, writes=["posT"])
        c.emit("pool", lambda e: e.iota(io_i[:], pattern=[[1, 32]], base=0, channel_multiplier=0), writes=["io"])
        c.emit("dve", lambda e: e.tensor_copy(out=invf[:], in_=io_i[:]), reads=["io"], writes=["invf"])
        c.emit("act", lambda e: e.activation(out=invf[:], in_=invf[:], func=AF.Exp, scale=-math.log(10000.0) / 32.0),
               reads=["invf"], writes=["invf"])
        for t in range(NT):
            s = t % 2
            c.emit("dve", lambda e, t=t: e.tensor_scalar(out=ang[:], in0=invf[:], scalar1=posT[:, t:t + 1], scalar2=0.0,
                                                         op0=ALU.mult, op1=ALU.add), reads=["invf", "posT"], writes=["ang"])
            c.emit("dve", lambda e, s=s: e.tensor_scalar(out=ob[s][:, 32:64], in0=ang[:], scalar1=1.0 / (2 * math.pi), scalar2=0.5,
                                                         op0=ALU.mult, op1=ALU.add), reads=["ang"], writes=[("ob", s)])
            c.emit("dve", lambda e, s=s: e.tensor_scalar(out=ob[s][:, 0:32], in0=ang[:], scalar1=1.0 / (2 * math.pi), scalar2=0.75,
                                                         op0=ALU.mult, op1=ALU.add), reads=["ang"], writes=[("ob", s)])
            c.emit("dve", lambda e, s=s: e.tensor_copy(out=ni[:], in_=ob[s][:]), reads=[("ob", s)], writes=["ni"])
            c.emit("dve", lambda e: e.tensor_copy(out=nf[:], in_=ni[:]), reads=["ni"], writes=["nf"])
            c.emit("dve", lambda e, s=s: e.tensor_tensor(out=ob[s][:], in0=ob[s][:], in1=nf[:], op=ALU.subtract),
                   reads=[("ob", s), "nf"], writes=[("ob", s)])
            c.emit("dve", lambda e, s=s: e.tensor_scalar(out=nf[:], in0=ob[s][:], scalar1=0.0, scalar2=0.0,
                                                         op0=ALU.is_lt, op1=ALU.add), reads=[("ob", s)], writes=["nf"])
            c.emit("dve", lambda e, s=s: e.tensor_tensor(out=ob[s][:], in0=ob[s][:], in1=nf[:], op=ALU.add),
                   reads=[("ob", s), "nf"], writes=[("ob", s)])
            c.emit("act", lambda e, s=s: e.activation(out=ob[s][:], in_=ob[s][:], func=AF.Sin, bias=nb[:, 0:1], scale=2 * math.pi),
                   reads=[("ob", s), "nb"], writes=[("ob", s)])
            c.dma("sync", cos_d[t * 128:(t + 1) * 128, :], ob[s][:, 0:32], reads=[("ob", s)], writes=[("cd", t)])
            c.dma("sync", sin_d[t * 128:(t + 1) * 128, :], ob[s][:, 32:64], reads=[("ob", s)], writes=[("sd", t)])
        c.flush()


def rope_epi_ops(c, ps, pk, nh, cs_tile, cskey, dst, dkey, tmp, tkey):
    p3 = ps[:, 0:nh * 64].rearrange("p (h d) -> p h d", d=64)
    cosb = cs_tile[:, 0:32].unsqueeze(1).to_broadcast([128, nh, 32])
    sinb = cs_tile[:, 32:64].unsqueeze(1).to_broadcast([128, nh, 32])
    d3 = dst.rearrange("p (h d) -> p h d", d=64)
    t3 = tmp[:, 0:nh * 64].rearrange("p (h d) -> p h d", d=64)
    R = pk + [cskey]
    c.emit("dve", lambda e: e.tensor_tensor(out=d3[:, :, 0:32], in0=p3[:, :, 0:32], in1=cosb, op=ALU.mult), reads=R, writes=[dkey])
    c.emit("dve", lambda e: e.tensor_tensor(out=t3[:, :, 0:32], in0=p3[:, :, 32:64], in1=sinb, op=ALU.mult), reads=R, writes=[tkey])
    c.emit("dve", lambda e: e.tensor_tensor(out=d3[:, :, 32:64], in0=p3[:, :, 32:64], in1=cosb, op=ALU.mult), reads=R, writes=[dkey])
    c.emit("dve", lambda e: e.tensor_tensor(out=t3[:, :, 32:64], in0=p3[:, :, 0:32], in1=sinb, op=ALU.mult), reads=R, writes=[tkey])
    c.emit("dve", lambda e: e.tensor_tensor(out=d3[:, :, 0:32], in0=d3[:, :, 0:32], in1=t3[:, :, 0:32], op=ALU.subtract),
           reads=[dkey, tkey], writes=[dkey])
    c.emit("dve", lambda e: e.tensor_tensor(out=d3[:, :, 32:64], in0=d3[:, :, 32:64], in1=t3[:, :, 32:64], op=ALU.add),
           reads=[dkey, tkey], writes=[dkey])


def norm_tile(c, xt, gam, junk, hbuf, stat, jkeys, hkeys):
    c.emit("act", lambda e: e.activation(out=junk, in_=xt[:], func=AF.Square, accum_out=stat[:, 0:1]),
           reads=["xt"], writes=jkeys + ["stat"])
    c.emit("dve", lambda e: e.tensor_scalar(out=stat[:, 1:2], in0=stat[:, 0:1], scalar1=1.0 / D, scalar2=RMS_EPS,
                                            op0=ALU.mult, op1=ALU.add), reads=["stat"], writes=["stat"])
    c.emit("act", lambda e: e.activation(out=stat[:, 3:4], in_=stat[:, 1:2], func=AF.Sqrt), reads=["stat"], writes=["stat"])
    c.emit("dve", lambda e: e.reciprocal(out=stat[:, 2:3], in_=stat[:, 3:4]), reads=["stat"], writes=["stat"])
    c.emit("dve", lambda e: e.scalar_tensor_tensor(out=hbuf, in0=xt[:], scalar=stat[:, 2:3], in1=gam[:],
                                                   op0=ALU.mult, op1=ALU.mult), reads=["xt", "stat", "gam"], writes=hkeys)


def kv_phase(c, x_d, g_row, w_kv, cos_d, sin_d, kT_d, v_d, wdt=F32):
    with ExitStack() as pes:
        xt = c.sb("xt", [128, D], es=pes)
        gam = c.sb("gam", [128, D], es=pes)
        junk = c.sb("junk", [128, D], es=pes)
        hbuf = c.sb("hbuf", [128, D], es=pes)
        stat = c.sb("stat", [128, 8], es=pes)
        hT = c.sb("hT", [128, KC, 128], wdt, es=pes)
        wb = [c.sb("wb", [128, KC, 256], wdt, es=pes) for _ in range(2)]
        cs = [c.sb("cs", [128, 64], es=pes) for _ in range(2)]
        krot = c.sb("krot", [128, 512], es=pes)
        tmp = c.sb("tmp", [128, 256], es=pes)
        vb = [c.sb("vb", [128, 256], es=pes) for _ in range(2)]
        kTt = c.sb("kTt", [64, 8, 128], es=pes)
        c.dma("sync", gam[:], g_row.partition_broadcast(128), writes=["gam"])
        for t in range(c.S // 128):
            rs = slice(t * 128, (t + 1) * 128)
            s = t % 2
            c.dma("sync", xt[:], x_d[rs, :], writes=["xt"])
            c.dma("pool", cs[s][:, 0:32], cos_d[rs, :], writes=[("cs", s)])
            c.dma("pool", cs[s][:, 32:64], sin_d[rs, :], writes=[("cs", s)])
            norm_tile(c, xt, gam, junk[:], hbuf[:], stat, ["junk"], ["hbuf"])
            transpose_tile(c, hbuf, "hbuf", hT, lambda g: ("hT", g), 0)

            def epi(cb, ps, pk, s=s, rs=rs):
                if cb < 2:
                    rope_epi_ops(c, ps, pk, 4, cs[s], ("cs", s), krot[:, cb * 256:(cb + 1) * 256], ("krot", cb), tmp, "tmp")
                else:
                    vs = c.wslot("vb")
                    c.emit("act", lambda e, vs=vs, ps=ps: e.copy(out=vb[vs][:], in_=ps[:, 0:256]), reads=pk, writes=[("vb", vs)])
                    c.dma("sync", v_d[rs, (cb - 2) * 256:(cb - 1) * 256], vb[vs][:], reads=[("vb", vs)], writes=[("vd", rs.start, cb)])
            proj(c, hT, lambda kc: [("hT", kc // 4)], 1, w_kv, 1024, wb, "wb", epi)
            for hg in range(2):
                ps, pk = c.psum()
                for j in range(4):
                    h = hg * 4 + j
                    c.emit("pe", lambda e, ps=ps, j=j, h=h: e.transpose(ps[0:64, j * 128:(j + 1) * 128], krot[:, h * 64:(h + 1) * 64], c.ident[:]),
                           reads=[("krot", h // 4), "const"], writes=pk)
                c.emit("act", lambda e, ps=ps, hg=hg: e.copy(out=kTt[:, hg * 4:(hg + 1) * 4, :],
                                                             in_=ps[0:64, :].rearrange("p (a b) -> p a b", a=4)),
                       reads=pk, writes=["kTt"])
            c.dma("sync", kT_d[:, :, rs].rearrange("g d s -> d g s"), kTt[:], reads=["kTt"], writes=[("kTd", t)])
        c.flush()


def attn_phase(c, x_d, g_row, w_q, w_o, sinks_row, cos_d, sin_d, kT_d, v_d, dbg=None, wdt=F32):
    with ExitStack() as pes:
        xt = c.sb("xt", [128, D], es=pes)
        gam = c.sb("gam", [128, D], es=pes)
        junk = c.sb("junk", [128, D], es=pes)
        hbuf = c.sb("hbuf", [128, D], es=pes)
        stat = c.sb("stat", [128, 8], es=pes)
        hT = c.sb("hT", [128, KC, 128], wdt, es=pes)
        wb = [c.sb("wb", [128, KC, 256], wdt, es=pes) for _ in range(2)]
        cs = [c.sb("cs", [128, 64], es=pes) for _ in range(2)]
        tmp = c.sb("tmp", [128, 256], es=pes)
        qT = c.sb("qT", [64, NH, 128], es=pes)
        kTs = [c.sb("kTs", [64, 8, 256], es=pes) for _ in range(2)]
        v1 = [c.sb("v1", [128, 2, 8, 65], es=pes) for _ in range(2)]
        E = [[c.sb("E", [128, 512], es=pes) for _ in range(2)] for _ in range(2)]
        mC = c.sb("mC", [128, 512], es=pes)
        mP = c.sb("mP", [128, 512], es=pes)
        sk = c.sb("sk", [128, NH], es=pes)
        dn = c.sb("dn", [128, 8], es=pes)
        attn = junk
        c.dma("sync", gam[:], g_row.partition_broadcast(128), writes=["gam"])
        c.dma("sync", sk[:], sinks_row.partition_broadcast(128), writes=["sk"])
        c.emit("act", lambda e: e.activation(out=sk[:], in_=sk[:], func=AF.Exp), reads=["sk"], writes=["sk"])
        for q in range(4):
            c.emit("pool", lambda e, q=q: e.tensor_copy(out=mC[:, q * 128:(q + 1) * 128], in_=c.mask4[:, 128:256]),
                   reads=["const"], writes=["mC"])
            c.emit("pool", lambda e, q=q: e.tensor_copy(out=mP[:, q * 128:(q + 1) * 128], in_=c.msl[:]),
                   reads=["const"], writes=["mP"])
        for s in range(2):
            c.emit("pool", lambda e, s=s: e.memset(v1[s][:], 1.0), writes=[("v1", s)])
        for t in range(c.S // 128):
            rs = slice(t * 128, (t + 1) * 128)
            s = t % 2
            nkb = 1 if t == 0 else 2
            k0 = (t - 1) * 128 if t > 0 else 0
            c.dma("sync", xt[:], x_d[rs, :], writes=["xt"])
            c.dma("pool", cs[s][:, 0:32], cos_d[rs, :], writes=[("cs", s)])
            c.dma("pool", cs[s][:, 32:64], sin_d[rs, :], writes=[("cs", s)])
            kb0 = 2 - nkb
            c.dma("pool", kTs[s][:, :, kb0 * 128:256], kT_d[:, :, k0:(t + 1) * 128].rearrange("g d s -> d g s"),
                  writes=[("kTs", s)])
            for kb in range(kb0, 2):
                r1 = (t - 1 + kb) * 128
                c.dma("sync", v1[s][:, kb, :, 0:64], v_d[r1:r1 + 128, :].rearrange("p (g d) -> p g d", d=64), writes=[("v1", s)])
            norm_tile(c, xt, gam, junk[:], hbuf[:], stat, ["junk"], ["hbuf"])
            transpose_tile(c, hbuf, "hbuf", hT, lambda g: ("hT", g), 0)

            def epi(cb, ps, pk, s=s):
                rope_epi_ops(c, ps, pk, 4, cs[s], ("cs", s), hbuf[:, cb * 256:(cb + 1) * 256], "hbuf", tmp, "tmp")
            proj(c, hT, lambda kc: [("hT", kc // 4)], 1, w_q, D, wb, "wb", epi)
            for hg in range(8):
                ps, pk = c.psum()
                for j in range(4):
                    h = hg * 4 + j
                    c.emit("pe", lambda e, ps=ps, j=j, h=h: e.transpose(ps[0:64, j * 128:(j + 1) * 128], hbuf[:, h * 64:(h + 1) * 64], c.ident[:]),
                           reads=["hbuf", "const"], writes=pk)
                c.emit("act", lambda e, ps=ps, hg=hg: e.copy(out=qT[:, hg * 4:(hg + 1) * 4, :],
                                                             in_=ps[0:64, :].rearrange("p (a b) -> p a b", a=4)),
                       reads=pk, writes=[("qT", hg)])
            for g in range(8):
                es_ = g % 2
                for kb in range(kb0, 2):
                    ps, pk = c.psum()
                    for j in range(4):
                        c.emit("pe", lambda e, ps=ps, j=j, g=g, kb=kb, s=s: e.matmul(
                            ps[:, j * 128:(j + 1) * 128], lhsT=kTs[s][:, g, kb * 128:(kb + 1) * 128], rhs=qT[:, 4 * g + j, :],
                            start=True, stop=True), reads=[("kTs", s), ("qT", g)], writes=pk)
                    c.emit("act", lambda e, ps=ps, kb=kb, es_=es_: e.activation(out=E[es_][kb][:], in_=ps[:, :], func=AF.Exp, scale=0.125),
                           reads=pk, writes=[("E", es_, kb)])
                    m = mC if kb == 1 else mP
                    c.emit("pool", lambda e, kb=kb, es_=es_, m=m: e.tensor_tensor(out=E[es_][kb][:], in0=E[es_][kb][:], in1=m[:], op=ALU.mult),
                           reads=[("E", es_, kb), "mC", "mP"], writes=[("E", es_, kb)])
                po, pok = c.psum()
                for j in range(4):
                    for kb in range(kb0, 2):
                        c.emit("pe", lambda e, po=po, j=j, kb=kb, g=g, s=s, es_=es_, kb0=kb0: e.matmul(
                            po[:, j * 65:(j + 1) * 65], lhsT=E[es_][kb][:, j * 128:(j + 1) * 128], rhs=v1[s][:, kb, g, :],
                            start=(kb == kb0), stop=(kb == 1)), reads=[("E", es_, kb), ("v1", s)], writes=pok)
                po3 = po[:, 0:260].rearrange("p (h d) -> p h d", d=65)
                c.emit("dve", lambda e, po3=po3, g=g: e.tensor_tensor(out=dn[:, 0:4], in0=po3[:, :, 64], in1=sk[:, 4 * g:4 * g + 4], op=ALU.add),
                       reads=pok + ["sk"], writes=["dn"])
                c.emit("dve", lambda e: e.reciprocal(out=dn[:, 4:8], in_=dn[:, 0:4]), reads=["dn"], writes=["dn"])
                c.emit("dve", lambda e, po3=po3, g=g: e.tensor_tensor(
                    out=attn[:, g * 256:(g + 1) * 256].rearrange("p (h d) -> p h d", d=64), in0=po3[:, :, 0:64],
                    in1=dn[:, 4:8].unsqueeze(2).to_broadcast([128, 4, 64]), op=ALU.mult), reads=pok + ["dn"], writes=["junk"])
            if dbg is not None:
                c.dma("sync", dbg[rs, :], attn[:], reads=["junk"], writes=[("dbg", t)])
            transpose_tile(c, attn, "junk", hT, lambda g: ("hT", g), 0)

            def epi2(cb, ps, pk):
                c.emit("dve", lambda e, cb=cb, ps=ps: e.tensor_tensor(out=xt[:, cb * 256:(cb + 1) * 256], in0=ps[:, 0:256],
                                                                       in1=xt[:, cb * 256:(cb + 1) * 256], op=ALU.add),
                       reads=pk + ["xt"], writes=["xt"])
            proj(c, hT, lambda kc: [("hT", kc // 4)], 1, w_o, D, wb, "wb", epi2)
            c.dma("sync", x_d[rs, :], xt[:], reads=["xt"], writes=[("xd", t)])
        c.flush()


def test_attn_program(S):
    nc = bass.Bass("TRN2", target_bir_lowering=False)
    x_in = nc.dram_tensor("x", [S, D], F32, kind="ExternalInput").ap()
    pos = nc.dram_tensor("pos", [S], I32, kind="ExternalInput").ap()
    gk = nc.dram_tensor("gk", [1, D], F32, kind="ExternalInput").ap()
    g1 = nc.dram_tensor("g1", [1, D], F32, kind="ExternalInput").ap()
    wkv = nc.dram_tensor("wkv", [D, 1024], F32, kind="ExternalInput").ap()
    wq = nc.dram_tensor("wq", [D, D], F32, kind="ExternalInput").ap()
    wo = nc.dram_tensor("wo", [D, D], F32, kind="ExternalInput").ap()
    sinks = nc.dram_tensor("sinks", [1, NH], F32, kind="ExternalInput").ap()
    xo = nc.dram_tensor("xo", [S, D], F32, kind="ExternalOutput").ap()
    cos_d = nc.dram_tensor("cos_d", [S, 32], F32, kind="ExternalOutput").ap()
    sin_d = nc.dram_tensor("sin_d", [S, 32], F32, kind="ExternalOutput").ap()
    kT_d = nc.dram_tensor("kT_d", [8, 64, S], F32, kind="ExternalOutput").ap()
    v_d = nc.dram_tensor("v_d", [S, 512], F32, kind="ExternalOutput").ap()
    with ExitStack() as es:
        c = Ctx(nc, es, S)
        make_consts(c)
        make_masks(c)
        copy_rows(c, xo, x_in, S)
        rope_tables_phase(c, pos, cos_d, sin_d)
        kv_phase(c, xo, gk, wkv, cos_d, sin_d, kT_d, v_d)
        dbg = nc.dram_tensor("dbg", [S, D], F32, kind="ExternalOutput").ap()
        attn_phase(c, xo, g1, wq, wo, sinks, cos_d, sin_d, kT_d, v_d, dbg=dbg)
    return nc


def ple_phase(c, x_d, g_row, w_gate, w_up, p_d, wdt=F32):
    with ExitStack() as pes:
        xt = c.sb("xt", [128, D], es=pes)
        gam = c.sb("gam", [128, D], es=pes)
        junk = c.sb("junk", [128, D], es=pes)
        hbuf = c.sb("hbuf", [128, D], es=pes)
        stat = c.sb("stat", [128, 8], es=pes)
        hT = c.sb("hT", [128, KC, 128], wdt, es=pes)
        wb = [c.sb("wb", [128, KC, 256], wdt, es=pes) for _ in range(2)]
        wup = c.sb("wup", [128, 2, D], es=pes)
        pt = c.sb("pt", [128, 256], es=pes)
        pT = c.sb("pT", [128, 2, 128], es=pes)
        sg = [c.sb("sg", [128, 256], es=pes) for _ in range(2)]
        c.dma("sync", gam[:], g_row.partition_broadcast(128), writes=["gam"])
        c.dma("sync", wup[:], w_up.rearrange("(ch p) n -> p ch n", p=128), writes=["wup"])
        for t in range(c.S // 128):
            rs = slice(t * 128, (t + 1) * 128)
            c.dma("sync", xt[:], x_d[rs, :], writes=["xt"])
            c.dma("pool", pt[:], p_d[rs, :], writes=["pt"])
            norm_tile(c, xt, gam, junk[:], hbuf[:], stat, ["junk"], ["hbuf"])
            transpose_tile(c, hbuf, "hbuf", hT, lambda g: ("hT", g), 0)
            transpose_tile(c, pt, "pt", pT, lambda g: "pT", 0, nchunks=2)

            def epi(cb, ps, pk):
                s = c.wslot("sg")
                c.emit("act", lambda e, s=s, ps=ps: e.activation(out=sg[s][:], in_=ps[:, 0:256], func=AF.Sigmoid),
                       reads=pk, writes=[("sg", s)])
                pu, puk = c.psum()
                for ch in range(2):
                    c.emit("pe", lambda e, pu=pu, ch=ch, cb=cb: e.matmul(pu[:, 0:256], lhsT=pT[:, ch, :],
                                                                         rhs=wup[:, ch, cb * 256:(cb + 1) * 256],
                                                                         start=(ch == 0), stop=(ch == 1)),
                           reads=["pT", "wup"], writes=puk)
                c.emit("dve", lambda e, s=s, pu=pu: e.tensor_tensor(out=sg[s][:], in0=sg[s][:], in1=pu[:, 0:256], op=ALU.mult),
                       reads=puk + [("sg", s)], writes=[("sg", s)])
                c.emit("dve", lambda e, s=s, cb=cb: e.tensor_tensor(out=xt[:, cb * 256:(cb + 1) * 256], in0=sg[s][:],
                                                                    in1=xt[:, cb * 256:(cb + 1) * 256], op=ALU.add),
                       reads=[("sg", s), "xt"], writes=["xt"])
            proj(c, hT, lambda kc: [("hT", kc // 4)], 1, w_gate, D, wb, "wb", epi)
            c.dma("sync", x_d[rs, :], xt[:], reads=["xt"], writes=[("xd", t)])
        c.flush()


def final_norm_phase(c, x_d, g_row, y_d):
    with ExitStack() as pes:
        xt = c.sb("xt", [128, D], es=pes)
        gam = c.sb("gam", [128, D], es=pes)
        junk = c.sb("junk", [128, D], es=pes)
        hb = [c.sb("hbuf", [128, D], es=pes) for _ in range(2)]
        stat = c.sb("stat", [128, 8], es=pes)
        c.dma("sync", gam[:], g_row.partition_broadcast(128), writes=["gam"])
        for t in range(c.S // 128):
            rs = slice(t * 128, (t + 1) * 128)
            s = t % 2
            c.dma("sync", xt[:], x_d[rs, :], writes=["xt"])
            norm_tile(c, xt, gam, junk[:], hb[s][:], stat, ["junk"], [("hb", s)])
            c.dma("pool", y_d[rs, :], hb[s][:], reads=[("hb", s)], writes=[("yd", t)])
        c.flush()


def convert_ffn_weights_phase(c, wg, wu, wd, wg_b, wu_b, wd_b, ff):
    HC = ff // 128
    CH = 2816 if ff % 2816 == 0 else ff
    with ExitStack() as pes:
        fb = [c.sb("cvf", [128, 2816], es=pes) for _ in range(3)]
        bb = [c.sb("cvb", [128, 2816], BF16, es=pes) for _ in range(3)]
        it = 0
        engs = ["act", "dve", "pool"]

        def cast(s, n, it):
            e_ = engs[it % 3]
            if e_ == "act":
                c.emit("act", lambda e: e.copy(out=bb[s][:, 0:n], in_=fb[s][:, 0:n]), reads=[("cvf", s)], writes=[("cvb", s)])
            else:
                c.emit(e_, lambda e: e.tensor_copy(out=bb[s][:, 0:n], in_=fb[s][:, 0:n]), reads=[("cvf", s)], writes=[("cvb", s)])
        for (w, wb_) in ((wg, wg_b), (wu, wu_b)):
            for kc in range(KC):
                for c0 in range(0, ff, CH):
                    s = it % 3
                    c.dma("sync", fb[s][:, 0:CH], w[kc * 128:(kc + 1) * 128, c0:c0 + CH], writes=[("cvf", s)])
                    cast(s, CH, it)
                    h0, nh = c0 // 128, CH // 128
                    c.dma("pool", wb_[h0:h0 + nh, :, kc, :].rearrange("h p c -> p h c"),
                          bb[s][:, 0:CH].rearrange("p (h c) -> p h c", c=128), reads=[("cvb", s)], writes=[("cvo", it)])
                    it += 1
        for hc in range(HC):
            s = it % 3
            c.dma("sync", fb[s][:, 0:D], wd[hc * 128:(hc + 1) * 128, :], writes=[("cvf", s)])
            cast(s, D, it)
            c.dma("pool", wd_b[hc // 4, :, :, hc % 4, :].rearrange("db p c -> p db c"),
                  bb[s][:, 0:D].rearrange("p (db c) -> p db c", c=512), reads=[("cvb", s)], writes=[("cvo", it)])
            it += 1
        c.flush()


def run_ffn_bf16(c, x_d, gam_row, wg_b, wu_b, wd_b, ff):
    TB, NTT = 512, 4
    HC = ff // 128
    HG = HC // 4
    with ExitStack() as pes:
        xt = [c.sb("xt", [128, D], es=pes) for _ in range(NTT)]
        gam = c.sb("gam", [128, D], es=pes)
        junk = c.sb("junk", [128, D], es=pes)
        hbuf = c.sb("hbuf", [128, D], es=pes)
        stat = c.sb("stat", [128, 8], es=pes)
        hT = c.sb("hT", [128, KC, TB], BF16, es=pes)
        act = c.sb("act", [128, HC, TB], BF16, es=pes)
        wgs = [c.sb("wgs", [128, KC, 128], BF16, es=pes) for _ in range(3)]
        wus = [c.sb("wus", [128, KC, 128], BF16, es=pes) for _ in range(3)]
        wds = [c.sb("wds", [128, 4, 512], BF16, es=pes) for _ in range(3)]
        sg = [c.sb("sg", [128, TB], es=pes) for _ in range(2)]
        c.dma("sync", gam[:], gam_row.partition_broadcast(128), writes=["gam"])
        B = dict(junk=junk, stat=stat, hbuf=hbuf)
        wi = 0
        di = 0
        for tb in range(c.S // TB):
            for tt in range(NTT):
                r0 = tb * TB + tt * 128
                c.dma("sync", xt[tt][:], x_d[r0:r0 + 128, :], writes=[("xt", tt)])
                norm_to_hT(c, xt[tt][:], ("xt", tt), gam[:], "gam", hT, lambda g: ("hT", g), tt * 128, B)
            for hc in range(HC):
                s = wi % 3
                wi += 1
                c.dma("sync", wgs[s][:], wg_b[hc], writes=[("wg", s)])
                c.dma("pool", wus[s][:], wu_b[hc], writes=[("wu", s)])
                psg, pgk = c.psum()
                psu, puk = c.psum()
                for kc in range(KC):
                    c.emit("pe", lambda e, psg=psg, s=s, kc=kc: e.matmul(psg[:, :], lhsT=wgs[s][:, kc, :], rhs=hT[:, kc, :],
                                                                          start=(kc == 0), stop=(kc == KC - 1)),
                           reads=[("wg", s), ("hT", kc // 4)], writes=pgk)
                for kc in range(KC):
                    c.emit("pe", lambda e, psu=psu, s=s, kc=kc: e.matmul(psu[:, :], lhsT=wus[s][:, kc, :], rhs=hT[:, kc, :],
                                                                          start=(kc == 0), stop=(kc == KC - 1)),
                           reads=[("wu", s), ("hT", kc // 4)], writes=puk)
                ss = hc % 2
                c.emit("act", lambda e, psg=psg, ss=ss: e.activation(out=sg[ss][:], in_=psg[:, :], func=AF.Silu),
                       reads=pgk, writes=[("sg", ss)])
                c.emit("dve", lambda e, psu=psu, ss=ss, hc=hc: e.tensor_tensor(out=act[:, hc, :], in0=sg[ss][:], in1=psu[:, :],
                                                                               op=ALU.mult),
                       reads=puk + [("sg", ss)], writes=[("act", hc)])
            for db in range(D // 512):
                pds = [c.psum() for _ in range(NTT)]
                for hg in range(HG):
                    s = di % 3
                    di += 1
                    c.dma("sync" if di % 2 else "pool", wds[s][:], wd_b[hg, db], writes=[("wd", s)])
                    for tt in range(NTT):
                        pd, pdk = pds[tt]
                        for j in range(4):
                            hc = hg * 4 + j
                            c.emit("pe", lambda e, pd=pd, s=s, j=j, hc=hc, tt=tt: e.matmul(
                                pd[:, :], lhsT=act[:, hc, tt * 128:(tt + 1) * 128], rhs=wds[s][:, j, :],
                                start=(hc == 0), stop=(hc == HC - 1)),
                                reads=[("act", hc), ("wd", s)], writes=pdk)
                for tt in range(NTT):
                    pd, pdk = pds[tt]
                    c.emit("dve", lambda e, pd=pd, tt=tt, db=db: e.scalar_tensor_tensor(
                        out=xt[tt][:, db * 512:(db + 1) * 512], in0=pd[:, :], scalar=0.5,
                        in1=xt[tt][:, db * 512:(db + 1) * 512], op0=ALU.mult, op1=ALU.add),
                        reads=pdk + [("xt", tt)], writes=[("xt", tt)])
            for tt in range(NTT):
                r0 = tb * TB + tt * 128
                c.dma("pool", x_d[r0:r0 + 128, :], xt[tt][:], reads=[("xt", tt)], writes=[("xd", r0 // 128)])
        c.flush()


def convert_mats_phase(c, pairs):
    with ExitStack() as pes:
        fb = [c.sb("cvf", [128, 2048], es=pes) for _ in range(3)]
        bb = [c.sb("cvb", [128, 2048], BF16, es=pes) for _ in range(3)]
        engs = ["act", "dve", "pool"]
        it = 0
        for (w, wb_) in pairs:
            K_, N_ = w.shape
            for kc in range(K_ // 128):
                for c0 in range(0, N_, 2048):
                    n = min(2048, N_ - c0)
                    s = it % 3
                    c.dma("sync", fb[s][:, 0:n], w[kc * 128:(kc + 1) * 128, c0:c0 + n], writes=[("cvf", s)])
                    e_ = engs[it % 3]
                    if e_ == "act":
                        c.emit("act", lambda e, s=s, n=n: e.copy(out=bb[s][:, 0:n], in_=fb[s][:, 0:n]), reads=[("cvf", s)], writes=[("cvb", s)])
                    else:
                        c.emit(e_, lambda e, s=s, n=n: e.tensor_copy(out=bb[s][:, 0:n], in_=fb[s][:, 0:n]), reads=[("cvf", s)], writes=[("cvb", s)])
                    c.dma("pool", wb_[kc * 128:(kc + 1) * 128, c0:c0 + n], bb[s][:, 0:n], reads=[("cvb", s)], writes=[("cvo", it)])
                    it += 1
        c.flush()


FF = 5632
DEPTH = 4
FULL_SHAPES = dict(
    norm_g=(4, 4, D), ffn_w_gate=(4, 2, D, FF), ffn_w_up=(4, 2, D, FF), ffn_w_down=(4, 2, FF, D),
    ple_w_up=(4, 256, D), ple_w_gate=(4, D, D), kv_norm_g=(1, D), w_kv=(D, 1024), attn_w_q=(2, D, D),
    attn_w_o=(2, D, D), attn_sinks=(2, NH), final_norm_g=(1, D))
FULL_SHAPES.update(RWKV_SHAPES)


def build_full(S, ff=FF, layers=(0, 1, 2, 3)):
    nc = bass.Bass("TRN2", target_bir_lowering=False)
    shapes = dict(FULL_SHAPES)
    shapes["ffn_w_gate"] = (4, 2, D, ff)
    shapes["ffn_w_up"] = (4, 2, D, ff)
    shapes["ffn_w_down"] = (4, 2, ff, D)
    I = {k: nc.dram_tensor(k, list(v), F32, kind="ExternalInput").ap() for k, v in shapes.items()}
    x_in = nc.dram_tensor("x", [S, D], F32, kind="ExternalInput").ap()
    p_in = nc.dram_tensor("p", [DEPTH, S, 256], F32, kind="ExternalInput").ap()
    pos = nc.dram_tensor("positions", [S], I32, kind="ExternalInput").ap()
    y = nc.dram_tensor("y", [S, D], F32, kind="ExternalOutput").ap()
    scr = lambda n, shp=None: nc.dram_tensor("scr_" + n, list(shp or [S, D]), F32).ap()
    xs = scr("x")
    O = {k: scr(k) for k in ["r", "k", "v", "sw", "a", "g", "sv", "At", "Rt", "Bh", "Kh", "Bt", "Kt", "bonus", "y"]}
    O["el"] = scr("el", [S // 128, D])
    V0, V1 = scr("V0"), scr("V1")
    cos_d, sin_d = scr("cos", [S, 32]), scr("sin", [S, 32])
    kT_d, v_d = scr("kT", [8, 64, S]), scr("vkv", [S, 512])
    HC = ff // 128
    WB = {}
    for i in layers:
        for hf in range(2):
            WB[(i, hf)] = (nc.dram_tensor("wgb_%d_%d" % (i, hf), [HC, 128, KC, 128], BF16).ap(),
                           nc.dram_tensor("wub_%d_%d" % (i, hf), [HC, 128, KC, 128], BF16).ap(),
                           nc.dram_tensor("wdb_%d_%d" % (i, hf), [HC // 4, 4, 128, 4, 512], BF16).ap())
    with ExitStack() as es:
        c = Ctx(nc, es, S)
        make_consts(c)
        make_masks(c)
        copy_rows(c, xs, x_in, S)
        rope_tables_phase(c, pos, cos_d, sin_d)
        bf = lambda n, shp: nc.dram_tensor("bf_" + n, list(shp), BF16).ap()
        MB = {}
        pairs = []
        for i in layers:
            MB[("ple", i)] = bf("ple%d" % i, [D, D])
            pairs.append((I["ple_w_gate"][i], MB[("ple", i)]))
            if i < 2:
                MB[("wo", i)] = bf("rwo%d" % i, [D, D])
                pairs.append((I["rwkv_w_o"][i], MB[("wo", i)]))
            else:
                MB[("aq", i)] = bf("aq%d" % i, [D, D])
                MB[("ao", i)] = bf("ao%d" % i, [D, D])
                pairs.append((I["attn_w_q"][i - 2], MB[("aq", i)]))
                pairs.append((I["attn_w_o"][i - 2], MB[("ao", i)]))
        MB["kv"] = bf("wkv", [D, 1024])
        pairs.append((I["w_kv"], MB["kv"]))
        convert_mats_phase(c, pairs)
        for i in layers:
            for hf in range(2):
                convert_ffn_weights_phase(c, I["ffn_w_gate"][i, hf], I["ffn_w_up"][i, hf], I["ffn_w_down"][i, hf],
                                          WB[(i, hf)][0], WB[(i, hf)][1], WB[(i, hf)][2], ff)
        for i in layers:
            if i == 2:
                kv_phase(c, xs, I["kv_norm_g"], MB["kv"], cos_d, sin_d, kT_d, v_d, wdt=BF16)
            run_ffn_bf16(c, xs, I["norm_g"][i, 0:1, :], WB[(i, 0)][0], WB[(i, 0)][1], WB[(i, 0)][2], ff)
            if i < 2:
                P = rwkv_param_aps(I, i, i == 1)
                P["g"] = I["norm_g"][i, 1:2, :]
                Oi = dict(O)
                Oi["V"] = V0 if i == 0 else V1
                Oi["vf"] = V0
                rwkv_proj_phase(c, xs, P, Oi, i == 1)
                rwkv_prep_phase(c, P, Oi, i == 1)
                rwkv_scan_phase(c, Oi)
                P["wo"] = MB[("wo", i)]
                rwkv_post_phase(c, xs, P, Oi, wdt=BF16)
            else:
                j = i - 2
                attn_phase(c, xs, I["norm_g"][i, 1:2, :], MB[("aq", i)], MB[("ao", i)], I["attn_sinks"][j:j + 1, :],
                           cos_d, sin_d, kT_d, v_d, wdt=BF16)
            run_ffn_bf16(c, xs, I["norm_g"][i, 2:3, :], WB[(i, 1)][0], WB[(i, 1)][1], WB[(i, 1)][2], ff)
            ple_phase(c, xs, I["norm_g"][i, 3:4, :], MB[("ple", i)], I["ple_w_up"][i], p_in[i], wdt=BF16)
        final_norm_phase(c, xs, I["final_norm_g"], y)
    return nc


N_CORES = 4
_NC_CACHE = {}


def kernel(**inputs):
    x = np.ascontiguousarray(inputs["x"], dtype=np.float32)
    B, S, _ = x.shape
    if S not in _NC_CACHE:
        _NC_CACHE[S] = build_full(S)
    nc = _NC_CACHE[S]
    shared = {}
    for k, shp in FULL_SHAPES.items():
        shared[k] = np.ascontiguousarray(np.asarray(inputs[k], dtype=np.float32).reshape(shp))
    p = np.asarray(inputs["p"], dtype=np.float32)
    pos = np.asarray(inputs["positions"], dtype=np.int32)
    in_maps = []
    for b in range(B):
        m = dict(shared)
        m["x"] = x[b]
        m["p"] = np.ascontiguousarray(p[:, b])
        m["positions"] = np.ascontiguousarray(pos[b])
        in_maps.append(m)
    res = run_bass_kernel_spmd(nc, in_maps, core_ids=list(range(B)))
    return np.stack([np.asarray(r["y"], dtype=np.float32) for r in res.results], axis=0)
```

```python
import numpy as np
import os
DUMP = int(os.environ.get('DUMP', '0'))
INV_SUB = int(os.environ.get('INV_SUB', '9'))
SCAN_DBG = float(os.environ.get('SCAN_DBG', '99'))
from contextlib import ExitStack
import concourse.bass as bass
import concourse.mybir as mybir
from concourse.bass_utils import run_bass_kernel_spmd

F32 = mybir.dt.float32
BF16 = mybir.dt.bfloat16
I32 = mybir.dt.int32
ALU = mybir.AluOpType
AF = mybir.ActivationFunctionType
AX = mybir.AxisListType

D = 2048
KC = D // 128
NH = 32
HD = 64
RMS_EPS = 1e-6
GN_EPS = 64e-5

ENGS = ("act", "dve", "pool", "pe", "sync")
NS_DMA = 8


class Sched:
    def __init__(self):
        self.ops = {e: [] for e in ENGS}
        self.lastw = {}
        self.readers = {}

    def emit(self, eng, fn, reads=(), writes=(), dma=False):
        idx = len(self.ops[eng])
        me = (eng, idx)
        deps = set()
        for k in tuple(reads) + tuple(writes):
            w = self.lastw.get(k)
            if w is not None:
                deps.add(w)
        for k in writes:
            rd = self.readers.get(k)
            if rd:
                deps.update(rd)
            self.readers[k] = []
        for k in reads:
            lst = self.readers.setdefault(k, [])
            if not dma:
                lst[:] = [r for r in lst if not (r[0] == eng and not self.ops[eng][r[1]][2])]
            lst.append(me)
        for k in writes:
            self.lastw[k] = me
        deps.discard(me)
        if eng == "pe":
            deps = {d for d in deps if d[0] != "pe"}
        self.ops[eng].append((fn, deps, dma))

    def setup(self, nc, es):
        self.nc = nc
        self.sems = {e: es.enter_context(nc.semaphore("s_" + e)) for e in ENGS}
        self.dsems = {e: [es.enter_context(nc.semaphore("d_%s%d" % (e, i))) for i in range(NS_DMA)]
                      for e in ("sync", "pool", "act")}
        self.cnt = {e: 0 for e in ENGS}
        self.dcnt = {e: 0 for e in ENGS}
        self.waited = {e: {} for e in ENGS}
        self.dlast = {e: [] for e in ENGS}

    def flush(self):
        nc = self.nc
        sems, dsems = self.sems, self.dsems
        signaled = {e: set() for e in ENGS}
        for e, lst in self.ops.items():
            for fn, deps, dma in lst:
                for (e2, i2) in deps:
                    signaled[e2].add(i2)
        val = {}
        for e in ENGS:
            c = self.cnt[e]
            j = self.dcnt[e]
            for i, (fn, deps, dma) in enumerate(self.ops[e]):
                if dma:
                    val[(e, i)] = (dsems[e][j % NS_DMA], 16 * (j // NS_DMA + 1))
                    j += 1
                elif i in signaled[e]:
                    c += 1
                    val[(e, i)] = (sems[e], c)
            self.cnt[e] = c
            self.dcnt[e] = j

        def run(e, engobj):
            waited = self.waited[e]
            dlast = self.dlast[e]
            for i, (fn, deps, dma) in enumerate(self.ops[e]):
                need = {}

                def want(s, v):
                    key = id(s)
                    if waited.get(key, 0) >= v:
                        return
                    if key not in need or need[key][1] < v:
                        need[key] = (s, v)
                for d in deps:
                    want(*val[d])
                if dma and len(dlast) >= NS_DMA:
                    want(*dlast[-NS_DMA])
                for key, (s, v) in need.items():
                    engobj.wait_ge(s, v)
                    waited[key] = v
                    if DUMP:
                        print("W", e, i, getattr(s, "name", s), v)
                ins = fn(engobj)
                if DUMP:
                    print("I", e, i, "dma" if dma else "", "SIG" if (dma or i in signaled[e]) else "", getattr(val.get((e, i), ("", ""))[0], "name", "-"), val.get((e, i), ("", ""))[1], str(ins)[:150].replace("\n", " "))
                if dma:
                    ins.then_inc(val[(e, i)][0], 16)
                    dlast.append(val[(e, i)])
                    del dlast[:-NS_DMA]
                elif i in signaled[e]:
                    ins.then_inc(sems[e], 1)
            for (s, v) in dlast:
                if waited.get(id(s), 0) < v:
                    engobj.wait_ge(s, v)
                    waited[id(s)] = v

        if DUMP:
            print("F flush")
        with nc.Block() as block:
            @block.sync
            def _(eng):
                run("sync", eng)

            @block.gpsimd
            def _(eng):
                run("pool", eng)

            @block.scalar
            def _(eng):
                run("act", eng)

            @block.vector
            def _(eng):
                run("dve", eng)

            @block.tensor
            def _(eng):
                run("pe", eng)
        nc.all_engine_barrier()
        self.ops = {e: [] for e in ENGS}
        self.lastw = {}
        self.readers = {}


class Ctx:
    def __init__(self, nc, es, S):
        self.nc = nc
        self.es = es
        self.S = S
        self.NT = S // 128
        self.sch = Sched()
        self.sch.setup(nc, es)
        self.ps = [es.enter_context(nc.psum_tensor("ps%d" % i, [128, 512], F32)) for i in range(8)]
        self.ps_i = 0
        self.uid = 0

    def sb(self, name, shape, dt=F32, es=None):
        self.uid += 1
        return (es or self.es).enter_context(self.nc.sbuf_tensor("%s_%d" % (name, self.uid), shape, dt))

    def flush(self):
        self.sch.flush()

    def psum(self):
        i = self.ps_i
        self.ps_i = (i + 1) % 8
        return self.ps[i], [("ps", i, q) for q in range(4)]

    @staticmethod
    def q(keys, lo, hi):
        return list(keys)

    def wslot(self, name):
        d = self.__dict__.setdefault("_ws", {})
        d[name] = d.get(name, -1) + 1
        return d[name] % 2

    def emit(self, eng, fn, reads=(), writes=()):
        self.sch.emit(eng, fn, reads, writes)

    def dma(self, eng, out, in_, reads=(), writes=()):
        self.sch.emit(eng, lambda e: e.dma_start(out=out, in_=in_), reads, writes, dma=True)


def make_consts(c):
    ident = c.sb("ident", [128, 128])
    c.emit("pool", lambda e: e.memset(ident[:], 0.0), writes=["const"])
    c.emit("pool", lambda e: e.affine_select(out=ident[:], in_=ident[:], compare_op=ALU.not_equal, fill=1.0,
                                             base=0, pattern=[[-1, 128]], channel_multiplier=1),
           writes=["const"])
    c.ident = ident


def transpose_tile(c, src, skey, dstT, dkeyf, tcol, nchunks=KC, evac=("act", "dve")):
    ident = c.ident
    for g in range(0, nchunks, 4):
        n = min(4, nchunks - g)
        ps, pk = c.psum()
        for j in range(n):
            kc = g + j
            c.emit("pe", lambda e, ps=ps, j=j, kc=kc: e.transpose(ps[:, j * 128:(j + 1) * 128],
                                                                     src[:, kc * 128:(kc + 1) * 128], ident[:]),
                   reads=[skey, "const"], writes=pk)
        eng = evac[(g // 4) % len(evac)]
        dst = dstT[:, g:g + n, tcol:tcol + 128]
        srcp = ps[:, 0:n * 128].rearrange("p (a b) -> p a b", a=n)
        if eng == "act":
            c.emit("act", lambda e, dst=dst, srcp=srcp: e.copy(out=dst, in_=srcp), reads=pk, writes=[dkeyf(g // 4)])
        else:
            c.emit("dve", lambda e, dst=dst, srcp=srcp: e.tensor_copy(out=dst, in_=srcp), reads=pk,
                   writes=[dkeyf(g // 4)])


def norm_to_hT(c, xt, xkey, gam, gkey, hT, hkeyf, tcol, B):
    junk, stat, hbuf = B["junk"], B["stat"], B["hbuf"]
    c.emit("act", lambda e: e.activation(out=junk[:], in_=xt, func=AF.Square, accum_out=stat[:, 0:1]),
           reads=[xkey], writes=["junk", "stat"])
    c.emit("dve", lambda e: e.tensor_scalar(out=stat[:, 1:2], in0=stat[:, 0:1], scalar1=1.0 / D, scalar2=RMS_EPS,
                                            op0=ALU.mult, op1=ALU.add), reads=["stat"], writes=["stat"])
    c.emit("act", lambda e: e.activation(out=stat[:, 3:4], in_=stat[:, 1:2], func=AF.Sqrt), reads=["stat"], writes=["stat"])
    c.emit("dve", lambda e: e.reciprocal(out=stat[:, 2:3], in_=stat[:, 3:4]), reads=["stat"], writes=["stat"])
    c.emit("dve", lambda e: e.scalar_tensor_tensor(out=hbuf[:], in0=xt, scalar=stat[:, 2:3], in1=gam,
                                                   op0=ALU.mult, op1=ALU.mult),
           reads=[xkey, "stat", gkey], writes=["hbuf"])
    transpose_tile(c, hbuf, "hbuf", hT, hkeyf, tcol)


def ffn_phase(c, x_d, gam_row, wg, wu, wd, FF, B):
    TB = 256
    NTT = TB // 128
    HC = FF // 128
    xt, gam, hT, act, wgs, wus, wds, sg = (B[k] for k in ("xt", "gam", "hT", "act", "wgs", "wus", "wds", "sg"))
    c.dma("sync", gam[:], gam_row.partition_broadcast(128), writes=["gam"])
    wg_v = wg.rearrange("(kc p) f -> p kc f", p=128)
    wu_v = wu.rearrange("(kc p) f -> p kc f", p=128)
    wd_v = wd.rearrange("(hc p) d -> p hc d", p=128)
    GH = 4
    assert HC % GH == 0
    hkeyf = lambda g: ("hT", g)
    for tb in range(c.S // TB):
        for tt in range(NTT):
            r0 = tb * TB + tt * 128
            c.dma("sync", xt[tt][:], x_d[r0:r0 + 128, :], reads=[("xd", r0 // 128)], writes=[("xt", tt)])
            norm_to_hT(c, xt[tt][:], ("xt", tt), gam[:], "gam", hT, hkeyf, tt * 128, B)
        for hc in range(HC):
            s = hc % 2
            c.dma("sync", wgs[s][:], wg_v[:, :, hc * 128:(hc + 1) * 128], writes=[("wg", s)])
            c.dma("pool", wus[s][:], wu_v[:, :, hc * 128:(hc + 1) * 128], writes=[("wu", s)])
            psg, pgk = c.psum()
            psu, puk = c.psum()
            for kc in range(KC):
                c.emit("pe", lambda e, psg=psg, s=s, kc=kc: e.matmul(psg[:, 0:TB], lhsT=wgs[s][:, kc, :], rhs=hT[:, kc, 0:TB],
                                                                      start=(kc == 0), stop=(kc == KC - 1)),
                       reads=[("wg", s), ("hT", kc // 4)], writes=pgk)
            for kc in range(KC):
                c.emit("pe", lambda e, psu=psu, s=s, kc=kc: e.matmul(psu[:, 0:TB], lhsT=wus[s][:, kc, :], rhs=hT[:, kc, 0:TB],
                                                                      start=(kc == 0), stop=(kc == KC - 1)),
                       reads=[("wu", s), ("hT", kc // 4)], writes=puk)
            c.emit("act", lambda e, psg=psg, s=s: e.activation(out=sg[s][:], in_=psg[:, 0:TB], func=AF.Silu),
                   reads=pgk, writes=[("sg", s)])
            c.emit("dve", lambda e, psu=psu, s=s, hc=hc: e.tensor_tensor(out=act[:, hc, :], in0=sg[s][:], in1=psu[:, 0:TB],
                                                                          op=ALU.mult),
                   reads=puk + [("sg", s)], writes=[("act", hc)])
        for db in range(D // 512):
            pds = [c.psum() for _ in range(NTT)]
            for hg in range(HC // GH):
                s = (db * (HC // GH) + hg) % 2
                q = "sync" if hg % 2 == 0 else "pool"
                c.dma(q, wds[s][:], wd_v[:, hg * GH:(hg + 1) * GH, db * 512:(db + 1) * 512], writes=[("wd", s)])
                for tt in range(NTT):
                    pd, pdk = pds[tt]
                    for j in range(GH):
                        hc = hg * GH + j
                        c.emit("pe", lambda e, pd=pd, s=s, j=j, hc=hc, tt=tt: e.matmul(
                            pd[:, :], lhsT=act[:, hc, tt * 128:(tt + 1) * 128], rhs=wds[s][:, j, :],
                            start=(hc == 0), stop=(hc == HC - 1)),
                            reads=[("act", hc), ("wd", s)], writes=pdk)
            for tt in range(NTT):
                pd, pdk = pds[tt]
                c.emit("dve", lambda e, pd=pd, tt=tt, db=db: e.scalar_tensor_tensor(
                    out=xt[tt][:, db * 512:(db + 1) * 512], in0=pd[:, :], scalar=0.5,
                    in1=xt[tt][:, db * 512:(db + 1) * 512], op0=ALU.mult, op1=ALU.add),
                    reads=pdk + [("xt", tt)], writes=[("xt", tt)])
        for tt in range(NTT):
            r0 = tb * TB + tt * 128
            c.dma("sync", x_d[r0:r0 + 128, :], xt[tt][:], reads=[("xt", tt)], writes=[("xd", r0 // 128)])


def alloc_common(c, es):
    B = {}
    B["xt"] = [c.sb("xt%d" % i, [128, D], es=es) for i in range(2)]
    B["gam"] = c.sb("gam", [128, D], es=es)
    B["junk"] = c.sb("junk", [128, D], es=es)
    B["hbuf"] = c.sb("hbuf", [128, D], es=es)
    B["stat"] = c.sb("stat", [128, 8], es=es)
    B["hT"] = c.sb("hT", [128, KC, 256], es=es)
    return B


def run_ffn(c, x_d, gam_row, wg, wu, wd, FF):
    with ExitStack() as pes:
        B = alloc_common(c, pes)
        B["act"] = c.sb("act", [128, FF // 128, 256], es=pes)
        B["wgs"] = [c.sb("wgs%d" % i, [128, KC, 128], es=pes) for i in range(2)]
        B["wus"] = [c.sb("wus%d" % i, [128, KC, 128], es=pes) for i in range(2)]
        B["wds"] = [c.sb("wds%d" % i, [128, 4, 512], es=pes) for i in range(2)]
        B["sg"] = [c.sb("sg%d" % i, [128, 256], es=pes) for i in range(2)]
        ffn_phase(c, x_d, gam_row, wg, wu, wd, FF, B)
        c.flush()


def copy_rows(c, dst, src, S):
    for t in range(S // 128):
        c.dma("sync", dst[t * 128:(t + 1) * 128, :], src[t * 128:(t + 1) * 128, :])
    c.flush()


def test_ffn_program(S, FF):
    nc = bass.Bass("TRN2", target_bir_lowering=False)
    x_in = nc.dram_tensor("x", [S, D], F32, kind="ExternalInput").ap()
    g_in = nc.dram_tensor("g", [1, D], F32, kind="ExternalInput").ap()
    wg = nc.dram_tensor("wg", [D, FF], F32, kind="ExternalInput").ap()
    wu = nc.dram_tensor("wu", [D, FF], F32, kind="ExternalInput").ap()
    wd = nc.dram_tensor("wd", [FF, D], F32, kind="ExternalInput").ap()
    y = nc.dram_tensor("y", [S, D], F32, kind="ExternalOutput").ap()
    xs = nc.dram_tensor("xs", [S, D], F32).ap()
    with ExitStack() as es:
        c = Ctx(nc, es, S)
        make_consts(c)
        copy_rows(c, xs, x_in, S)
        run_ffn(c, xs, g_in, wg, wu, wd, FF)
        run_ffn(c, xs, g_in, wg, wu, wd, FF)
        copy_rows(c, y, xs, S)
    return nc


def proj(c, xT, xkeyf, ntt, W, N, wbufs, wname, epi, kcn=KC, CW=256, qi=[0]):
    Wv = W.rearrange("(kc p) n -> p kc n", p=128)
    for cb in range(N // CW):
        s = c.wslot(wname)
        qi[0] += 1
        c.dma("sync" if qi[0] % 2 else "pool", wbufs[s][:, 0:kcn, 0:CW], Wv[:, :, cb * CW:(cb + 1) * CW],
              writes=[(wname, s)])
        ps, pk = c.psum()
        for tt in range(ntt):
            for kc in range(kcn):
                c.emit("pe", lambda e, ps=ps, tt=tt, kc=kc, s=s: e.matmul(
                    ps[:, tt * CW:(tt + 1) * CW], lhsT=xT[:, kc, tt * 128:(tt + 1) * 128], rhs=wbufs[s][:, kc, 0:CW],
                    start=(kc == 0), stop=(kc == kcn - 1)),
                    reads=[(wname, s)] + xkeyf(kc), writes=c.q(pk, tt * CW, (tt + 1) * CW))
        epi(cb, ps, pk)


def store_epi(c, out_d, r0, ntt, ost, func, CW=256):
    def epi(cb, ps, pk):
        s = c.wslot("ost")
        c.emit("act", lambda e: e.activation(out=ost[s][:, 0:ntt * CW], in_=ps[:, 0:ntt * CW], func=func),
               reads=c.q(pk, 0, ntt * CW), writes=[("ost", s)])
        dst = out_d[r0:r0 + ntt * 128, cb * CW:(cb + 1) * CW].rearrange("(tt p) n -> p tt n", p=128)
        c.dma("sync", dst, ost[s][:, 0:ntt * CW].rearrange("p (tt n) -> p tt n", tt=ntt),
              reads=[("ost", s)], writes=[("od", id(out_d), r0, cb)])
    return epi


def load_rows_T(c, rows_ap, R, dst, pes):
    tmp = c.sb("rowsT", [128, D], es=pes)
    c.dma("sync", tmp[0:R, :], rows_ap, writes=["rowsT"])
    for g in range(0, KC, 4):
        ps, pk = c.psum()
        for j in range(4):
            kc = g + j
            c.emit("pe", lambda e, ps=ps, j=j, kc=kc: e.transpose(ps[:, j * R:(j + 1) * R],
                                                                     tmp[0:R, kc * 128:(kc + 1) * 128], c.ident[0:R, 0:R]),
                   reads=["rowsT", "const"], writes=pk)
        c.emit("dve", lambda e, ps=ps, g=g: e.tensor_copy(out=dst[:, g:g + 4, 0:R],
                                                         in_=ps[:, 0:4 * R].rearrange("p (a b) -> p a b", a=4)),
               reads=pk, writes=["rowsTd"])


C0 = -float(np.exp(-0.5))


def rwkv_proj_phase(c, x_d, P, O, has_vmix):
    TB, NTT = 256, 2
    with ExitStack() as pes:
        xt = c.sb("xt", [128, D], es=pes)
        gam = c.sb("gam", [128, D], es=pes)
        stat = c.sb("stat", [128, 8], es=pes)
        hTw = c.sb("hTw", [128, KC, 257], es=pes)
        dxT = c.sb("dxT", [128, KC, 256], es=pes)
        xcT = [c.sb("xcT", [128, KC, 256], es=pes) for _ in range(2)]
        wb = [c.sb("wb", [128, KC, 256], es=pes) for _ in range(2)]
        muT = c.sb("muT", [128, KC, 6], es=pes)
        w2a = c.sb("w2a", [128, D], es=pes)
        a2a = c.sb("a2a", [128, D], es=pes)
        v2a = c.sb("v2a", [128, D], es=pes)
        g2s = c.sb("g2s", [128, 2, D], es=pes)
        t1 = {k: c.sb("t1" + k, [128, 256], es=pes) for k in "wav"}
        t1g = c.sb("t1g", [128, 2, 256], es=pes)
        ost = [c.sb("ost", [128, 512], es=pes) for _ in range(2)]
        use_b = "wr_b" in P
        if use_b:
            xcb = c.sb("xcb", [128, KC, 256], BF16, es=pes)
            wbb = [c.sb("wbb", [128, KC, 256], BF16, es=pes) for _ in range(2)]
        B = dict(junk=dxT[:, 0:8, :].rearrange("p a b -> p (a b)"), stat=stat,
                 hbuf=xcT[1][:, 0:8, :].rearrange("p a b -> p (a b)"))
        c.dma("sync", gam[:], P["g"].partition_broadcast(128), writes=["gam"])
        load_rows_T(c, P["mu"], 6, muT, pes)
        c.dma("sync", w2a[0:96, :], P["w2"], writes=["w2a"])
        c.dma("sync", w2a[96:97, :], P["w0"], writes=["w2a"])
        c.dma("sync", a2a[0:96, :], P["a2"], writes=["a2a"])
        c.dma("sync", a2a[96:97, :], P["a0"], writes=["a2a"])
        if has_vmix:
            c.dma("sync", v2a[0:64, :], P["v2"], writes=["v2a"])
            c.dma("sync", v2a[64:65, :], P["v0"], writes=["v2a"])
        c.dma("sync", g2s[:], P["g2"].rearrange("(ch p) n -> p ch n", p=128), writes=["g2s"])
        for k in "wav":
            c.emit("dve", lambda e, k=k: e.memset(t1[k][:], 1.0), writes=[("t1", k)])
        c.emit("dve", lambda e: e.memset(hTw[:, :, 0:1], 0.0), writes=[("hT", g) for g in range(4)])

        hkeys = [("hT", g) for g in range(4)]
        for tb in range(c.S // TB):
            r0 = tb * TB
            for tt in range(NTT):
                c.dma("sync", xt[:], x_d[r0 + tt * 128:r0 + (tt + 1) * 128, :], writes=["xt"])
                junk = B["junk"]
                c.emit("act", lambda e: e.activation(out=junk, in_=xt[:], func=AF.Square, accum_out=stat[:, 0:1]),
                       reads=["xt"], writes=["dxT", "stat"])
                c.emit("dve", lambda e: e.tensor_scalar(out=stat[:, 1:2], in0=stat[:, 0:1], scalar1=1.0 / D,
                                                        scalar2=RMS_EPS, op0=ALU.mult, op1=ALU.add),
                       reads=["stat"], writes=["stat"])
                c.emit("act", lambda e: e.activation(out=stat[:, 3:4], in_=stat[:, 1:2], func=AF.Sqrt),
                       reads=["stat"], writes=["stat"])
                c.emit("dve", lambda e: e.reciprocal(out=stat[:, 2:3], in_=stat[:, 3:4]), reads=["stat"], writes=["stat"])
                hb = B["hbuf"]
                c.emit("dve", lambda e: e.scalar_tensor_tensor(out=hb, in0=xt[:], scalar=stat[:, 2:3], in1=gam[:],
                                                               op0=ALU.mult, op1=ALU.mult),
                       reads=["xt", "stat", "gam"], writes=[("xc", 1, g) for g in range(4)])
                ident = c.ident
                for g in range(0, KC, 4):
                    ps, pk = c.psum()
                    for j in range(4):
                        kc = g + j
                        c.emit("pe", lambda e, ps=ps, j=j, kc=kc: e.transpose(
                            ps[:, j * 128:(j + 1) * 128], hb[:, kc * 128:(kc + 1) * 128], ident[:]),
                            reads=[("xc", 1, q) for q in range(4)] + ["const"], writes=pk)
                    dst = hTw[:, g:g + 4, 1 + tt * 128:1 + (tt + 1) * 128]
                    srcp = ps[:, :].rearrange("p (a b) -> p a b", a=4)
                    if (g // 4) % 2 == 0:
                        c.emit("act", lambda e, dst=dst, srcp=srcp: e.copy(out=dst, in_=srcp), reads=pk,
                               writes=[("hT", g // 4)])
                    else:
                        c.emit("dve", lambda e, dst=dst, srcp=srcp: e.tensor_copy(out=dst, in_=srcp), reads=pk,
                               writes=[("hT", g // 4)])
            c.emit("dve", lambda e: e.tensor_tensor(out=dxT[:], in0=hTw[:, :, 0:256], in1=hTw[:, :, 1:257],
                                                    op=ALU.subtract), reads=hkeys, writes=["dxT"])

            def mix(m):
                s = c.wslot("xc")
                for kc in range(KC):
                    c.emit("dve", lambda e, s=s, kc=kc, m=m: e.scalar_tensor_tensor(
                        out=xcT[s][:, kc, :], in0=dxT[:, kc, :], scalar=muT[:, kc, m:m + 1], in1=hTw[:, kc, 1:257],
                        op0=ALU.mult, op1=ALU.add),
                        reads=["dxT", "rowsTd", ("hT", kc // 4)], writes=[("xc", s, kc // 4)])
                return xcT[s], (lambda kc, s=s: [("xc", s, kc // 4)])

            def mixb(m):
                for kc in range(KC):
                    c.emit("dve", lambda e, kc=kc, m=m: e.scalar_tensor_tensor(
                        out=xcb[:, kc, :], in0=dxT[:, kc, :], scalar=muT[:, kc, m:m + 1], in1=hTw[:, kc, 1:257],
                        op0=ALU.mult, op1=ALU.add),
                        reads=["dxT", "rowsTd", ("hT", kc // 4)], writes=[("xcb", kc // 4)])
                return xcb, (lambda kc: [("xcb", kc // 4)])

            def lora(xT, xkf, w1, R, func, t1tile, t1key, nch=1):
                s = c.wslot("wb")
                c.dma("pool", wb[s][:, :, 0:R * nch], w1.rearrange("(kc p) r -> p kc r", p=128), writes=[("wb", s)])
                for ch in range(nch):
                    ps, pk = c.psum()
                    for kc in range(KC):
                        c.emit("pe", lambda e, ps=ps, kc=kc, s=s, ch=ch: e.matmul(
                            ps[0:R, 0:256], lhsT=wb[s][:, kc, ch * R:(ch + 1) * R], rhs=xT[:, kc, 0:256],
                            start=(kc == 0), stop=(kc == KC - 1)),
                            reads=[("wb", s)] + xkf(kc), writes=c.q(pk, 0, 256))
                    dst = t1tile[0:R, :] if nch == 1 else t1tile[0:R, ch, :]
                    c.emit("act", lambda e, ps=ps, dst=dst: e.activation(out=dst, in_=ps[0:R, 0:256], func=func),
                           reads=c.q(pk, 0, 256), writes=[t1key])

            def lora2(t1tile, t1key, K, w2tile, w2key, out_d, func, nch=1):
                epi = store_epi(c, out_d, r0, NTT, ost, func)
                for cb in range(D // 256):
                    ps, pk = c.psum()
                    for tt in range(NTT):
                        for ch in range(nch):
                            lhsT = t1tile[0:K, tt * 128:(tt + 1) * 128] if nch == 1 else t1tile[0:K, ch, tt * 128:(tt + 1) * 128]
                            rhs = w2tile[0:K, cb * 256:(cb + 1) * 256] if nch == 1 else w2tile[0:K, ch, cb * 256:(cb + 1) * 256]
                            c.emit("pe", lambda e, ps=ps, tt=tt, lhsT=lhsT, rhs=rhs, ch=ch: e.matmul(
                                ps[:, tt * 256:(tt + 1) * 256], lhsT=lhsT, rhs=rhs, start=(ch == 0), stop=(ch == nch - 1)),
                                reads=[t1key, w2key], writes=c.q(pk, tt * 256, (tt + 1) * 256))
                    epi(cb, ps, pk)

            if use_b:
                xT, xkf = mixb(0)
                proj(c, xT, xkf, NTT, P["wr_b"], D, wbb, "wbb", store_epi(c, O["r"], r0, NTT, ost, AF.Copy))
                xT, xkf = mixb(2)
                proj(c, xT, xkf, NTT, P["wk_b"], D, wbb, "wbb", store_epi(c, O["k"], r0, NTT, ost, AF.Copy))
                xT, xkf = mixb(3)
                proj(c, xT, xkf, NTT, P["wv_b"], D, wbb, "wbb", store_epi(c, O["v"], r0, NTT, ost, AF.Copy))
                if has_vmix:
                    xT, xkf = mix(3)
            else:
                xT, xkf = mix(0)
                proj(c, xT, xkf, NTT, P["wr"], D, wb, "wb", store_epi(c, O["r"], r0, NTT, ost, AF.Copy))
                xT, xkf = mix(2)
                proj(c, xT, xkf, NTT, P["wk"], D, wb, "wb", store_epi(c, O["k"], r0, NTT, ost, AF.Copy))
                xT, xkf = mix(3)
                proj(c, xT, xkf, NTT, P["wv"], D, wb, "wb", store_epi(c, O["v"], r0, NTT, ost, AF.Copy))
            if has_vmix:
                lora(xT, xkf, P["v1"], 64, AF.Copy, t1["v"], ("t1", "v"))
                lora2(t1["v"], ("t1", "v"), 65, v2a, "v2a", O["sv"], AF.Sigmoid)
            xT, xkf = mix(1)
            lora(xT, xkf, P["w1"], 96, AF.Tanh, t1["w"], ("t1", "w"))
            lora2(t1["w"], ("t1", "w"), 97, w2a, "w2a", O["sw"], AF.Sigmoid)
            xT, xkf = mix(4)
            lora(xT, xkf, P["a1"], 96, AF.Copy, t1["a"], ("t1", "a"))
            lora2(t1["a"], ("t1", "a"), 97, a2a, "a2a", O["a"], AF.Sigmoid)
            xT, xkf = mix(5)
            lora(xT, xkf, P["g1"], 128, AF.Sigmoid, t1g, ("t1", "g"), nch=2)
            lora2(t1g, ("t1", "g"), 128, g2s, "g2s", O["g"], AF.Copy, nch=2)
            c.emit("dve", lambda e: e.tensor_copy(out=hTw[:, :, 0:1], in_=hTw[:, :, 256:257]), reads=hkeys, writes=hkeys)
        c.flush()


def rwkv_param_aps(nc_inputs, j, has_vmix):
    I = nc_inputs
    P = dict(
        mu=I["rwkv_mu"][j], wr=I["rwkv_w_rkv"][j, 0], wk=I["rwkv_w_rkv"][j, 1], wv=I["rwkv_w_rkv"][j, 2],
        wo=I["rwkv_w_o"][j], w0=I["rwkv_w0"][j:j + 1, :], w1=I["rwkv_w1"][j], w2=I["rwkv_w2"][j],
        a0=I["rwkv_a0"][j:j + 1, :], a1=I["rwkv_a1"][j], a2=I["rwkv_a2"][j], g1=I["rwkv_g1"][j], g2=I["rwkv_g2"][j],
        k_k=I["rwkv_k_k"][j:j + 1, :], k_a=I["rwkv_k_a"][j:j + 1, :],
        r_k=I["rwkv_r_k"][j:j + 1].rearrange("o h n -> o (h n)"),
        gn_g=I["rwkv_gn_g"][j:j + 1, :], gn_b=I["rwkv_gn_b"][j:j + 1, :])
    if has_vmix:
        P.update(v0=I["rwkv_v0"][j - 1:j, :], v1=I["rwkv_v1"][j - 1], v2=I["rwkv_v2"][j - 1])
    return P


RWKV_SHAPES = dict(
    rwkv_mu=(2, 6, D), rwkv_w_rkv=(2, 3, D, D), rwkv_w_o=(2, D, D), rwkv_w0=(2, D), rwkv_w1=(2, D, 96),
    rwkv_w2=(2, 96, D), rwkv_a0=(2, D), rwkv_a1=(2, D, 96), rwkv_a2=(2, 96, D), rwkv_v0=(1, D), rwkv_v1=(1, D, 64),
    rwkv_v2=(1, 64, D), rwkv_g1=(2, D, 256), rwkv_g2=(2, 256, D), rwkv_k_k=(2, D), rwkv_k_a=(2, D),
    rwkv_r_k=(2, 32, 64), rwkv_gn_g=(2, D), rwkv_gn_b=(2, D))


def test_rwkv_program(S, stage):
    nc = bass.Bass("TRN2", target_bir_lowering=False)
    I = {k: nc.dram_tensor(k, list(v), F32, kind="ExternalInput").ap() for k, v in RWKV_SHAPES.items()}
    x_in = nc.dram_tensor("x", [S, D], F32, kind="ExternalInput").ap()
    g_in = nc.dram_tensor("g", [1, D], F32, kind="ExternalInput").ap()
    vf_in = nc.dram_tensor("vf", [S, D], F32, kind="ExternalInput").ap()
    names = ["r", "k", "v", "sw", "a", "g", "sv", "At", "Rt", "Bh", "Kh", "Bt", "Kt", "V", "bonus", "y", "xo"]
    O = {k: nc.dram_tensor("o_" + k, [S, D], F32, kind="ExternalOutput").ap() for k in names}
    O["el"] = nc.dram_tensor("o_el", [S // 128, D], F32, kind="ExternalOutput").ap()
    O["vf"] = vf_in
    with ExitStack() as es:
        c = Ctx(nc, es, S)
        make_consts(c)
        make_masks(c)
        P = rwkv_param_aps(I, 1, True)
        P["g"] = g_in
        copy_rows(c, O["xo"], x_in, S)
        rwkv_proj_phase(c, O["xo"], P, O, True)
        if stage >= 2:
            rwkv_prep_phase(c, P, O, True)
        if stage >= 3:
            rwkv_scan_phase(c, O)
        if stage >= 4:
            rwkv_post_phase(c, O["xo"], P, O)
    return nc


def make_masks(c):
    tri = c.sb("tri", [128, 128])
    ones = c.sb("ones", [128, 128])
    mask4 = c.sb("mask4", [128, 512])
    msl = c.sb("msl", [128, 128])
    c.emit("pool", lambda e: e.memset(ones[:], 1.0), writes=["const"])
    c.emit("pool", lambda e: e.memset(tri[:], 1.0), writes=["const"])
    c.emit("pool", lambda e: e.affine_select(out=tri[:], in_=tri[:], compare_op=ALU.is_ge, fill=0.0, base=0,
                                             pattern=[[1, 128]], channel_multiplier=-1), writes=["const"])
    c.emit("pool", lambda e: e.memset(mask4[:], 1.0), writes=["const"])
    for q in range(4):
        op = ALU.is_gt if q % 2 == 0 else ALU.is_ge
        c.emit("pool", lambda e, q=q, op=op: e.affine_select(
            out=mask4[:, q * 128:(q + 1) * 128], in_=mask4[:, q * 128:(q + 1) * 128], compare_op=op, fill=0.0, base=0,
            pattern=[[1, 128]], channel_multiplier=-1), writes=["const"])
    c.emit("pool", lambda e: e.memset(msl[:], 1.0), writes=["const"])
    c.emit("pool", lambda e: e.affine_select(out=msl[:], in_=msl[:], compare_op=ALU.is_gt, fill=0.0, base=0,
                                             pattern=[[-1, 128]], channel_multiplier=1), writes=["const"])
    c.tri, c.ones, c.mask4, c.msl = tri, ones, mask4, msl


def rwkv_prep_phase(c, P, O, has_vmix):
    CB = 512
    with ExitStack() as pes:
        names_in = ["r", "k", "v", "sw", "a"] + (["sv", "vf"] if has_vmix else [])
        ld = {n: [c.sb("ld" + n, [128, CB], es=pes) for _ in range(2)] for n in names_in}
        names_out = ["At", "Rt", "Bh", "Kh", "Bt", "Kt", "V", "bonus"]
        ob = {n: [c.sb("ob" + n, [128, CB], es=pes) for _ in range(2)] for n in names_out}
        tm = {n: c.sb("tm" + n, [128, CB], es=pes) for n in ["lw", "epos", "eneg", "eprev", "EL", "kk", "sq", "b", "t", "kh", "t2", "d"]}
        st = c.sb("st", [128, 64], es=pes)
        kkp = c.sb("kkp", [128, D], es=pes)
        kap = c.sb("kap", [128, D], es=pes)
        rkp = c.sb("rkp", [128, D], es=pes)
        c.dma("sync", kkp[:], P["k_k"].partition_broadcast(128), writes=["par"])
        c.dma("sync", kap[:], P["k_a"].partition_broadcast(128), writes=["par"])
        c.dma("sync", rkp[:], P["r_k"].partition_broadcast(128), writes=["par"])
        it = 0
        for ch in range(c.S // 128):
            rs = slice(ch * 128, (ch + 1) * 128)
            for cb in range(D // CB):
                cs = slice(cb * CB, (cb + 1) * CB)
                s = it % 2
                it += 1
                L = {}
                for i, n in enumerate(names_in):
                    c.dma("sync" if i % 2 == 0 else "pool", ld[n][s][:], O[n][rs, cs], writes=[("ld", n, s)])
                    L[n] = ld[n][s]
                lk = lambda *ns: [("ld", n, s) for n in ns]
                T = tm
                ok = lambda *ns: [("ob", n, s) for n in ns]
                OB = {n: ob[n][s] for n in names_out}
                c.emit("dve", lambda e, L=L: e.tensor_scalar(out=T["lw"][:], in0=L["sw"][:], scalar1=C0, scalar2=0.0,
                                                             op0=ALU.mult, op1=ALU.add), reads=lk("sw"), writes=["lw"])
                psc, pck = c.psum()
                pst, ptk = c.psum()
                c.emit("pe", lambda e, psc=psc: e.matmul(psc[:, :], lhsT=c.tri[:], rhs=T["lw"][:], start=True, stop=True),
                       reads=["lw", "const"], writes=pck)
                c.emit("pe", lambda e, pst=pst: e.matmul(pst[:, :], lhsT=c.ones[:], rhs=T["lw"][:], start=True, stop=True),
                       reads=["lw", "const"], writes=ptk)
                c.emit("act", lambda e, psc=psc: e.activation(out=T["epos"][:], in_=psc[:, :], func=AF.Exp), reads=pck, writes=["epos"])
                c.emit("act", lambda e, psc=psc: e.activation(out=T["eneg"][:], in_=psc[:, :], func=AF.Exp, scale=-1.0),
                       reads=pck, writes=["eneg"])
                c.emit("dve", lambda e, psc=psc: e.tensor_tensor(out=T["eprev"][:], in0=psc[:, :], in1=T["lw"][:], op=ALU.subtract),
                       reads=pck + ["lw"], writes=["eprev"])
                c.emit("act", lambda e: e.activation(out=T["eprev"][:], in_=T["eprev"][:], func=AF.Exp), reads=["eprev"], writes=["eprev"])
                c.emit("act", lambda e, pst=pst: e.activation(out=T["EL"][:], in_=pst[:, :], func=AF.Exp), reads=ptk, writes=["EL"])
                c.emit("pool", lambda e, L=L, cs=cs: e.tensor_tensor(out=T["kk"][:], in0=L["k"][:], in1=kkp[:, cs], op=ALU.mult),
                       reads=lk("k") + ["par"], writes=["kk"])
                c.emit("pool", lambda e: e.tensor_tensor(out=T["sq"][:], in0=T["kk"][:], in1=T["kk"][:], op=ALU.mult),
                       reads=["kk"], writes=["sq"])
                c.emit("dve", lambda e: e.tensor_reduce(out=st[:, 0:8], in_=T["sq"][:].rearrange("p (h j) -> p h j", j=64),
                                                        axis=AX.X, op=ALU.add), reads=["sq"], writes=["st"])
                c.emit("act", lambda e: e.activation(out=st[:, 8:16], in_=st[:, 0:8], func=AF.Sqrt), reads=["st"], writes=["st"])
                c.emit("dve", lambda e: e.tensor_scalar(out=st[:, 8:16], in0=st[:, 8:16], scalar1=1e-12, scalar2=0.0,
                                                        op0=ALU.max, op1=ALU.add), reads=["st"], writes=["st"])
                c.emit("dve", lambda e: e.reciprocal(out=st[:, 16:24], in_=st[:, 8:16]), reads=["st"], writes=["st"])
                c.emit("dve", lambda e: e.tensor_tensor(
                    out=T["kk"][:].rearrange("p (h j) -> p h j", j=64), in0=T["kk"][:].rearrange("p (h j) -> p h j", j=64),
                    in1=st[:, 16:24].unsqueeze(2).to_broadcast([128, 8, 64]), op=ALU.mult), reads=["kk", "st"], writes=["kk"])
                c.emit("dve", lambda e, OB=OB: e.scalar_tensor_tensor(out=OB["At"][:], in0=T["kk"][:], scalar=-1.0, in1=T["eprev"][:],
                                                                     op0=ALU.mult, op1=ALU.mult),
                       reads=["kk", "eprev"], writes=ok("At"))
                c.emit("pool", lambda e, L=L: e.tensor_tensor(out=T["b"][:], in0=T["kk"][:], in1=L["a"][:], op=ALU.mult),
                       reads=["kk"] + lk("a"), writes=["b"])
                c.emit("pool", lambda e, OB=OB: e.tensor_tensor(out=OB["Bh"][:], in0=T["b"][:], in1=T["eneg"][:], op=ALU.mult),
                       reads=["b", "eneg"], writes=ok("Bh"))
                c.emit("pool", lambda e, OB=OB: e.tensor_tensor(out=OB["Bt"][:], in0=OB["Bh"][:], in1=T["EL"][:], op=ALU.mult),
                       reads=ok("Bh") + ["EL"], writes=ok("Bt"))
                c.emit("dve", lambda e, L=L, cs=cs: e.scalar_tensor_tensor(out=T["t"][:], in0=L["a"][:], scalar=-1.0, in1=kap[:, cs],
                                                                          op0=ALU.add, op1=ALU.mult),
                       reads=lk("a") + ["par"], writes=["t"])
                c.emit("dve", lambda e, L=L: e.scalar_tensor_tensor(out=T["kh"][:], in0=T["t"][:], scalar=1.0, in1=L["k"][:],
                                                                   op0=ALU.add, op1=ALU.mult),
                       reads=["t"] + lk("k"), writes=["kh"])
                c.emit("pool", lambda e, OB=OB: e.tensor_tensor(out=OB["Kh"][:], in0=T["kh"][:], in1=T["eneg"][:], op=ALU.mult),
                       reads=["kh", "eneg"], writes=ok("Kh"))
                c.emit("pool", lambda e, OB=OB: e.tensor_tensor(out=OB["Kt"][:], in0=OB["Kh"][:], in1=T["EL"][:], op=ALU.mult),
                       reads=ok("Kh") + ["EL"], writes=ok("Kt"))
                c.emit("pool", lambda e, OB=OB, L=L: e.tensor_tensor(out=OB["Rt"][:], in0=L["r"][:], in1=T["epos"][:], op=ALU.mult),
                       reads=lk("r") + ["epos"], writes=ok("Rt"))
                if has_vmix:
                    c.emit("pool", lambda e, L=L: e.tensor_tensor(out=T["d"][:], in0=L["vf"][:], in1=L["v"][:], op=ALU.subtract),
                           reads=lk("vf", "v"), writes=["d"])
                    c.emit("pool", lambda e, L=L: e.tensor_tensor(out=T["d"][:], in0=T["d"][:], in1=L["sv"][:], op=ALU.mult),
                           reads=lk("sv") + ["d"], writes=["d"])
                    c.emit("pool", lambda e, L=L, OB=OB: e.tensor_tensor(out=OB["V"][:], in0=T["d"][:], in1=L["v"][:], op=ALU.add),
                           reads=lk("v") + ["d"], writes=ok("V"))
                else:
                    c.emit("pool", lambda e, L=L, OB=OB: e.tensor_copy(out=OB["V"][:], in_=L["v"][:]), reads=lk("v"), writes=ok("V"))
                c.emit("pool", lambda e, L=L: e.tensor_tensor(out=T["t2"][:], in0=L["r"][:], in1=T["kh"][:], op=ALU.mult),
                       reads=lk("r") + ["kh"], writes=["t2"])
                c.emit("pool", lambda e, cs=cs: e.tensor_tensor(out=T["t2"][:], in0=T["t2"][:], in1=rkp[:, cs], op=ALU.mult),
                       reads=["t2", "par"], writes=["t2"])
                c.emit("dve", lambda e: e.tensor_reduce(out=st[:, 32:40], in_=T["t2"][:].rearrange("p (h j) -> p h j", j=64),
                                                        axis=AX.X, op=ALU.add), reads=["t2"], writes=["st2"])
                c.emit("dve", lambda e, OB=OB: e.tensor_tensor(
                    out=OB["bonus"][:].rearrange("p (h j) -> p h j", j=64), in0=OB["V"][:].rearrange("p (h j) -> p h j", j=64),
                    in1=st[:, 32:40].unsqueeze(2).to_broadcast([128, 8, 64]), op=ALU.mult), reads=ok("V") + ["st2"], writes=ok("bonus"))
                for i, n in enumerate(names_out):
                    c.dma("sync" if i % 2 == 0 else "pool", O[n][rs, cs], OB[n][:], reads=ok(n), writes=[("od", n, ch, cb)])
                c.dma("sync", O["el"][ch:ch + 1, cs], T["EL"][0:1, :], reads=["EL"], writes=[("od", "el", ch, cb)])
        c.flush()


def rwkv_scan_phase(c, O):
    CB, G = 512, 8
    NB = G // 4
    ident = c.ident
    with ExitStack() as pes:
        names_in = ["At", "Rt", "Bh", "Kh", "Bt", "Kt", "V"]
        ld = {n: [c.sb("sl" + n, [128, CB], es=pes) for _ in range(2)] for n in names_in}
        elrow = [c.sb("elrow", [1, CB], es=pes) for _ in range(2)]
        gl = [c.sb("gl", [64, 8], es=pes) for _ in range(2)]
        Tst = c.sb("Tst", [64, NH, 64], es=pes)
        XT = [c.sb("XT", [64, 512], es=pes) for _ in range(G)]
        ABK = [c.sb("ABK", [128, 512], es=pes) for _ in range(G)]
        Xb = [[c.sb("Xb", [128, 128], es=pes) for _ in range(2)] for _ in range(G)]
        XTb = [[c.sb("XTb", [128, 128], es=pes) for _ in range(3)] for _ in range(G)]
        PTb = [[c.sb("PTb", [128, 128], es=pes) for _ in range(2)] for _ in range(G)]
        Wb = [c.sb("Wb", [128, 64], es=pes) for _ in range(G)]
        Ub = [c.sb("Ub", [128, 64], es=pes) for _ in range(G)]
        Yst = [c.sb("Yst", [128, CB], es=pes) for _ in range(2)]
        print("scan sbuf remaining", c.nc.sbuf_bytes_remaining)
        c.emit("dve", lambda e: e.memset(Tst[:], 0.0), writes=[("T", h) for h in range(NH)])
        it = 0
        for ch in range(c.S // 128):
            rs = slice(ch * 128, (ch + 1) * 128)
            for cb in range(D // CB):
                cs = slice(cb * CB, (cb + 1) * CB)
                s = it % 2
                it += 1
                L = {}
                for i, n in enumerate(names_in):
                    c.dma("sync" if i % 2 == 0 else "pool", ld[n][s][:], O[n][rs, cs], writes=[("sl", n, s)])
                    L[n] = ld[n][s]
                lk = lambda *ns: [("sl", n, s) for n in ns]
                c.dma("sync", elrow[s][:], O["el"][ch:ch + 1, cs], writes=[("elrow", s)])
                psg, pgk = c.psum()
                for hl in range(8):
                    c.emit("pe", lambda e, psg=psg, hl=hl, s=s: e.matmul(
                        psg[0:64, hl:hl + 1], lhsT=elrow[s][0:1, hl * 64:(hl + 1) * 64], rhs=c.ones[0:1, 0:1], start=True, stop=True),
                        reads=[("elrow", s), "const"], writes=c.q(pgk, 0, 8))
                c.emit("dve", lambda e, psg=psg, s=s: e.tensor_copy(out=gl[s][:], in_=psg[0:64, 0:8]),
                       reads=c.q(pgk, 0, 8), writes=[("gl", s)])
                for grp in range(8 // G):
                    if SCAN_DBG < 1:
                        continue
                    heads = [grp * G + i for i in range(G)]
                    for i, hl in enumerate(heads):
                        hc = slice(hl * 64, (hl + 1) * 64)
                        ps, pk = c.psum()
                        for qn, n in enumerate(["At", "Rt", "Bh", "Kh"]):
                            c.emit("pe", lambda e, ps=ps, qn=qn, n=n, hc=hc, L=L: e.transpose(
                                ps[0:64, qn * 128:(qn + 1) * 128], L[n][:, hc], ident[:]),
                                reads=lk(n) + ["const"], writes=c.q(pk, qn * 128, (qn + 1) * 128))
                        if i % 2 == 0:
                            c.emit("act", lambda e, ps=ps, i=i: e.copy(out=XT[i][:], in_=ps[0:64, :]), reads=pk, writes=[("XT", i)])
                        else:
                            c.emit("dve", lambda e, ps=ps, i=i: e.tensor_copy(out=XT[i][:], in_=ps[0:64, :]), reads=pk,
                                   writes=[("XT", i)])
                    if SCAN_DBG < 2:
                        continue
                    psCs = [c.psum() for _ in range(NB)]
                    for i, hl in enumerate(heads):
                        psC, pCk = psCs[i // 4]
                        c.emit("pe", lambda e, psC=psC, i=i: e.matmul(psC[:, (i % 4) * 128:(i % 4 + 1) * 128], lhsT=XT[i][:, 0:128],
                                                                      rhs=XT[i][:, 256:384], start=True, stop=True),
                               reads=[("XT", i)], writes=pCk)
                    for i, hl in enumerate(heads):
                        psC, pCk = psCs[i // 4]
                        c.emit("dve", lambda e, psC=psC, i=i: e.tensor_tensor(out=XTb[i][2][:], in0=psC[:, (i % 4) * 128:(i % 4 + 1) * 128],
                                                                              in1=c.msl[:], op=ALU.mult),
                               reads=pCk + ["const"], writes=[("XTb", i, 2)])
                    for i, hl in enumerate(heads):
                        ps, pk = c.psum()
                        c.emit("pe", lambda e, ps=ps, i=i: e.matmul(ps[:, 0:256], lhsT=XT[i][:, 256:384], rhs=XT[i][:, 0:256],
                                                                    start=True, stop=True), reads=[("XT", i)], writes=pk)
                        c.emit("pe", lambda e, ps=ps, i=i: e.matmul(ps[:, 256:512], lhsT=XT[i][:, 384:512], rhs=XT[i][:, 0:256],
                                                                    start=True, stop=True), reads=[("XT", i)], writes=pk)
                        c.emit("dve", lambda e, ps=ps, i=i: e.tensor_tensor(out=ABK[i][:], in0=ps[:, :], in1=c.mask4[:], op=ALU.mult),
                               reads=pk + ["const"], writes=[("ABK", i)])
                        c.emit("dve", lambda e, i=i: e.tensor_tensor(out=PTb[i][0][:], in0=ABK[i][:, 0:128], in1=ident[:], op=ALU.add),
                               reads=[("ABK", i), "const"], writes=[("PTb", i, 0)])
                    if SCAN_DBG < 3:
                        continue
                    Xc = [(ABK[i][:, 0:128], ("ABK", i)) for i in range(G)]
                    XTc = [(XTb[i][2][:], ("XTb", i, 2)) for i in range(G)]
                    PTc = [(PTb[i][0][:], ("PTb", i, 0)) for i in range(G)]
                    for lev in range(int(os.environ.get("NLEV", "6"))):
                        pXs = [c.psum() for _ in range(NB)]
                        pXTs = [c.psum() for _ in range(NB)]
                        newX, newXT = [], []
                        for i in range(G):
                            pX, pXk = pXs[i // 4]
                            pXT, pXTk = pXTs[i // 4]
                            qs = pXk
                            qt = pXTk
                            xa, xk_ = Xc[i]
                            xta, xtk = XTc[i]
                            if lev < 5:
                                c.emit("pe", lambda e, pX=pX, i=i, xa=xa, xta=xta: e.matmul(
                                    pX[:, (i % 4) * 128:(i % 4 + 1) * 128], lhsT=xta, rhs=xa, start=True, stop=True),
                                    reads=[xk_, xtk], writes=qs)
                            c.emit("pe", lambda e, pXT=pXT, i=i, xa=xa, xta=xta: e.matmul(
                                pXT[:, (i % 4) * 128:(i % 4 + 1) * 128], lhsT=xa, rhs=xta, start=True, stop=True),
                                reads=[xk_, xtk], writes=qt)
                        if INV_SUB < 2:
                            continue
                        for i in range(G):
                            pX, pXk = pXs[i // 4]
                            pXT, pXTk = pXTs[i // 4]
                            b = lev % 2
                            if lev < 5:
                                c.emit("act", lambda e, pX=pX, i=i, b=b: e.copy(out=Xb[i][b][:], in_=pX[:, (i % 4) * 128:(i % 4 + 1) * 128]),
                                       reads=pXk, writes=[("Xb", i, b)])
                                newX.append((Xb[i][b][:], ("Xb", i, b)))
                            else:
                                newX.append(None)
                            c.emit("act", lambda e, pXT=pXT, i=i, b=b: e.copy(out=XTb[i][b][:], in_=pXT[:, (i % 4) * 128:(i % 4 + 1) * 128]),
                                   reads=pXTk, writes=[("XTb", i, b)])
                            newXT.append((XTb[i][b][:], ("XTb", i, b)))
                        if INV_SUB < 3:
                            continue
                        pPs = [c.psum() for _ in range(NB)]
                        for i in range(G):
                            pP, pPk = pPs[i // 4]
                            qp = pPk
                            pa, pkk = PTc[i]
                            c.emit("pe", lambda e, pP=pP, i=i, pa=pa, l=newXT[i][0]: e.matmul(
                                pP[:, (i % 4) * 128:(i % 4 + 1) * 128], lhsT=l, rhs=pa, start=True, stop=True),
                                reads=[newXT[i][1], pkk], writes=qp)
                        for i in range(G):
                            pP, pPk = pPs[i // 4]
                            qp = pPk
                            pa, pkk = PTc[i]
                            nb = (lev + 1) % 2
                            c.emit("dve", lambda e, pP=pP, i=i, pa=pa, nb=nb: e.tensor_tensor(
                                out=PTb[i][nb][:], in0=pP[:, (i % 4) * 128:(i % 4 + 1) * 128], in1=pa, op=ALU.add),
                                reads=qp + [pkk], writes=[("PTb", i, nb)])
                            PTc[i] = (PTb[i][nb][:], ("PTb", i, nb))
                        Xc, XTc = newX, newXT
                    if SCAN_DBG < 4:
                        continue
                    pWs = [c.psum() for _ in range(NB)]
                    pUs = [c.psum() for _ in range(NB)]
                    pYs = [c.psum() for _ in range(NB)]
                    pTs = [c.psum() for _ in range(NB)]
                    H = [(i, hl, cb * 8 + hl, slice(hl * 64, (hl + 1) * 64), slice((i % 4) * 128, (i % 4) * 128 + 64)) for i, hl in enumerate(heads)]
                    for i, hl, h, hc, q0 in H:
                        c.emit("pe", lambda e, pW=pWs[i // 4][0], pU=pUs[i // 4][0], pY=pYs[i // 4][0], pT=pTs[i // 4][0], i=i, h=h, q0=q0: e.matmul(pW[:, q0], lhsT=XT[i][:, 0:128], rhs=Tst[:, h, :],
                                                                      start=True, stop=False),
                               reads=[("XT", i), ("T", h)], writes=pWs[i // 4][1])
                        c.emit("pe", lambda e, pW=pWs[i // 4][0], pU=pUs[i // 4][0], pY=pYs[i // 4][0], pT=pTs[i // 4][0], i=i, hc=hc, q0=q0, L=L: e.matmul(pW[:, q0], lhsT=ABK[i][:, 256:384], rhs=L["V"][:, hc],
                                                                             start=False, stop=True),
                               reads=[("ABK", i)] + lk("V"), writes=pWs[i // 4][1])
                    for i, hl, h, hc, q0 in H:
                        c.emit("act", lambda e, pW=pWs[i // 4][0], pU=pUs[i // 4][0], pY=pYs[i // 4][0], pT=pTs[i // 4][0], i=i, q0=q0: e.copy(out=Wb[i][:], in_=pW[:, q0]), reads=pWs[i // 4][1], writes=[("Wb", i)])
                    for i, hl, h, hc, q0 in H:
                        pa, pkk = PTc[i]
                        c.emit("pe", lambda e, pW=pWs[i // 4][0], pU=pUs[i // 4][0], pY=pYs[i // 4][0], pT=pTs[i // 4][0], i=i, pa=pa, q0=q0: e.matmul(pU[:, q0], lhsT=pa, rhs=Wb[i][:], start=True, stop=True),
                               reads=[pkk, ("Wb", i)], writes=pUs[i // 4][1])
                    for i, hl, h, hc, q0 in H:
                        c.emit("act", lambda e, pW=pWs[i // 4][0], pU=pUs[i // 4][0], pY=pYs[i // 4][0], pT=pTs[i // 4][0], i=i, q0=q0: e.copy(out=Ub[i][:], in_=pU[:, q0]), reads=pUs[i // 4][1], writes=[("Ub", i)])
                    for i, hl, h, hc, q0 in H:
                        c.emit("pe", lambda e, pW=pWs[i // 4][0], pU=pUs[i // 4][0], pY=pYs[i // 4][0], pT=pTs[i // 4][0], i=i, h=h, q0=q0: e.matmul(pY[:, q0], lhsT=XT[i][:, 128:256], rhs=Tst[:, h, :],
                                                                      start=True, stop=False),
                               reads=[("XT", i), ("T", h)], writes=pYs[i // 4][1])
                        c.emit("pe", lambda e, pW=pWs[i // 4][0], pU=pUs[i // 4][0], pY=pYs[i // 4][0], pT=pTs[i // 4][0], i=i, q0=q0: e.matmul(pY[:, q0], lhsT=ABK[i][:, 128:256], rhs=Ub[i][:],
                                                                 start=False, stop=False),
                               reads=[("ABK", i), ("Ub", i)], writes=pYs[i // 4][1])
                        c.emit("pe", lambda e, pW=pWs[i // 4][0], pU=pUs[i // 4][0], pY=pYs[i // 4][0], pT=pTs[i // 4][0], i=i, hc=hc, q0=q0, L=L: e.matmul(pY[:, q0], lhsT=ABK[i][:, 384:512], rhs=L["V"][:, hc],
                                                                             start=False, stop=True),
                               reads=[("ABK", i)] + lk("V"), writes=pYs[i // 4][1])
                        c.emit("pe", lambda e, pW=pWs[i // 4][0], pU=pUs[i // 4][0], pY=pYs[i // 4][0], pT=pTs[i // 4][0], i=i, hc=hc, q0=q0, L=L: e.matmul(pT[0:64, q0], lhsT=L["Bt"][:, hc], rhs=Ub[i][:],
                                                                             start=True, stop=False),
                               reads=lk("Bt") + [("Ub", i)], writes=pTs[i // 4][1])
                        c.emit("pe", lambda e, pW=pWs[i // 4][0], pU=pUs[i // 4][0], pY=pYs[i // 4][0], pT=pTs[i // 4][0], i=i, hc=hc, q0=q0, L=L: e.matmul(pT[0:64, q0], lhsT=L["Kt"][:, hc], rhs=L["V"][:, hc],
                                                                             start=False, stop=True),
                               reads=lk("Kt", "V"), writes=pTs[i // 4][1])
                    for i, hl, h, hc, q0 in H:
                        c.emit("act", lambda e, pW=pWs[i // 4][0], pU=pUs[i // 4][0], pY=pYs[i // 4][0], pT=pTs[i // 4][0], hc=hc, q0=q0, s=s: e.copy(out=Yst[s][:, hc], in_=pY[:, q0]),
                               reads=pYs[i // 4][1], writes=[("Yst", s)])
                        c.emit("dve", lambda e, pW=pWs[i // 4][0], pU=pUs[i // 4][0], pY=pYs[i // 4][0], pT=pTs[i // 4][0], h=h, hl=hl, q0=q0, s=s: e.scalar_tensor_tensor(
                            out=Tst[:, h, :], in0=Tst[:, h, :], scalar=gl[s][:, hl:hl + 1], in1=pT[0:64, q0],
                            op0=ALU.mult, op1=ALU.add), reads=pTs[i // 4][1] + [("T", h), ("gl", s)], writes=[("T", h)])
                c.dma("sync", O["y"][rs, cs], Yst[s][:], reads=[("Yst", s)], writes=[("od", "y", ch, cb)])
        c.flush()


def rwkv_post_phase(c, x_d, P, O, wdt=F32):
    with ExitStack() as pes:
        yb = c.sb("yb", [128, D], es=pes)
        bb = c.sb("bb", [128, D], es=pes)
        gb = c.sb("gb", [128, D], es=pes)
        xt = c.sb("xt", [128, D], es=pes)
        sq = c.sb("sq", [128, D], es=pes)
        gng = c.sb("gng", [128, D], es=pes)
        gnb = c.sb("gnb", [128, D], es=pes)
        st = c.sb("st", [128, 160], es=pes)
        zT = c.sb("zT", [128, KC, 128], wdt, es=pes)
        wb = [c.sb("wb", [128, KC, 256], wdt, es=pes) for _ in range(2)]
        c.dma("sync", gng[:], P["gn_g"].partition_broadcast(128), writes=["par"])
        c.dma("sync", gnb[:], P["gn_b"].partition_broadcast(128), writes=["par"])
        y3 = yb[:].rearrange("p (h j) -> p h j", j=64)
        s3 = sq[:].rearrange("p (h j) -> p h j", j=64)
        bc = lambda a: a.unsqueeze(2).to_broadcast([128, NH, 64])
        for t in range(c.S // 128):
            rs = slice(t * 128, (t + 1) * 128)
            c.dma("sync", yb[:], O["y"][rs, :], writes=["yb"])
            c.dma("pool", bb[:], O["bonus"][rs, :], writes=["bb"])
            c.dma("sync", gb[:], O["g"][rs, :], writes=["gb"])
            c.dma("pool", xt[:], x_d[rs, :], writes=["xt"])
            c.emit("dve", lambda e: e.tensor_reduce(out=st[:, 0:32], in_=y3, axis=AX.X, op=ALU.add), reads=["yb"], writes=["st"])
            c.emit("dve", lambda e: e.tensor_scalar(out=st[:, 32:64], in0=st[:, 0:32], scalar1=1.0 / 64, scalar2=0.0,
                                                    op0=ALU.mult, op1=ALU.add), reads=["st"], writes=["st"])
            c.emit("dve", lambda e: e.tensor_tensor(out=y3, in0=y3, in1=bc(st[:, 32:64]), op=ALU.subtract),
                   reads=["yb", "st"], writes=["yb"])
            c.emit("pool", lambda e: e.tensor_tensor(out=sq[:], in0=yb[:], in1=yb[:], op=ALU.mult), reads=["yb"], writes=["sq"])
            c.emit("dve", lambda e: e.tensor_reduce(out=st[:, 64:96], in_=s3, axis=AX.X, op=ALU.add), reads=["sq"], writes=["st"])
            c.emit("dve", lambda e: e.tensor_scalar(out=st[:, 96:128], in0=st[:, 64:96], scalar1=1.0 / 64, scalar2=GN_EPS,
                                                    op0=ALU.mult, op1=ALU.add), reads=["st"], writes=["st"])
            c.emit("act", lambda e: e.activation(out=st[:, 96:128], in_=st[:, 96:128], func=AF.Sqrt), reads=["st"], writes=["st"])
            c.emit("dve", lambda e: e.reciprocal(out=st[:, 128:160], in_=st[:, 96:128]), reads=["st"], writes=["st"])
            c.emit("dve", lambda e: e.tensor_tensor(out=y3, in0=y3, in1=bc(st[:, 128:160]), op=ALU.mult),
                   reads=["yb", "st"], writes=["yb"])
            c.emit("pool", lambda e: e.tensor_tensor(out=yb[:], in0=yb[:], in1=gng[:], op=ALU.mult), reads=["yb", "par"], writes=["yb"])
            c.emit("pool", lambda e: e.tensor_tensor(out=yb[:], in0=yb[:], in1=gnb[:], op=ALU.add), reads=["yb", "par"], writes=["yb"])
            c.emit("pool", lambda e: e.tensor_tensor(out=yb[:], in0=yb[:], in1=bb[:], op=ALU.add), reads=["yb", "bb"], writes=["yb"])
            c.emit("pool", lambda e: e.tensor_tensor(out=sq[:], in0=yb[:], in1=gb[:], op=ALU.mult), reads=["yb", "gb"], writes=["sq"])
            transpose_tile(c, sq, "sq", zT, lambda g: ("zT", g), 0)

            def epi(cb, ps, pk):
                c.emit("dve", lambda e, cb=cb, ps=ps: e.tensor_tensor(out=xt[:, cb * 256:(cb + 1) * 256], in0=ps[:, 0:256],
                                                                       in1=xt[:, cb * 256:(cb + 1) * 256], op=ALU.add),
                       reads=pk + ["xt"], writes=["xt"])
            proj(c, zT, lambda kc: [("zT", kc // 4)], 1, P["wo"], D, wb, "wb", epi)
            c.dma("sync", x_d[rs, :], xt[:], reads=["xt"], writes=[("xd", t)])
        c.flush()


import math


def rope_tables_phase(c, pos_ap, cos_d, sin_d):
    NT = c.S // 128
    with ExitStack() as pes:
        pi_ = c.sb("pos_i", [NT, 128], I32, es=pes)
        pf = c.sb("pos_f", [NT, 128], es=pes)
        posT = c.sb("posT", [128, NT], es=pes)
        io_i = c.sb("io_i", [128, 32], I32, es=pes)
        invf = c.sb("invf", [128, 32], es=pes)
        ang = c.sb("ang", [128, 32], es=pes)
        ob = [c.sb("ropeo", [128, 64], es=pes) for _ in range(2)]
        nb = c.sb("negpi", [128, 1], es=pes)
        ni = c.sb("ni", [128, 64], I32, es=pes)
        nf = c.sb("nf", [128, 64], es=pes)
        c.emit("pool", lambda e: e.memset(nb[:], -math.pi), writes=["nb"])
        c.dma("sync", pi_[:], pos_ap.rearrange("(t p) -> t p", p=128), writes=["pi"])
        c.emit("dve", lambda e: e.tensor_copy(out=pf[:], in_=pi_[:]), reads=["pi"], writes=["pf"])
        ps, pk = c.psum()
        c.emit("pe", lambda e: e.transpose(ps[:, 0:NT], pf[:], c.ident[0:NT, 0:NT]), reads=["pf", "const"], writes=pk)
        c.emit("dve", lambda e: e.tensor_copy(out=posT[:], in_=ps[:, 0:NT]), reads=pk, writes=["posT"])
        c.emit("pool", lambda e: e.iota(io_i[:], pattern=[[1, 32]], base=0, channel_multiplier=0), writes=["io"])
        c.emit("dve", lambda e: e.tensor_copy(out=invf[:], in_=io_i[:]), reads=["io"], writes=["invf"])
        c.emit("act", lambda e: e.activation(out=invf[:], in_=invf[:], func=AF.Exp, scale=-math.log(10000.0) / 32.0),
               reads=["invf"], writes=["invf"])
        for t in range(NT):
            s = t % 2
            c.emit("dve", lambda e, t=t: e.tensor_scalar(out=ang[:], in0=invf[:], scalar1=posT[:, t:t + 1], scalar2=0.0,
                                                         op0=ALU.mult, op1=ALU.add), reads=["invf", "posT"], writes=["ang"])
            c.emit("dve", lambda e, s=s: e.tensor_scalar(out=ob[s][:, 32:64], in0=ang[:], scalar1=1.0 / (2 * math.pi), scalar2=0.5,
                                                         op0=ALU.mult, op1=ALU.add), reads=["ang"], writes=[("ob", s)])
            c.emit("dve", lambda e, s=s: e.tensor_scalar(out=ob[s][:, 0:32], in0=ang[:], scalar1=1.0 / (2 * math.pi), scalar2=0.75,
                                                         op0=ALU.mult, op1=ALU.add), reads=["ang"], writes=[("ob", s)])
            c.emit("dve", lambda e, s=s: e.tensor_copy(out=ni[:], in_=ob[s][:]), reads=[("ob", s)], writes=["ni"])
            c.emit("dve", lambda e: e.tensor_copy(out=nf[:], in_=ni[:]), reads=["ni"], writes=["nf"])
            c.emit("dve", lambda e, s=s: e.tensor_tensor(out=ob[s][:], in0=ob[s][:], in1=nf[:], op=ALU.subtract),
                   reads=[("ob", s), "nf"], writes=[("ob", s)])
            c.emit("dve", lambda e, s=s: e.tensor_scalar(out=nf[:], in0=ob[s][:], scalar1=0.0, scalar2=0.0,
                                                         op0=ALU.is_lt, op1=ALU.add), reads=[("ob", s)], writes=["nf"])
            c.emit("dve", lambda e, s=s: e.tensor_tensor(out=ob[s][:], in0=ob[s][:], in1=nf[:], op=ALU.add),
                   reads=[("ob", s), "nf"], writes=[("ob", s)])
            c.emit("act", lambda e, s=s: e.activation(out=ob[s][:], in_=ob[s][:], func=AF.Sin, bias=nb[:, 0:1], scale=2 * math.pi),
                   reads=[("ob", s), "nb"], writes=[("ob", s)])
            c.dma("sync", cos_d[t * 128:(t + 1) * 128, :], ob[s][:, 0:32], reads=[("ob", s)], writes=[("cd", t)])
            c.dma("sync", sin_d[t * 128:(t + 1) * 128, :], ob[s][:, 32:64], reads=[("ob", s)], writes=[("sd", t)])
        c.flush()


def rope_epi_ops(c, ps, pk, nh, cs_tile, cskey, dst, dkey, tmp, tkey):
    p3 = ps[:, 0:nh * 64].rearrange("p (h d) -> p h d", d=64)
    cosb = cs_tile[:, 0:32].unsqueeze(1).to_broadcast([128, nh, 32])
    sinb = cs_tile[:, 32:64].unsqueeze(1).to_broadcast([128, nh, 32])
    d3 = dst.rearrange("p (h d) -> p h d", d=64)
    t3 = tmp[:, 0:nh * 64].rearrange("p (h d) -> p h d", d=64)
    R = pk + [cskey]
    c.emit("dve", lambda e: e.tensor_tensor(out=d3[:, :, 0:32], in0=p3[:, :, 0:32], in1=cosb, op=ALU.mult), reads=R, writes=[dkey])
    c.emit("dve", lambda e: e.tensor_tensor(out=t3[:, :, 0:32], in0=p3[:, :, 32:64], in1=sinb, op=ALU.mult), reads=R, writes=[tkey])
    c.emit("dve", lambda e: e.tensor_tensor(out=d3[:, :, 32:64], in0=p3[:, :, 32:64], in1=cosb, op=ALU.mult), reads=R, writes=[dkey])
    c.emit("dve", lambda e: e.tensor_tensor(out=t3[:, :, 32:64], in0=p3[:, :, 0:32], in1=sinb, op=ALU.mult), reads=R, writes=[tkey])
    c.emit("dve", lambda e: e.tensor_tensor(out=d3[:, :, 0:32], in0=d3[:, :, 0:32], in1=t3[:, :, 0:32], op=ALU.subtract),
           reads=[dkey, tkey], writes=[dkey])
    c.emit("dve", lambda e: e.tensor_tensor(out=d3[:, :, 32:64], in0=d3[:, :, 32:64], in1=t3[:, :, 32:64], op=ALU.add),
           reads=[dkey, tkey], writes=[dkey])


def norm_tile(c, xt, gam, junk, hbuf, stat, jkeys, hkeys):
    c.emit("act", lambda e: e.activation(out=junk, in_=xt[:], func=AF.Square, accum_out=stat[:, 0:1]),
           reads=["xt"], writes=jkeys + ["stat"])
    c.emit("dve", lambda e: e.tensor_scalar(out=stat[:, 1:2], in0=stat[:, 0:1], scalar1=1.0 / D, scalar2=RMS_EPS,
                                            op0=ALU.mult, op1=ALU.add), reads=["stat"], writes=["stat"])
    c.emit("act", lambda e: e.activation(out=stat[:, 3:4], in_=stat[:, 1:2], func=AF.Sqrt), reads=["stat"], writes=["stat"])
    c.emit("dve", lambda e: e.reciprocal(out=stat[:, 2:3], in_=stat[:, 3:4]), reads=["stat"], writes=["stat"])
    c.emit("dve", lambda e: e.scalar_tensor_tensor(out=hbuf, in0=xt[:], scalar=stat[:, 2:3], in1=gam[:],
                                                   op0=ALU.mult, op1=ALU.mult), reads=["xt", "stat", "gam"], writes=hkeys)


def kv_phase(c, x_d, g_row, w_kv, cos_d, sin_d, kT_d, v_d, wdt=F32):
    with ExitStack() as pes:
        xt = c.sb("xt", [128, D], es=pes)
        gam = c.sb("gam", [128, D], es=pes)
        junk = c.sb("junk", [128, D], es=pes)
        hbuf = c.sb("hbuf", [128, D], es=pes)
        stat = c.sb("stat", [128, 8], es=pes)
        hT = c.sb("hT", [128, KC, 128], wdt, es=pes)
        wb = [c.sb("wb", [128, KC, 256], wdt, es=pes) for _ in range(2)]
        cs = [c.sb("cs", [128, 64], es=pes) for _ in range(2)]
        krot = c.sb("krot", [128, 512], es=pes)
        tmp = c.sb("tmp", [128, 256], es=pes)
        vb = [c.sb("vb", [128, 256], es=pes) for _ in range(2)]
        kTt = c.sb("kTt", [64, 8, 128], es=pes)
        c.dma("sync", gam[:], g_row.partition_broadcast(128), writes=["gam"])
        for t in range(c.S // 128):
            rs = slice(t * 128, (t + 1) * 128)
            s = t % 2
            c.dma("sync", xt[:], x_d[rs, :], writes=["xt"])
            c.dma("pool", cs[s][:, 0:32], cos_d[rs, :], writes=[("cs", s)])
            c.dma("pool", cs[s][:, 32:64], sin_d[rs, :], writes=[("cs", s)])
            norm_tile(c, xt, gam, junk[:], hbuf[:], stat, ["junk"], ["hbuf"])
            transpose_tile(c, hbuf, "hbuf", hT, lambda g: ("hT", g), 0)

            def epi(cb, ps, pk, s=s, rs=rs):
                if cb < 2:
                    rope_epi_ops(c, ps, pk, 4, cs[s], ("cs", s), krot[:, cb * 256:(cb + 1) * 256], ("krot", cb), tmp, "tmp")
                else:
                    vs = c.wslot("vb")
                    c.emit("act", lambda e, vs=vs, ps=ps: e.copy(out=vb[vs][:], in_=ps[:, 0:256]), reads=pk, writes=[("vb", vs)])
                    c.dma("sync", v_d[rs, (cb - 2) * 256:(cb - 1) * 256], vb[vs][:], reads=[("vb", vs)], writes=[("vd", rs.start, cb)])
            proj(c, hT, lambda kc: [("hT", kc // 4)], 1, w_kv, 1024, wb, "wb", epi)
            for hg in range(2):
                ps, pk = c.psum()
                for j in range(4):
                    h = hg * 4 + j
                    c.emit("pe", lambda e, ps=ps, j=j, h=h: e.transpose(ps[0:64, j * 128:(j + 1) * 128], krot[:, h * 64:(h + 1) * 64], c.ident[:]),
                           reads=[("krot", h // 4), "const"], writes=pk)
                c.emit("act", lambda e, ps=ps, hg=hg: e.copy(out=kTt[:, hg * 4:(hg + 1) * 4, :],
                                                             in_=ps[0:64, :].rearrange("p (a b) -> p a b", a=4)),
                       reads=pk, writes=["kTt"])
            c.dma("sync", kT_d[:, :, rs].rearrange("g d s -> d g s"), kTt[:], reads=["kTt"], writes=[("kTd", t)])
        c.flush()


def attn_phase(c, x_d, g_row, w_q, w_o, sinks_row, cos_d, sin_d, kT_d, v_d, dbg=None, wdt=F32):
    with ExitStack() as pes:
        xt = c.sb("xt", [128, D], es=pes)
        gam = c.sb("gam", [128, D], es=pes)
        junk = c.sb("junk", [128, D], es=pes)
        hbuf = c.sb("hbuf", [128, D], es=pes)
        stat = c.sb("stat", [128, 8], es=pes)
        hT = c.sb("hT", [128, KC, 128], wdt, es=pes)
        wb = [c.sb("wb", [128, KC, 256], wdt, es=pes) for _ in range(2)]
        cs = [c.sb("cs", [128, 64], es=pes) for _ in range(2)]
        tmp = c.sb("tmp", [128, 256], es=pes)
        qT = c.sb("qT", [64, NH, 128], es=pes)
        kTs = [c.sb("kTs", [64, 8, 256], es=pes) for _ in range(2)]
        v1 = [c.sb("v1", [128, 2, 8, 65], es=pes) for _ in range(2)]
        E = [[c.sb("E", [128, 512], es=pes) for _ in range(2)] for _ in range(2)]
        mC = c.sb("mC", [128, 512], es=pes)
        mP = c.sb("mP", [128, 512], es=pes)
        sk = c.sb("sk", [128, NH], es=pes)
        dn = c.sb("dn", [128, 8], es=pes)
        attn = junk
        c.dma("sync", gam[:], g_row.partition_broadcast(128), writes=["gam"])
        c.dma("sync", sk[:], sinks_row.partition_broadcast(128), writes=["sk"])
        c.emit("act", lambda e: e.activation(out=sk[:], in_=sk[:], func=AF.Exp), reads=["sk"], writes=["sk"])
        for q in range(4):
            c.emit("pool", lambda e, q=q: e.tensor_copy(out=mC[:, q * 128:(q + 1) * 128], in_=c.mask4[:, 128:256]),
                   reads=["const"], writes=["mC"])
            c.emit("pool", lambda e, q=q: e.tensor_copy(out=mP[:, q * 128:(q + 1) * 128], in_=c.msl[:]),
                   reads=["const"], writes=["mP"])
        for s in range(2):
            c.emit("pool", lambda e, s=s: e.memset(v1[s][:], 1.0), writes=[("v1", s)])
        for t in range(c.S // 128):
            rs = slice(t * 128, (t + 1) * 128)
            s = t % 2
            nkb = 1 if t == 0 else 2
            k0 = (t - 1) * 128 if t > 0 else 0
            c.dma("sync", xt[:], x_d[rs, :], writes=["xt"])
            c.dma("pool", cs[s][:, 0:32], cos_d[rs, :], writes=[("cs", s)])
            c.dma("pool", cs[s][:, 32:64], sin_d[rs, :], writes=[("cs", s)])
            kb0 = 2 - nkb
            c.dma("pool", kTs[s][:, :, kb0 * 128:256], kT_d[:, :, k0:(t + 1) * 128].rearrange("g d s -> d g s"),
                  writes=[("kTs", s)])
            for kb in range(kb0, 2):
                r1 = (t - 1 + kb) * 128
                c.dma("sync", v1[s][:, kb, :, 0:64], v_d[r1:r1 + 128, :].rearrange("p (g d) -> p g d", d=64), writes=[("v1", s)])
            norm_tile(c, xt, gam, junk[:], hbuf[:], stat, ["junk"], ["hbuf"])
            transpose_tile(c, hbuf, "hbuf", hT, lambda g: ("hT", g), 0)

            def epi(cb, ps, pk, s=s):
                rope_epi_ops(c, ps, pk, 4, cs[s], ("cs", s), hbuf[:, cb * 256:(cb + 1) * 256], "hbuf", tmp, "tmp")
            proj(c, hT, lambda kc: [("hT", kc // 4)], 1, w_q, D, wb, "wb", epi)
            for hg in range(8):
                ps, pk = c.psum()
                for j in range(4):
                    h = hg * 4 + j
                    c.emit("pe", lambda e, ps=ps, j=j, h=h: e.transpose(ps[0:64, j * 128:(j + 1) * 128], hbuf[:, h * 64:(h + 1) * 64], c.ident[:]),
                           reads=["hbuf", "const"], writes=pk)
                c.emit("act", lambda e, ps=ps, hg=hg: e.copy(out=qT[:, hg * 4:(hg + 1) * 4, :],
                                                             in_=ps[0:64, :].rearrange("p (a b) -> p a b", a=4)),
                       reads=pk, writes=[("qT", hg)])
            for g in range(8):
                es_ = g % 2
                for kb in range(kb0, 2):
                    ps, pk = c.psum()
                    for j in range(4):
                        c.emit("pe", lambda e, ps=ps, j=j, g=g, kb=kb, s=s: e.matmul(
                            ps[:, j * 128:(j + 1) * 128], lhsT=kTs[s][:, g, kb * 128:(kb + 1) * 128], rhs=qT[:, 4 * g + j, :],
                            start=True, stop=True), reads=[("kTs", s), ("qT", g)], writes=pk)
                    c.emit("act", lambda e, ps=ps, kb=kb, es_=es_: e.activation(out=E[es_][kb][:], in_=ps[:, :], func=AF.Exp, scale=0.125),
                           reads=pk, writes=[("E", es_, kb)])
                    m = mC if kb == 1 else mP
                    c.emit("pool", lambda e, kb=kb, es_=es_, m=m: e.tensor_tensor(out=E[es_][kb][:], in0=E[es_][kb][:], in1=m[:], op=ALU.mult),
                           reads=[("E", es_, kb), "mC", "mP"], writes=[("E", es_, kb)])
                po, pok = c.psum()
                for j in range(4):
                    for kb in range(kb0, 2):
                        c.emit("pe", lambda e, po=po, j=j, kb=kb, g=g, s=s, es_=es_, kb0=kb0: e.matmul(
                            po[:, j * 65:(j + 1) * 65], lhsT=E[es_][kb][:, j * 128:(j + 1) * 128], rhs=v1[s][:, kb, g, :],
                            start=(kb == kb0), stop=(kb == 1)), reads=[("E", es_, kb), ("v1", s)], writes=pok)
                po3 = po[:, 0:260].rearrange("p (h d) -> p h d", d=65)
                c.emit("dve", lambda e, po3=po3, g=g: e.tensor_tensor(out=dn[:, 0:4], in0=po3[:, :, 64], in1=sk[:, 4 * g:4 * g + 4], op=ALU.add),
                       reads=pok + ["sk"], writes=["dn"])
                c.emit("dve", lambda e: e.reciprocal(out=dn[:, 4:8], in_=dn[:, 0:4]), reads=["dn"], writes=["dn"])
                c.emit("dve", lambda e, po3=po3, g=g: e.tensor_tensor(
                    out=attn[:, g * 256:(g + 1) * 256].rearrange("p (h d) -> p h d", d=64), in0=po3[:, :, 0:64],
                    in1=dn[:, 4:8].unsqueeze(2).to_broadcast([128, 4, 64]), op=ALU.mult), reads=pok + ["dn"], writes=["junk"])
            if dbg is not None:
                c.dma("sync", dbg[rs, :], attn[:], reads=["junk"], writes=[("dbg", t)])
            transpose_tile(c, attn, "junk", hT, lambda g: ("hT", g), 0)

            def epi2(cb, ps, pk):
                c.emit("dve", lambda e, cb=cb, ps=ps: e.tensor_tensor(out=xt[:, cb * 256:(cb + 1) * 256], in0=ps[:, 0:256],
                                                                       in1=xt[:, cb * 256:(cb + 1) * 256], op=ALU.add),
                       reads=pk + ["xt"], writes=["xt"])
            proj(c, hT, lambda kc: [("hT", kc // 4)], 1, w_o, D, wb, "wb", epi2)
            c.dma("sync", x_d[rs, :], xt[:], reads=["xt"], writes=[("xd", t)])
        c.flush()


def test_attn_program(S):
    nc = bass.Bass("TRN2", target_bir_lowering=False)
    x_in = nc.dram_tensor("x", [S, D], F32, kind="ExternalInput").ap()
    pos = nc.dram_tensor("pos", [S], I32, kind="ExternalInput").ap()
    gk = nc.dram_tensor("gk", [1, D], F32, kind="ExternalInput").ap()
    g1 = nc.dram_tensor("g1", [1, D], F32, kind="ExternalInput").ap()
    wkv = nc.dram_tensor("wkv", [D, 1024], F32, kind="ExternalInput").ap()
    wq = nc.dram_tensor("wq", [D, D], F32, kind="ExternalInput").ap()
    wo = nc.dram_tensor("wo", [D, D], F32, kind="ExternalInput").ap()
    sinks = nc.dram_tensor("sinks", [1, NH], F32, kind="ExternalInput").ap()
    xo = nc.dram_tensor("xo", [S, D], F32, kind="ExternalOutput").ap()
    cos_d = nc.dram_tensor("cos_d", [S, 32], F32, kind="ExternalOutput").ap()
    sin_d = nc.dram_tensor("sin_d", [S, 32], F32, kind="ExternalOutput").ap()
    kT_d = nc.dram_tensor("kT_d", [8, 64, S], F32, kind="ExternalOutput").ap()
    v_d = nc.dram_tensor("v_d", [S, 512], F32, kind="ExternalOutput").ap()
    with ExitStack() as es:
        c = Ctx(nc, es, S)
        make_consts(c)
        make_masks(c)
        copy_rows(c, xo, x_in, S)
        rope_tables_phase(c, pos, cos_d, sin_d)
        kv_phase(c, xo, gk, wkv, cos_d, sin_d, kT_d, v_d)
        dbg = nc.dram_tensor("dbg", [S, D], F32, kind="ExternalOutput").ap()
        attn_phase(c, xo, g1, wq, wo, sinks, cos_d, sin_d, kT_d, v_d, dbg=dbg)
    return nc


def ple_phase(c, x_d, g_row, w_gate, w_up, p_d, wdt=F32):
    with ExitStack() as pes:
        xt = c.sb("xt", [128, D], es=pes)
        gam = c.sb("gam", [128, D], es=pes)
        junk = c.sb("junk", [128, D], es=pes)
        hbuf = c.sb("hbuf", [128, D], es=pes)
        stat = c.sb("stat", [128, 8], es=pes)
        hT = c.sb("hT", [128, KC, 128], wdt, es=pes)
        wb = [c.sb("wb", [128, KC, 256], wdt, es=pes) for _ in range(2)]
        wup = c.sb("wup", [128, 2, D], es=pes)
        pt = c.sb("pt", [128, 256], es=pes)
        pT = c.sb("pT", [128, 2, 128], es=pes)
        sg = [c.sb("sg", [128, 256], es=pes) for _ in range(2)]
        c.dma("sync", gam[:], g_row.partition_broadcast(128), writes=["gam"])
        c.dma("sync", wup[:], w_up.rearrange("(ch p) n -> p ch n", p=128), writes=["wup"])
        for t in range(c.S // 128):
            rs = slice(t * 128, (t + 1) * 128)
            c.dma("sync", xt[:], x_d[rs, :], writes=["xt"])
            c.dma("pool", pt[:], p_d[rs, :], writes=["pt"])
            norm_tile(c, xt, gam, junk[:], hbuf[:], stat, ["junk"], ["hbuf"])
            transpose_tile(c, hbuf, "hbuf", hT, lambda g: ("hT", g), 0)
            transpose_tile(c, pt, "pt", pT, lambda g: "pT", 0, nchunks=2)

            def epi(cb, ps, pk):
                s = c.wslot("sg")
                c.emit("act", lambda e, s=s, ps=ps: e.activation(out=sg[s][:], in_=ps[:, 0:256], func=AF.Sigmoid),
                       reads=pk, writes=[("sg", s)])
                pu, puk = c.psum()
                for ch in range(2):
                    c.emit("pe", lambda e, pu=pu, ch=ch, cb=cb: e.matmul(pu[:, 0:256], lhsT=pT[:, ch, :],
                                                                         rhs=wup[:, ch, cb * 256:(cb + 1) * 256],
                                                                         start=(ch == 0), stop=(ch == 1)),
                           reads=["pT", "wup"], writes=puk)
                c.emit("dve", lambda e, s=s, pu=pu: e.tensor_tensor(out=sg[s][:], in0=sg[s][:], in1=pu[:, 0:256], op=ALU.mult),
                       reads=puk + [("sg", s)], writes=[("sg", s)])
                c.emit("dve", lambda e, s=s, cb=cb: e.tensor_tensor(out=xt[:, cb * 256:(cb + 1) * 256], in0=sg[s][:],
                                                                    in1=xt[:, cb * 256:(cb + 1) * 256], op=ALU.add),
                       reads=[("sg", s), "xt"], writes=["xt"])
            proj(c, hT, lambda kc: [("hT", kc // 4)], 1, w_gate, D, wb, "wb", epi)
            c.dma("sync", x_d[rs, :], xt[:], reads=["xt"], writes=[("xd", t)])
        c.flush()


def final_norm_phase(c, x_d, g_row, y_d):
    with ExitStack() as pes:
        xt = c.sb("xt", [128, D], es=pes)
        gam = c.sb("gam", [128, D], es=pes)
        junk = c.sb("junk", [128, D], es=pes)
        hb = [c.sb("hbuf", [128, D], es=pes) for _ in range(2)]
        stat = c.sb("stat", [128, 8], es=pes)
        c.dma("sync", gam[:], g_row.partition_broadcast(128), writes=["gam"])
        for t in range(c.S // 128):
            rs = slice(t * 128, (t + 1) * 128)
            s = t % 2
            c.dma("sync", xt[:], x_d[rs, :], writes=["xt"])
            norm_tile(c, xt, gam, junk[:], hb[s][:], stat, ["junk"], [("hb", s)])
            c.dma("pool", y_d[rs, :], hb[s][:], reads=[("hb", s)], writes=[("yd", t)])
        c.flush()


def convert_ffn_weights_phase(c, wg, wu, wd, wg_b, wu_b, wd_b, ff):
    HC = ff // 128
    CH = 2816 if ff % 2816 == 0 else ff
    with ExitStack() as pes:
        fb = [c.sb("cvf", [128, 2816], es=pes) for _ in range(3)]
        bb = [c.sb("cvb", [128, 2816], BF16, es=pes) for _ in range(3)]
        it = 0
        engs = ["act", "dve", "pool"]

        def cast(s, n, it):
            e_ = engs[it % 3]
            if e_ == "act":
                c.emit("act", lambda e: e.copy(out=bb[s][:, 0:n], in_=fb[s][:, 0:n]), reads=[("cvf", s)], writes=[("cvb", s)])
            else:
                c.emit(e_, lambda e: e.tensor_copy(out=bb[s][:, 0:n], in_=fb[s][:, 0:n]), reads=[("cvf", s)], writes=[("cvb", s)])
        for (w, wb_) in ((wg, wg_b), (wu, wu_b)):
            for kc in range(KC):
                for c0 in range(0, ff, CH):
                    s = it % 3
                    c.dma("sync", fb[s][:, 0:CH], w[kc * 128:(kc + 1) * 128, c0:c0 + CH], writes=[("cvf", s)])
                    cast(s, CH, it)
                    h0, nh = c0 // 128, CH // 128
                    c.dma("pool", wb_[h0:h0 + nh, :, kc, :].rearrange("h p c -> p h c"),
                          bb[s][:, 0:CH].rearrange("p (h c) -> p h c", c=128), reads=[("cvb", s)], writes=[("cvo", it)])
                    it += 1
        for hc in range(HC):
            s = it % 3
            c.dma("sync", fb[s][:, 0:D], wd[hc * 128:(hc + 1) * 128, :], writes=[("cvf", s)])
            cast(s, D, it)
            c.dma("pool", wd_b[hc // 4, :, :, hc % 4, :].rearrange("db p c -> p db c"),
                  bb[s][:, 0:D].rearrange("p (db c) -> p db c", c=512), reads=[("cvb", s)], writes=[("cvo", it)])
            it += 1
        c.flush()


def run_ffn_bf16(c, x_d, gam_row, wg_b, wu_b, wd_b, ff):
    TB, NTT = 512, 4
    HC = ff // 128
    HG = HC // 4
    with ExitStack() as pes:
        xt = [c.sb("xt", [128, D], es=pes) for _ in range(NTT)]
        gam = c.sb("gam", [128, D], es=pes)
        junk = c.sb("junk", [128, D], es=pes)
        hbuf = c.sb("hbuf", [128, D], es=pes)
        stat = c.sb("stat", [128, 8], es=pes)
        hT = c.sb("hT", [128, KC, TB], BF16, es=pes)
        act = c.sb("act", [128, HC, TB], BF16, es=pes)
        wgs = [c.sb("wgs", [128, KC, 128], BF16, es=pes) for _ in range(3)]
        wus = [c.sb("wus", [128, KC, 128], BF16, es=pes) for _ in range(3)]
        wds = [c.sb("wds", [128, 4, 512], BF16, es=pes) for _ in range(3)]
        sg = [c.sb("sg", [128, TB], es=pes) for _ in range(2)]
        c.dma("sync", gam[:], gam_row.partition_broadcast(128), writes=["gam"])
        B = dict(junk=junk, stat=stat, hbuf=hbuf)
        wi = 0
        di = 0
        for tb in range(c.S // TB):
            for tt in range(NTT):
                r0 = tb * TB + tt * 128
                c.dma("sync", xt[tt][:], x_d[r0:r0 + 128, :], writes=[("xt", tt)])
                norm_to_hT(c, xt[tt][:], ("xt", tt), gam[:], "gam", hT, lambda g: ("hT", g), tt * 128, B)
            for hc in range(HC):
                s = wi % 3
                wi += 1
                c.dma("sync", wgs[s][:], wg_b[hc], writes=[("wg", s)])
                c.dma("pool", wus[s][:], wu_b[hc], writes=[("wu", s)])
                psg, pgk = c.psum()
                psu, puk = c.psum()
                for kc in range(KC):
                    c.emit("pe", lambda e, psg=psg, s=s, kc=kc: e.matmul(psg[:, :], lhsT=wgs[s][:, kc, :], rhs=hT[:, kc, :],
                                                                          start=(kc == 0), stop=(kc == KC - 1)),
                           reads=[("wg", s), ("hT", kc // 4)], writes=pgk)
                for kc in range(KC):
                    c.emit("pe", lambda e, psu=psu, s=s, kc=kc: e.matmul(psu[:, :], lhsT=wus[s][:, kc, :], rhs=hT[:, kc, :],
                                                                          start=(kc == 0), stop=(kc == KC - 1)),
                           reads=[("wu", s), ("hT", kc // 4)], writes=puk)
                ss = hc % 2
                c.emit("act", lambda e, psg=psg, ss=ss: e.activation(out=sg[ss][:], in_=psg[:, :], func=AF.Silu),
                       reads=pgk, writes=[("sg", ss)])
                c.emit("dve", lambda e, psu=psu, ss=ss, hc=hc: e.tensor_tensor(out=act[:, hc, :], in0=sg[ss][:], in1=psu[:, :],
                                                                               op=ALU.mult),
                       reads=puk + [("sg", ss)], writes=[("act", hc)])
            for db in range(D // 512):
                pds = [c.psum() for _ in range(NTT)]
                for hg in range(HG):
                    s = di % 3
                    di += 1
                    c.dma("sync" if di % 2 else "pool", wds[s][:], wd_b[hg, db], writes=[("wd", s)])
                    for tt in range(NTT):
                        pd, pdk = pds[tt]
                        for j in range(4):
                            hc = hg * 4 + j
                            c.emit("pe", lambda e, pd=pd, s=s, j=j, hc=hc, tt=tt: e.matmul(
                                pd[:, :], lhsT=act[:, hc, tt * 128:(tt + 1) * 128], rhs=wds[s][:, j, :],
                                start=(hc == 0), stop=(hc == HC - 1)),
                                reads=[("act", hc), ("wd", s)], writes=pdk)
                for tt in range(NTT):
                    pd, pdk = pds[tt]
                    c.emit("dve", lambda e, pd=pd, tt=tt, db=db: e.scalar_tensor_tensor(
                        out=xt[tt][:, db * 512:(db + 1) * 512], in0=pd[:, :], scalar=0.5,
                        in1=xt[tt][:, db * 512:(db + 1) * 512], op0=ALU.mult, op1=ALU.add),
                        reads=pdk + [("xt", tt)], writes=[("xt", tt)])
            for tt in range(NTT):
                r0 = tb * TB + tt * 128
                c.dma("pool", x_d[r0:r0 + 128, :], xt[tt][:], reads=[("xt", tt)], writes=[("xd", r0 // 128)])
        c.flush()


def convert_mats_phase(c, pairs):
    with ExitStack() as pes:
        fb = [c.sb("cvf", [128, 2048], es=pes) for _ in range(3)]
        bb = [c.sb("cvb", [128, 2048], BF16, es=pes) for _ in range(3)]
        engs = ["act", "dve", "pool"]
        it = 0
        for (w, wb_) in pairs:
            K_, N_ = w.shape
            for kc in range(K_ // 128):
                for c0 in range(0, N_, 2048):
                    n = min(2048, N_ - c0)
                    s = it % 3
                    c.dma("sync", fb[s][:, 0:n], w[kc * 128:(kc + 1) * 128, c0:c0 + n], writes=[("cvf", s)])
                    e_ = engs[it % 3]
                    if e_ == "act":
                        c.emit("act", lambda e, s=s, n=n: e.copy(out=bb[s][:, 0:n], in_=fb[s][:, 0:n]), reads=[("cvf", s)], writes=[("cvb", s)])
                    else:
                        c.emit(e_, lambda e, s=s, n=n: e.tensor_copy(out=bb[s][:, 0:n], in_=fb[s][:, 0:n]), reads=[("cvf", s)], writes=[("cvb", s)])
                    c.dma("pool", wb_[kc * 128:(kc + 1) * 128, c0:c0 + n], bb[s][:, 0:n], reads=[("cvb", s)], writes=[("cvo", it)])
                    it += 1
        c.flush()


FF = 5632
DEPTH = 4
FULL_SHAPES = dict(
    norm_g=(4, 4, D), ffn_w_gate=(4, 2, D, FF), ffn_w_up=(4, 2, D, FF), ffn_w_down=(4, 2, FF, D),
    ple_w_up=(4, 256, D), ple_w_gate=(4, D, D), kv_norm_g=(1, D), w_kv=(D, 1024), attn_w_q=(2, D, D),
    attn_w_o=(2, D, D), attn_sinks=(2, NH), final_norm_g=(1, D))
FULL_SHAPES.update(RWKV_SHAPES)


def build_full(S, ff=FF, layers=(0, 1, 2, 3)):
    nc = bass.Bass("TRN2", target_bir_lowering=False)
    shapes = dict(FULL_SHAPES)
    shapes["ffn_w_gate"] = (4, 2, D, ff)
    shapes["ffn_w_up"] = (4, 2, D, ff)
    shapes["ffn_w_down"] = (4, 2, ff, D)
    I = {k: nc.dram_tensor(k, list(v), F32, kind="ExternalInput").ap() for k, v in shapes.items()}
    x_in = nc.dram_tensor("x", [S, D], F32, kind="ExternalInput").ap()
    p_in = nc.dram_tensor("p", [DEPTH, S, 256], F32, kind="ExternalInput").ap()
    pos = nc.dram_tensor("positions", [S], I32, kind="ExternalInput").ap()
    y = nc.dram_tensor("y", [S, D], F32, kind="ExternalOutput").ap()
    scr = lambda n, shp=None: nc.dram_tensor("scr_" + n, list(shp or [S, D]), F32).ap()
    xs = scr("x")
    O = {k: scr(k) for k in ["r", "k", "v", "sw", "a", "g", "sv", "At", "Rt", "Bh", "Kh", "Bt", "Kt", "bonus", "y"]}
    O["el"] = scr("el", [S // 128, D])
    V0, V1 = scr("V0"), scr("V1")
    cos_d, sin_d = scr("cos", [S, 32]), scr("sin", [S, 32])
    kT_d, v_d = scr("kT", [8, 64, S]), scr("vkv", [S, 512])
    HC = ff // 128
    WB = {}
    for i in layers:
        for hf in range(2):
            WB[(i, hf)] = (nc.dram_tensor("wgb_%d_%d" % (i, hf), [HC, 128, KC, 128], BF16).ap(),
                           nc.dram_tensor("wub_%d_%d" % (i, hf), [HC, 128, KC, 128], BF16).ap(),
                           nc.dram_tensor("wdb_%d_%d" % (i, hf), [HC // 4, 4, 128, 4, 512], BF16).ap())
    with ExitStack() as es:
        c = Ctx(nc, es, S)
        make_consts(c)
        make_masks(c)
        copy_rows(c, xs, x_in, S)
        rope_tables_phase(c, pos, cos_d, sin_d)
        bf = lambda n, shp: nc.dram_tensor("bf_" + n, list(shp), BF16).ap()
        MB = {}
        pairs = []
        for i in layers:
            MB[("ple", i)] = bf("ple%d" % i, [D, D])
            pairs.append((I["ple_w_gate"][i], MB[("ple", i)]))
            if i < 2:
                MB[("wo", i)] = bf("rwo%d" % i, [D, D])
                pairs.append((I["rwkv_w_o"][i], MB[("wo", i)]))
                for q_, nm in enumerate(["wr_b", "wk_b", "wv_b"]):
                    MB[(nm, i)] = bf("%s%d" % (nm, i), [D, D])
                    pairs.append((I["rwkv_w_rkv"][i, q_], MB[(nm, i)]))
            else:
                MB[("aq", i)] = bf("aq%d" % i, [D, D])
                MB[("ao", i)] = bf("ao%d" % i, [D, D])
                pairs.append((I["attn_w_q"][i - 2], MB[("aq", i)]))
                pairs.append((I["attn_w_o"][i - 2], MB[("ao", i)]))
        MB["kv"] = bf("wkv", [D, 1024])
        pairs.append((I["w_kv"], MB["kv"]))
        convert_mats_phase(c, pairs)
        for i in layers:
            for hf in range(2):
                convert_ffn_weights_phase(c, I["ffn_w_gate"][i, hf], I["ffn_w_up"][i, hf], I["ffn_w_down"][i, hf],
                                          WB[(i, hf)][0], WB[(i, hf)][1], WB[(i, hf)][2], ff)
        for i in layers:
            if i == 2:
                kv_phase(c, xs, I["kv_norm_g"], MB["kv"], cos_d, sin_d, kT_d, v_d, wdt=BF16)
            run_ffn_bf16(c, xs, I["norm_g"][i, 0:1, :], WB[(i, 0)][0], WB[(i, 0)][1], WB[(i, 0)][2], ff)
            if i < 2:
                P = rwkv_param_aps(I, i, i == 1)
                P["g"] = I["norm_g"][i, 1:2, :]
                for nm in ["wr_b", "wk_b", "wv_b"]:
                    P[nm] = MB[(nm, i)]
                Oi = dict(O)
                Oi["V"] = V0 if i == 0 else V1
                Oi["vf"] = V0
                rwkv_proj_phase(c, xs, P, Oi, i == 1)
                rwkv_prep_phase(c, P, Oi, i == 1)
                rwkv_scan_phase(c, Oi)
                P["wo"] = MB[("wo", i)]
                rwkv_post_phase(c, xs, P, Oi, wdt=BF16)
            else:
                j = i - 2
                attn_phase(c, xs, I["norm_g"][i, 1:2, :], MB[("aq", i)], MB[("ao", i)], I["attn_sinks"][j:j + 1, :],
                           cos_d, sin_d, kT_d, v_d, wdt=BF16)
            run_ffn_bf16(c, xs, I["norm_g"][i, 2:3, :], WB[(i, 1)][0], WB[(i, 1)][1], WB[(i, 1)][2], ff)
            ple_phase(c, xs, I["norm_g"][i, 3:4, :], MB[("ple", i)], I["ple_w_up"][i], p_in[i], wdt=BF16)
        final_norm_phase(c, xs, I["final_norm_g"], y)
    return nc


N_CORES = 4
_NC_CACHE = {}


def kernel(**inputs):
    x = np.ascontiguousarray(inputs["x"], dtype=np.float32)
    B, S, _ = x.shape
    if S not in _NC_CACHE:
        _NC_CACHE[S] = build_full(S)
    nc = _NC_CACHE[S]
    shared = {}
    for k, shp in FULL_SHAPES.items():
        shared[k] = np.ascontiguousarray(np.asarray(inputs[k], dtype=np.float32).reshape(shp))
    p = np.asarray(inputs["p"], dtype=np.float32)
    pos = np.asarray(inputs["positions"], dtype=np.int32)
    in_maps = []
    for b in range(B):
        m = dict(shared)
        m["x"] = x[b]
        m["p"] = np.ascontiguousarray(p[:, b])
        m["positions"] = np.ascontiguousarray(pos[b])
        in_maps.append(m)
    res = run_bass_kernel_spmd(nc, in_maps, core_ids=list(range(B)))
    return np.stack([np.asarray(r["y"], dtype=np.float32) for r in res.results], axis=0)
```

```python
import numpy as np
import os
DUMP = int(os.environ.get('DUMP', '0'))
INV_SUB = int(os.environ.get('INV_SUB', '9'))
SCAN_DBG = float(os.environ.get('SCAN_DBG', '99'))
from contextlib import ExitStack
import concourse.bass as bass
import concourse.mybir as mybir
from concourse.bass_utils import run_bass_kernel_spmd

F32 = mybir.dt.float32
BF16 = mybir.dt.bfloat16
I32 = mybir.dt.int32
ALU = mybir.AluOpType
AF = mybir.ActivationFunctionType
AX = mybir.AxisListType

D = 2048
KC = D // 128
NH = 32
HD = 64
RMS_EPS = 1e-6
GN_EPS = 64e-5

ENGS = ("act", "dve", "pool", "pe", "sync")
NS_DMA = 8


class Sched:
    def __init__(self):
        self.ops = {e: [] for e in ENGS}
        self.lastw = {}
        self.readers = {}

    def emit(self, eng, fn, reads=(), writes=(), dma=False):
        idx = len(self.ops[eng])
        me = (eng, idx)
        deps = set()
        for k in tuple(reads) + tuple(writes):
            w = self.lastw.get(k)
            if w is not None:
                deps.add(w)
        for k in writes:
            rd = self.readers.get(k)
            if rd:
                deps.update(rd)
            self.readers[k] = []
        for k in reads:
            lst = self.readers.setdefault(k, [])
            if not dma:
                lst[:] = [r for r in lst if not (r[0] == eng and not self.ops[eng][r[1]][2])]
            lst.append(me)
        for k in writes:
            self.lastw[k] = me
        deps.discard(me)
        if eng == "pe":
            deps = {d for d in deps if d[0] != "pe"}
        self.ops[eng].append((fn, deps, dma))

    def setup(self, nc, es):
        self.nc = nc
        self.sems = {e: es.enter_context(nc.semaphore("s_" + e)) for e in ENGS}
        self.dsems = {e: [es.enter_context(nc.semaphore("d_%s%d" % (e, i))) for i in range(NS_DMA)]
                      for e in ("sync", "pool", "act")}
        self.cnt = {e: 0 for e in ENGS}
        self.dcnt = {e: 0 for e in ENGS}
        self.waited = {e: {} for e in ENGS}
        self.dlast = {e: [] for e in ENGS}

    def flush(self):
        nc = self.nc
        sems, dsems = self.sems, self.dsems
        signaled = {e: set() for e in ENGS}
        for e, lst in self.ops.items():
            for fn, deps, dma in lst:
                for (e2, i2) in deps:
                    signaled[e2].add(i2)
        val = {}
        for e in ENGS:
            c = self.cnt[e]
            j = self.dcnt[e]
            for i, (fn, deps, dma) in enumerate(self.ops[e]):
                if dma:
                    val[(e, i)] = (dsems[e][j % NS_DMA], 16 * (j // NS_DMA + 1))
                    j += 1
                elif i in signaled[e]:
                    c += 1
                    val[(e, i)] = (sems[e], c)
            self.cnt[e] = c
            self.dcnt[e] = j

        def run(e, engobj):
            waited = self.waited[e]
            dlast = self.dlast[e]
            for i, (fn, deps, dma) in enumerate(self.ops[e]):
                need = {}

                def want(s, v):
                    key = id(s)
                    if waited.get(key, 0) >= v:
                        return
                    if key not in need or need[key][1] < v:
                        need[key] = (s, v)
                for d in deps:
                    want(*val[d])
                if dma and len(dlast) >= NS_DMA:
                    want(*dlast[-NS_DMA])
                for key, (s, v) in need.items():
                    engobj.wait_ge(s, v)
                    waited[key] = v
                    if DUMP:
                        print("W", e, i, getattr(s, "name", s), v)
                ins = fn(engobj)
                if DUMP:
                    print("I", e, i, "dma" if dma else "", "SIG" if (dma or i in signaled[e]) else "", getattr(val.get((e, i), ("", ""))[0], "name", "-"), val.get((e, i), ("", ""))[1], str(ins)[:150].replace("\n", " "))
                if dma:
                    ins.then_inc(val[(e, i)][0], 16)
                    dlast.append(val[(e, i)])
                    del dlast[:-NS_DMA]
                elif i in signaled[e]:
                    ins.then_inc(sems[e], 1)
            for (s, v) in dlast:
                if waited.get(id(s), 0) < v:
                    engobj.wait_ge(s, v)
                    waited[id(s)] = v

        if DUMP:
            print("F flush")
        with nc.Block() as block:
            @block.sync
            def _(eng):
                run("sync", eng)

            @block.gpsimd
            def _(eng):
                run("pool", eng)

            @block.scalar
            def _(eng):
                run("act", eng)

            @block.vector
            def _(eng):
                run("dve", eng)

            @block.tensor
            def _(eng):
                run("pe", eng)
        nc.all_engine_barrier()
        self.ops = {e: [] for e in ENGS}
        self.lastw = {}
        self.readers = {}


class Ctx:
    def __init__(self, nc, es, S):
        self.nc = nc
        self.es = es
        self.S = S
        self.NT = S // 128
        self.sch = Sched()
        self.sch.setup(nc, es)
        self.ps = [es.enter_context(nc.psum_tensor("ps%d" % i, [128, 512], F32)) for i in range(8)]
        self.ps_i = 0
        self.uid = 0

    def sb(self, name, shape, dt=F32, es=None):
        self.uid += 1
        return (es or self.es).enter_context(self.nc.sbuf_tensor("%s_%d" % (name, self.uid), shape, dt))

    def flush(self):
        self.sch.flush()

    def psum(self):
        i = self.ps_i
        self.ps_i = (i + 1) % 8
        return self.ps[i], [("ps", i, q) for q in range(4)]

    @staticmethod
    def q(keys, lo, hi):
        return list(keys)

    def wslot(self, name):
        d = self.__dict__.setdefault("_ws", {})
        d[name] = d.get(name, -1) + 1
        return d[name] % 2

    def emit(self, eng, fn, reads=(), writes=()):
        self.sch.emit(eng, fn, reads, writes)

    def dma(self, eng, out, in_, reads=(), writes=()):
        self.sch.emit(eng, lambda e: e.dma_start(out=out, in_=in_), reads, writes, dma=True)


def make_consts(c):
    ident = c.sb("ident", [128, 128])
    c.emit("pool", lambda e: e.memset(ident[:], 0.0), writes=["const"])
    c.emit("pool", lambda e: e.affine_select(out=ident[:], in_=ident[:], compare_op=ALU.not_equal, fill=1.0,
                                             base=0, pattern=[[-1, 128]], channel_multiplier=1),
           writes=["const"])
    c.ident = ident


def transpose_tile(c, src, skey, dstT, dkeyf, tcol, nchunks=KC, evac=("act", "dve")):
    ident = c.ident
    for g in range(0, nchunks, 4):
        n = min(4, nchunks - g)
        ps, pk = c.psum()
        for j in range(n):
            kc = g + j
            c.emit("pe", lambda e, ps=ps, j=j, kc=kc: e.transpose(ps[:, j * 128:(j + 1) * 128],
                                                                     src[:, kc * 128:(kc + 1) * 128], ident[:]),
                   reads=[skey, "const"], writes=pk)
        eng = evac[(g // 4) % len(evac)]
        dst = dstT[:, g:g + n, tcol:tcol + 128]
        srcp = ps[:, 0:n * 128].rearrange("p (a b) -> p a b", a=n)
        if eng == "act":
            c.emit("act", lambda e, dst=dst, srcp=srcp: e.copy(out=dst, in_=srcp), reads=pk, writes=[dkeyf(g // 4)])
        else:
            c.emit("dve", lambda e, dst=dst, srcp=srcp: e.tensor_copy(out=dst, in_=srcp), reads=pk,
                   writes=[dkeyf(g // 4)])


def norm_to_hT(c, xt, xkey, gam, gkey, hT, hkeyf, tcol, B):
    junk, stat, hbuf = B["junk"], B["stat"], B["hbuf"]
    c.emit("act", lambda e: e.activation(out=junk[:], in_=xt, func=AF.Square, accum_out=stat[:, 0:1]),
           reads=[xkey], writes=["junk", "stat"])
    c.emit("dve", lambda e: e.tensor_scalar(out=stat[:, 1:2], in0=stat[:, 0:1], scalar1=1.0 / D, scalar2=RMS_EPS,
                                            op0=ALU.mult, op1=ALU.add), reads=["stat"], writes=["stat"])
    c.emit("act", lambda e: e.activation(out=stat[:, 3:4], in_=stat[:, 1:2], func=AF.Sqrt), reads=["stat"], writes=["stat"])
    c.emit("dve", lambda e: e.reciprocal(out=stat[:, 2:3], in_=stat[:, 3:4]), reads=["stat"], writes=["stat"])
    c.emit("dve", lambda e: e.scalar_tensor_tensor(out=hbuf[:], in0=xt, scalar=stat[:, 2:3], in1=gam,
                                                   op0=ALU.mult, op1=ALU.mult),
           reads=[xkey, "stat", gkey], writes=["hbuf"])
    transpose_tile(c, hbuf, "hbuf", hT, hkeyf, tcol)


def ffn_phase(c, x_d, gam_row, wg, wu, wd, FF, B):
    TB = 256
    NTT = TB // 128
    HC = FF // 128
    xt, gam, hT, act, wgs, wus, wds, sg = (B[k] for k in ("xt", "gam", "hT", "act", "wgs", "wus", "wds", "sg"))
    c.dma("sync", gam[:], gam_row.partition_broadcast(128), writes=["gam"])
    wg_v = wg.rearrange("(kc p) f -> p kc f", p=128)
    wu_v = wu.rearrange("(kc p) f -> p kc f", p=128)
    wd_v = wd.rearrange("(hc p) d -> p hc d", p=128)
    GH = 4
    assert HC % GH == 0
    hkeyf = lambda g: ("hT", g)
    for tb in range(c.S // TB):
        for tt in range(NTT):
            r0 = tb * TB + tt * 128
            c.dma("sync", xt[tt][:], x_d[r0:r0 + 128, :], reads=[("xd", r0 // 128)], writes=[("xt", tt)])
            norm_to_hT(c, xt[tt][:], ("xt", tt), gam[:], "gam", hT, hkeyf, tt * 128, B)
        for hc in range(HC):
            s = hc % 2
            c.dma("sync", wgs[s][:], wg_v[:, :, hc * 128:(hc + 1) * 128], writes=[("wg", s)])
            c.dma("pool", wus[s][:], wu_v[:, :, hc * 128:(hc + 1) * 128], writes=[("wu", s)])
            psg, pgk = c.psum()
            psu, puk = c.psum()
            for kc in range(KC):
                c.emit("pe", lambda e, psg=psg, s=s, kc=kc: e.matmul(psg[:, 0:TB], lhsT=wgs[s][:, kc, :], rhs=hT[:, kc, 0:TB],
                                                                      start=(kc == 0), stop=(kc == KC - 1)),
                       reads=[("wg", s), ("hT", kc // 4)], writes=pgk)
            for kc in range(KC):
                c.emit("pe", lambda e, psu=psu, s=s, kc=kc: e.matmul(psu[:, 0:TB], lhsT=wus[s][:, kc, :], rhs=hT[:, kc, 0:TB],
                                                                      start=(kc == 0), stop=(kc == KC - 1)),
                       reads=[("wu", s), ("hT", kc // 4)], writes=puk)
            c.emit("act", lambda e, psg=psg, s=s: e.activation(out=sg[s][:], in_=psg[:, 0:TB], func=AF.Silu),
                   reads=pgk, writes=[("sg", s)])
            c.emit("dve", lambda e, psu=psu, s=s, hc=hc: e.tensor_tensor(out=act[:, hc, :], in0=sg[s][:], in1=psu[:, 0:TB],
                                                                          op=ALU.mult),
                   reads=puk + [("sg", s)], writes=[("act", hc)])
        for db in range(D // 512):
            pds = [c.psum() for _ in range(NTT)]
            for hg in range(HC // GH):
                s = (db * (HC // GH) + hg) % 2
                q = "sync" if hg % 2 == 0 else "pool"
                c.dma(q, wds[s][:], wd_v[:, hg * GH:(hg + 1) * GH, db * 512:(db + 1) * 512], writes=[("wd", s)])
                for tt in range(NTT):
                    pd, pdk = pds[tt]
                    for j in range(GH):
                        hc = hg * GH + j
                        c.emit("pe", lambda e, pd=pd, s=s, j=j, hc=hc, tt=tt: e.matmul(
                            pd[:, :], lhsT=act[:, hc, tt * 128:(tt + 1) * 128], rhs=wds[s][:, j, :],
                            start=(hc == 0), stop=(hc == HC - 1)),
                            reads=[("act", hc), ("wd", s)], writes=pdk)
            for tt in range(NTT):
                pd, pdk = pds[tt]
                c.emit("dve", lambda e, pd=pd, tt=tt, db=db: e.scalar_tensor_tensor(
                    out=xt[tt][:, db * 512:(db + 1) * 512], in0=pd[:, :], scalar=0.5,
                    in1=xt[tt][:, db * 512:(db + 1) * 512], op0=ALU.mult, op1=ALU.add),
                    reads=pdk + [("xt", tt)], writes=[("xt", tt)])
        for tt in range(NTT):
            r0 = tb * TB + tt * 128
            c.dma("sync", x_d[r0:r0 + 128, :], xt[tt][:], reads=[("xt", tt)], writes=[("xd", r0 // 128)])


def alloc_common(c, es):
    B = {}
    B["xt"] = [c.sb("xt%d" % i, [128, D], es=es) for i in range(2)]
    B["gam"] = c.sb("gam", [128, D], es=es)
    B["junk"] = c.sb("junk", [128, D], es=es)
    B["hbuf"] = c.sb("hbuf", [128, D], es=es)
    B["stat"] = c.sb("stat", [128, 8], es=es)
    B["hT"] = c.sb("hT", [128, KC, 256], es=es)
    return B


def run_ffn(c, x_d, gam_row, wg, wu, wd, FF):
    with ExitStack() as pes:
        B = alloc_common(c, pes)
        B["act"] = c.sb("act", [128, FF // 128, 256], es=pes)
        B["wgs"] = [c.sb("wgs%d" % i, [128, KC, 128], es=pes) for i in range(2)]
        B["wus"] = [c.sb("wus%d" % i, [128, KC, 128], es=pes) for i in range(2)]
        B["wds"] = [c.sb("wds%d" % i, [128, 4, 512], es=pes) for i in range(2)]
        B["sg"] = [c.sb("sg%d" % i, [128, 256], es=pes) for i in range(2)]
        ffn_phase(c, x_d, gam_row, wg, wu, wd, FF, B)
        c.flush()


def copy_rows(c, dst, src, S):
    for t in range(S // 128):
        c.dma("sync", dst[t * 128:(t + 1) * 128, :], src[t * 128:(t + 1) * 128, :])
    c.flush()


def test_ffn_program(S, FF):
    nc = bass.Bass("TRN2", target_bir_lowering=False)
    x_in = nc.dram_tensor("x", [S, D], F32, kind="ExternalInput").ap()
    g_in = nc.dram_tensor("g", [1, D], F32, kind="ExternalInput").ap()
    wg = nc.dram_tensor("wg", [D, FF], F32, kind="ExternalInput").ap()
    wu = nc.dram_tensor("wu", [D, FF], F32, kind="ExternalInput").ap()
    wd = nc.dram_tensor("wd", [FF, D], F32, kind="ExternalInput").ap()
    y = nc.dram_tensor("y", [S, D], F32, kind="ExternalOutput").ap()
    xs = nc.dram_tensor("xs", [S, D], F32).ap()
    with ExitStack() as es:
        c = Ctx(nc, es, S)
        make_consts(c)
        copy_rows(c, xs, x_in, S)
        run_ffn(c, xs, g_in, wg, wu, wd, FF)
        run_ffn(c, xs, g_in, wg, wu, wd, FF)
        copy_rows(c, y, xs, S)
    return nc


def proj(c, xT, xkeyf, ntt, W, N, wbufs, wname, epi, kcn=KC, CW=256, qi=[0]):
    Wv = W.rearrange("(kc p) n -> p kc n", p=128)
    for cb in range(N // CW):
        s = c.wslot(wname)
        qi[0] += 1
        c.dma("sync" if qi[0] % 2 else "pool", wbufs[s][:, 0:kcn, 0:CW], Wv[:, :, cb * CW:(cb + 1) * CW],
              writes=[(wname, s)])
        ps, pk = c.psum()
        for tt in range(ntt):
            for kc in range(kcn):
                c.emit("pe", lambda e, ps=ps, tt=tt, kc=kc, s=s: e.matmul(
                    ps[:, tt * CW:(tt + 1) * CW], lhsT=xT[:, kc, tt * 128:(tt + 1) * 128], rhs=wbufs[s][:, kc, 0:CW],
                    start=(kc == 0), stop=(kc == kcn - 1)),
                    reads=[(wname, s)] + xkeyf(kc), writes=c.q(pk, tt * CW, (tt + 1) * CW))
        epi(cb, ps, pk)


def store_epi(c, out_d, r0, ntt, ost, func, CW=256):
    def epi(cb, ps, pk):
        s = c.wslot("ost")
        c.emit("act", lambda e: e.activation(out=ost[s][:, 0:ntt * CW], in_=ps[:, 0:ntt * CW], func=func),
               reads=c.q(pk, 0, ntt * CW), writes=[("ost", s)])
        dst = out_d[r0:r0 + ntt * 128, cb * CW:(cb + 1) * CW].rearrange("(tt p) n -> p tt n", p=128)
        c.dma("sync", dst, ost[s][:, 0:ntt * CW].rearrange("p (tt n) -> p tt n", tt=ntt),
              reads=[("ost", s)], writes=[("od", id(out_d), r0, cb)])
    return epi


def load_rows_T(c, rows_ap, R, dst, pes):
    tmp = c.sb("rowsT", [128, D], es=pes)
    c.dma("sync", tmp[0:R, :], rows_ap, writes=["rowsT"])
    for g in range(0, KC, 4):
        ps, pk = c.psum()
        for j in range(4):
            kc = g + j
            c.emit("pe", lambda e, ps=ps, j=j, kc=kc: e.transpose(ps[:, j * R:(j + 1) * R],
                                                                     tmp[0:R, kc * 128:(kc + 1) * 128], c.ident[0:R, 0:R]),
                   reads=["rowsT", "const"], writes=pk)
        c.emit("dve", lambda e, ps=ps, g=g: e.tensor_copy(out=dst[:, g:g + 4, 0:R],
                                                         in_=ps[:, 0:4 * R].rearrange("p (a b) -> p a b", a=4)),
               reads=pk, writes=["rowsTd"])


C0 = -float(np.exp(-0.5))


def rwkv_proj_phase(c, x_d, P, O, has_vmix):
    TB, NTT = 256, 2
    with ExitStack() as pes:
        xt = c.sb("xt", [128, D], es=pes)
        gam = c.sb("gam", [128, D], es=pes)
        stat = c.sb("stat", [128, 8], es=pes)
        hTw = c.sb("hTw", [128, KC, 257], es=pes)
        dxT = c.sb("dxT", [128, KC, 256], es=pes)
        xcT = [c.sb("xcT", [128, KC, 256], es=pes) for _ in range(2)]
        wb = [c.sb("wb", [128, KC, 256], es=pes) for _ in range(2)]
        muT = c.sb("muT", [128, KC, 6], es=pes)
        w2a = c.sb("w2a", [128, D], es=pes)
        a2a = c.sb("a2a", [128, D], es=pes)
        v2a = c.sb("v2a", [128, D], es=pes)
        g2s = c.sb("g2s", [128, 2, D], es=pes)
        t1 = {k: c.sb("t1" + k, [128, 256], es=pes) for k in "wav"}
        t1g = c.sb("t1g", [128, 2, 256], es=pes)
        ost = [c.sb("ost", [128, 512], es=pes) for _ in range(2)]
        use_b = "wr_b" in P
        if use_b:
            xcb = c.sb("xcb", [128, KC, 256], BF16, es=pes)
            wbb = [c.sb("wbb", [128, KC, 256], BF16, es=pes) for _ in range(2)]
        B = dict(junk=dxT[:, 0:8, :].rearrange("p a b -> p (a b)"), stat=stat,
                 hbuf=xcT[1][:, 0:8, :].rearrange("p a b -> p (a b)"))
        c.dma("sync", gam[:], P["g"].partition_broadcast(128), writes=["gam"])
        load_rows_T(c, P["mu"], 6, muT, pes)
        c.dma("sync", w2a[0:96, :], P["w2"], writes=["w2a"])
        c.dma("sync", w2a[96:97, :], P["w0"], writes=["w2a"])
        c.dma("sync", a2a[0:96, :], P["a2"], writes=["a2a"])
        c.dma("sync", a2a[96:97, :], P["a0"], writes=["a2a"])
        if has_vmix:
            c.dma("sync", v2a[0:64, :], P["v2"], writes=["v2a"])
            c.dma("sync", v2a[64:65, :], P["v0"], writes=["v2a"])
        c.dma("sync", g2s[:], P["g2"].rearrange("(ch p) n -> p ch n", p=128), writes=["g2s"])
        for k in "wav":
            c.emit("dve", lambda e, k=k: e.memset(t1[k][:], 1.0), writes=[("t1", k)])
        c.emit("dve", lambda e: e.memset(hTw[:, :, 0:1], 0.0), writes=[("hT", g) for g in range(4)])

        hkeys = [("hT", g) for g in range(4)]
        for tb in range(c.S // TB):
            r0 = tb * TB
            for tt in range(NTT):
                c.dma("sync", xt[:], x_d[r0 + tt * 128:r0 + (tt + 1) * 128, :], writes=["xt"])
                junk = B["junk"]
                c.emit("act", lambda e: e.activation(out=junk, in_=xt[:], func=AF.Square, accum_out=stat[:, 0:1]),
                       reads=["xt"], writes=["dxT", "stat"])
                c.emit("dve", lambda e: e.tensor_scalar(out=stat[:, 1:2], in0=stat[:, 0:1], scalar1=1.0 / D,
                                                        scalar2=RMS_EPS, op0=ALU.mult, op1=ALU.add),
                       reads=["stat"], writes=["stat"])
                c.emit("act", lambda e: e.activation(out=stat[:, 3:4], in_=stat[:, 1:2], func=AF.Sqrt),
                       reads=["stat"], writes=["stat"])
                c.emit("dve", lambda e: e.reciprocal(out=stat[:, 2:3], in_=stat[:, 3:4]), reads=["stat"], writes=["stat"])
                hb = B["hbuf"]
                c.emit("dve", lambda e: e.scalar_tensor_tensor(out=hb, in0=xt[:], scalar=stat[:, 2:3], in1=gam[:],
                                                               op0=ALU.mult, op1=ALU.mult),
                       reads=["xt", "stat", "gam"], writes=[("xc", 1, g) for g in range(4)])
                ident = c.ident
                for g in range(0, KC, 4):
                    ps, pk = c.psum()
                    for j in range(4):
                        kc = g + j
                        c.emit("pe", lambda e, ps=ps, j=j, kc=kc: e.transpose(
                            ps[:, j * 128:(j + 1) * 128], hb[:, kc * 128:(kc + 1) * 128], ident[:]),
                            reads=[("xc", 1, q) for q in range(4)] + ["const"], writes=pk)
                    dst = hTw[:, g:g + 4, 1 + tt * 128:1 + (tt + 1) * 128]
                    srcp = ps[:, :].rearrange("p (a b) -> p a b", a=4)
                    if (g // 4) % 2 == 0:
                        c.emit("act", lambda e, dst=dst, srcp=srcp: e.copy(out=dst, in_=srcp), reads=pk,
                               writes=[("hT", g // 4)])
                    else:
                        c.emit("dve", lambda e, dst=dst, srcp=srcp: e.tensor_copy(out=dst, in_=srcp), reads=pk,
                               writes=[("hT", g // 4)])
            c.emit("dve", lambda e: e.tensor_tensor(out=dxT[:], in0=hTw[:, :, 0:256], in1=hTw[:, :, 1:257],
                                                    op=ALU.subtract), reads=hkeys, writes=["dxT"])

            def mix(m):
                s = c.wslot("xc")
                for kc in range(KC):
                    c.emit("dve", lambda e, s=s, kc=kc, m=m: e.scalar_tensor_tensor(
                        out=xcT[s][:, kc, :], in0=dxT[:, kc, :], scalar=muT[:, kc, m:m + 1], in1=hTw[:, kc, 1:257],
                        op0=ALU.mult, op1=ALU.add),
                        reads=["dxT", "rowsTd", ("hT", kc // 4)], writes=[("xc", s, kc // 4)])
                return xcT[s], (lambda kc, s=s: [("xc", s, kc // 4)])

            def mixb(m):
                for kc in range(KC):
                    c.emit("dve", lambda e, kc=kc, m=m: e.scalar_tensor_tensor(
                        out=xcb[:, kc, :], in0=dxT[:, kc, :], scalar=muT[:, kc, m:m + 1], in1=hTw[:, kc, 1:257],
                        op0=ALU.mult, op1=ALU.add),
                        reads=["dxT", "rowsTd", ("hT", kc // 4)], writes=[("xcb", kc // 4)])
                return xcb, (lambda kc: [("xcb", kc // 4)])

            def lora(xT, xkf, w1, R, func, t1tile, t1key, nch=1):
                s = c.wslot("wb")
                c.dma("pool", wb[s][:, :, 0:R * nch], w1.rearrange("(kc p) r -> p kc r", p=128), writes=[("wb", s)])
                for ch in range(nch):
                    ps, pk = c.psum()
                    for kc in range(KC):
                        c.emit("pe", lambda e, ps=ps, kc=kc, s=s, ch=ch: e.matmul(
                            ps[0:R, 0:256], lhsT=wb[s][:, kc, ch * R:(ch + 1) * R], rhs=xT[:, kc, 0:256],
                            start=(kc == 0), stop=(kc == KC - 1)),
                            reads=[("wb", s)] + xkf(kc), writes=c.q(pk, 0, 256))
                    dst = t1tile[0:R, :] if nch == 1 else t1tile[0:R, ch, :]
                    c.emit("act", lambda e, ps=ps, dst=dst: e.activation(out=dst, in_=ps[0:R, 0:256], func=func),
                           reads=c.q(pk, 0, 256), writes=[t1key])

            def lora2(t1tile, t1key, K, w2tile, w2key, out_d, func, nch=1):
                epi = store_epi(c, out_d, r0, NTT, ost, func)
                for cb in range(D // 256):
                    ps, pk = c.psum()
                    for tt in range(NTT):
                        for ch in range(nch):
                            lhsT = t1tile[0:K, tt * 128:(tt + 1) * 128] if nch == 1 else t1tile[0:K, ch, tt * 128:(tt + 1) * 128]
                            rhs = w2tile[0:K, cb * 256:(cb + 1) * 256] if nch == 1 else w2tile[0:K, ch, cb * 256:(cb + 1) * 256]
                            c.emit("pe", lambda e, ps=ps, tt=tt, lhsT=lhsT, rhs=rhs, ch=ch: e.matmul(
                                ps[:, tt * 256:(tt + 1) * 256], lhsT=lhsT, rhs=rhs, start=(ch == 0), stop=(ch == nch - 1)),
                                reads=[t1key, w2key], writes=c.q(pk, tt * 256, (tt + 1) * 256))
                    epi(cb, ps, pk)

            if use_b:
                xT, xkf = mixb(0)
                proj(c, xT, xkf, NTT, P["wr_b"], D, wbb, "wbb", store_epi(c, O["r"], r0, NTT, ost, AF.Copy))
                xT, xkf = mixb(2)
                proj(c, xT, xkf, NTT, P["wk_b"], D, wbb, "wbb", store_epi(c, O["k"], r0, NTT, ost, AF.Copy))
                xT, xkf = mixb(3)
                proj(c, xT, xkf, NTT, P["wv_b"], D, wbb, "wbb", store_epi(c, O["v"], r0, NTT, ost, AF.Copy))
                if has_vmix:
                    xT, xkf = mix(3)
            else:
                xT, xkf = mix(0)
                proj(c, xT, xkf, NTT, P["wr"], D, wb, "wb", store_epi(c, O["r"], r0, NTT, ost, AF.Copy))
                xT, xkf = mix(2)
                proj(c, xT, xkf, NTT, P["wk"], D, wb, "wb", store_epi(c, O["k"], r0, NTT, ost, AF.Copy))
                xT, xkf = mix(3)
                proj(c, xT, xkf, NTT, P["wv"], D, wb, "wb", store_epi(c, O["v"], r0, NTT, ost, AF.Copy))
            if has_vmix:
                lora(xT, xkf, P["v1"], 64, AF.Copy, t1["v"], ("t1", "v"))
                lora2(t1["v"], ("t1", "v"), 65, v2a, "v2a", O["sv"], AF.Sigmoid)
            xT, xkf = mix(1)
            lora(xT, xkf, P["w1"], 96, AF.Tanh, t1["w"], ("t1", "w"))
            lora2(t1["w"], ("t1", "w"), 97, w2a, "w2a", O["sw"], AF.Sigmoid)
            xT, xkf = mix(4)
            lora(xT, xkf, P["a1"], 96, AF.Copy, t1["a"], ("t1", "a"))
            lora2(t1["a"], ("t1", "a"), 97, a2a, "a2a", O["a"], AF.Sigmoid)
            xT, xkf = mix(5)
            lora(xT, xkf, P["g1"], 128, AF.Sigmoid, t1g, ("t1", "g"), nch=2)
            lora2(t1g, ("t1", "g"), 128, g2s, "g2s", O["g"], AF.Copy, nch=2)
            c.emit("dve", lambda e: e.tensor_copy(out=hTw[:, :, 0:1], in_=hTw[:, :, 256:257]), reads=hkeys, writes=hkeys)
        c.flush()


def rwkv_param_aps(nc_inputs, j, has_vmix):
    I = nc_inputs
    P = dict(
        mu=I["rwkv_mu"][j], wr=I["rwkv_w_rkv"][j, 0], wk=I["rwkv_w_rkv"][j, 1], wv=I["rwkv_w_rkv"][j, 2],
        wo=I["rwkv_w_o"][j], w0=I["rwkv_w0"][j:j + 1, :], w1=I["rwkv_w1"][j], w2=I["rwkv_w2"][j],
        a0=I["rwkv_a0"][j:j + 1, :], a1=I["rwkv_a1"][j], a2=I["rwkv_a2"][j], g1=I["rwkv_g1"][j], g2=I["rwkv_g2"][j],
        k_k=I["rwkv_k_k"][j:j + 1, :], k_a=I["rwkv_k_a"][j:j + 1, :],
        r_k=I["rwkv_r_k"][j:j + 1].rearrange("o h n -> o (h n)"),
        gn_g=I["rwkv_gn_g"][j:j + 1, :], gn_b=I["rwkv_gn_b"][j:j + 1, :])
    if has_vmix:
        P.update(v0=I["rwkv_v0"][j - 1:j, :], v1=I["rwkv_v1"][j - 1], v2=I["rwkv_v2"][j - 1])
    return P


RWKV_SHAPES = dict(
    rwkv_mu=(2, 6, D), rwkv_w_rkv=(2, 3, D, D), rwkv_w_o=(2, D, D), rwkv_w0=(2, D), rwkv_w1=(2, D, 96),
    rwkv_w2=(2, 96, D), rwkv_a0=(2, D), rwkv_a1=(2, D, 96), rwkv_a2=(2, 96, D), rwkv_v0=(1, D), rwkv_v1=(1, D, 64),
    rwkv_v2=(1, 64, D), rwkv_g1=(2, D, 256), rwkv_g2=(2, 256, D), rwkv_k_k=(2, D), rwkv_k_a=(2, D),
    rwkv_r_k=(2, 32, 64), rwkv_gn_g=(2, D), rwkv_gn_b=(2, D))


def test_rwkv_program(S, stage):
    nc = bass.Bass("TRN2", target_bir_lowering=False)
    I = {k: nc.dram_tensor(k, list(v), F32, kind="ExternalInput").ap() for k, v in RWKV_SHAPES.items()}
    x_in = nc.dram_tensor("x", [S, D], F32, kind="ExternalInput").ap()
    g_in = nc.dram_tensor("g", [1, D], F32, kind="ExternalInput").ap()
    vf_in = nc.dram_tensor("vf", [S, D], F32, kind="ExternalInput").ap()
    names = ["r", "k", "v", "sw", "a", "g", "sv", "At", "Rt", "Bh", "Kh", "Bt", "Kt", "V", "bonus", "y", "xo"]
    O = {k: nc.dram_tensor("o_" + k, [S, D], F32, kind="ExternalOutput").ap() for k in names}
    O["el"] = nc.dram_tensor("o_el", [S // 128, D], F32, kind="ExternalOutput").ap()
    O["vf"] = vf_in
    with ExitStack() as es:
        c = Ctx(nc, es, S)
        make_consts(c)
        make_masks(c)
        P = rwkv_param_aps(I, 1, True)
        P["g"] = g_in
        copy_rows(c, O["xo"], x_in, S)
        rwkv_proj_phase(c, O["xo"], P, O, True)
        if stage >= 2:
            rwkv_prep_phase(c, P, O, True)
        if stage >= 3:
            rwkv_scan_phase(c, O)
        if stage >= 4:
            rwkv_post_phase(c, O["xo"], P, O)
    return nc


def make_masks(c):
    tri = c.sb("tri", [128, 128])
    ones = c.sb("ones", [128, 128])
    mask4 = c.sb("mask4", [128, 512])
    msl = c.sb("msl", [128, 128])
    c.emit("pool", lambda e: e.memset(ones[:], 1.0), writes=["const"])
    c.emit("pool", lambda e: e.memset(tri[:], 1.0), writes=["const"])
    c.emit("pool", lambda e: e.affine_select(out=tri[:], in_=tri[:], compare_op=ALU.is_ge, fill=0.0, base=0,
                                             pattern=[[1, 128]], channel_multiplier=-1), writes=["const"])
    c.emit("pool", lambda e: e.memset(mask4[:], 1.0), writes=["const"])
    for q in range(4):
        op = ALU.is_gt if q % 2 == 0 else ALU.is_ge
        c.emit("pool", lambda e, q=q, op=op: e.affine_select(
            out=mask4[:, q * 128:(q + 1) * 128], in_=mask4[:, q * 128:(q + 1) * 128], compare_op=op, fill=0.0, base=0,
            pattern=[[1, 128]], channel_multiplier=-1), writes=["const"])
    c.emit("pool", lambda e: e.memset(msl[:], 1.0), writes=["const"])
    c.emit("pool", lambda e: e.affine_select(out=msl[:], in_=msl[:], compare_op=ALU.is_gt, fill=0.0, base=0,
                                             pattern=[[-1, 128]], channel_multiplier=1), writes=["const"])
    c.tri, c.ones, c.mask4, c.msl = tri, ones, mask4, msl


def rwkv_prep_phase(c, P, O, has_vmix):
    CB = 512
    with ExitStack() as pes:
        names_in = ["r", "k", "v", "sw", "a"] + (["sv", "vf"] if has_vmix else [])
        ld = {n: [c.sb("ld" + n, [128, CB], es=pes) for _ in range(2)] for n in names_in}
        names_out = ["At", "Rt", "Bh", "Kh", "Bt", "Kt", "V", "bonus"]
        ob = {n: [c.sb("ob" + n, [128, CB], es=pes) for _ in range(2)] for n in names_out}
        tm = {n: c.sb("tm" + n, [128, CB], es=pes) for n in ["lw", "epos", "eneg", "eprev", "EL", "kk", "sq", "b", "t", "kh", "t2", "d"]}
        st = c.sb("st", [128, 64], es=pes)
        kkp = c.sb("kkp", [128, D], es=pes)
        kap = c.sb("kap", [128, D], es=pes)
        rkp = c.sb("rkp", [128, D], es=pes)
        c.dma("sync", kkp[:], P["k_k"].partition_broadcast(128), writes=["par"])
        c.dma("sync", kap[:], P["k_a"].partition_broadcast(128), writes=["par"])
        c.dma("sync", rkp[:], P["r_k"].partition_broadcast(128), writes=["par"])
        it = 0
        for ch in range(c.S // 128):
            rs = slice(ch * 128, (ch + 1) * 128)
            for cb in range(D // CB):
                cs = slice(cb * CB, (cb + 1) * CB)
                s = it % 2
                it += 1
                L = {}
                for i, n in enumerate(names_in):
                    c.dma("sync" if i % 2 == 0 else "pool", ld[n][s][:], O[n][rs, cs], writes=[("ld", n, s)])
                    L[n] = ld[n][s]
                lk = lambda *ns: [("ld", n, s) for n in ns]
                T = tm
                ok = lambda *ns: [("ob", n, s) for n in ns]
                OB = {n: ob[n][s] for n in names_out}
                c.emit("dve", lambda e, L=L: e.tensor_scalar(out=T["lw"][:], in0=L["sw"][:], scalar1=C0, scalar2=0.0,
                                                             op0=ALU.mult, op1=ALU.add), reads=lk("sw"), writes=["lw"])
                psc, pck = c.psum()
                pst, ptk = c.psum()
                c.emit("pe", lambda e, psc=psc: e.matmul(psc[:, :], lhsT=c.tri[:], rhs=T["lw"][:], start=True, stop=True),
                       reads=["lw", "const"], writes=pck)
                c.emit("pe", lambda e, pst=pst: e.matmul(pst[:, :], lhsT=c.ones[:], rhs=T["lw"][:], start=True, stop=True),
                       reads=["lw", "const"], writes=ptk)
                c.emit("act", lambda e, psc=psc: e.activation(out=T["epos"][:], in_=psc[:, :], func=AF.Exp), reads=pck, writes=["epos"])
                c.emit("act", lambda e, psc=psc: e.activation(out=T["eneg"][:], in_=psc[:, :], func=AF.Exp, scale=-1.0),
                       reads=pck, writes=["eneg"])
                c.emit("dve", lambda e, psc=psc: e.tensor_tensor(out=T["eprev"][:], in0=psc[:, :], in1=T["lw"][:], op=ALU.subtract),
                       reads=pck + ["lw"], writes=["eprev"])
                c.emit("act", lambda e: e.activation(out=T["eprev"][:], in_=T["eprev"][:], func=AF.Exp), reads=["eprev"], writes=["eprev"])
                c.emit("act", lambda e, pst=pst: e.activation(out=T["EL"][:], in_=pst[:, :], func=AF.Exp), reads=ptk, writes=["EL"])
                c.emit("pool", lambda e, L=L, cs=cs: e.tensor_tensor(out=T["kk"][:], in0=L["k"][:], in1=kkp[:, cs], op=ALU.mult),
                       reads=lk("k") + ["par"], writes=["kk"])
                c.emit("pool", lambda e: e.tensor_tensor(out=T["sq"][:], in0=T["kk"][:], in1=T["kk"][:], op=ALU.mult),
                       reads=["kk"], writes=["sq"])
                c.emit("dve", lambda e: e.tensor_reduce(out=st[:, 0:8], in_=T["sq"][:].rearrange("p (h j) -> p h j", j=64),
                                                        axis=AX.X, op=ALU.add), reads=["sq"], writes=["st"])
                c.emit("act", lambda e: e.activation(out=st[:, 8:16], in_=st[:, 0:8], func=AF.Sqrt), reads=["st"], writes=["st"])
                c.emit("dve", lambda e: e.tensor_scalar(out=st[:, 8:16], in0=st[:, 8:16], scalar1=1e-12, scalar2=0.0,
                                                        op0=ALU.max, op1=ALU.add), reads=["st"], writes=["st"])
                c.emit("dve", lambda e: e.reciprocal(out=st[:, 16:24], in_=st[:, 8:16]), reads=["st"], writes=["st"])
                c.emit("dve", lambda e: e.tensor_tensor(
                    out=T["kk"][:].rearrange("p (h j) -> p h j", j=64), in0=T["kk"][:].rearrange("p (h j) -> p h j", j=64),
                    in1=st[:, 16:24].unsqueeze(2).to_broadcast([128, 8, 64]), op=ALU.mult), reads=["kk", "st"], writes=["kk"])
                c.emit("dve", lambda e, OB=OB: e.scalar_tensor_tensor(out=OB["At"][:], in0=T["kk"][:], scalar=-1.0, in1=T["eprev"][:],
                                                                     op0=ALU.mult, op1=ALU.mult),
                       reads=["kk", "eprev"], writes=ok("At"))
                c.emit("pool", lambda e, L=L: e.tensor_tensor(out=T["b"][:], in0=T["kk"][:], in1=L["a"][:], op=ALU.mult),
                       reads=["kk"] + lk("a"), writes=["b"])
                c.emit("pool", lambda e, OB=OB: e.tensor_tensor(out=OB["Bh"][:], in0=T["b"][:], in1=T["eneg"][:], op=ALU.mult),
                       reads=["b", "eneg"], writes=ok("Bh"))
                c.emit("pool", lambda e, OB=OB: e.tensor_tensor(out=OB["Bt"][:], in0=OB["Bh"][:], in1=T["EL"][:], op=ALU.mult),
                       reads=ok("Bh") + ["EL"], writes=ok("Bt"))
                c.emit("dve", lambda e, L=L, cs=cs: e.scalar_tensor_tensor(out=T["t"][:], in0=L["a"][:], scalar=-1.0, in1=kap[:, cs],
                                                                          op0=ALU.add, op1=ALU.mult),
                       reads=lk("a") + ["par"], writes=["t"])
                c.emit("dve", lambda e, L=L: e.scalar_tensor_tensor(out=T["kh"][:], in0=T["t"][:], scalar=1.0, in1=L["k"][:],
                                                                   op0=ALU.add, op1=ALU.mult),
                       reads=["t"] + lk("k"), writes=["kh"])
                c.emit("pool", lambda e, OB=OB: e.tensor_tensor(out=OB["Kh"][:], in0=T["kh"][:], in1=T["eneg"][:], op=ALU.mult),
                       reads=["kh", "eneg"], writes=ok("Kh"))
                c.emit("pool", lambda e, OB=OB: e.tensor_tensor(out=OB["Kt"][:], in0=OB["Kh"][:], in1=T["EL"][:], op=ALU.mult),
                       reads=ok("Kh") + ["EL"], writes=ok("Kt"))
                c.emit("pool", lambda e, OB=OB, L=L: e.tensor_tensor(out=OB["Rt"][:], in0=L["r"][:], in1=T["epos"][:], op=ALU.mult),
                       reads=lk("r") + ["epos"], writes=ok("Rt"))
                if has_vmix:
                    c.emit("pool", lambda e, L=L: e.tensor_tensor(out=T["d"][:], in0=L["vf"][:], in1=L["v"][:], op=ALU.subtract),
                           reads=lk("vf", "v"), writes=["d"])
                    c.emit("pool", lambda e, L=L: e.tensor_tensor(out=T["d"][:], in0=T["d"][:], in1=L["sv"][:], op=ALU.mult),
                           reads=lk("sv") + ["d"], writes=["d"])
                    c.emit("pool", lambda e, L=L, OB=OB: e.tensor_tensor(out=OB["V"][:], in0=T["d"][:], in1=L["v"][:], op=ALU.add),
                           reads=lk("v") + ["d"], writes=ok("V"))
                else:
                    c.emit("pool", lambda e, L=L, OB=OB: e.tensor_copy(out=OB["V"][:], in_=L["v"][:]), reads=lk("v"), writes=ok("V"))
                c.emit("pool", lambda e, L=L: e.tensor_tensor(out=T["t2"][:], in0=L["r"][:], in1=T["kh"][:], op=ALU.mult),
                       reads=lk("r") + ["kh"], writes=["t2"])
                c.emit("pool", lambda e, cs=cs: e.tensor_tensor(out=T["t2"][:], in0=T["t2"][:], in1=rkp[:, cs], op=ALU.mult),
                       reads=["t2", "par"], writes=["t2"])
                c.emit("dve", lambda e: e.tensor_reduce(out=st[:, 32:40], in_=T["t2"][:].rearrange("p (h j) -> p h j", j=64),
                                                        axis=AX.X, op=ALU.add), reads=["t2"], writes=["st2"])
                c.emit("dve", lambda e, OB=OB: e.tensor_tensor(
                    out=OB["bonus"][:].rearrange("p (h j) -> p h j", j=64), in0=OB["V"][:].rearrange("p (h j) -> p h j", j=64),
                    in1=st[:, 32:40].unsqueeze(2).to_broadcast([128, 8, 64]), op=ALU.mult), reads=ok("V") + ["st2"], writes=ok("bonus"))
                for i, n in enumerate(names_out):
                    c.dma("sync" if i % 2 == 0 else "pool", O[n][rs, cs], OB[n][:], reads=ok(n), writes=[("od", n, ch, cb)])
                c.dma("sync", O["el"][ch:ch + 1, cs], T["EL"][0:1, :], reads=["EL"], writes=[("od", "el", ch, cb)])
        c.flush()


def rwkv_scan_phase(c, O):
    CB, G = 512, 8
    NB = G // 4
    ident = c.ident
    with ExitStack() as pes:
        names_in = ["At", "Rt", "Bh", "Kh", "Bt", "Kt", "V"]
        ld = {n: [c.sb("sl" + n, [128, CB], es=pes) for _ in range(2)] for n in names_in}
        elrow = [c.sb("elrow", [1, CB], es=pes) for _ in range(2)]
        gl = [c.sb("gl", [64, 8], es=pes) for _ in range(2)]
        Tst = c.sb("Tst", [64, NH, 64], es=pes)
        XT = [c.sb("XT", [64, 512], es=pes) for _ in range(G)]
        ABK = [c.sb("ABK", [128, 512], es=pes) for _ in range(G)]
        Xb = [[c.sb("Xb", [128, 128], es=pes) for _ in range(2)] for _ in range(G)]
        XTb = [[c.sb("XTb", [128, 128], es=pes) for _ in range(3)] for _ in range(G)]
        PTb = [[c.sb("PTb", [128, 128], es=pes) for _ in range(2)] for _ in range(G)]
        Wb = [c.sb("Wb", [128, 64], es=pes) for _ in range(G)]
        Ub = [c.sb("Ub", [128, 64], es=pes) for _ in range(G)]
        Yst = [c.sb("Yst", [128, CB], es=pes) for _ in range(2)]
        print("scan sbuf remaining", c.nc.sbuf_bytes_remaining)
        c.emit("dve", lambda e: e.memset(Tst[:], 0.0), writes=[("T", h) for h in range(NH)])
        it = 0
        for ch in range(c.S // 128):
            rs = slice(ch * 128, (ch + 1) * 128)
            for cb in range(D // CB):
                cs = slice(cb * CB, (cb + 1) * CB)
                s = it % 2
                it += 1
                L = {}
                for i, n in enumerate(names_in):
                    c.dma("sync" if i % 2 == 0 else "pool", ld[n][s][:], O[n][rs, cs], writes=[("sl", n, s)])
                    L[n] = ld[n][s]
                lk = lambda *ns: [("sl", n, s) for n in ns]
                c.dma("sync", elrow[s][:], O["el"][ch:ch + 1, cs], writes=[("elrow", s)])
                psg, pgk = c.psum()
                for hl in range(8):
                    c.emit("pe", lambda e, psg=psg, hl=hl, s=s: e.matmul(
                        psg[0:64, hl:hl + 1], lhsT=elrow[s][0:1, hl * 64:(hl + 1) * 64], rhs=c.ones[0:1, 0:1], start=True, stop=True),
                        reads=[("elrow", s), "const"], writes=c.q(pgk, 0, 8))
                c.emit("dve", lambda e, psg=psg, s=s: e.tensor_copy(out=gl[s][:], in_=psg[0:64, 0:8]),
                       reads=c.q(pgk, 0, 8), writes=[("gl", s)])
                for grp in range(8 // G):
                    if SCAN_DBG < 1:
                        continue
                    heads = [grp * G + i for i in range(G)]
                    for i, hl in enumerate(heads):
                        hc = slice(hl * 64, (hl + 1) * 64)
                        ps, pk = c.psum()
                        for qn, n in enumerate(["At", "Rt", "Bh", "Kh"]):
                            c.emit("pe", lambda e, ps=ps, qn=qn, n=n, hc=hc, L=L: e.transpose(
                                ps[0:64, qn * 128:(qn + 1) * 128], L[n][:, hc], ident[:]),
                                reads=lk(n) + ["const"], writes=c.q(pk, qn * 128, (qn + 1) * 128))
                        if i % 2 == 0:
                            c.emit("act", lambda e, ps=ps, i=i: e.copy(out=XT[i][:], in_=ps[0:64, :]), reads=pk, writes=[("XT", i)])
                        else:
                            c.emit("dve", lambda e, ps=ps, i=i: e.tensor_copy(out=XT[i][:], in_=ps[0:64, :]), reads=pk,
                                   writes=[("XT", i)])
                    if SCAN_DBG < 2:
                        continue
                    psCs = [c.psum() for _ in range(NB)]
                    for i, hl in enumerate(heads):
                        psC, pCk = psCs[i // 4]
                        c.emit("pe", lambda e, psC=psC, i=i: e.matmul(psC[:, (i % 4) * 128:(i % 4 + 1) * 128], lhsT=XT[i][:, 0:128],
                                                                      rhs=XT[i][:, 256:384], start=True, stop=True),
                               reads=[("XT", i)], writes=pCk)
                    for i, hl in enumerate(heads):
                        psC, pCk = psCs[i // 4]
                        c.emit("dve", lambda e, psC=psC, i=i: e.tensor_tensor(out=XTb[i][2][:], in0=psC[:, (i % 4) * 128:(i % 4 + 1) * 128],
                                                                              in1=c.msl[:], op=ALU.mult),
                               reads=pCk + ["const"], writes=[("XTb", i, 2)])
                    for i, hl in enumerate(heads):
                        ps, pk = c.psum()
                        c.emit("pe", lambda e, ps=ps, i=i: e.matmul(ps[:, 0:256], lhsT=XT[i][:, 256:384], rhs=XT[i][:, 0:256],
                                                                    start=True, stop=True), reads=[("XT", i)], writes=pk)
                        c.emit("pe", lambda e, ps=ps, i=i: e.matmul(ps[:, 256:512], lhsT=XT[i][:, 384:512], rhs=XT[i][:, 0:256],
                                                                    start=True, stop=True), reads=[("XT", i)], writes=pk)
                        c.emit("dve", lambda e, ps=ps, i=i: e.tensor_tensor(out=ABK[i][:], in0=ps[:, :], in1=c.mask4[:], op=ALU.mult),
                               reads=pk + ["const"], writes=[("ABK", i)])
                        c.emit("dve", lambda e, i=i: e.tensor_tensor(out=PTb[i][0][:], in0=ABK[i][:, 0:128], in1=ident[:], op=ALU.add),
                               reads=[("ABK", i), "const"], writes=[("PTb", i, 0)])
                    if SCAN_DBG < 3:
                        continue
                    Xc = [(ABK[i][:, 0:128], ("ABK", i)) for i in range(G)]
                    XTc = [(XTb[i][2][:], ("XTb", i, 2)) for i in range(G)]
                    PTc = [(PTb[i][0][:], ("PTb", i, 0)) for i in range(G)]
                    for lev in range(int(os.environ.get("NLEV", "6"))):
                        pXs = [c.psum() for _ in range(NB)]
                        pXTs = [c.psum() for _ in range(NB)]
                        newX, newXT = [], []
                        for i in range(G):
                            pX, pXk = pXs[i // 4]
                            pXT, pXTk = pXTs[i // 4]
                            qs = pXk
                            qt = pXTk
                            xa, xk_ = Xc[i]
                            xta, xtk = XTc[i]
                            if lev < 5:
                                c.emit("pe", lambda e, pX=pX, i=i, xa=xa, xta=xta: e.matmul(
                                    pX[:, (i % 4) * 128:(i % 4 + 1) * 128], lhsT=xta, rhs=xa, start=True, stop=True),
                                    reads=[xk_, xtk], writes=qs)
                            c.emit("pe", lambda e, pXT=pXT, i=i, xa=xa, xta=xta: e.matmul(
                                pXT[:, (i % 4) * 128:(i % 4 + 1) * 128], lhsT=xa, rhs=xta, start=True, stop=True),
                                reads=[xk_, xtk], writes=qt)
                        if INV_SUB < 2:
                            continue
                        for i in range(G):
                            pX, pXk = pXs[i // 4]
                            pXT, pXTk = pXTs[i // 4]
                            b = lev % 2
                            if lev < 5:
                                c.emit("act", lambda e, pX=pX, i=i, b=b: e.copy(out=Xb[i][b][:], in_=pX[:, (i % 4) * 128:(i % 4 + 1) * 128]),
                                       reads=pXk, writes=[("Xb", i, b)])
                                newX.append((Xb[i][b][:], ("Xb", i, b)))
                            else:
                                newX.append(None)
                            c.emit("act", lambda e, pXT=pXT, i=i, b=b: e.copy(out=XTb[i][b][:], in_=pXT[:, (i % 4) * 128:(i % 4 + 1) * 128]),
                                   reads=pXTk, writes=[("XTb", i, b)])
                            newXT.append((XTb[i][b][:], ("XTb", i, b)))
                        if INV_SUB < 3:
                            continue
                        pPs = [c.psum() for _ in range(NB)]
                        for i in range(G):
                            pP, pPk = pPs[i // 4]
                            qp = pPk
                            pa, pkk = PTc[i]
                            c.emit("pe", lambda e, pP=pP, i=i, pa=pa, l=newXT[i][0]: e.matmul(
                                pP[:, (i % 4) * 128:(i % 4 + 1) * 128], lhsT=l, rhs=pa, start=True, stop=True),
                                reads=[newXT[i][1], pkk], writes=qp)
                        for i in range(G):
                            pP, pPk = pPs[i // 4]
                            qp = pPk
                            pa, pkk = PTc[i]
                            nb = (lev + 1) % 2
                            c.emit("dve", lambda e, pP=pP, i=i, pa=pa, nb=nb: e.tensor_tensor(
                                out=PTb[i][nb][:], in0=pP[:, (i % 4) * 128:(i % 4 + 1) * 128], in1=pa, op=ALU.add),
                                reads=qp + [pkk], writes=[("PTb", i, nb)])
                            PTc[i] = (PTb[i][nb][:], ("PTb", i, nb))
                        Xc, XTc = newX, newXT
                    if SCAN_DBG < 4:
                        continue
                    pWs = [c.psum() for _ in range(NB)]
                    pUs = [c.psum() for _ in range(NB)]
                    pYs = [c.psum() for _ in range(NB)]
                    pTs = [c.psum() for _ in range(NB)]
                    H = [(i, hl, cb * 8 + hl, slice(hl * 64, (hl + 1) * 64), slice((i % 4) * 128, (i % 4) * 128 + 64)) for i, hl in enumerate(heads)]
                    for i, hl, h, hc, q0 in H:
                        c.emit("pe", lambda e, pW=pWs[i // 4][0], pU=pUs[i // 4][0], pY=pYs[i // 4][0], pT=pTs[i // 4][0], i=i, h=h, q0=q0: e.matmul(pW[:, q0], lhsT=XT[i][:, 0:128], rhs=Tst[:, h, :],
                                                                      start=True, stop=False),
                               reads=[("XT", i), ("T", h)], writes=pWs[i // 4][1])
                        c.emit("pe", lambda e, pW=pWs[i // 4][0], pU=pUs[i // 4][0], pY=pYs[i // 4][0], pT=pTs[i // 4][0], i=i, hc=hc, q0=q0, L=L: e.matmul(pW[:, q0], lhsT=ABK[i][:, 256:384], rhs=L["V"][:, hc],
                                                                             start=False, stop=True),
                               reads=[("ABK", i)] + lk("V"), writes=pWs[i // 4][1])
                    for i, hl, h, hc, q0 in H:
                        c.emit("act", lambda e, pW=pWs[i // 4][0], pU=pUs[i // 4][0], pY=pYs[i // 4][0], pT=pTs[i // 4][0], i=i, q0=q0: e.copy(out=Wb[i][:], in_=pW[:, q0]), reads=pWs[i // 4][1], writes=[("Wb", i)])
                    for i, hl, h, hc, q0 in H:
                        pa, pkk = PTc[i]
                        c.emit("pe", lambda e, pW=pWs[i // 4][0], pU=pUs[i // 4][0], pY=pYs[i // 4][0], pT=pTs[i // 4][0], i=i, pa=pa, q0=q0: e.matmul(pU[:, q0], lhsT=pa, rhs=Wb[i][:], start=True, stop=True),
                               reads=[pkk, ("Wb", i)], writes=pUs[i // 4][1])
                    for i, hl, h, hc, q0 in H:
                        c.emit("act", lambda e, pW=pWs[i // 4][0], pU=pUs[i // 4][0], pY=pYs[i // 4][0], pT=pTs[i // 4][0], i=i, q0=q0: e.copy(out=Ub[i][:], in_=pU[:, q0]), reads=pUs[i // 4][1], writes=[("Ub", i)])
                    for i, hl, h, hc, q0 in H:
                        c.emit("pe", lambda e, pW=pWs[i // 4][0], pU=pUs[i // 4][0], pY=pYs[i // 4][0], pT=pTs[i // 4][0], i=i, h=h, q0=q0: e.matmul(pY[:, q0], lhsT=XT[i][:, 128:256], rhs=Tst[:, h, :],
                                                                      start=True, stop=False),
                               reads=[("XT", i), ("T", h)], writes=pYs[i // 4][1])
                        c.emit("pe", lambda e, pW=pWs[i // 4][0], pU=pUs[i // 4][0], pY=pYs[i // 4][0], pT=pTs[i // 4][0], i=i, q0=q0: e.matmul(pY[:, q0], lhsT=ABK[i][:, 128:256], rhs=Ub[i][:],
                                                                 start=False, stop=False),
                               reads=[("ABK", i), ("Ub", i)], writes=pYs[i // 4][1])
                        c.emit("pe", lambda e, pW=pWs[i // 4][0], pU=pUs[i // 4][0], pY=pYs[i // 4][0], pT=pTs[i // 4][0], i=i, hc=hc, q0=q0, L=L: e.matmul(pY[:, q0], lhsT=ABK[i][:, 384:512], rhs=L["V"][:, hc],
                                                                             start=False, stop=True),
                               reads=[("ABK", i)] + lk("V"), writes=pYs[i // 4][1])
                        c.emit("pe", lambda e, pW=pWs[i // 4][0], pU=pUs[i // 4][0], pY=pYs[i // 4][0], pT=pTs[i // 4][0], i=i, hc=hc, q0=q0, L=L: e.matmul(pT[0:64, q0], lhsT=L["Bt"][:, hc], rhs=Ub[i][:],
                                                                             start=True, stop=False),
                               reads=lk("Bt") + [("Ub", i)], writes=pTs[i // 4][1])
                        c.emit("pe", lambda e, pW=pWs[i // 4][0], pU=pUs[i // 4][0], pY=pYs[i // 4][0], pT=pTs[i // 4][0], i=i, hc=hc, q0=q0, L=L: e.matmul(pT[0:64, q0], lhsT=L["Kt"][:, hc], rhs=L["V"][:, hc],
                                                                             start=False, stop=True),
                               reads=lk("Kt", "V"), writes=pTs[i // 4][1])
                    for i, hl, h, hc, q0 in H:
                        c.emit("act", lambda e, pW=pWs[i // 4][0], pU=pUs[i // 4][0], pY=pYs[i // 4][0], pT=pTs[i // 4][0], hc=hc, q0=q0, s=s: e.copy(out=Yst[s][:, hc], in_=pY[:, q0]),
                               reads=pYs[i // 4][1], writes=[("Yst", s)])
                        c.emit("dve", lambda e, pW=pWs[i // 4][0], pU=pUs[i // 4][0], pY=pYs[i // 4][0], pT=pTs[i // 4][0], h=h, hl=hl, q0=q0, s=s: e.scalar_tensor_tensor(
                            out=Tst[:, h, :], in0=Tst[:, h, :], scalar=gl[s][:, hl:hl + 1], in1=pT[0:64, q0],
                            op0=ALU.mult, op1=ALU.add), reads=pTs[i // 4][1] + [("T", h), ("gl", s)], writes=[("T", h)])
                c.dma("sync", O["y"][rs, cs], Yst[s][:], reads=[("Yst", s)], writes=[("od", "y", ch, cb)])
        c.flush()


def rwkv_post_phase(c, x_d, P, O, wdt=F32):
    with ExitStack() as pes:
        yb = c.sb("yb", [128, D], es=pes)
        bb = c.sb("bb", [128, D], es=pes)
        gb = c.sb("gb", [128, D], es=pes)
        xt = c.sb("xt", [128, D], es=pes)
        sq = c.sb("sq", [128, D], es=pes)
        gng = c.sb("gng", [128, D], es=pes)
        gnb = c.sb("gnb", [128, D], es=pes)
        st = c.sb("st", [128, 160], es=pes)
        zT = c.sb("zT", [128, KC, 128], wdt, es=pes)
        wb = [c.sb("wb", [128, KC, 256], wdt, es=pes) for _ in range(2)]
        c.dma("sync", gng[:], P["gn_g"].partition_broadcast(128), writes=["par"])
        c.dma("sync", gnb[:], P["gn_b"].partition_broadcast(128), writes=["par"])
        y3 = yb[:].rearrange("p (h j) -> p h j", j=64)
        s3 = sq[:].rearrange("p (h j) -> p h j", j=64)
        bc = lambda a: a.unsqueeze(2).to_broadcast([128, NH, 64])
        for t in range(c.S // 128):
            rs = slice(t * 128, (t + 1) * 128)
            c.dma("sync", yb[:], O["y"][rs, :], writes=["yb"])
            c.dma("pool", bb[:], O["bonus"][rs, :], writes=["bb"])
            c.dma("sync", gb[:], O["g"][rs, :], writes=["gb"])
            c.dma("pool", xt[:], x_d[rs, :], writes=["xt"])
            c.emit("dve", lambda e: e.tensor_reduce(out=st[:, 0:32], in_=y3, axis=AX.X, op=ALU.add), reads=["yb"], writes=["st"])
            c.emit("dve", lambda e: e.tensor_scalar(out=st[:, 32:64], in0=st[:, 0:32], scalar1=1.0 / 64, scalar2=0.0,
                                                    op0=ALU.mult, op1=ALU.add), reads=["st"], writes=["st"])
            c.emit("dve", lambda e: e.tensor_tensor(out=y3, in0=y3, in1=bc(st[:, 32:64]), op=ALU.subtract),
                   reads=["yb", "st"], writes=["yb"])
            c.emit("pool", lambda e: e.tensor_tensor(out=sq[:], in0=yb[:], in1=yb[:], op=ALU.mult), reads=["yb"], writes=["sq"])
            c.emit("dve", lambda e: e.tensor_reduce(out=st[:, 64:96], in_=s3, axis=AX.X, op=ALU.add), reads=["sq"], writes=["st"])
            c.emit("dve", lambda e: e.tensor_scalar(out=st[:, 96:128], in0=st[:, 64:96], scalar1=1.0 / 64, scalar2=GN_EPS,
                                                    op0=ALU.mult, op1=ALU.add), reads=["st"], writes=["st"])
            c.emit("act", lambda e: e.activation(out=st[:, 96:128], in_=st[:, 96:128], func=AF.Sqrt), reads=["st"], writes=["st"])
            c.emit("dve", lambda e: e.reciprocal(out=st[:, 128:160], in_=st[:, 96:128]), reads=["st"], writes=["st"])
            c.emit("dve", lambda e: e.tensor_tensor(out=y3, in0=y3, in1=bc(st[:, 128:160]), op=ALU.mult),
                   reads=["yb", "st"], writes=["yb"])
            c.emit("pool", lambda e: e.tensor_tensor(out=yb[:], in0=yb[:], in1=gng[:], op=ALU.mult), reads=["yb", "par"], writes=["yb"])
            c.emit("pool", lambda e: e.tensor_tensor(out=yb[:], in0=yb[:], in1=gnb[:], op=ALU.add), reads=["yb", "par"], writes=["yb"])
            c.emit("pool", lambda e: e.tensor_tensor(out=yb[:], in0=yb[:], in1=bb[:], op=ALU.add), reads=["yb", "bb"], writes=["yb"])
            c.emit("pool", lambda e: e.tensor_tensor(out=sq[:], in0=yb[:], in1=gb[:], op=ALU.mult), reads=["yb", "gb"], writes=["sq"])
            transpose_tile(c, sq, "sq", zT, lambda g: ("zT", g), 0)

            def epi(cb, ps, pk):
                c.emit("dve", lambda e, cb=cb, ps=ps: e.tensor_tensor(out=xt[:, cb * 256:(cb + 1) * 256], in0=ps[:, 0:256],
                                                                       in1=xt[:, cb * 256:(cb + 1) * 256], op=ALU.add),
                       reads=pk + ["xt"], writes=["xt"])
            proj(c, zT, lambda kc: [("zT", kc // 4)], 1, P["wo"], D, wb, "wb", epi)
            c.dma("sync", x_d[rs, :], xt[:], reads=["xt"], writes=[("xd", t)])
        c.flush()


def rwkv_post_phase4(c, x_d, P, O, wdt=F32):
    with ExitStack() as pes:
        yb = c.sb("yb", [128, D], es=pes)
        bb = c.sb("bb", [128, D], es=pes)
        gb = c.sb("gb", [128, D], es=pes)
        xt = [c.sb("xt", [128, D], es=pes) for _ in range(4)]
        sq = c.sb("sq", [128, D], es=pes)
        gng = c.sb("gng", [128, D], es=pes)
        gnb = c.sb("gnb", [128, D], es=pes)
        st = c.sb("st", [128, 160], es=pes)
        zT = c.sb("zT", [128, KC, 512], wdt, es=pes)
        wb = [c.sb("wb", [128, KC, 256], wdt, es=pes) for _ in range(2)]
        c.dma("sync", gng[:], P["gn_g"].partition_broadcast(128), writes=["par"])
        c.dma("sync", gnb[:], P["gn_b"].partition_broadcast(128), writes=["par"])
        y3 = yb[:].rearrange("p (h j) -> p h j", j=64)
        s3 = sq[:].rearrange("p (h j) -> p h j", j=64)
        bc = lambda a: a.unsqueeze(2).to_broadcast([128, NH, 64])
        for t in range(c.S // 128):
            rs = slice(t * 128, (t + 1) * 128)
            tt = t % 4
            c.dma("sync", yb[:], O["y"][rs, :], writes=["yb"])
            c.dma("pool", bb[:], O["bonus"][rs, :], writes=["bb"])
            c.dma("sync", gb[:], O["g"][rs, :], writes=["gb"])
            c.dma("pool", xt[tt][:], x_d[rs, :], writes=[("xt", tt)])
            c.emit("dve", lambda e: e.tensor_reduce(out=st[:, 0:32], in_=y3, axis=AX.X, op=ALU.add), reads=["yb"], writes=["st"])
            c.emit("dve", lambda e: e.tensor_scalar(out=st[:, 32:64], in0=st[:, 0:32], scalar1=1.0 / 64, scalar2=0.0,
                                                    op0=ALU.mult, op1=ALU.add), reads=["st"], writes=["st"])
            c.emit("dve", lambda e: e.tensor_tensor(out=y3, in0=y3, in1=bc(st[:, 32:64]), op=ALU.subtract),
                   reads=["yb", "st"], writes=["yb"])
            c.emit("pool", lambda e: e.tensor_tensor(out=sq[:], in0=yb[:], in1=yb[:], op=ALU.mult), reads=["yb"], writes=["sq"])
            c.emit("dve", lambda e: e.tensor_reduce(out=st[:, 64:96], in_=s3, axis=AX.X, op=ALU.add), reads=["sq"], writes=["st"])
            c.emit("dve", lambda e: e.tensor_scalar(out=st[:, 96:128], in0=st[:, 64:96], scalar1=1.0 / 64, scalar2=GN_EPS,
                                                    op0=ALU.mult, op1=ALU.add), reads=["st"], writes=["st"])
            c.emit("act", lambda e: e.activation(out=st[:, 96:128], in_=st[:, 96:128], func=AF.Sqrt), reads=["st"], writes=["st"])
            c.emit("dve", lambda e: e.reciprocal(out=st[:, 128:160], in_=st[:, 96:128]), reads=["st"], writes=["st"])
            c.emit("dve", lambda e: e.tensor_tensor(out=y3, in0=y3, in1=bc(st[:, 128:160]), op=ALU.mult),
                   reads=["yb", "st"], writes=["yb"])
            c.emit("pool", lambda e: e.tensor_tensor(out=yb[:], in0=yb[:], in1=gng[:], op=ALU.mult), reads=["yb", "par"], writes=["yb"])
            c.emit("pool", lambda e: e.tensor_tensor(out=yb[:], in0=yb[:], in1=gnb[:], op=ALU.add), reads=["yb", "par"], writes=["yb"])
            c.emit("pool", lambda e: e.tensor_tensor(out=yb[:], in0=yb[:], in1=bb[:], op=ALU.add), reads=["yb", "bb"], writes=["yb"])
            c.emit("pool", lambda e: e.tensor_tensor(out=sq[:], in0=yb[:], in1=gb[:], op=ALU.mult), reads=["yb", "gb"], writes=["sq"])
            transpose_tile(c, sq, "sq", zT, lambda g: ("zT", g), tt * 128)
            if tt != 3:
                continue

            def epi(cb, half, ps, pk):
                for t2 in range(2):
                    t4 = half * 2 + t2
                    c.emit("dve", lambda e, cb=cb, ps=ps, t2=t2, t4=t4: e.tensor_tensor(
                        out=xt[t4][:, cb * 256:(cb + 1) * 256], in0=ps[:, t2 * 256:(t2 + 1) * 256],
                        in1=xt[t4][:, cb * 256:(cb + 1) * 256], op=ALU.add),
                        reads=pk + [("xt", t4)], writes=[("xt", t4)])
            proj4(c, zT, lambda kc: [("zT", kc // 4)], P["wo"], D, wb, "wb", epi)
            for t4 in range(4):
                r0 = (t - 3 + t4) * 128
                c.dma("sync", x_d[r0:r0 + 128, :], xt[t4][:], reads=[("xt", t4)], writes=[("xd", r0 // 128)])
        c.flush()


import math


def rope_tables_phase(c, pos_ap, cos_d, sin_d):
    NT = c.S // 128
    with ExitStack() as pes:
        pi_ = c.sb("pos_i", [NT, 128], I32, es=pes)
        pf = c.sb("pos_f", [NT, 128], es=pes)
        posT = c.sb("posT", [128, NT], es=pes)
        io_i = c.sb("io_i", [128, 32], I32, es=pes)
        invf = c.sb("invf", [128, 32], es=pes)
        ang = c.sb("ang", [128, 32], es=pes)
        ob = [c.sb("ropeo", [128, 64], es=pes) for _ in range(2)]
        nb = c.sb("negpi", [128, 1], es=pes)
        ni = c.sb("ni", [128, 64], I32, es=pes)
        nf = c.sb("nf", [128, 64], es=pes)
        c.emit("pool", lambda e: e.memset(nb[:], -math.pi), writes=["nb"])
        c.dma("sync", pi_[:], pos_ap.rearrange("(t p) -> t p", p=128), writes=["pi"])
        c.emit("dve", lambda e: e.tensor_copy(out=pf[:], in_=pi_[:]), reads=["pi"], writes=["pf"])
        ps, pk = c.psum()
        c.emit("pe", lambda e: e.transpose(ps[:, 0:NT], pf[:], c.ident[0:NT, 0:NT]), reads=["pf", "const"], writes=pk)
        c.emit("dve", lambda e: e.tensor_copy(out=posT[:], in_=ps[:, 0:NT]), reads=pk, writes=["posT"])
        c.emit("pool", lambda e: e.iota(io_i[:], pattern=[[1, 32]], base=0, channel_multiplier=0), writes=["io"])
        c.emit("dve", lambda e: e.tensor_copy(out=invf[:], in_=io_i[:]), reads=["io"], writes=["invf"])
        c.emit("act", lambda e: e.activation(out=invf[:], in_=invf[:], func=AF.Exp, scale=-math.log(10000.0) / 32.0),
               reads=["invf"], writes=["invf"])
        for t in range(NT):
            s = t % 2
            c.emit("dve", lambda e, t=t: e.tensor_scalar(out=ang[:], in0=invf[:], scalar1=posT[:, t:t + 1], scalar2=0.0,
                                                         op0=ALU.mult, op1=ALU.add), reads=["invf", "posT"], writes=["ang"])
            c.emit("dve", lambda e, s=s: e.tensor_scalar(out=ob[s][:, 32:64], in0=ang[:], scalar1=1.0 / (2 * math.pi), scalar2=0.5,
                                                         op0=ALU.mult, op1=ALU.add), reads=["ang"], writes=[("ob", s)])
            c.emit("dve", lambda e, s=s: e.tensor_scalar(out=ob[s][:, 0:32], in0=ang[:], scalar1=1.0 / (2 * math.pi), scalar2=0.75,
                                                         op0=ALU.mult, op1=ALU.add), reads=["ang"], writes=[("ob", s)])
            c.emit("dve", lambda e, s=s: e.tensor_copy(out=ni[:], in_=ob[s][:]), reads=[("ob", s)], writes=["ni"])
            c.emit("dve", lambda e: e.tensor_copy(out=nf[:], in_=ni[:]), reads=["ni"], writes=["nf"])
            c.emit("dve", lambda e, s=s: e.tensor_tensor(out=ob[s][:], in0=ob[s][:], in1=nf[:], op=ALU.subtract),
                   reads=[("ob", s), "nf"], writes=[("ob", s)])
            c.emit("dve", lambda e, s=s: e.tensor_scalar(out=nf[:], in0=ob[s][:], scalar1=0.0, scalar2=0.0,
                                                         op0=ALU.is_lt, op1=ALU.add), reads=[("ob", s)], writes=["nf"])
            c.emit("dve", lambda e, s=s: e.tensor_tensor(out=ob[s][:], in0=ob[s][:], in1=nf[:], op=ALU.add),
                   reads=[("ob", s), "nf"], writes=[("ob", s)])
            c.emit("act", lambda e, s=s: e.activation(out=ob[s][:], in_=ob[s][:], func=AF.Sin, bias=nb[:, 0:1], scale=2 * math.pi),
                   reads=[("ob", s), "nb"], writes=[("ob", s)])
            c.dma("sync", cos_d[t * 128:(t + 1) * 128, :], ob[s][:, 0:32], reads=[("ob", s)], writes=[("cd", t)])
            c.dma("sync", sin_d[t * 128:(t + 1) * 128, :], ob[s][:, 32:64], reads=[("ob", s)], writes=[("sd", t)])
        c.flush()


def rope_epi_ops(c, ps, pk, nh, cs_tile, cskey, dst, dkey, tmp, tkey):
    p3 = ps[:, 0:nh * 64].rearrange("p (h d) -> p h d", d=64)
    cosb = cs_tile[:, 0:32].unsqueeze(1).to_broadcast([128, nh, 32])
    sinb = cs_tile[:, 32:64].unsqueeze(1).to_broadcast([128, nh, 32])
    d3 = dst.rearrange("p (h d) -> p h d", d=64)
    t3 = tmp[:, 0:nh * 64].rearrange("p (h d) -> p h d", d=64)
    R = pk + [cskey]
    c.emit("dve", lambda e: e.tensor_tensor(out=d3[:, :, 0:32], in0=p3[:, :, 0:32], in1=cosb, op=ALU.mult), reads=R, writes=[dkey])
    c.emit("dve", lambda e: e.tensor_tensor(out=t3[:, :, 0:32], in0=p3[:, :, 32:64], in1=sinb, op=ALU.mult), reads=R, writes=[tkey])
    c.emit("dve", lambda e: e.tensor_tensor(out=d3[:, :, 32:64], in0=p3[:, :, 32:64], in1=cosb, op=ALU.mult), reads=R, writes=[dkey])
    c.emit("dve", lambda e: e.tensor_tensor(out=t3[:, :, 32:64], in0=p3[:, :, 0:32], in1=sinb, op=ALU.mult), reads=R, writes=[tkey])
    c.emit("dve", lambda e: e.tensor_tensor(out=d3[:, :, 0:32], in0=d3[:, :, 0:32], in1=t3[:, :, 0:32], op=ALU.subtract),
           reads=[dkey, tkey], writes=[dkey])
    c.emit("dve", lambda e: e.tensor_tensor(out=d3[:, :, 32:64], in0=d3[:, :, 32:64], in1=t3[:, :, 32:64], op=ALU.add),
           reads=[dkey, tkey], writes=[dkey])


def norm_tile(c, xt, gam, junk, hbuf, stat, jkeys, hkeys):
    c.emit("act", lambda e: e.activation(out=junk, in_=xt[:], func=AF.Square, accum_out=stat[:, 0:1]),
           reads=["xt"], writes=jkeys + ["stat"])
    c.emit("dve", lambda e: e.tensor_scalar(out=stat[:, 1:2], in0=stat[:, 0:1], scalar1=1.0 / D, scalar2=RMS_EPS,
                                            op0=ALU.mult, op1=ALU.add), reads=["stat"], writes=["stat"])
    c.emit("act", lambda e: e.activation(out=stat[:, 3:4], in_=stat[:, 1:2], func=AF.Sqrt), reads=["stat"], writes=["stat"])
    c.emit("dve", lambda e: e.reciprocal(out=stat[:, 2:3], in_=stat[:, 3:4]), reads=["stat"], writes=["stat"])
    c.emit("dve", lambda e: e.scalar_tensor_tensor(out=hbuf, in0=xt[:], scalar=stat[:, 2:3], in1=gam[:],
                                                   op0=ALU.mult, op1=ALU.mult), reads=["xt", "stat", "gam"], writes=hkeys)


def kv_phase(c, x_d, g_row, w_kv, cos_d, sin_d, kT_d, v_d, wdt=F32):
    with ExitStack() as pes:
        xt = c.sb("xt", [128, D], es=pes)
        gam = c.sb("gam", [128, D], es=pes)
        junk = c.sb("junk", [128, D], es=pes)
        hbuf = c.sb("hbuf", [128, D], es=pes)
        stat = c.sb("stat", [128, 8], es=pes)
        hT = c.sb("hT", [128, KC, 128], wdt, es=pes)
        wb = [c.sb("wb", [128, KC, 256], wdt, es=pes) for _ in range(2)]
        cs = [c.sb("cs", [128, 64], es=pes) for _ in range(2)]
        krot = c.sb("krot", [128, 512], es=pes)
        tmp = c.sb("tmp", [128, 256], es=pes)
        vb = [c.sb("vb", [128, 256], es=pes) for _ in range(2)]
        kTt = c.sb("kTt", [64, 8, 128], es=pes)
        c.dma("sync", gam[:], g_row.partition_broadcast(128), writes=["gam"])
        for t in range(c.S // 128):
            rs = slice(t * 128, (t + 1) * 128)
            s = t % 2
            c.dma("sync", xt[:], x_d[rs, :], writes=["xt"])
            c.dma("pool", cs[s][:, 0:32], cos_d[rs, :], writes=[("cs", s)])
            c.dma("pool", cs[s][:, 32:64], sin_d[rs, :], writes=[("cs", s)])
            norm_tile(c, xt, gam, junk[:], hbuf[:], stat, ["junk"], ["hbuf"])
            transpose_tile(c, hbuf, "hbuf", hT, lambda g: ("hT", g), 0)

            def epi(cb, ps, pk, s=s, rs=rs):
                if cb < 2:
                    rope_epi_ops(c, ps, pk, 4, cs[s], ("cs", s), krot[:, cb * 256:(cb + 1) * 256], ("krot", cb), tmp, "tmp")
                else:
                    vs = c.wslot("vb")
                    c.emit("act", lambda e, vs=vs, ps=ps: e.copy(out=vb[vs][:], in_=ps[:, 0:256]), reads=pk, writes=[("vb", vs)])
                    c.dma("sync", v_d[rs, (cb - 2) * 256:(cb - 1) * 256], vb[vs][:], reads=[("vb", vs)], writes=[("vd", rs.start, cb)])
            proj(c, hT, lambda kc: [("hT", kc // 4)], 1, w_kv, 1024, wb, "wb", epi)
            for hg in range(2):
                ps, pk = c.psum()
                for j in range(4):
                    h = hg * 4 + j
                    c.emit("pe", lambda e, ps=ps, j=j, h=h: e.transpose(ps[0:64, j * 128:(j + 1) * 128], krot[:, h * 64:(h + 1) * 64], c.ident[:]),
                           reads=[("krot", h // 4), "const"], writes=pk)
                c.emit("act", lambda e, ps=ps, hg=hg: e.copy(out=kTt[:, hg * 4:(hg + 1) * 4, :],
                                                             in_=ps[0:64, :].rearrange("p (a b) -> p a b", a=4)),
                       reads=pk, writes=["kTt"])
            c.dma("sync", kT_d[:, :, rs].rearrange("g d s -> d g s"), kTt[:], reads=["kTt"], writes=[("kTd", t)])
        c.flush()


def attn_phase(c, x_d, g_row, w_q, w_o, sinks_row, cos_d, sin_d, kT_d, v_d, dbg=None, wdt=F32):
    with ExitStack() as pes:
        xt = c.sb("xt", [128, D], es=pes)
        gam = c.sb("gam", [128, D], es=pes)
        junk = c.sb("junk", [128, D], es=pes)
        hbuf = c.sb("hbuf", [128, D], es=pes)
        stat = c.sb("stat", [128, 8], es=pes)
        hT = c.sb("hT", [128, KC, 128], wdt, es=pes)
        wb = [c.sb("wb", [128, KC, 256], wdt, es=pes) for _ in range(2)]
        cs = [c.sb("cs", [128, 64], es=pes) for _ in range(2)]
        tmp = c.sb("tmp", [128, 256], es=pes)
        qT = c.sb("qT", [64, NH, 128], es=pes)
        kTs = [c.sb("kTs", [64, 8, 256], es=pes) for _ in range(2)]
        v1 = [c.sb("v1", [128, 2, 8, 65], es=pes) for _ in range(2)]
        E = [[c.sb("E", [128, 512], es=pes) for _ in range(2)] for _ in range(2)]
        mC = c.sb("mC", [128, 512], es=pes)
        mP = c.sb("mP", [128, 512], es=pes)
        sk = c.sb("sk", [128, NH], es=pes)
        dn = c.sb("dn", [128, 8], es=pes)
        attn = junk
        c.dma("sync", gam[:], g_row.partition_broadcast(128), writes=["gam"])
        c.dma("sync", sk[:], sinks_row.partition_broadcast(128), writes=["sk"])
        c.emit("act", lambda e: e.activation(out=sk[:], in_=sk[:], func=AF.Exp), reads=["sk"], writes=["sk"])
        for q in range(4):
            c.emit("pool", lambda e, q=q: e.tensor_copy(out=mC[:, q * 128:(q + 1) * 128], in_=c.mask4[:, 128:256]),
                   reads=["const"], writes=["mC"])
            c.emit("pool", lambda e, q=q: e.tensor_copy(out=mP[:, q * 128:(q + 1) * 128], in_=c.msl[:]),
                   reads=["const"], writes=["mP"])
        for s in range(2):
            c.emit("pool", lambda e, s=s: e.memset(v1[s][:], 1.0), writes=[("v1", s)])
        for t in range(c.S // 128):
            rs = slice(t * 128, (t + 1) * 128)
            s = t % 2
            nkb = 1 if t == 0 else 2
            k0 = (t - 1) * 128 if t > 0 else 0
            c.dma("sync", xt[:], x_d[rs, :], writes=["xt"])
            c.dma("pool", cs[s][:, 0:32], cos_d[rs, :], writes=[("cs", s)])
            c.dma("pool", cs[s][:, 32:64], sin_d[rs, :], writes=[("cs", s)])
            kb0 = 2 - nkb
            c.dma("pool", kTs[s][:, :, kb0 * 128:256], kT_d[:, :, k0:(t + 1) * 128].rearrange("g d s -> d g s"),
                  writes=[("kTs", s)])
            for kb in range(kb0, 2):
                r1 = (t - 1 + kb) * 128
                c.dma("sync", v1[s][:, kb, :, 0:64], v_d[r1:r1 + 128, :].rearrange("p (g d) -> p g d", d=64), writes=[("v1", s)])
            norm_tile(c, xt, gam, junk[:], hbuf[:], stat, ["junk"], ["hbuf"])
            transpose_tile(c, hbuf, "hbuf", hT, lambda g: ("hT", g), 0)

            def epi(cb, ps, pk, s=s):
                rope_epi_ops(c, ps, pk, 4, cs[s], ("cs", s), hbuf[:, cb * 256:(cb + 1) * 256], "hbuf", tmp, "tmp")
            proj(c, hT, lambda kc: [("hT", kc // 4)], 1, w_q, D, wb, "wb", epi)
            for hg in range(8):
                ps, pk = c.psum()
                for j in range(4):
                    h = hg * 4 + j
                    c.emit("pe", lambda e, ps=ps, j=j, h=h: e.transpose(ps[0:64, j * 128:(j + 1) * 128], hbuf[:, h * 64:(h + 1) * 64], c.ident[:]),
                           reads=["hbuf", "const"], writes=pk)
                c.emit("act", lambda e, ps=ps, hg=hg: e.copy(out=qT[:, hg * 4:(hg + 1) * 4, :],
                                                             in_=ps[0:64, :].rearrange("p (a b) -> p a b", a=4)),
                       reads=pk, writes=[("qT", hg)])
            for g in range(8):
                es_ = g % 2
                for kb in range(kb0, 2):
                    ps, pk = c.psum()
                    for j in range(4):
                        c.emit("pe", lambda e, ps=ps, j=j, g=g, kb=kb, s=s: e.matmul(
                            ps[:, j * 128:(j + 1) * 128], lhsT=kTs[s][:, g, kb * 128:(kb + 1) * 128], rhs=qT[:, 4 * g + j, :],
                            start=True, stop=True), reads=[("kTs", s), ("qT", g)], writes=pk)
                    c.emit("act", lambda e, ps=ps, kb=kb, es_=es_: e.activation(out=E[es_][kb][:], in_=ps[:, :], func=AF.Exp, scale=0.125),
                           reads=pk, writes=[("E", es_, kb)])
                    m = mC if kb == 1 else mP
                    c.emit("pool", lambda e, kb=kb, es_=es_, m=m: e.tensor_tensor(out=E[es_][kb][:], in0=E[es_][kb][:], in1=m[:], op=ALU.mult),
                           reads=[("E", es_, kb), "mC", "mP"], writes=[("E", es_, kb)])
                po, pok = c.psum()
                for j in range(4):
                    for kb in range(kb0, 2):
                        c.emit("pe", lambda e, po=po, j=j, kb=kb, g=g, s=s, es_=es_, kb0=kb0: e.matmul(
                            po[:, j * 65:(j + 1) * 65], lhsT=E[es_][kb][:, j * 128:(j + 1) * 128], rhs=v1[s][:, kb, g, :],
                            start=(kb == kb0), stop=(kb == 1)), reads=[("E", es_, kb), ("v1", s)], writes=pok)
                po3 = po[:, 0:260].rearrange("p (h d) -> p h d", d=65)
                c.emit("dve", lambda e, po3=po3, g=g: e.tensor_tensor(out=dn[:, 0:4], in0=po3[:, :, 64], in1=sk[:, 4 * g:4 * g + 4], op=ALU.add),
                       reads=pok + ["sk"], writes=["dn"])
                c.emit("dve", lambda e: e.reciprocal(out=dn[:, 4:8], in_=dn[:, 0:4]), reads=["dn"], writes=["dn"])
                c.emit("dve", lambda e, po3=po3, g=g: e.tensor_tensor(
                    out=attn[:, g * 256:(g + 1) * 256].rearrange("p (h d) -> p h d", d=64), in0=po3[:, :, 0:64],
                    in1=dn[:, 4:8].unsqueeze(2).to_broadcast([128, 4, 64]), op=ALU.mult), reads=pok + ["dn"], writes=["junk"])
            if dbg is not None:
                c.dma("sync", dbg[rs, :], attn[:], reads=["junk"], writes=[("dbg", t)])
            transpose_tile(c, attn, "junk", hT, lambda g: ("hT", g), 0)

            def epi2(cb, ps, pk):
                c.emit("dve", lambda e, cb=cb, ps=ps: e.tensor_tensor(out=xt[:, cb * 256:(cb + 1) * 256], in0=ps[:, 0:256],
                                                                       in1=xt[:, cb * 256:(cb + 1) * 256], op=ALU.add),
                       reads=pk + ["xt"], writes=["xt"])
            proj(c, hT, lambda kc: [("hT", kc // 4)], 1, w_o, D, wb, "wb", epi2)
            c.dma("sync", x_d[rs, :], xt[:], reads=["xt"], writes=[("xd", t)])
        c.flush()


def test_attn_program(S):
    nc = bass.Bass("TRN2", target_bir_lowering=False)
    x_in = nc.dram_tensor("x", [S, D], F32, kind="ExternalInput").ap()
    pos = nc.dram_tensor("pos", [S], I32, kind="ExternalInput").ap()
    gk = nc.dram_tensor("gk", [1, D], F32, kind="ExternalInput").ap()
    g1 = nc.dram_tensor("g1", [1, D], F32, kind="ExternalInput").ap()
    wkv = nc.dram_tensor("wkv", [D, 1024], F32, kind="ExternalInput").ap()
    wq = nc.dram_tensor("wq", [D, D], F32, kind="ExternalInput").ap()
    wo = nc.dram_tensor("wo", [D, D], F32, kind="ExternalInput").ap()
    sinks = nc.dram_tensor("sinks", [1, NH], F32, kind="ExternalInput").ap()
    xo = nc.dram_tensor("xo", [S, D], F32, kind="ExternalOutput").ap()
    cos_d = nc.dram_tensor("cos_d", [S, 32], F32, kind="ExternalOutput").ap()
    sin_d = nc.dram_tensor("sin_d", [S, 32], F32, kind="ExternalOutput").ap()
    kT_d = nc.dram_tensor("kT_d", [8, 64, S], F32, kind="ExternalOutput").ap()
    v_d = nc.dram_tensor("v_d", [S, 512], F32, kind="ExternalOutput").ap()
    with ExitStack() as es:
        c = Ctx(nc, es, S)
        make_consts(c)
        make_masks(c)
        copy_rows(c, xo, x_in, S)
        rope_tables_phase(c, pos, cos_d, sin_d)
        kv_phase(c, xo, gk, wkv, cos_d, sin_d, kT_d, v_d)
        dbg = nc.dram_tensor("dbg", [S, D], F32, kind="ExternalOutput").ap()
        attn_phase(c, xo, g1, wq, wo, sinks, cos_d, sin_d, kT_d, v_d, dbg=dbg)
    return nc


def ple_phase(c, x_d, g_row, w_gate, w_up, p_d, wdt=F32):
    with ExitStack() as pes:
        xt = c.sb("xt", [128, D], es=pes)
        gam = c.sb("gam", [128, D], es=pes)
        junk = c.sb("junk", [128, D], es=pes)
        hbuf = c.sb("hbuf", [128, D], es=pes)
        stat = c.sb("stat", [128, 8], es=pes)
        hT = c.sb("hT", [128, KC, 128], wdt, es=pes)
        wb = [c.sb("wb", [128, KC, 256], wdt, es=pes) for _ in range(2)]
        wup = c.sb("wup", [128, 2, D], es=pes)
        pt = c.sb("pt", [128, 256], es=pes)
        pT = c.sb("pT", [128, 2, 128], es=pes)
        sg = [c.sb("sg", [128, 256], es=pes) for _ in range(2)]
        c.dma("sync", gam[:], g_row.partition_broadcast(128), writes=["gam"])
        c.dma("sync", wup[:], w_up.rearrange("(ch p) n -> p ch n", p=128), writes=["wup"])
        for t in range(c.S // 128):
            rs = slice(t * 128, (t + 1) * 128)
            c.dma("sync", xt[:], x_d[rs, :], writes=["xt"])
            c.dma("pool", pt[:], p_d[rs, :], writes=["pt"])
            norm_tile(c, xt, gam, junk[:], hbuf[:], stat, ["junk"], ["hbuf"])
            transpose_tile(c, hbuf, "hbuf", hT, lambda g: ("hT", g), 0)
            transpose_tile(c, pt, "pt", pT, lambda g: "pT", 0, nchunks=2)

            def epi(cb, ps, pk):
                s = c.wslot("sg")
                c.emit("act", lambda e, s=s, ps=ps: e.activation(out=sg[s][:], in_=ps[:, 0:256], func=AF.Sigmoid),
                       reads=pk, writes=[("sg", s)])
                pu, puk = c.psum()
                for ch in range(2):
                    c.emit("pe", lambda e, pu=pu, ch=ch, cb=cb: e.matmul(pu[:, 0:256], lhsT=pT[:, ch, :],
                                                                         rhs=wup[:, ch, cb * 256:(cb + 1) * 256],
                                                                         start=(ch == 0), stop=(ch == 1)),
                           reads=["pT", "wup"], writes=puk)
                c.emit("dve", lambda e, s=s, pu=pu: e.tensor_tensor(out=sg[s][:], in0=sg[s][:], in1=pu[:, 0:256], op=ALU.mult),
                       reads=puk + [("sg", s)], writes=[("sg", s)])
                c.emit("dve", lambda e, s=s, cb=cb: e.tensor_tensor(out=xt[:, cb * 256:(cb + 1) * 256], in0=sg[s][:],
                                                                    in1=xt[:, cb * 256:(cb + 1) * 256], op=ALU.add),
                       reads=[("sg", s), "xt"], writes=["xt"])
            proj(c, hT, lambda kc: [("hT", kc // 4)], 1, w_gate, D, wb, "wb", epi)
            c.dma("sync", x_d[rs, :], xt[:], reads=["xt"], writes=[("xd", t)])
        c.flush()


def proj4(c, xT, xkeyf, W, N, wbufs, wname, epi, kcn=KC, qi=[0]):
    CW = 256
    Wv = W.rearrange("(kc p) n -> p kc n", p=128)
    for cb in range(N // CW):
        s = c.wslot(wname)
        qi[0] += 1
        c.dma("sync" if qi[0] % 2 else "pool", wbufs[s][:, 0:kcn, 0:CW], Wv[:, :, cb * CW:(cb + 1) * CW],
              writes=[(wname, s)])
        for half in range(2):
            ps, pk = c.psum()
            for t2 in range(2):
                tt = half * 2 + t2
                for kc in range(kcn):
                    c.emit("pe", lambda e, ps=ps, t2=t2, tt=tt, kc=kc, s=s: e.matmul(
                        ps[:, t2 * CW:(t2 + 1) * CW], lhsT=xT[:, kc, tt * 128:(tt + 1) * 128], rhs=wbufs[s][:, kc, 0:CW],
                        start=(kc == 0), stop=(kc == kcn - 1)),
                        reads=[(wname, s)] + xkeyf(kc), writes=pk)
            epi(cb, half, ps, pk)


def ple_phase4(c, x_d, g_row, w_gate, w_up, p_d, wdt=F32):
    TT = 4
    with ExitStack() as pes:
        xt = [c.sb("xt", [128, D], es=pes) for _ in range(TT)]
        gam = c.sb("gam", [128, D], es=pes)
        junk = c.sb("junk", [128, D], es=pes)
        hbuf = c.sb("hbuf", [128, D], es=pes)
        stat = c.sb("stat", [128, 8], es=pes)
        hT = c.sb("hT", [128, KC, TT * 128], wdt, es=pes)
        wb = [c.sb("wb", [128, KC, 256], wdt, es=pes) for _ in range(2)]
        wup = c.sb("wup", [128, 2, D], es=pes)
        pt = [c.sb("pt", [128, 256], es=pes) for _ in range(2)]
        pT = c.sb("pT", [128, 2, TT * 128], es=pes)
        sg = [c.sb("sg", [128, 512], es=pes) for _ in range(2)]
        c.dma("sync", gam[:], g_row.partition_broadcast(128), writes=["gam"])
        c.dma("sync", wup[:], w_up.rearrange("(ch p) n -> p ch n", p=128), writes=["wup"])
        for tb in range(c.S // (TT * 128)):
            for tt in range(TT):
                r0 = (tb * TT + tt) * 128
                c.dma("sync", xt[tt][:], x_d[r0:r0 + 128, :], writes=[("xt", tt)])
                ps_ = tt % 2
                c.dma("pool", pt[ps_][:], p_d[r0:r0 + 128, :], writes=[("pt", ps_)])
                c.emit("act", lambda e, tt=tt: e.activation(out=junk[:], in_=xt[tt][:], func=AF.Square, accum_out=stat[:, 0:1]),
                       reads=[("xt", tt)], writes=["junk", "stat"])
                c.emit("dve", lambda e: e.tensor_scalar(out=stat[:, 1:2], in0=stat[:, 0:1], scalar1=1.0 / D, scalar2=RMS_EPS,
                                                        op0=ALU.mult, op1=ALU.add), reads=["stat"], writes=["stat"])
                c.emit("act", lambda e: e.activation(out=stat[:, 3:4], in_=stat[:, 1:2], func=AF.Sqrt), reads=["stat"], writes=["stat"])
                c.emit("dve", lambda e: e.reciprocal(out=stat[:, 2:3], in_=stat[:, 3:4]), reads=["stat"], writes=["stat"])
                c.emit("dve", lambda e, tt=tt: e.scalar_tensor_tensor(out=hbuf[:], in0=xt[tt][:], scalar=stat[:, 2:3], in1=gam[:],
                                                                      op0=ALU.mult, op1=ALU.mult),
                       reads=[("xt", tt), "stat", "gam"], writes=["hbuf"])
                transpose_tile(c, hbuf, "hbuf", hT, lambda g: ("hT", g), tt * 128)
                transpose_tile(c, pt[ps_], ("pt", ps_), pT, lambda g: "pT", tt * 128, nchunks=2)

            def epi(cb, half, ps, pk):
                s = c.wslot("sg")
                c.emit("act", lambda e, s=s, ps=ps: e.activation(out=sg[s][:], in_=ps[:, :], func=AF.Sigmoid),
                       reads=pk, writes=[("sg", s)])
                pu, puk = c.psum()
                for t2 in range(2):
                    tt = half * 2 + t2
                    for ch in range(2):
                        c.emit("pe", lambda e, pu=pu, ch=ch, cb=cb, t2=t2, tt=tt: e.matmul(
                            pu[:, t2 * 256:(t2 + 1) * 256], lhsT=pT[:, ch, tt * 128:(tt + 1) * 128],
                            rhs=wup[:, ch, cb * 256:(cb + 1) * 256], start=(ch == 0), stop=(ch == 1)),
                            reads=["pT", "wup"], writes=puk)
                c.emit("dve", lambda e, s=s, pu=pu: e.tensor_tensor(out=sg[s][:], in0=sg[s][:], in1=pu[:, :], op=ALU.mult),
                       reads=puk + [("sg", s)], writes=[("sg", s)])
                for t2 in range(2):
                    tt = half * 2 + t2
                    c.emit("pool", lambda e, s=s, cb=cb, t2=t2, tt=tt: e.tensor_tensor(
                        out=xt[tt][:, cb * 256:(cb + 1) * 256], in0=sg[s][:, t2 * 256:(t2 + 1) * 256],
                        in1=xt[tt][:, cb * 256:(cb + 1) * 256], op=ALU.add),
                        reads=[("sg", s), ("xt", tt)], writes=[("xt", tt)])
            proj4(c, hT, lambda kc: [("hT", kc // 4)], w_gate, D, wb, "wb", epi)
            for tt in range(TT):
                r0 = (tb * TT + tt) * 128
                c.dma("sync", x_d[r0:r0 + 128, :], xt[tt][:], reads=[("xt", tt)], writes=[("xd", r0 // 128)])
        c.flush()


def final_norm_phase(c, x_d, g_row, y_d):
    with ExitStack() as pes:
        xt = c.sb("xt", [128, D], es=pes)
        gam = c.sb("gam", [128, D], es=pes)
        junk = c.sb("junk", [128, D], es=pes)
        hb = [c.sb("hbuf", [128, D], es=pes) for _ in range(2)]
        stat = c.sb("stat", [128, 8], es=pes)
        c.dma("sync", gam[:], g_row.partition_broadcast(128), writes=["gam"])
        for t in range(c.S // 128):
            rs = slice(t * 128, (t + 1) * 128)
            s = t % 2
            c.dma("sync", xt[:], x_d[rs, :], writes=["xt"])
            norm_tile(c, xt, gam, junk[:], hb[s][:], stat, ["junk"], [("hb", s)])
            c.dma("pool", y_d[rs, :], hb[s][:], reads=[("hb", s)], writes=[("yd", t)])
        c.flush()


def convert_ffn_weights_phase(c, wg, wu, wd, wg_b, wu_b, wd_b, ff):
    HC = ff // 128
    CH = 2816 if ff % 2816 == 0 else ff
    with ExitStack() as pes:
        fb = [c.sb("cvf", [128, 2816], es=pes) for _ in range(3)]
        bb = [c.sb("cvb", [128, 2816], BF16, es=pes) for _ in range(3)]
        it = 0
        engs = ["act", "dve", "pool"]

        def cast(s, n, it):
            e_ = engs[it % 3]
            if e_ == "act":
                c.emit("act", lambda e: e.copy(out=bb[s][:, 0:n], in_=fb[s][:, 0:n]), reads=[("cvf", s)], writes=[("cvb", s)])
            else:
                c.emit(e_, lambda e: e.tensor_copy(out=bb[s][:, 0:n], in_=fb[s][:, 0:n]), reads=[("cvf", s)], writes=[("cvb", s)])
        for (w, wb_) in ((wg, wg_b), (wu, wu_b)):
            for kc in range(KC):
                for c0 in range(0, ff, CH):
                    s = it % 3
                    c.dma("sync", fb[s][:, 0:CH], w[kc * 128:(kc + 1) * 128, c0:c0 + CH], writes=[("cvf", s)])
                    cast(s, CH, it)
                    h0, nh = c0 // 128, CH // 128
                    c.dma("pool", wb_[h0:h0 + nh, :, kc, :].rearrange("h p c -> p h c"),
                          bb[s][:, 0:CH].rearrange("p (h c) -> p h c", c=128), reads=[("cvb", s)], writes=[("cvo", it)])
                    it += 1
        for hc in range(HC):
            s = it % 3
            c.dma("sync", fb[s][:, 0:D], wd[hc * 128:(hc + 1) * 128, :], writes=[("cvf", s)])
            cast(s, D, it)
            c.dma("pool", wd_b[hc // 4, :, :, hc % 4, :].rearrange("db p c -> p db c"),
                  bb[s][:, 0:D].rearrange("p (db c) -> p db c", c=512), reads=[("cvb", s)], writes=[("cvo", it)])
            it += 1
        c.flush()


def run_ffn_bf16(c, x_d, gam_row, wg_b, wu_b, wd_b, ff):
    TB, NTT = 512, 4
    HC = ff // 128
    HG = HC // 4
    with ExitStack() as pes:
        xt = [c.sb("xt", [128, D], es=pes) for _ in range(NTT)]
        gam = c.sb("gam", [128, D], es=pes)
        junk = c.sb("junk", [128, D], es=pes)
        hbuf = c.sb("hbuf", [128, D], es=pes)
        stat = c.sb("stat", [128, 8], es=pes)
        hT = c.sb("hT", [128, KC, TB], BF16, es=pes)
        act = c.sb("act", [128, HC, TB], BF16, es=pes)
        wgs = [c.sb("wgs", [128, KC, 128], BF16, es=pes) for _ in range(3)]
        wus = [c.sb("wus", [128, KC, 128], BF16, es=pes) for _ in range(3)]
        wds = [c.sb("wds", [128, 4, 512], BF16, es=pes) for _ in range(3)]
        sg = [c.sb("sg", [128, TB], es=pes) for _ in range(2)]
        c.dma("sync", gam[:], gam_row.partition_broadcast(128), writes=["gam"])
        B = dict(junk=junk, stat=stat, hbuf=hbuf)
        wi = 0
        di = 0
        for tb in range(c.S // TB):
            for tt in range(NTT):
                r0 = tb * TB + tt * 128
                c.dma("sync", xt[tt][:], x_d[r0:r0 + 128, :], writes=[("xt", tt)])
                norm_to_hT(c, xt[tt][:], ("xt", tt), gam[:], "gam", hT, lambda g: ("hT", g), tt * 128, B)
            for hc in range(HC):
                s = wi % 3
                wi += 1
                c.dma("sync", wgs[s][:], wg_b[hc], writes=[("wg", s)])
                c.dma("pool", wus[s][:], wu_b[hc], writes=[("wu", s)])
                psg, pgk = c.psum()
                psu, puk = c.psum()
                for kc in range(KC):
                    c.emit("pe", lambda e, psg=psg, s=s, kc=kc: e.matmul(psg[:, :], lhsT=wgs[s][:, kc, :], rhs=hT[:, kc, :],
                                                                          start=(kc == 0), stop=(kc == KC - 1)),
                           reads=[("wg", s), ("hT", kc // 4)], writes=pgk)
                for kc in range(KC):
                    c.emit("pe", lambda e, psu=psu, s=s, kc=kc: e.matmul(psu[:, :], lhsT=wus[s][:, kc, :], rhs=hT[:, kc, :],
                                                                          start=(kc == 0), stop=(kc == KC - 1)),
                           reads=[("wu", s), ("hT", kc // 4)], writes=puk)
                ss = hc % 2
                c.emit("act", lambda e, psg=psg, ss=ss: e.activation(out=sg[ss][:], in_=psg[:, :], func=AF.Silu),
                       reads=pgk, writes=[("sg", ss)])
                c.emit("dve", lambda e, psu=psu, ss=ss, hc=hc: e.tensor_tensor(out=act[:, hc, :], in0=sg[ss][:], in1=psu[:, :],
                                                                               op=ALU.mult),
                       reads=puk + [("sg", ss)], writes=[("act", hc)])
            for db in range(D // 512):
                pds = [c.psum() for _ in range(NTT)]
                for hg in range(HG):
                    s = di % 3
                    di += 1
                    c.dma("sync" if di % 2 else "pool", wds[s][:], wd_b[hg, db], writes=[("wd", s)])
                    for tt in range(NTT):
                        pd, pdk = pds[tt]
                        for j in range(4):
                            hc = hg * 4 + j
                            c.emit("pe", lambda e, pd=pd, s=s, j=j, hc=hc, tt=tt: e.matmul(
                                pd[:, :], lhsT=act[:, hc, tt * 128:(tt + 1) * 128], rhs=wds[s][:, j, :],
                                start=(hc == 0), stop=(hc == HC - 1)),
                                reads=[("act", hc), ("wd", s)], writes=pdk)
                for tt in range(NTT):
                    pd, pdk = pds[tt]
                    c.emit("dve", lambda e, pd=pd, tt=tt, db=db: e.scalar_tensor_tensor(
                        out=xt[tt][:, db * 512:(db + 1) * 512], in0=pd[:, :], scalar=0.5,
                        in1=xt[tt][:, db * 512:(db + 1) * 512], op0=ALU.mult, op1=ALU.add),
                        reads=pdk + [("xt", tt)], writes=[("xt", tt)])
            for tt in range(NTT):
                r0 = tb * TB + tt * 128
                c.dma("pool", x_d[r0:r0 + 128, :], xt[tt][:], reads=[("xt", tt)], writes=[("xd", r0 // 128)])
        c.flush()


def convert_mats_phase(c, pairs):
    with ExitStack() as pes:
        fb = [c.sb("cvf", [128, 2048], es=pes) for _ in range(3)]
        bb = [c.sb("cvb", [128, 2048], BF16, es=pes) for _ in range(3)]
        engs = ["act", "dve", "pool"]
        it = 0
        for (w, wb_) in pairs:
            K_, N_ = w.shape
            for kc in range(K_ // 128):
                for c0 in range(0, N_, 2048):
                    n = min(2048, N_ - c0)
                    s = it % 3
                    c.dma("sync", fb[s][:, 0:n], w[kc * 128:(kc + 1) * 128, c0:c0 + n], writes=[("cvf", s)])
                    e_ = engs[it % 3]
                    if e_ == "act":
                        c.emit("act", lambda e, s=s, n=n: e.copy(out=bb[s][:, 0:n], in_=fb[s][:, 0:n]), reads=[("cvf", s)], writes=[("cvb", s)])
                    else:
                        c.emit(e_, lambda e, s=s, n=n: e.tensor_copy(out=bb[s][:, 0:n], in_=fb[s][:, 0:n]), reads=[("cvf", s)], writes=[("cvb", s)])
                    c.dma("pool", wb_[kc * 128:(kc + 1) * 128, c0:c0 + n], bb[s][:, 0:n], reads=[("cvb", s)], writes=[("cvo", it)])
                    it += 1
        c.flush()


FF = 5632
DEPTH = 4
FULL_SHAPES = dict(
    norm_g=(4, 4, D), ffn_w_gate=(4, 2, D, FF), ffn_w_up=(4, 2, D, FF), ffn_w_down=(4, 2, FF, D),
    ple_w_up=(4, 256, D), ple_w_gate=(4, D, D), kv_norm_g=(1, D), w_kv=(D, 1024), attn_w_q=(2, D, D),
    attn_w_o=(2, D, D), attn_sinks=(2, NH), final_norm_g=(1, D))
FULL_SHAPES.update(RWKV_SHAPES)


def build_full(S, ff=FF, layers=(0, 1, 2, 3)):
    nc = bass.Bass("TRN2", target_bir_lowering=False)
    shapes = dict(FULL_SHAPES)
    shapes["ffn_w_gate"] = (4, 2, D, ff)
    shapes["ffn_w_up"] = (4, 2, D, ff)
    shapes["ffn_w_down"] = (4, 2, ff, D)
    I = {k: nc.dram_tensor(k, list(v), F32, kind="ExternalInput").ap() for k, v in shapes.items()}
    x_in = nc.dram_tensor("x", [S, D], F32, kind="ExternalInput").ap()
    p_in = nc.dram_tensor("p", [DEPTH, S, 256], F32, kind="ExternalInput").ap()
    pos = nc.dram_tensor("positions", [S], I32, kind="ExternalInput").ap()
    y = nc.dram_tensor("y", [S, D], F32, kind="ExternalOutput").ap()
    scr = lambda n, shp=None: nc.dram_tensor("scr_" + n, list(shp or [S, D]), F32).ap()
    xs = scr("x")
    O = {k: scr(k) for k in ["r", "k", "v", "sw", "a", "g", "sv", "At", "Rt", "Bh", "Kh", "Bt", "Kt", "bonus", "y"]}
    O["el"] = scr("el", [S // 128, D])
    V0, V1 = scr("V0"), scr("V1")
    cos_d, sin_d = scr("cos", [S, 32]), scr("sin", [S, 32])
    kT_d, v_d = scr("kT", [8, 64, S]), scr("vkv", [S, 512])
    HC = ff // 128
    WB = {}
    for i in layers:
        for hf in range(2):
            WB[(i, hf)] = (nc.dram_tensor("wgb_%d_%d" % (i, hf), [HC, 128, KC, 128], BF16).ap(),
                           nc.dram_tensor("wub_%d_%d" % (i, hf), [HC, 128, KC, 128], BF16).ap(),
                           nc.dram_tensor("wdb_%d_%d" % (i, hf), [HC // 4, 4, 128, 4, 512], BF16).ap())
    with ExitStack() as es:
        c = Ctx(nc, es, S)
        make_consts(c)
        make_masks(c)
        copy_rows(c, xs, x_in, S)
        rope_tables_phase(c, pos, cos_d, sin_d)
        bf = lambda n, shp: nc.dram_tensor("bf_" + n, list(shp), BF16).ap()
        MB = {}
        pairs = []
        for i in layers:
            MB[("ple", i)] = bf("ple%d" % i, [D, D])
            pairs.append((I["ple_w_gate"][i], MB[("ple", i)]))
            if i < 2:
                MB[("wo", i)] = bf("rwo%d" % i, [D, D])
                pairs.append((I["rwkv_w_o"][i], MB[("wo", i)]))
                for q_, nm in enumerate(["wr_b", "wk_b", "wv_b"]):
                    MB[(nm, i)] = bf("%s%d" % (nm, i), [D, D])
                    pairs.append((I["rwkv_w_rkv"][i, q_], MB[(nm, i)]))
            else:
                MB[("aq", i)] = bf("aq%d" % i, [D, D])
                MB[("ao", i)] = bf("ao%d" % i, [D, D])
                pairs.append((I["attn_w_q"][i - 2], MB[("aq", i)]))
                pairs.append((I["attn_w_o"][i - 2], MB[("ao", i)]))
        MB["kv"] = bf("wkv", [D, 1024])
        pairs.append((I["w_kv"], MB["kv"]))
        convert_mats_phase(c, pairs)
        for i in layers:
            for hf in range(2):
                convert_ffn_weights_phase(c, I["ffn_w_gate"][i, hf], I["ffn_w_up"][i, hf], I["ffn_w_down"][i, hf],
                                          WB[(i, hf)][0], WB[(i, hf)][1], WB[(i, hf)][2], ff)
        for i in layers:
            if i == 2:
                kv_phase(c, xs, I["kv_norm_g"], MB["kv"], cos_d, sin_d, kT_d, v_d, wdt=BF16)
            run_ffn_bf16(c, xs, I["norm_g"][i, 0:1, :], WB[(i, 0)][0], WB[(i, 0)][1], WB[(i, 0)][2], ff)
            if i < 2:
                P = rwkv_param_aps(I, i, i == 1)
                P["g"] = I["norm_g"][i, 1:2, :]
                for nm in ["wr_b", "wk_b", "wv_b"]:
                    P[nm] = MB[(nm, i)]
                Oi = dict(O)
                Oi["V"] = V0 if i == 0 else V1
                Oi["vf"] = V0
                rwkv_proj_phase(c, xs, P, Oi, i == 1)
                rwkv_prep_phase(c, P, Oi, i == 1)
                rwkv_scan_phase(c, Oi)
                P["wo"] = MB[("wo", i)]
                if S % 512 == 0:
                    rwkv_post_phase4(c, xs, P, Oi, wdt=BF16)
                else:
                    rwkv_post_phase(c, xs, P, Oi, wdt=BF16)
            else:
                j = i - 2
                attn_phase(c, xs, I["norm_g"][i, 1:2, :], MB[("aq", i)], MB[("ao", i)], I["attn_sinks"][j:j + 1, :],
                           cos_d, sin_d, kT_d, v_d, wdt=BF16)
            run_ffn_bf16(c, xs, I["norm_g"][i, 2:3, :], WB[(i, 1)][0], WB[(i, 1)][1], WB[(i, 1)][2], ff)
            if S % 512 == 0:
                ple_phase4(c, xs, I["norm_g"][i, 3:4, :], MB[("ple", i)], I["ple_w_up"][i], p_in[i], wdt=BF16)
            else:
                ple_phase(c, xs, I["norm_g"][i, 3:4, :], MB[("ple", i)], I["ple_w_up"][i], p_in[i], wdt=BF16)
        final_norm_phase(c, xs, I["final_norm_g"], y)
    return nc


N_CORES = 4
_NC_CACHE = {}


def kernel(**inputs):
    x = np.ascontiguousarray(inputs["x"], dtype=np.float32)
    B, S, _ = x.shape
    if S not in _NC_CACHE:
        _NC_CACHE[S] = build_full(S)
    nc = _NC_CACHE[S]
    shared = {}
    for k, shp in FULL_SHAPES.items():
        shared[k] = np.ascontiguousarray(np.asarray(inputs[k], dtype=np.float32).reshape(shp))
    p = np.asarray(inputs["p"], dtype=np.float32)
    pos = np.asarray(inputs["positions"], dtype=np.int32)
    in_maps = []
    for b in range(B):
        m = dict(shared)
        m["x"] = x[b]
        m["p"] = np.ascontiguousarray(p[:, b])
        m["positions"] = np.ascontiguousarray(pos[b])
        in_maps.append(m)
    res = run_bass_kernel_spmd(nc, in_maps, core_ids=list(range(B)))
    return np.stack([np.asarray(r["y"], dtype=np.float32) for r in res.results], axis=0)
```

```python
import numpy as np
import os
DUMP = int(os.environ.get('DUMP', '0'))
INV_SUB = int(os.environ.get('INV_SUB', '9'))
SCAN_DBG = float(os.environ.get('SCAN_DBG', '99'))
from contextlib import ExitStack
import concourse.bass as bass
import concourse.mybir as mybir
from concourse.bass_utils import run_bass_kernel_spmd

F32 = mybir.dt.float32
BF16 = mybir.dt.bfloat16
I32 = mybir.dt.int32
ALU = mybir.AluOpType
AF = mybir.ActivationFunctionType
AX = mybir.AxisListType

D = 2048
KC = D // 128
NH = 32
HD = 64
RMS_EPS = 1e-6
GN_EPS = 64e-5

ENGS = ("act", "dve", "pool", "pe", "sync")
NS_DMA = 8


class Sched:
    def __init__(self):
        self.ops = {e: [] for e in ENGS}
        self.lastw = {}
        self.readers = {}

    def emit(self, eng, fn, reads=(), writes=(), dma=False):
        idx = len(self.ops[eng])
        me = (eng, idx)
        deps = set()
        for k in tuple(reads) + tuple(writes):
            w = self.lastw.get(k)
            if w is not None:
                deps.add(w)
        for k in writes:
            rd = self.readers.get(k)
            if rd:
                deps.update(rd)
            self.readers[k] = []
        for k in reads:
            lst = self.readers.setdefault(k, [])
            if not dma:
                lst[:] = [r for r in lst if not (r[0] == eng and not self.ops[eng][r[1]][2])]
            lst.append(me)
        for k in writes:
            self.lastw[k] = me
        deps.discard(me)
        if eng == "pe":
            deps = {d for d in deps if d[0] != "pe"}
        self.ops[eng].append((fn, deps, dma))

    def setup(self, nc, es):
        self.nc = nc
        self.sems = {e: es.enter_context(nc.semaphore("s_" + e)) for e in ENGS}
        self.dsems = {e: [es.enter_context(nc.semaphore("d_%s%d" % (e, i))) for i in range(NS_DMA)]
                      for e in ("sync", "pool", "act")}
        self.cnt = {e: 0 for e in ENGS}
        self.dcnt = {e: 0 for e in ENGS}
        self.waited = {e: {} for e in ENGS}
        self.dlast = {e: [] for e in ENGS}

    def flush(self):
        nc = self.nc
        sems, dsems = self.sems, self.dsems
        signaled = {e: set() for e in ENGS}
        for e, lst in self.ops.items():
            for fn, deps, dma in lst:
                for (e2, i2) in deps:
                    signaled[e2].add(i2)
        val = {}
        for e in ENGS:
            c = self.cnt[e]
            j = self.dcnt[e]
            for i, (fn, deps, dma) in enumerate(self.ops[e]):
                if dma:
                    val[(e, i)] = (dsems[e][j % NS_DMA], 16 * (j // NS_DMA + 1))
                    j += 1
                elif i in signaled[e]:
                    c += 1
                    val[(e, i)] = (sems[e], c)
            self.cnt[e] = c
            self.dcnt[e] = j

        def run(e, engobj):
            waited = self.waited[e]
            dlast = self.dlast[e]
            for i, (fn, deps, dma) in enumerate(self.ops[e]):
                need = {}

                def want(s, v):
                    key = id(s)
                    if waited.get(key, 0) >= v:
                        return
                    if key not in need or need[key][1] < v:
                        need[key] = (s, v)
                for d in deps:
                    want(*val[d])
                if dma and len(dlast) >= NS_DMA:
                    want(*dlast[-NS_DMA])
                for key, (s, v) in need.items():
                    engobj.wait_ge(s, v)
                    waited[key] = v
                    if DUMP:
                        print("W", e, i, getattr(s, "name", s), v)
                ins = fn(engobj)
                if DUMP:
                    print("I", e, i, "dma" if dma else "", "SIG" if (dma or i in signaled[e]) else "", getattr(val.get((e, i), ("", ""))[0], "name", "-"), val.get((e, i), ("", ""))[1], str(ins)[:150].replace("\n", " "))
                if dma:
                    ins.then_inc(val[(e, i)][0], 16)
                    dlast.append(val[(e, i)])
                    del dlast[:-NS_DMA]
                elif i in signaled[e]:
                    ins.then_inc(sems[e], 1)
            for (s, v) in dlast:
                if waited.get(id(s), 0) < v:
                    engobj.wait_ge(s, v)
                    waited[id(s)] = v

        if DUMP:
            print("F flush")
        with nc.Block() as block:
            @block.sync
            def _(eng):
                run("sync", eng)

            @block.gpsimd
            def _(eng):
                run("pool", eng)

            @block.scalar
            def _(eng):
                run("act", eng)

            @block.vector
            def _(eng):
                run("dve", eng)

            @block.tensor
            def _(eng):
                run("pe", eng)
        nc.all_engine_barrier()
        self.ops = {e: [] for e in ENGS}
        self.lastw = {}
        self.readers = {}


class Ctx:
    def __init__(self, nc, es, S):
        self.nc = nc
        self.es = es
        self.S = S
        self.NT = S // 128
        self.sch = Sched()
        self.sch.setup(nc, es)
        self.ps = [es.enter_context(nc.psum_tensor("ps%d" % i, [128, 512], F32)) for i in range(8)]
        self.ps_i = 0
        self.uid = 0

    def sb(self, name, shape, dt=F32, es=None):
        self.uid += 1
        return (es or self.es).enter_context(self.nc.sbuf_tensor("%s_%d" % (name, self.uid), shape, dt))

    def flush(self):
        self.sch.flush()

    def psum(self):
        i = self.ps_i
        self.ps_i = (i + 1) % 8
        return self.ps[i], [("ps", i, q) for q in range(4)]

    @staticmethod
    def q(keys, lo, hi):
        return list(keys)

    def wslot(self, name):
        d = self.__dict__.setdefault("_ws", {})
        d[name] = d.get(name, -1) + 1
        return d[name] % 2

    def emit(self, eng, fn, reads=(), writes=()):
        self.sch.emit(eng, fn, reads, writes)

    def dma(self, eng, out, in_, reads=(), writes=()):
        self.sch.emit(eng, lambda e: e.dma_start(out=out, in_=in_), reads, writes, dma=True)


def make_consts(c):
    ident = c.sb("ident", [128, 128])
    c.emit("pool", lambda e: e.memset(ident[:], 0.0), writes=["const"])
    c.emit("pool", lambda e: e.affine_select(out=ident[:], in_=ident[:], compare_op=ALU.not_equal, fill=1.0,
                                             base=0, pattern=[[-1, 128]], channel_multiplier=1),
           writes=["const"])
    c.ident = ident


def transpose_tile(c, src, skey, dstT, dkeyf, tcol, nchunks=KC, evac=("act", "dve")):
    ident = c.ident
    for g in range(0, nchunks, 4):
        n = min(4, nchunks - g)
        ps, pk = c.psum()
        for j in range(n):
            kc = g + j
            c.emit("pe", lambda e, ps=ps, j=j, kc=kc: e.transpose(ps[:, j * 128:(j + 1) * 128],
                                                                     src[:, kc * 128:(kc + 1) * 128], ident[:]),
                   reads=[skey, "const"], writes=pk)
        eng = evac[(g // 4) % len(evac)]
        dst = dstT[:, g:g + n, tcol:tcol + 128]
        srcp = ps[:, 0:n * 128].rearrange("p (a b) -> p a b", a=n)
        if eng == "act":
            c.emit("act", lambda e, dst=dst, srcp=srcp: e.copy(out=dst, in_=srcp), reads=pk, writes=[dkeyf(g // 4)])
        else:
            c.emit("dve", lambda e, dst=dst, srcp=srcp: e.tensor_copy(out=dst, in_=srcp), reads=pk,
                   writes=[dkeyf(g // 4)])


def norm_to_hT(c, xt, xkey, gam, gkey, hT, hkeyf, tcol, B):
    junk, stat, hbuf = B["junk"], B["stat"], B["hbuf"]
    c.emit("act", lambda e: e.activation(out=junk[:], in_=xt, func=AF.Square, accum_out=stat[:, 0:1]),
           reads=[xkey], writes=["junk", "stat"])
    c.emit("dve", lambda e: e.tensor_scalar(out=stat[:, 1:2], in0=stat[:, 0:1], scalar1=1.0 / D, scalar2=RMS_EPS,
                                            op0=ALU.mult, op1=ALU.add), reads=["stat"], writes=["stat"])
    c.emit("act", lambda e: e.activation(out=stat[:, 3:4], in_=stat[:, 1:2], func=AF.Sqrt), reads=["stat"], writes=["stat"])
    c.emit("dve", lambda e: e.reciprocal(out=stat[:, 2:3], in_=stat[:, 3:4]), reads=["stat"], writes=["stat"])
    c.emit("dve", lambda e: e.scalar_tensor_tensor(out=hbuf[:], in0=xt, scalar=stat[:, 2:3], in1=gam,
                                                   op0=ALU.mult, op1=ALU.mult),
           reads=[xkey, "stat", gkey], writes=["hbuf"])
    transpose_tile(c, hbuf, "hbuf", hT, hkeyf, tcol)


def ffn_phase(c, x_d, gam_row, wg, wu, wd, FF, B):
    TB = 256
    NTT = TB // 128
    HC = FF // 128
    xt, gam, hT, act, wgs, wus, wds, sg = (B[k] for k in ("xt", "gam", "hT", "act", "wgs", "wus", "wds", "sg"))
    c.dma("sync", gam[:], gam_row.partition_broadcast(128), writes=["gam"])
    wg_v = wg.rearrange("(kc p) f -> p kc f", p=128)
    wu_v = wu.rearrange("(kc p) f -> p kc f", p=128)
    wd_v = wd.rearrange("(hc p) d -> p hc d", p=128)
    GH = 4
    assert HC % GH == 0
    hkeyf = lambda g: ("hT", g)
    for tb in range(c.S // TB):
        for tt in range(NTT):
            r0 = tb * TB + tt * 128
            c.dma("sync", xt[tt][:], x_d[r0:r0 + 128, :], reads=[("xd", r0 // 128)], writes=[("xt", tt)])
            norm_to_hT(c, xt[tt][:], ("xt", tt), gam[:], "gam", hT, hkeyf, tt * 128, B)
        for hc in range(HC):
            s = hc % 2
            c.dma("sync", wgs[s][:], wg_v[:, :, hc * 128:(hc + 1) * 128], writes=[("wg", s)])
            c.dma("pool", wus[s][:], wu_v[:, :, hc * 128:(hc + 1) * 128], writes=[("wu", s)])
            psg, pgk = c.psum()
            psu, puk = c.psum()
            for kc in range(KC):
                c.emit("pe", lambda e, psg=psg, s=s, kc=kc: e.matmul(psg[:, 0:TB], lhsT=wgs[s][:, kc, :], rhs=hT[:, kc, 0:TB],
                                                                      start=(kc == 0), stop=(kc == KC - 1)),
                       reads=[("wg", s), ("hT", kc // 4)], writes=pgk)
            for kc in range(KC):
                c.emit("pe", lambda e, psu=psu, s=s, kc=kc: e.matmul(psu[:, 0:TB], lhsT=wus[s][:, kc, :], rhs=hT[:, kc, 0:TB],
                                                                      start=(kc == 0), stop=(kc == KC - 1)),
                       reads=[("wu", s), ("hT", kc // 4)], writes=puk)
            c.emit("act", lambda e, psg=psg, s=s: e.activation(out=sg[s][:], in_=psg[:, 0:TB], func=AF.Silu),
                   reads=pgk, writes=[("sg", s)])
            c.emit("dve", lambda e, psu=psu, s=s, hc=hc: e.tensor_tensor(out=act[:, hc, :], in0=sg[s][:], in1=psu[:, 0:TB],
                                                                          op=ALU.mult),
                   reads=puk + [("sg", s)], writes=[("act", hc)])
        for db in range(D // 512):
            pds = [c.psum() for _ in range(NTT)]
            for hg in range(HC // GH):
                s = (db * (HC // GH) + hg) % 2
                q = "sync" if hg % 2 == 0 else "pool"
                c.dma(q, wds[s][:], wd_v[:, hg * GH:(hg + 1) * GH, db * 512:(db + 1) * 512], writes=[("wd", s)])
                for tt in range(NTT):
                    pd, pdk = pds[tt]
                    for j in range(GH):
                        hc = hg * GH + j
                        c.emit("pe", lambda e, pd=pd, s=s, j=j, hc=hc, tt=tt: e.matmul(
                            pd[:, :], lhsT=act[:, hc, tt * 128:(tt + 1) * 128], rhs=wds[s][:, j, :],
                            start=(hc == 0), stop=(hc == HC - 1)),
                            reads=[("act", hc), ("wd", s)], writes=pdk)
            for tt in range(NTT):
                pd, pdk = pds[tt]
                c.emit("dve", lambda e, pd=pd, tt=tt, db=db: e.scalar_tensor_tensor(
                    out=xt[tt][:, db * 512:(db + 1) * 512], in0=pd[:, :], scalar=0.5,
                    in1=xt[tt][:, db * 512:(db + 1) * 512], op0=ALU.mult, op1=ALU.add),
                    reads=pdk + [("xt", tt)], writes=[("xt", tt)])
        for tt in range(NTT):
            r0 = tb * TB + tt * 128
            c.dma("sync", x_d[r0:r0 + 128, :], xt[tt][:], reads=[("xt", tt)], writes=[("xd", r0 // 128)])


def alloc_common(c, es):
    B = {}
    B["xt"] = [c.sb("xt%d" % i, [128, D], es=es) for i in range(2)]
    B["gam"] = c.sb("gam", [128, D], es=es)
    B["junk"] = c.sb("junk", [128, D], es=es)
    B["hbuf"] = c.sb("hbuf", [128, D], es=es)
    B["stat"] = c.sb("stat", [128, 8], es=es)
    B["hT"] = c.sb("hT", [128, KC, 256], es=es)
    return B


def run_ffn(c, x_d, gam_row, wg, wu, wd, FF):
    with ExitStack() as pes:
        B = alloc_common(c, pes)
        B["act"] = c.sb("act", [128, FF // 128, 256], es=pes)
        B["wgs"] = [c.sb("wgs%d" % i, [128, KC, 128], es=pes) for i in range(2)]
        B["wus"] = [c.sb("wus%d" % i, [128, KC, 128], es=pes) for i in range(2)]
        B["wds"] = [c.sb("wds%d" % i, [128, 4, 512], es=pes) for i in range(2)]
        B["sg"] = [c.sb("sg%d" % i, [128, 256], es=pes) for i in range(2)]
        ffn_phase(c, x_d, gam_row, wg, wu, wd, FF, B)
        c.flush()


def copy_rows(c, dst, src, S):
    for t in range(S // 128):
        c.dma("sync", dst[t * 128:(t + 1) * 128, :], src[t * 128:(t + 1) * 128, :])
    c.flush()


def test_ffn_program(S, FF):
    nc = bass.Bass("TRN2", target_bir_lowering=False)
    x_in = nc.dram_tensor("x", [S, D], F32, kind="ExternalInput").ap()
    g_in = nc.dram_tensor("g", [1, D], F32, kind="ExternalInput").ap()
    wg = nc.dram_tensor("wg", [D, FF], F32, kind="ExternalInput").ap()
    wu = nc.dram_tensor("wu", [D, FF], F32, kind="ExternalInput").ap()
    wd = nc.dram_tensor("wd", [FF, D], F32, kind="ExternalInput").ap()
    y = nc.dram_tensor("y", [S, D], F32, kind="ExternalOutput").ap()
    xs = nc.dram_tensor("xs", [S, D], F32).ap()
    with ExitStack() as es:
        c = Ctx(nc, es, S)
        make_consts(c)
        copy_rows(c, xs, x_in, S)
        run_ffn(c, xs, g_in, wg, wu, wd, FF)
        run_ffn(c, xs, g_in, wg, wu, wd, FF)
        copy_rows(c, y, xs, S)
    return nc


def proj(c, xT, xkeyf, ntt, W, N, wbufs, wname, epi, kcn=KC, CW=256, qi=[0]):
    Wv = W.rearrange("(kc p) n -> p kc n", p=128)
    for cb in range(N // CW):
        s = c.wslot(wname)
        qi[0] += 1
        c.dma("sync" if qi[0] % 2 else "pool", wbufs[s][:, 0:kcn, 0:CW], Wv[:, :, cb * CW:(cb + 1) * CW],
              writes=[(wname, s)])
        ps, pk = c.psum()
        for tt in range(ntt):
            for kc in range(kcn):
                c.emit("pe", lambda e, ps=ps, tt=tt, kc=kc, s=s: e.matmul(
                    ps[:, tt * CW:(tt + 1) * CW], lhsT=xT[:, kc, tt * 128:(tt + 1) * 128], rhs=wbufs[s][:, kc, 0:CW],
                    start=(kc == 0), stop=(kc == kcn - 1)),
                    reads=[(wname, s)] + xkeyf(kc), writes=c.q(pk, tt * CW, (tt + 1) * CW))
        epi(cb, ps, pk)


def store_epi(c, out_d, r0, ntt, ost, func, CW=256):
    def epi(cb, ps, pk):
        s = c.wslot("ost")
        c.emit("act", lambda e: e.activation(out=ost[s][:, 0:ntt * CW], in_=ps[:, 0:ntt * CW], func=func),
               reads=c.q(pk, 0, ntt * CW), writes=[("ost", s)])
        dst = out_d[r0:r0 + ntt * 128, cb * CW:(cb + 1) * CW].rearrange("(tt p) n -> p tt n", p=128)
        c.dma("sync", dst, ost[s][:, 0:ntt * CW].rearrange("p (tt n) -> p tt n", tt=ntt),
              reads=[("ost", s)], writes=[("od", id(out_d), r0, cb)])
    return epi


def load_rows_T(c, rows_ap, R, dst, pes):
    tmp = c.sb("rowsT", [128, D], es=pes)
    c.dma("sync", tmp[0:R, :], rows_ap, writes=["rowsT"])
    for g in range(0, KC, 4):
        ps, pk = c.psum()
        for j in range(4):
            kc = g + j
            c.emit("pe", lambda e, ps=ps, j=j, kc=kc: e.transpose(ps[:, j * R:(j + 1) * R],
                                                                     tmp[0:R, kc * 128:(kc + 1) * 128], c.ident[0:R, 0:R]),
                   reads=["rowsT", "const"], writes=pk)
        c.emit("dve", lambda e, ps=ps, g=g: e.tensor_copy(out=dst[:, g:g + 4, 0:R],
                                                         in_=ps[:, 0:4 * R].rearrange("p (a b) -> p a b", a=4)),
               reads=pk, writes=["rowsTd"])


C0 = -float(np.exp(-0.5))


def rwkv_proj_phase(c, x_d, P, O, has_vmix):
    TB, NTT = 256, 2
    with ExitStack() as pes:
        xt = c.sb("xt", [128, D], es=pes)
        gam = c.sb("gam", [128, D], es=pes)
        stat = c.sb("stat", [128, 8], es=pes)
        hTw = c.sb("hTw", [128, KC, 257], es=pes)
        dxT = c.sb("dxT", [128, KC, 256], es=pes)
        xcT = [c.sb("xcT", [128, KC, 256], es=pes) for _ in range(2)]
        wb = [c.sb("wb", [128, KC, 256], es=pes) for _ in range(2)]
        muT = c.sb("muT", [128, KC, 6], es=pes)
        w2a = c.sb("w2a", [128, D], es=pes)
        a2a = c.sb("a2a", [128, D], es=pes)
        v2a = c.sb("v2a", [128, D], es=pes)
        g2s = c.sb("g2s", [128, 2, D], es=pes)
        t1 = {k: c.sb("t1" + k, [128, 256], es=pes) for k in "wav"}
        t1g = c.sb("t1g", [128, 2, 256], es=pes)
        ost = [c.sb("ost", [128, 512], es=pes) for _ in range(2)]
        use_b = "wr_b" in P
        if use_b:
            xcb = c.sb("xcb", [128, KC, 256], BF16, es=pes)
            wbb = [c.sb("wbb", [128, KC, 256], BF16, es=pes) for _ in range(2)]
        B = dict(junk=dxT[:, 0:8, :].rearrange("p a b -> p (a b)"), stat=stat,
                 hbuf=xcT[1][:, 0:8, :].rearrange("p a b -> p (a b)"))
        c.dma("sync", gam[:], P["g"].partition_broadcast(128), writes=["gam"])
        load_rows_T(c, P["mu"], 6, muT, pes)
        c.dma("sync", w2a[0:96, :], P["w2"], writes=["w2a"])
        c.dma("sync", w2a[96:97, :], P["w0"], writes=["w2a"])
        c.dma("sync", a2a[0:96, :], P["a2"], writes=["a2a"])
        c.dma("sync", a2a[96:97, :], P["a0"], writes=["a2a"])
        if has_vmix:
            c.dma("sync", v2a[0:64, :], P["v2"], writes=["v2a"])
            c.dma("sync", v2a[64:65, :], P["v0"], writes=["v2a"])
        c.dma("sync", g2s[:], P["g2"].rearrange("(ch p) n -> p ch n", p=128), writes=["g2s"])
        for k in "wav":
            c.emit("dve", lambda e, k=k: e.memset(t1[k][:], 1.0), writes=[("t1", k)])
        c.emit("dve", lambda e: e.memset(hTw[:, :, 0:1], 0.0), writes=[("hT", g) for g in range(4)])

        hkeys = [("hT", g) for g in range(4)]
        for tb in range(c.S // TB):
            r0 = tb * TB
            for tt in range(NTT):
                c.dma("sync", xt[:], x_d[r0 + tt * 128:r0 + (tt + 1) * 128, :], writes=["xt"])
                junk = B["junk"]
                c.emit("act", lambda e: e.activation(out=junk, in_=xt[:], func=AF.Square, accum_out=stat[:, 0:1]),
                       reads=["xt"], writes=["dxT", "stat"])
                c.emit("dve", lambda e: e.tensor_scalar(out=stat[:, 1:2], in0=stat[:, 0:1], scalar1=1.0 / D,
                                                        scalar2=RMS_EPS, op0=ALU.mult, op1=ALU.add),
                       reads=["stat"], writes=["stat"])
                c.emit("act", lambda e: e.activation(out=stat[:, 3:4], in_=stat[:, 1:2], func=AF.Sqrt),
                       reads=["stat"], writes=["stat"])
                c.emit("dve", lambda e: e.reciprocal(out=stat[:, 2:3], in_=stat[:, 3:4]), reads=["stat"], writes=["stat"])
                hb = B["hbuf"]
                c.emit("dve", lambda e: e.scalar_tensor_tensor(out=hb, in0=xt[:], scalar=stat[:, 2:3], in1=gam[:],
                                                               op0=ALU.mult, op1=ALU.mult),
                       reads=["xt", "stat", "gam"], writes=[("xc", 1, g) for g in range(4)])
                ident = c.ident
                for g in range(0, KC, 4):
                    ps, pk = c.psum()
                    for j in range(4):
                        kc = g + j
                        c.emit("pe", lambda e, ps=ps, j=j, kc=kc: e.transpose(
                            ps[:, j * 128:(j + 1) * 128], hb[:, kc * 128:(kc + 1) * 128], ident[:]),
                            reads=[("xc", 1, q) for q in range(4)] + ["const"], writes=pk)
                    dst = hTw[:, g:g + 4, 1 + tt * 128:1 + (tt + 1) * 128]
                    srcp = ps[:, :].rearrange("p (a b) -> p a b", a=4)
                    if (g // 4) % 2 == 0:
                        c.emit("act", lambda e, dst=dst, srcp=srcp: e.copy(out=dst, in_=srcp), reads=pk,
                               writes=[("hT", g // 4)])
                    else:
                        c.emit("dve", lambda e, dst=dst, srcp=srcp: e.tensor_copy(out=dst, in_=srcp), reads=pk,
                               writes=[("hT", g // 4)])
            c.emit("dve", lambda e: e.tensor_tensor(out=dxT[:], in0=hTw[:, :, 0:256], in1=hTw[:, :, 1:257],
                                                    op=ALU.subtract), reads=hkeys, writes=["dxT"])

            def mix(m):
                s = c.wslot("xc")
                for kc in range(KC):
                    c.emit("dve", lambda e, s=s, kc=kc, m=m: e.scalar_tensor_tensor(
                        out=xcT[s][:, kc, :], in0=dxT[:, kc, :], scalar=muT[:, kc, m:m + 1], in1=hTw[:, kc, 1:257],
                        op0=ALU.mult, op1=ALU.add),
                        reads=["dxT", "rowsTd", ("hT", kc // 4)], writes=[("xc", s, kc // 4)])
                return xcT[s], (lambda kc, s=s: [("xc", s, kc // 4)])

            def mixb(m):
                for kc in range(KC):
                    c.emit("dve", lambda e, kc=kc, m=m: e.scalar_tensor_tensor(
                        out=xcb[:, kc, :], in0=dxT[:, kc, :], scalar=muT[:, kc, m:m + 1], in1=hTw[:, kc, 1:257],
                        op0=ALU.mult, op1=ALU.add),
                        reads=["dxT", "rowsTd", ("hT", kc // 4)], writes=[("xcb", kc // 4)])
                return xcb, (lambda kc: [("xcb", kc // 4)])

            def lora(xT, xkf, w1, R, func, t1tile, t1key, nch=1):
                s = c.wslot("wb")
                c.dma("pool", wb[s][:, :, 0:R * nch], w1.rearrange("(kc p) r -> p kc r", p=128), writes=[("wb", s)])
                for ch in range(nch):
                    ps, pk = c.psum()
                    for kc in range(KC):
                        c.emit("pe", lambda e, ps=ps, kc=kc, s=s, ch=ch: e.matmul(
                            ps[0:R, 0:256], lhsT=wb[s][:, kc, ch * R:(ch + 1) * R], rhs=xT[:, kc, 0:256],
                            start=(kc == 0), stop=(kc == KC - 1)),
                            reads=[("wb", s)] + xkf(kc), writes=c.q(pk, 0, 256))
                    dst = t1tile[0:R, :] if nch == 1 else t1tile[0:R, ch, :]
                    c.emit("act", lambda e, ps=ps, dst=dst: e.activation(out=dst, in_=ps[0:R, 0:256], func=func),
                           reads=c.q(pk, 0, 256), writes=[t1key])

            def lora2(t1tile, t1key, K, w2tile, w2key, out_d, func, nch=1):
                epi = store_epi(c, out_d, r0, NTT, ost, func)
                for cb in range(D // 256):
                    ps, pk = c.psum()
                    for tt in range(NTT):
                        for ch in range(nch):
                            lhsT = t1tile[0:K, tt * 128:(tt + 1) * 128] if nch == 1 else t1tile[0:K, ch, tt * 128:(tt + 1) * 128]
                            rhs = w2tile[0:K, cb * 256:(cb + 1) * 256] if nch == 1 else w2tile[0:K, ch, cb * 256:(cb + 1) * 256]
                            c.emit("pe", lambda e, ps=ps, tt=tt, lhsT=lhsT, rhs=rhs, ch=ch: e.matmul(
                                ps[:, tt * 256:(tt + 1) * 256], lhsT=lhsT, rhs=rhs, start=(ch == 0), stop=(ch == nch - 1)),
                                reads=[t1key, w2key], writes=c.q(pk, tt * 256, (tt + 1) * 256))
                    epi(cb, ps, pk)

            if use_b:
                xT, xkf = mixb(0)
                proj(c, xT, xkf, NTT, P["wr_b"], D, wbb, "wbb", store_epi(c, O["r"], r0, NTT, ost, AF.Copy))
                xT, xkf = mixb(2)
                proj(c, xT, xkf, NTT, P["wk_b"], D, wbb, "wbb", store_epi(c, O["k"], r0, NTT, ost, AF.Copy))
                xT, xkf = mixb(3)
                proj(c, xT, xkf, NTT, P["wv_b"], D, wbb, "wbb", store_epi(c, O["v"], r0, NTT, ost, AF.Copy))
                if has_vmix:
                    xT, xkf = mix(3)
            else:
                xT, xkf = mix(0)
                proj(c, xT, xkf, NTT, P["wr"], D, wb, "wb", store_epi(c, O["r"], r0, NTT, ost, AF.Copy))
                xT, xkf = mix(2)
                proj(c, xT, xkf, NTT, P["wk"], D, wb, "wb", store_epi(c, O["k"], r0, NTT, ost, AF.Copy))
                xT, xkf = mix(3)
                proj(c, xT, xkf, NTT, P["wv"], D, wb, "wb", store_epi(c, O["v"], r0, NTT, ost, AF.Copy))
            if has_vmix:
                lora(xT, xkf, P["v1"], 64, AF.Copy, t1["v"], ("t1", "v"))
                lora2(t1["v"], ("t1", "v"), 65, v2a, "v2a", O["sv"], AF.Sigmoid)
            xT, xkf = mix(1)
            lora(xT, xkf, P["w1"], 96, AF.Tanh, t1["w"], ("t1", "w"))
            lora2(t1["w"], ("t1", "w"), 97, w2a, "w2a", O["sw"], AF.Sigmoid)
            xT, xkf = mix(4)
            lora(xT, xkf, P["a1"], 96, AF.Copy, t1["a"], ("t1", "a"))
            lora2(t1["a"], ("t1", "a"), 97, a2a, "a2a", O["a"], AF.Sigmoid)
            xT, xkf = mix(5)
            lora(xT, xkf, P["g1"], 128, AF.Sigmoid, t1g, ("t1", "g"), nch=2)
            lora2(t1g, ("t1", "g"), 128, g2s, "g2s", O["g"], AF.Copy, nch=2)
            c.emit("dve", lambda e: e.tensor_copy(out=hTw[:, :, 0:1], in_=hTw[:, :, 256:257]), reads=hkeys, writes=hkeys)
        c.flush()


def rwkv_param_aps(nc_inputs, j, has_vmix):
    I = nc_inputs
    P = dict(
        mu=I["rwkv_mu"][j], wr=I["rwkv_w_rkv"][j, 0], wk=I["rwkv_w_rkv"][j, 1], wv=I["rwkv_w_rkv"][j, 2],
        wo=I["rwkv_w_o"][j], w0=I["rwkv_w0"][j:j + 1, :], w1=I["rwkv_w1"][j], w2=I["rwkv_w2"][j],
        a0=I["rwkv_a0"][j:j + 1, :], a1=I["rwkv_a1"][j], a2=I["rwkv_a2"][j], g1=I["rwkv_g1"][j], g2=I["rwkv_g2"][j],
        k_k=I["rwkv_k_k"][j:j + 1, :], k_a=I["rwkv_k_a"][j:j + 1, :],
        r_k=I["rwkv_r_k"][j:j + 1].rearrange("o h n -> o (h n)"),
        gn_g=I["rwkv_gn_g"][j:j + 1, :], gn_b=I["rwkv_gn_b"][j:j + 1, :])
    if has_vmix:
        P.update(v0=I["rwkv_v0"][j - 1:j, :], v1=I["rwkv_v1"][j - 1], v2=I["rwkv_v2"][j - 1])
    return P


RWKV_SHAPES = dict(
    rwkv_mu=(2, 6, D), rwkv_w_rkv=(2, 3, D, D), rwkv_w_o=(2, D, D), rwkv_w0=(2, D), rwkv_w1=(2, D, 96),
    rwkv_w2=(2, 96, D), rwkv_a0=(2, D), rwkv_a1=(2, D, 96), rwkv_a2=(2, 96, D), rwkv_v0=(1, D), rwkv_v1=(1, D, 64),
    rwkv_v2=(1, 64, D), rwkv_g1=(2, D, 256), rwkv_g2=(2, 256, D), rwkv_k_k=(2, D), rwkv_k_a=(2, D),
    rwkv_r_k=(2, 32, 64), rwkv_gn_g=(2, D), rwkv_gn_b=(2, D))


def test_rwkv_program(S, stage):
    nc = bass.Bass("TRN2", target_bir_lowering=False)
    I = {k: nc.dram_tensor(k, list(v), F32, kind="ExternalInput").ap() for k, v in RWKV_SHAPES.items()}
    x_in = nc.dram_tensor("x", [S, D], F32, kind="ExternalInput").ap()
    g_in = nc.dram_tensor("g", [1, D], F32, kind="ExternalInput").ap()
    vf_in = nc.dram_tensor("vf", [S, D], F32, kind="ExternalInput").ap()
    names = ["r", "k", "v", "sw", "a", "g", "sv", "At", "Rt", "Bh", "Kh", "Bt", "Kt", "V", "bonus", "y", "xo"]
    O = {k: nc.dram_tensor("o_" + k, [S, D], F32, kind="ExternalOutput").ap() for k in names}
    O["el"] = nc.dram_tensor("o_el", [S // 128, D], F32, kind="ExternalOutput").ap()
    O["vf"] = vf_in
    with ExitStack() as es:
        c = Ctx(nc, es, S)
        make_consts(c)
        make_masks(c)
        P = rwkv_param_aps(I, 1, True)
        P["g"] = g_in
        copy_rows(c, O["xo"], x_in, S)
        rwkv_proj_phase(c, O["xo"], P, O, True)
        if stage >= 2:
            rwkv_prep_phase(c, P, O, True)
        if stage >= 3:
            rwkv_scan_phase(c, O)
        if stage >= 4:
            rwkv_post_phase(c, O["xo"], P, O)
    return nc


def make_masks(c):
    tri = c.sb("tri", [128, 128])
    ones = c.sb("ones", [128, 128])
    mask4 = c.sb("mask4", [128, 512])
    msl = c.sb("msl", [128, 128])
    c.emit("pool", lambda e: e.memset(ones[:], 1.0), writes=["const"])
    c.emit("pool", lambda e: e.memset(tri[:], 1.0), writes=["const"])
    c.emit("pool", lambda e: e.affine_select(out=tri[:], in_=tri[:], compare_op=ALU.is_ge, fill=0.0, base=0,
                                             pattern=[[1, 128]], channel_multiplier=-1), writes=["const"])
    c.emit("pool", lambda e: e.memset(mask4[:], 1.0), writes=["const"])
    for q in range(4):
        op = ALU.is_gt if q % 2 == 0 else ALU.is_ge
        c.emit("pool", lambda e, q=q, op=op: e.affine_select(
            out=mask4[:, q * 128:(q + 1) * 128], in_=mask4[:, q * 128:(q + 1) * 128], compare_op=op, fill=0.0, base=0,
            pattern=[[1, 128]], channel_multiplier=-1), writes=["const"])
    c.emit("pool", lambda e: e.memset(msl[:], 1.0), writes=["const"])
    c.emit("pool", lambda e: e.affine_select(out=msl[:], in_=msl[:], compare_op=ALU.is_gt, fill=0.0, base=0,
                                             pattern=[[-1, 128]], channel_multiplier=1), writes=["const"])
    c.tri, c.ones, c.mask4, c.msl = tri, ones, mask4, msl


def rwkv_prep_phase(c, P, O, has_vmix):
    CB = 512
    with ExitStack() as pes:
        names_in = ["r", "k", "v", "sw", "a"] + (["sv", "vf"] if has_vmix else [])
        ld = {n: [c.sb("ld" + n, [128, CB], es=pes) for _ in range(2)] for n in names_in}
        names_out = ["At", "Rt", "Bh", "Kh", "Bt", "Kt", "V", "bonus"]
        ob = {n: [c.sb("ob" + n, [128, CB], es=pes) for _ in range(2)] for n in names_out}
        tm = {n: c.sb("tm" + n, [128, CB], es=pes) for n in ["lw", "epos", "eneg", "eprev", "EL", "kk", "sq", "b", "t", "kh", "t2", "d"]}
        st = c.sb("st", [128, 64], es=pes)
        kkp = c.sb("kkp", [128, D], es=pes)
        kap = c.sb("kap", [128, D], es=pes)
        rkp = c.sb("rkp", [128, D], es=pes)
        c.dma("sync", kkp[:], P["k_k"].partition_broadcast(128), writes=["par"])
        c.dma("sync", kap[:], P["k_a"].partition_broadcast(128), writes=["par"])
        c.dma("sync", rkp[:], P["r_k"].partition_broadcast(128), writes=["par"])
        it = 0
        for ch in range(c.S // 128):
            rs = slice(ch * 128, (ch + 1) * 128)
            for cb in range(D // CB):
                cs = slice(cb * CB, (cb + 1) * CB)
                s = it % 2
                it += 1
                L = {}
                for i, n in enumerate(names_in):
                    c.dma("sync" if i % 2 == 0 else "pool", ld[n][s][:], O[n][rs, cs], writes=[("ld", n, s)])
                    L[n] = ld[n][s]
                lk = lambda *ns: [("ld", n, s) for n in ns]
                T = tm
                ok = lambda *ns: [("ob", n, s) for n in ns]
                OB = {n: ob[n][s] for n in names_out}
                c.emit("dve", lambda e, L=L: e.tensor_scalar(out=T["lw"][:], in0=L["sw"][:], scalar1=C0, scalar2=0.0,
                                                             op0=ALU.mult, op1=ALU.add), reads=lk("sw"), writes=["lw"])
                psc, pck = c.psum()
                pst, ptk = c.psum()
                c.emit("pe", lambda e, psc=psc: e.matmul(psc[:, :], lhsT=c.tri[:], rhs=T["lw"][:], start=True, stop=True),
                       reads=["lw", "const"], writes=pck)
                c.emit("pe", lambda e, pst=pst: e.matmul(pst[:, :], lhsT=c.ones[:], rhs=T["lw"][:], start=True, stop=True),
                       reads=["lw", "const"], writes=ptk)
                c.emit("act", lambda e, psc=psc: e.activation(out=T["epos"][:], in_=psc[:, :], func=AF.Exp), reads=pck, writes=["epos"])
                c.emit("act", lambda e, psc=psc: e.activation(out=T["eneg"][:], in_=psc[:, :], func=AF.Exp, scale=-1.0),
                       reads=pck, writes=["eneg"])
                c.emit("dve", lambda e, psc=psc: e.tensor_tensor(out=T["eprev"][:], in0=psc[:, :], in1=T["lw"][:], op=ALU.subtract),
                       reads=pck + ["lw"], writes=["eprev"])
                c.emit("act", lambda e: e.activation(out=T["eprev"][:], in_=T["eprev"][:], func=AF.Exp), reads=["eprev"], writes=["eprev"])
                c.emit("act", lambda e, pst=pst: e.activation(out=T["EL"][:], in_=pst[:, :], func=AF.Exp), reads=ptk, writes=["EL"])
                c.emit("pool", lambda e, L=L, cs=cs: e.tensor_tensor(out=T["kk"][:], in0=L["k"][:], in1=kkp[:, cs], op=ALU.mult),
                       reads=lk("k") + ["par"], writes=["kk"])
                c.emit("pool", lambda e: e.tensor_tensor(out=T["sq"][:], in0=T["kk"][:], in1=T["kk"][:], op=ALU.mult),
                       reads=["kk"], writes=["sq"])
                c.emit("dve", lambda e: e.tensor_reduce(out=st[:, 0:8], in_=T["sq"][:].rearrange("p (h j) -> p h j", j=64),
                                                        axis=AX.X, op=ALU.add), reads=["sq"], writes=["st"])
                c.emit("act", lambda e: e.activation(out=st[:, 8:16], in_=st[:, 0:8], func=AF.Sqrt), reads=["st"], writes=["st"])
                c.emit("dve", lambda e: e.tensor_scalar(out=st[:, 8:16], in0=st[:, 8:16], scalar1=1e-12, scalar2=0.0,
                                                        op0=ALU.max, op1=ALU.add), reads=["st"], writes=["st"])
                c.emit("dve", lambda e: e.reciprocal(out=st[:, 16:24], in_=st[:, 8:16]), reads=["st"], writes=["st"])
                c.emit("dve", lambda e: e.tensor_tensor(
                    out=T["kk"][:].rearrange("p (h j) -> p h j", j=64), in0=T["kk"][:].rearrange("p (h j) -> p h j", j=64),
                    in1=st[:, 16:24].unsqueeze(2).to_broadcast([128, 8, 64]), op=ALU.mult), reads=["kk", "st"], writes=["kk"])
                c.emit("dve", lambda e, OB=OB: e.scalar_tensor_tensor(out=OB["At"][:], in0=T["kk"][:], scalar=-1.0, in1=T["eprev"][:],
                                                                     op0=ALU.mult, op1=ALU.mult),
                       reads=["kk", "eprev"], writes=ok("At"))
                c.emit("pool", lambda e, L=L: e.tensor_tensor(out=T["b"][:], in0=T["kk"][:], in1=L["a"][:], op=ALU.mult),
                       reads=["kk"] + lk("a"), writes=["b"])
                c.emit("pool", lambda e, OB=OB: e.tensor_tensor(out=OB["Bh"][:], in0=T["b"][:], in1=T["eneg"][:], op=ALU.mult),
                       reads=["b", "eneg"], writes=ok("Bh"))
                c.emit("pool", lambda e, OB=OB: e.tensor_tensor(out=OB["Bt"][:], in0=OB["Bh"][:], in1=T["EL"][:], op=ALU.mult),
                       reads=ok("Bh") + ["EL"], writes=ok("Bt"))
                c.emit("dve", lambda e, L=L, cs=cs: e.scalar_tensor_tensor(out=T["t"][:], in0=L["a"][:], scalar=-1.0, in1=kap[:, cs],
                                                                          op0=ALU.add, op1=ALU.mult),
                       reads=lk("a") + ["par"], writes=["t"])
                c.emit("dve", lambda e, L=L: e.scalar_tensor_tensor(out=T["kh"][:], in0=T["t"][:], scalar=1.0, in1=L["k"][:],
                                                                   op0=ALU.add, op1=ALU.mult),
                       reads=["t"] + lk("k"), writes=["kh"])
                c.emit("pool", lambda e, OB=OB: e.tensor_tensor(out=OB["Kh"][:], in0=T["kh"][:], in1=T["eneg"][:], op=ALU.mult),
                       reads=["kh", "eneg"], writes=ok("Kh"))
                c.emit("pool", lambda e, OB=OB: e.tensor_tensor(out=OB["Kt"][:], in0=OB["Kh"][:], in1=T["EL"][:], op=ALU.mult),
                       reads=ok("Kh") + ["EL"], writes=ok("Kt"))
                c.emit("pool", lambda e, OB=OB, L=L: e.tensor_tensor(out=OB["Rt"][:], in0=L["r"][:], in1=T["epos"][:], op=ALU.mult),
                       reads=lk("r") + ["epos"], writes=ok("Rt"))
                if has_vmix:
                    c.emit("pool", lambda e, L=L: e.tensor_tensor(out=T["d"][:], in0=L["vf"][:], in1=L["v"][:], op=ALU.subtract),
                           reads=lk("vf", "v"), writes=["d"])
                    c.emit("pool", lambda e, L=L: e.tensor_tensor(out=T["d"][:], in0=T["d"][:], in1=L["sv"][:], op=ALU.mult),
                           reads=lk("sv") + ["d"], writes=["d"])
                    c.emit("pool", lambda e, L=L, OB=OB: e.tensor_tensor(out=OB["V"][:], in0=T["d"][:], in1=L["v"][:], op=ALU.add),
                           reads=lk("v") + ["d"], writes=ok("V"))
                else:
                    c.emit("pool", lambda e, L=L, OB=OB: e.tensor_copy(out=OB["V"][:], in_=L["v"][:]), reads=lk("v"), writes=ok("V"))
                c.emit("pool", lambda e, L=L: e.tensor_tensor(out=T["t2"][:], in0=L["r"][:], in1=T["kh"][:], op=ALU.mult),
                       reads=lk("r") + ["kh"], writes=["t2"])
                c.emit("pool", lambda e, cs=cs: e.tensor_tensor(out=T["t2"][:], in0=T["t2"][:], in1=rkp[:, cs], op=ALU.mult),
                       reads=["t2", "par"], writes=["t2"])
                c.emit("dve", lambda e: e.tensor_reduce(out=st[:, 32:40], in_=T["t2"][:].rearrange("p (h j) -> p h j", j=64),
                                                        axis=AX.X, op=ALU.add), reads=["t2"], writes=["st2"])
                c.emit("dve", lambda e, OB=OB: e.tensor_tensor(
                    out=OB["bonus"][:].rearrange("p (h j) -> p h j", j=64), in0=OB["V"][:].rearrange("p (h j) -> p h j", j=64),
                    in1=st[:, 32:40].unsqueeze(2).to_broadcast([128, 8, 64]), op=ALU.mult), reads=ok("V") + ["st2"], writes=ok("bonus"))
                for i, n in enumerate(names_out):
                    c.dma("sync" if i % 2 == 0 else "pool", O[n][rs, cs], OB[n][:], reads=ok(n), writes=[("od", n, ch, cb)])
                c.dma("sync", O["el"][ch:ch + 1, cs], T["EL"][0:1, :], reads=["EL"], writes=[("od", "el", ch, cb)])
        c.flush()


def rwkv_scan_phase(c, O):
    CB, G = 512, 8
    NB = G // 4
    ident = c.ident
    with ExitStack() as pes:
        names_in = ["At", "Rt", "Bh", "Kh", "Bt", "Kt", "V"]
        ld = {n: [c.sb("sl" + n, [128, CB], es=pes) for _ in range(2)] for n in names_in}
        elrow = [c.sb("elrow", [1, CB], es=pes) for _ in range(2)]
        gl = [c.sb("gl", [64, 8], es=pes) for _ in range(2)]
        Tst = c.sb("Tst", [64, NH, 64], es=pes)
        XT = [c.sb("XT", [64, 512], es=pes) for _ in range(G)]
        ABK = [c.sb("ABK", [128, 512], es=pes) for _ in range(G)]
        Xb = [[c.sb("Xb", [128, 128], es=pes) for _ in range(2)] for _ in range(G)]
        XTb = [[c.sb("XTb", [128, 128], es=pes) for _ in range(3)] for _ in range(G)]
        PTb = [[c.sb("PTb", [128, 128], es=pes) for _ in range(2)] for _ in range(G)]
        Wb = [c.sb("Wb", [128, 64], es=pes) for _ in range(G)]
        Ub = [c.sb("Ub", [128, 64], es=pes) for _ in range(G)]
        Yst = [c.sb("Yst", [128, CB], es=pes) for _ in range(2)]
        print("scan sbuf remaining", c.nc.sbuf_bytes_remaining)
        c.emit("dve", lambda e: e.memset(Tst[:], 0.0), writes=[("T", h) for h in range(NH)])
        it = 0
        for ch in range(c.S // 128):
            rs = slice(ch * 128, (ch + 1) * 128)
            for cb in range(D // CB):
                cs = slice(cb * CB, (cb + 1) * CB)
                s = it % 2
                it += 1
                L = {}
                for i, n in enumerate(names_in):
                    c.dma("sync" if i % 2 == 0 else "pool", ld[n][s][:], O[n][rs, cs], writes=[("sl", n, s)])
                    L[n] = ld[n][s]
                lk = lambda *ns: [("sl", n, s) for n in ns]
                c.dma("sync", elrow[s][:], O["el"][ch:ch + 1, cs], writes=[("elrow", s)])
                psg, pgk = c.psum()
                for hl in range(8):
                    c.emit("pe", lambda e, psg=psg, hl=hl, s=s: e.matmul(
                        psg[0:64, hl:hl + 1], lhsT=elrow[s][0:1, hl * 64:(hl + 1) * 64], rhs=c.ones[0:1, 0:1], start=True, stop=True),
                        reads=[("elrow", s), "const"], writes=c.q(pgk, 0, 8))
                c.emit("dve", lambda e, psg=psg, s=s: e.tensor_copy(out=gl[s][:], in_=psg[0:64, 0:8]),
                       reads=c.q(pgk, 0, 8), writes=[("gl", s)])
                for grp in range(8 // G):
                    if SCAN_DBG < 1:
                        continue
                    heads = [grp * G + i for i in range(G)]
                    for i, hl in enumerate(heads):
                        hc = slice(hl * 64, (hl + 1) * 64)
                        ps, pk = c.psum()
                        for qn, n in enumerate(["At", "Rt", "Bh", "Kh"]):
                            c.emit("pe", lambda e, ps=ps, qn=qn, n=n, hc=hc, L=L: e.transpose(
                                ps[0:64, qn * 128:(qn + 1) * 128], L[n][:, hc], ident[:]),
                                reads=lk(n) + ["const"], writes=c.q(pk, qn * 128, (qn + 1) * 128))
                        if i % 2 == 0:
                            c.emit("act", lambda e, ps=ps, i=i: e.copy(out=XT[i][:], in_=ps[0:64, :]), reads=pk, writes=[("XT", i)])
                        else:
                            c.emit("dve", lambda e, ps=ps, i=i: e.tensor_copy(out=XT[i][:], in_=ps[0:64, :]), reads=pk,
                                   writes=[("XT", i)])
                    if SCAN_DBG < 2:
                        continue
                    psCs = [c.psum() for _ in range(NB)]
                    for i, hl in enumerate(heads):
                        psC, pCk = psCs[i // 4]
                        c.emit("pe", lambda e, psC=psC, i=i: e.matmul(psC[:, (i % 4) * 128:(i % 4 + 1) * 128], lhsT=XT[i][:, 0:128],
                                                                      rhs=XT[i][:, 256:384], start=True, stop=True),
                               reads=[("XT", i)], writes=pCk)
                    for i, hl in enumerate(heads):
                        psC, pCk = psCs[i // 4]
                        c.emit("dve", lambda e, psC=psC, i=i: e.tensor_tensor(out=XTb[i][2][:], in0=psC[:, (i % 4) * 128:(i % 4 + 1) * 128],
                                                                              in1=c.msl[:], op=ALU.mult),
                               reads=pCk + ["const"], writes=[("XTb", i, 2)])
                    for i, hl in enumerate(heads):
                        ps, pk = c.psum()
                        c.emit("pe", lambda e, ps=ps, i=i: e.matmul(ps[:, 0:256], lhsT=XT[i][:, 256:384], rhs=XT[i][:, 0:256],
                                                                    start=True, stop=True), reads=[("XT", i)], writes=pk)
                        c.emit("pe", lambda e, ps=ps, i=i: e.matmul(ps[:, 256:512], lhsT=XT[i][:, 384:512], rhs=XT[i][:, 0:256],
                                                                    start=True, stop=True), reads=[("XT", i)], writes=pk)
                        c.emit("dve", lambda e, ps=ps, i=i: e.tensor_tensor(out=ABK[i][:], in0=ps[:, :], in1=c.mask4[:], op=ALU.mult),
                               reads=pk + ["const"], writes=[("ABK", i)])
                        c.emit("dve", lambda e, i=i: e.tensor_tensor(out=PTb[i][0][:], in0=ABK[i][:, 0:128], in1=ident[:], op=ALU.add),
                               reads=[("ABK", i), "const"], writes=[("PTb", i, 0)])
                    if SCAN_DBG < 3:
                        continue
                    Xc = [(ABK[i][:, 0:128], ("ABK", i)) for i in range(G)]
                    XTc = [(XTb[i][2][:], ("XTb", i, 2)) for i in range(G)]
                    PTc = [(PTb[i][0][:], ("PTb", i, 0)) for i in range(G)]
                    for lev in range(int(os.environ.get("NLEV", "6"))):
                        pXs = [c.psum() for _ in range(NB)]
                        pXTs = [c.psum() for _ in range(NB)]
                        newX, newXT = [], []
                        for i in range(G):
                            pX, pXk = pXs[i // 4]
                            pXT, pXTk = pXTs[i // 4]
                            qs = pXk
                            qt = pXTk
                            xa, xk_ = Xc[i]
                            xta, xtk = XTc[i]
                            if lev < 5:
                                c.emit("pe", lambda e, pX=pX, i=i, xa=xa, xta=xta: e.matmul(
                                    pX[:, (i % 4) * 128:(i % 4 + 1) * 128], lhsT=xta, rhs=xa, start=True, stop=True),
                                    reads=[xk_, xtk], writes=qs)
                            c.emit("pe", lambda e, pXT=pXT, i=i, xa=xa, xta=xta: e.matmul(
                                pXT[:, (i % 4) * 128:(i % 4 + 1) * 128], lhsT=xa, rhs=xta, start=True, stop=True),
                                reads=[xk_, xtk], writes=qt)
                        if INV_SUB < 2:
                            continue
                        for i in range(G):
                            pX, pXk = pXs[i // 4]
                            pXT, pXTk = pXTs[i // 4]
                            b = lev % 2
                            if lev < 5:
                                c.emit("act", lambda e, pX=pX, i=i, b=b: e.copy(out=Xb[i][b][:], in_=pX[:, (i % 4) * 128:(i % 4 + 1) * 128]),
                                       reads=pXk, writes=[("Xb", i, b)])
                                newX.append((Xb[i][b][:], ("Xb", i, b)))
                            else:
                                newX.append(None)
                            c.emit("act", lambda e, pXT=pXT, i=i, b=b: e.copy(out=XTb[i][b][:], in_=pXT[:, (i % 4) * 128:(i % 4 + 1) * 128]),
                                   reads=pXTk, writes=[("XTb", i, b)])
                            newXT.append((XTb[i][b][:], ("XTb", i, b)))
                        if INV_SUB < 3:
                            continue
                        pPs = [c.psum() for _ in range(NB)]
                        for i in range(G):
                            pP, pPk = pPs[i // 4]
                            qp = pPk
                            pa, pkk = PTc[i]
                            c.emit("pe", lambda e, pP=pP, i=i, pa=pa, l=newXT[i][0]: e.matmul(
                                pP[:, (i % 4) * 128:(i % 4 + 1) * 128], lhsT=l, rhs=pa, start=True, stop=True),
                                reads=[newXT[i][1], pkk], writes=qp)
                        for i in range(G):
                            pP, pPk = pPs[i // 4]
                            qp = pPk
                            pa, pkk = PTc[i]
                            nb = (lev + 1) % 2
                            c.emit("dve", lambda e, pP=pP, i=i, pa=pa, nb=nb: e.tensor_tensor(
                                out=PTb[i][nb][:], in0=pP[:, (i % 4) * 128:(i % 4 + 1) * 128], in1=pa, op=ALU.add),
                                reads=qp + [pkk], writes=[("PTb", i, nb)])
                            PTc[i] = (PTb[i][nb][:], ("PTb", i, nb))
                        Xc, XTc = newX, newXT
                    if SCAN_DBG < 4:
                        continue
                    pWs = [c.psum() for _ in range(NB)]
                    pUs = [c.psum() for _ in range(NB)]
                    pYs = [c.psum() for _ in range(NB)]
                    pTs = [c.psum() for _ in range(NB)]
                    H = [(i, hl, cb * 8 + hl, slice(hl * 64, (hl + 1) * 64), slice((i % 4) * 128, (i % 4) * 128 + 64)) for i, hl in enumerate(heads)]
                    for i, hl, h, hc, q0 in H:
                        c.emit("pe", lambda e, pW=pWs[i // 4][0], pU=pUs[i // 4][0], pY=pYs[i // 4][0], pT=pTs[i // 4][0], i=i, h=h, q0=q0: e.matmul(pW[:, q0], lhsT=XT[i][:, 0:128], rhs=Tst[:, h, :],
                                                                      start=True, stop=False),
                               reads=[("XT", i), ("T", h)], writes=pWs[i // 4][1])
                        c.emit("pe", lambda e, pW=pWs[i // 4][0], pU=pUs[i // 4][0], pY=pYs[i // 4][0], pT=pTs[i // 4][0], i=i, hc=hc, q0=q0, L=L: e.matmul(pW[:, q0], lhsT=ABK[i][:, 256:384], rhs=L["V"][:, hc],
                                                                             start=False, stop=True),
                               reads=[("ABK", i)] + lk("V"), writes=pWs[i // 4][1])
                    for i, hl, h, hc, q0 in H:
                        c.emit("act", lambda e, pW=pWs[i // 4][0], pU=pUs[i // 4][0], pY=pYs[i // 4][0], pT=pTs[i // 4][0], i=i, q0=q0: e.copy(out=Wb[i][:], in_=pW[:, q0]), reads=pWs[i // 4][1], writes=[("Wb", i)])
                    for i, hl, h, hc, q0 in H:
                        pa, pkk = PTc[i]
                        c.emit("pe", lambda e, pW=pWs[i // 4][0], pU=pUs[i // 4][0], pY=pYs[i // 4][0], pT=pTs[i // 4][0], i=i, pa=pa, q0=q0: e.matmul(pU[:, q0], lhsT=pa, rhs=Wb[i][:], start=True, stop=True),
                               reads=[pkk, ("Wb", i)], writes=pUs[i // 4][1])
                    for i, hl, h, hc, q0 in H:
                        c.emit("act", lambda e, pW=pWs[i // 4][0], pU=pUs[i // 4][0], pY=pYs[i // 4][0], pT=pTs[i // 4][0], i=i, q0=q0: e.copy(out=Ub[i][:], in_=pU[:, q0]), reads=pUs[i // 4][1], writes=[("Ub", i)])
                    for i, hl, h, hc, q0 in H:
                        c.emit("pe", lambda e, pW=pWs[i // 4][0], pU=pUs[i // 4][0], pY=pYs[i // 4][0], pT=pTs[i // 4][0], i=i, h=h, q0=q0: e.matmul(pY[:, q0], lhsT=XT[i][:, 128:256], rhs=Tst[:, h, :],
                                                                      start=True, stop=False),
                               reads=[("XT", i), ("T", h)], writes=pYs[i // 4][1])
                        c.emit("pe", lambda e, pW=pWs[i // 4][0], pU=pUs[i // 4][0], pY=pYs[i // 4][0], pT=pTs[i // 4][0], i=i, q0=q0: e.matmul(pY[:, q0], lhsT=ABK[i][:, 128:256], rhs=Ub[i][:],
                                                                 start=False, stop=False),
                               reads=[("ABK", i), ("Ub", i)], writes=pYs[i // 4][1])
                        c.emit("pe", lambda e, pW=pWs[i // 4][0], pU=pUs[i // 4][0], pY=pYs[i // 4][0], pT=pTs[i // 4][0], i=i, hc=hc, q0=q0, L=L: e.matmul(pY[:, q0], lhsT=ABK[i][:, 384:512], rhs=L["V"][:, hc],
                                                                             start=False, stop=True),
                               reads=[("ABK", i)] + lk("V"), writes=pYs[i // 4][1])
                        c.emit("pe", lambda e, pW=pWs[i // 4][0], pU=pUs[i // 4][0], pY=pYs[i // 4][0], pT=pTs[i // 4][0], i=i, hc=hc, q0=q0, L=L: e.matmul(pT[0:64, q0], lhsT=L["Bt"][:, hc], rhs=Ub[i][:],
                                                                             start=True, stop=False),
                               reads=lk("Bt") + [("Ub", i)], writes=pTs[i // 4][1])
                        c.emit("pe", lambda e, pW=pWs[i // 4][0], pU=pUs[i // 4][0], pY=pYs[i // 4][0], pT=pTs[i // 4][0], i=i, hc=hc, q0=q0, L=L: e.matmul(pT[0:64, q0], lhsT=L["Kt"][:, hc], rhs=L["V"][:, hc],
                                                                             start=False, stop=True),
                               reads=lk("Kt", "V"), writes=pTs[i // 4][1])
                    for i, hl, h, hc, q0 in H:
                        c.emit("act", lambda e, pW=pWs[i // 4][0], pU=pUs[i // 4][0], pY=pYs[i // 4][0], pT=pTs[i // 4][0], hc=hc, q0=q0, s=s: e.copy(out=Yst[s][:, hc], in_=pY[:, q0]),
                               reads=pYs[i // 4][1], writes=[("Yst", s)])
                        c.emit("dve", lambda e, pW=pWs[i // 4][0], pU=pUs[i // 4][0], pY=pYs[i // 4][0], pT=pTs[i // 4][0], h=h, hl=hl, q0=q0, s=s: e.scalar_tensor_tensor(
                            out=Tst[:, h, :], in0=Tst[:, h, :], scalar=gl[s][:, hl:hl + 1], in1=pT[0:64, q0],
                            op0=ALU.mult, op1=ALU.add), reads=pTs[i // 4][1] + [("T", h), ("gl", s)], writes=[("T", h)])
                c.dma("sync", O["y"][rs, cs], Yst[s][:], reads=[("Yst", s)], writes=[("od", "y", ch, cb)])
        c.flush()


def rwkv_post_phase(c, x_d, P, O, wdt=F32):
    with ExitStack() as pes:
        yb = c.sb("yb", [128, D], es=pes)
        bb = c.sb("bb", [128, D], es=pes)
        gb = c.sb("gb", [128, D], es=pes)
        xt = c.sb("xt", [128, D], es=pes)
        sq = c.sb("sq", [128, D], es=pes)
        gng = c.sb("gng", [128, D], es=pes)
        gnb = c.sb("gnb", [128, D], es=pes)
        st = c.sb("st", [128, 160], es=pes)
        zT = c.sb("zT", [128, KC, 128], wdt, es=pes)
        wb = [c.sb("wb", [128, KC, 256], wdt, es=pes) for _ in range(2)]
        c.dma("sync", gng[:], P["gn_g"].partition_broadcast(128), writes=["par"])
        c.dma("sync", gnb[:], P["gn_b"].partition_broadcast(128), writes=["par"])
        y3 = yb[:].rearrange("p (h j) -> p h j", j=64)
        s3 = sq[:].rearrange("p (h j) -> p h j", j=64)
        bc = lambda a: a.unsqueeze(2).to_broadcast([128, NH, 64])
        for t in range(c.S // 128):
            rs = slice(t * 128, (t + 1) * 128)
            c.dma("sync", yb[:], O["y"][rs, :], writes=["yb"])
            c.dma("pool", bb[:], O["bonus"][rs, :], writes=["bb"])
            c.dma("sync", gb[:], O["g"][rs, :], writes=["gb"])
            c.dma("pool", xt[:], x_d[rs, :], writes=["xt"])
            c.emit("dve", lambda e: e.tensor_reduce(out=st[:, 0:32], in_=y3, axis=AX.X, op=ALU.add), reads=["yb"], writes=["st"])
            c.emit("dve", lambda e: e.tensor_scalar(out=st[:, 32:64], in0=st[:, 0:32], scalar1=1.0 / 64, scalar2=0.0,
                                                    op0=ALU.mult, op1=ALU.add), reads=["st"], writes=["st"])
            c.emit("dve", lambda e: e.tensor_tensor(out=y3, in0=y3, in1=bc(st[:, 32:64]), op=ALU.subtract),
                   reads=["yb", "st"], writes=["yb"])
            c.emit("pool", lambda e: e.tensor_tensor(out=sq[:], in0=yb[:], in1=yb[:], op=ALU.mult), reads=["yb"], writes=["sq"])
            c.emit("dve", lambda e: e.tensor_reduce(out=st[:, 64:96], in_=s3, axis=AX.X, op=ALU.add), reads=["sq"], writes=["st"])
            c.emit("dve", lambda e: e.tensor_scalar(out=st[:, 96:128], in0=st[:, 64:96], scalar1=1.0 / 64, scalar2=GN_EPS,
                                                    op0=ALU.mult, op1=ALU.add), reads=["st"], writes=["st"])
            c.emit("act", lambda e: e.activation(out=st[:, 96:128], in_=st[:, 96:128], func=AF.Sqrt), reads=["st"], writes=["st"])
            c.emit("dve", lambda e: e.reciprocal(out=st[:, 128:160], in_=st[:, 96:128]), reads=["st"], writes=["st"])
            c.emit("dve", lambda e: e.tensor_tensor(out=y3, in0=y3, in1=bc(st[:, 128:160]), op=ALU.mult),
                   reads=["yb", "st"], writes=["yb"])
            c.emit("pool", lambda e: e.tensor_tensor(out=yb[:], in0=yb[:], in1=gng[:], op=ALU.mult), reads=["yb", "par"], writes=["yb"])
            c.emit("pool", lambda e: e.tensor_tensor(out=yb[:], in0=yb[:], in1=gnb[:], op=ALU.add), reads=["yb", "par"], writes=["yb"])
            c.emit("pool", lambda e: e.tensor_tensor(out=yb[:], in0=yb[:], in1=bb[:], op=ALU.add), reads=["yb", "bb"], writes=["yb"])
            c.emit("pool", lambda e: e.tensor_tensor(out=sq[:], in0=yb[:], in1=gb[:], op=ALU.mult), reads=["yb", "gb"], writes=["sq"])
            transpose_tile(c, sq, "sq", zT, lambda g: ("zT", g), 0)

            def epi(cb, ps, pk):
                c.emit("dve", lambda e, cb=cb, ps=ps: e.tensor_tensor(out=xt[:, cb * 256:(cb + 1) * 256], in0=ps[:, 0:256],
                                                                       in1=xt[:, cb * 256:(cb + 1) * 256], op=ALU.add),
                       reads=pk + ["xt"], writes=["xt"])
            proj(c, zT, lambda kc: [("zT", kc // 4)], 1, P["wo"], D, wb, "wb", epi)
            c.dma("sync", x_d[rs, :], xt[:], reads=["xt"], writes=[("xd", t)])
        c.flush()


def rwkv_post_phase4(c, x_d, P, O, wdt=F32):
    with ExitStack() as pes:
        yb = c.sb("yb", [128, D], es=pes)
        bb = c.sb("bb", [128, D], es=pes)
        gb = c.sb("gb", [128, D], es=pes)
        xt = [c.sb("xt", [128, D], es=pes) for _ in range(4)]
        sq = c.sb("sq", [128, D], es=pes)
        gng = c.sb("gng", [128, D], es=pes)
        gnb = c.sb("gnb", [128, D], es=pes)
        st = c.sb("st", [128, 160], es=pes)
        zT = c.sb("zT", [128, KC, 512], wdt, es=pes)
        wb = [c.sb("wb", [128, KC, 256], wdt, es=pes) for _ in range(2)]
        c.dma("sync", gng[:], P["gn_g"].partition_broadcast(128), writes=["par"])
        c.dma("sync", gnb[:], P["gn_b"].partition_broadcast(128), writes=["par"])
        y3 = yb[:].rearrange("p (h j) -> p h j", j=64)
        s3 = sq[:].rearrange("p (h j) -> p h j", j=64)
        bc = lambda a: a.unsqueeze(2).to_broadcast([128, NH, 64])
        for t in range(c.S // 128):
            rs = slice(t * 128, (t + 1) * 128)
            tt = t % 4
            c.dma("sync", yb[:], O["y"][rs, :], writes=["yb"])
            c.dma("pool", bb[:], O["bonus"][rs, :], writes=["bb"])
            c.dma("sync", gb[:], O["g"][rs, :], writes=["gb"])
            c.dma("pool", xt[tt][:], x_d[rs, :], writes=[("xt", tt)])
            c.emit("dve", lambda e: e.tensor_reduce(out=st[:, 0:32], in_=y3, axis=AX.X, op=ALU.add), reads=["yb"], writes=["st"])
            c.emit("dve", lambda e: e.tensor_scalar(out=st[:, 32:64], in0=st[:, 0:32], scalar1=1.0 / 64, scalar2=0.0,
                                                    op0=ALU.mult, op1=ALU.add), reads=["st"], writes=["st"])
            c.emit("dve", lambda e: e.tensor_tensor(out=y3, in0=y3, in1=bc(st[:, 32:64]), op=ALU.subtract),
                   reads=["yb", "st"], writes=["yb"])
            c.emit("pool", lambda e: e.tensor_tensor(out=sq[:], in0=yb[:], in1=yb[:], op=ALU.mult), reads=["yb"], writes=["sq"])
            c.emit("dve", lambda e: e.tensor_reduce(out=st[:, 64:96], in_=s3, axis=AX.X, op=ALU.add), reads=["sq"], writes=["st"])
            c.emit("dve", lambda e: e.tensor_scalar(out=st[:, 96:128], in0=st[:, 64:96], scalar1=1.0 / 64, scalar2=GN_EPS,
                                                    op0=ALU.mult, op1=ALU.add), reads=["st"], writes=["st"])
            c.emit("act", lambda e: e.activation(out=st[:, 96:128], in_=st[:, 96:128], func=AF.Sqrt), reads=["st"], writes=["st"])
            c.emit("dve", lambda e: e.reciprocal(out=st[:, 128:160], in_=st[:, 96:128]), reads=["st"], writes=["st"])
            c.emit("dve", lambda e: e.tensor_tensor(out=y3, in0=y3, in1=bc(st[:, 128:160]), op=ALU.mult),
                   reads=["yb", "st"], writes=["yb"])
            c.emit("pool", lambda e: e.tensor_tensor(out=yb[:], in0=yb[:], in1=gng[:], op=ALU.mult), reads=["yb", "par"], writes=["yb"])
            c.emit("pool", lambda e: e.tensor_tensor(out=yb[:], in0=yb[:], in1=gnb[:], op=ALU.add), reads=["yb", "par"], writes=["yb"])
            c.emit("pool", lambda e: e.tensor_tensor(out=yb[:], in0=yb[:], in1=bb[:], op=ALU.add), reads=["yb", "bb"], writes=["yb"])
            c.emit("pool", lambda e: e.tensor_tensor(out=sq[:], in0=yb[:], in1=gb[:], op=ALU.mult), reads=["yb", "gb"], writes=["sq"])
            transpose_tile(c, sq, "sq", zT, lambda g: ("zT", g), tt * 128)
            if tt != 3:
                continue

            def epi(cb, half, ps, pk):
                for t2 in range(2):
                    t4 = half * 2 + t2
                    c.emit("dve", lambda e, cb=cb, ps=ps, t2=t2, t4=t4: e.tensor_tensor(
                        out=xt[t4][:, cb * 256:(cb + 1) * 256], in0=ps[:, t2 * 256:(t2 + 1) * 256],
                        in1=xt[t4][:, cb * 256:(cb + 1) * 256], op=ALU.add),
                        reads=pk + [("xt", t4)], writes=[("xt", t4)])
            proj4(c, zT, lambda kc: [("zT", kc // 4)], P["wo"], D, wb, "wb", epi)
            for t4 in range(4):
                r0 = (t - 3 + t4) * 128
                c.dma("sync", x_d[r0:r0 + 128, :], xt[t4][:], reads=[("xt", t4)], writes=[("xd", r0 // 128)])
        c.flush()


import math


def rope_tables_phase(c, pos_ap, cos_d, sin_d):
    NT = c.S // 128
    with ExitStack() as pes:
        pi_ = c.sb("pos_i", [NT, 128], I32, es=pes)
        pf = c.sb("pos_f", [NT, 128], es=pes)
        posT = c.sb("posT", [128, NT], es=pes)
        io_i = c.sb("io_i", [128, 32], I32, es=pes)
        invf = c.sb("invf", [128, 32], es=pes)
        ang = c.sb("ang", [128, 32], es=pes)
        ob = [c.sb("ropeo", [128, 64], es=pes) for _ in range(2)]
        nb = c.sb("negpi", [128, 1], es=pes)
        ni = c.sb("ni", [128, 64], I32, es=pes)
        nf = c.sb("nf", [128, 64], es=pes)
        c.emit("pool", lambda e: e.memset(nb[:], -math.pi), writes=["nb"])
        c.dma("sync", pi_[:], pos_ap.rearrange("(t p) -> t p", p=128), writes=["pi"])
        c.emit("dve", lambda e: e.tensor_copy(out=pf[:], in_=pi_[:]), reads=["pi"], writes=["pf"])
        ps, pk = c.psum()
        c.emit("pe", lambda e: e.transpose(ps[:, 0:NT], pf[:], c.ident[0:NT, 0:NT]), reads=["pf", "const"], writes=pk)
        c.emit("dve", lambda e: e.tensor_copy(out=posT[:], in_=ps[:, 0:NT]), reads=pk, writes=["posT"])
        c.emit("pool", lambda e: e.iota(io_i[:], pattern=[[1, 32]], base=0, channel_multiplier=0), writes=["io"])
        c.emit("dve", lambda e: e.tensor_copy(out=invf[:], in_=io_i[:]), reads=["io"], writes=["invf"])
        c.emit("act", lambda e: e.activation(out=invf[:], in_=invf[:], func=AF.Exp, scale=-math.log(10000.0) / 32.0),
               reads=["invf"], writes=["invf"])
        for t in range(NT):
            s = t % 2
            c.emit("dve", lambda e, t=t: e.tensor_scalar(out=ang[:], in0=invf[:], scalar1=posT[:, t:t + 1], scalar2=0.0,
                                                         op0=ALU.mult, op1=ALU.add), reads=["invf", "posT"], writes=["ang"])
            c.emit("dve", lambda e, s=s: e.tensor_scalar(out=ob[s][:, 32:64], in0=ang[:], scalar1=1.0 / (2 * math.pi), scalar2=0.5,
                                                         op0=ALU.mult, op1=ALU.add), reads=["ang"], writes=[("ob", s)])
            c.emit("dve", lambda e, s=s: e.tensor_scalar(out=ob[s][:, 0:32], in0=ang[:], scalar1=1.0 / (2 * math.pi), scalar2=0.75,
                                                         op0=ALU.mult, op1=ALU.add), reads=["ang"], writes=[("ob", s)])
            c.emit("dve", lambda e, s=s: e.tensor_copy(out=ni[:], in_=ob[s][:]), reads=[("ob", s)], writes=["ni"])
            c.emit("dve", lambda e: e.tensor_copy(out=nf[:], in_=ni[:]), reads=["ni"], writes=["nf"])
            c.emit("dve", lambda e, s=s: e.tensor_tensor(out=ob[s][:], in0=ob[s][:], in1=nf[:], op=ALU.subtract),
                   reads=[("ob", s), "nf"], writes=[("ob", s)])
            c.emit("dve", lambda e, s=s: e.tensor_scalar(out=nf[:], in0=ob[s][:], scalar1=0.0, scalar2=0.0,
                                                         op0=ALU.is_lt, op1=ALU.add), reads=[("ob", s)], writes=["nf"])
            c.emit("dve", lambda e, s=s: e.tensor_tensor(out=ob[s][:], in0=ob[s][:], in1=nf[:], op=ALU.add),
                   reads=[("ob", s), "nf"], writes=[("ob", s)])
            c.emit("act", lambda e, s=s: e.activation(out=ob[s][:], in_=ob[s][:], func=AF.Sin, bias=nb[:, 0:1], scale=2 * math.pi),
                   reads=[("ob", s), "nb"], writes=[("ob", s)])
            c.dma("sync", cos_d[t * 128:(t + 1) * 128, :], ob[s][:, 0:32], reads=[("ob", s)], writes=[("cd", t)])
            c.dma("sync", sin_d[t * 128:(t + 1) * 128, :], ob[s][:, 32:64], reads=[("ob", s)], writes=[("sd", t)])
        c.flush()


def rope_epi_ops(c, ps, pk, nh, cs_tile, cskey, dst, dkey, tmp, tkey):
    p3 = ps[:, 0:nh * 64].rearrange("p (h d) -> p h d", d=64)
    cosb = cs_tile[:, 0:32].unsqueeze(1).to_broadcast([128, nh, 32])
    sinb = cs_tile[:, 32:64].unsqueeze(1).to_broadcast([128, nh, 32])
    d3 = dst.rearrange("p (h d) -> p h d", d=64)
    t3 = tmp[:, 0:nh * 64].rearrange("p (h d) -> p h d", d=64)
    R = pk + [cskey]
    c.emit("dve", lambda e: e.tensor_tensor(out=d3[:, :, 0:32], in0=p3[:, :, 0:32], in1=cosb, op=ALU.mult), reads=R, writes=[dkey])
    c.emit("dve", lambda e: e.tensor_tensor(out=t3[:, :, 0:32], in0=p3[:, :, 32:64], in1=sinb, op=ALU.mult), reads=R, writes=[tkey])
    c.emit("dve", lambda e: e.tensor_tensor(out=d3[:, :, 32:64], in0=p3[:, :, 32:64], in1=cosb, op=ALU.mult), reads=R, writes=[dkey])
    c.emit("dve", lambda e: e.tensor_tensor(out=t3[:, :, 32:64], in0=p3[:, :, 0:32], in1=sinb, op=ALU.mult), reads=R, writes=[tkey])
    c.emit("dve", lambda e: e.tensor_tensor(out=d3[:, :, 0:32], in0=d3[:, :, 0:32], in1=t3[:, :, 0:32], op=ALU.subtract),
           reads=[dkey, tkey], writes=[dkey])
    c.emit("dve", lambda e: e.tensor_tensor(out=d3[:, :, 32:64], in0=d3[:, :, 32:64], in1=t3[:, :, 32:64], op=ALU.add),
           reads=[dkey, tkey], writes=[dkey])


def norm_tile(c, xt, gam, junk, hbuf, stat, jkeys, hkeys):
    c.emit("act", lambda e: e.activation(out=junk, in_=xt[:], func=AF.Square, accum_out=stat[:, 0:1]),
           reads=["xt"], writes=jkeys + ["stat"])
    c.emit("dve", lambda e: e.tensor_scalar(out=stat[:, 1:2], in0=stat[:, 0:1], scalar1=1.0 / D, scalar2=RMS_EPS,
                                            op0=ALU.mult, op1=ALU.add), reads=["stat"], writes=["stat"])
    c.emit("act", lambda e: e.activation(out=stat[:, 3:4], in_=stat[:, 1:2], func=AF.Sqrt), reads=["stat"], writes=["stat"])
    c.emit("dve", lambda e: e.reciprocal(out=stat[:, 2:3], in_=stat[:, 3:4]), reads=["stat"], writes=["stat"])
    c.emit("dve", lambda e: e.scalar_tensor_tensor(out=hbuf, in0=xt[:], scalar=stat[:, 2:3], in1=gam[:],
                                                   op0=ALU.mult, op1=ALU.mult), reads=["xt", "stat", "gam"], writes=hkeys)


def kv_phase(c, x_d, g_row, w_kv, cos_d, sin_d, kT_d, v_d, wdt=F32):
    with ExitStack() as pes:
        xt = c.sb("xt", [128, D], es=pes)
        gam = c.sb("gam", [128, D], es=pes)
        junk = c.sb("junk", [128, D], es=pes)
        hbuf = c.sb("hbuf", [128, D], es=pes)
        stat = c.sb("stat", [128, 8], es=pes)
        hT = c.sb("hT", [128, KC, 128], wdt, es=pes)
        wb = [c.sb("wb", [128, KC, 256], wdt, es=pes) for _ in range(2)]
        cs = [c.sb("cs", [128, 64], es=pes) for _ in range(2)]
        krot = c.sb("krot", [128, 512], es=pes)
        tmp = c.sb("tmp", [128, 256], es=pes)
        vb = [c.sb("vb", [128, 256], es=pes) for _ in range(2)]
        kTt = c.sb("kTt", [64, 8, 128], es=pes)
        c.dma("sync", gam[:], g_row.partition_broadcast(128), writes=["gam"])
        for t in range(c.S // 128):
            rs = slice(t * 128, (t + 1) * 128)
            s = t % 2
            c.dma("sync", xt[:], x_d[rs, :], writes=["xt"])
            c.dma("pool", cs[s][:, 0:32], cos_d[rs, :], writes=[("cs", s)])
            c.dma("pool", cs[s][:, 32:64], sin_d[rs, :], writes=[("cs", s)])
            norm_tile(c, xt, gam, junk[:], hbuf[:], stat, ["junk"], ["hbuf"])
            transpose_tile(c, hbuf, "hbuf", hT, lambda g: ("hT", g), 0)

            def epi(cb, ps, pk, s=s, rs=rs):
                if cb < 2:
                    rope_epi_ops(c, ps, pk, 4, cs[s], ("cs", s), krot[:, cb * 256:(cb + 1) * 256], ("krot", cb), tmp, "tmp")
                else:
                    vs = c.wslot("vb")
                    c.emit("act", lambda e, vs=vs, ps=ps: e.copy(out=vb[vs][:], in_=ps[:, 0:256]), reads=pk, writes=[("vb", vs)])
                    c.dma("sync", v_d[rs, (cb - 2) * 256:(cb - 1) * 256], vb[vs][:], reads=[("vb", vs)], writes=[("vd", rs.start, cb)])
            proj(c, hT, lambda kc: [("hT", kc // 4)], 1, w_kv, 1024, wb, "wb", epi)
            for hg in range(2):
                ps, pk = c.psum()
                for j in range(4):
                    h = hg * 4 + j
                    c.emit("pe", lambda e, ps=ps, j=j, h=h: e.transpose(ps[0:64, j * 128:(j + 1) * 128], krot[:, h * 64:(h + 1) * 64], c.ident[:]),
                           reads=[("krot", h // 4), "const"], writes=pk)
                c.emit("act", lambda e, ps=ps, hg=hg: e.copy(out=kTt[:, hg * 4:(hg + 1) * 4, :],
                                                             in_=ps[0:64, :].rearrange("p (a b) -> p a b", a=4)),
                       reads=pk, writes=["kTt"])
            c.dma("sync", kT_d[:, :, rs].rearrange("g d s -> d g s"), kTt[:], reads=["kTt"], writes=[("kTd", t)])
        c.flush()


def attn_phase(c, x_d, g_row, w_q, w_o, sinks_row, cos_d, sin_d, kT_d, v_d, dbg=None, wdt=F32):
    with ExitStack() as pes:
        xt = c.sb("xt", [128, D], es=pes)
        gam = c.sb("gam", [128, D], es=pes)
        junk = c.sb("junk", [128, D], es=pes)
        hbuf = c.sb("hbuf", [128, D], es=pes)
        stat = c.sb("stat", [128, 8], es=pes)
        hT = c.sb("hT", [128, KC, 128], wdt, es=pes)
        wb = [c.sb("wb", [128, KC, 256], wdt, es=pes) for _ in range(2)]
        cs = [c.sb("cs", [128, 64], es=pes) for _ in range(2)]
        tmp = c.sb("tmp", [128, 256], es=pes)
        qT = c.sb("qT", [64, NH, 128], es=pes)
        kTs = [c.sb("kTs", [64, 8, 256], es=pes) for _ in range(2)]
        v1 = [c.sb("v1", [128, 2, 8, 65], es=pes) for _ in range(2)]
        E = [[c.sb("E", [128, 512], es=pes) for _ in range(2)] for _ in range(2)]
        mC = c.sb("mC", [128, 512], es=pes)
        mP = c.sb("mP", [128, 512], es=pes)
        sk = c.sb("sk", [128, NH], es=pes)
        dn = c.sb("dn", [128, 8], es=pes)
        attn = junk
        c.dma("sync", gam[:], g_row.partition_broadcast(128), writes=["gam"])
        c.dma("sync", sk[:], sinks_row.partition_broadcast(128), writes=["sk"])
        c.emit("act", lambda e: e.activation(out=sk[:], in_=sk[:], func=AF.Exp), reads=["sk"], writes=["sk"])
        for q in range(4):
            c.emit("pool", lambda e, q=q: e.tensor_copy(out=mC[:, q * 128:(q + 1) * 128], in_=c.mask4[:, 128:256]),
                   reads=["const"], writes=["mC"])
            c.emit("pool", lambda e, q=q: e.tensor_copy(out=mP[:, q * 128:(q + 1) * 128], in_=c.msl[:]),
                   reads=["const"], writes=["mP"])
        for s in range(2):
            c.emit("pool", lambda e, s=s: e.memset(v1[s][:], 1.0), writes=[("v1", s)])
        for t in range(c.S // 128):
            rs = slice(t * 128, (t + 1) * 128)
            s = t % 2
            nkb = 1 if t == 0 else 2
            k0 = (t - 1) * 128 if t > 0 else 0
            c.dma("sync", xt[:], x_d[rs, :], writes=["xt"])
            c.dma("pool", cs[s][:, 0:32], cos_d[rs, :], writes=[("cs", s)])
            c.dma("pool", cs[s][:, 32:64], sin_d[rs, :], writes=[("cs", s)])
            kb0 = 2 - nkb
            c.dma("pool", kTs[s][:, :, kb0 * 128:256], kT_d[:, :, k0:(t + 1) * 128].rearrange("g d s -> d g s"),
                  writes=[("kTs", s)])
            for kb in range(kb0, 2):
                r1 = (t - 1 + kb) * 128
                c.dma("sync", v1[s][:, kb, :, 0:64], v_d[r1:r1 + 128, :].rearrange("p (g d) -> p g d", d=64), writes=[("v1", s)])
            norm_tile(c, xt, gam, junk[:], hbuf[:], stat, ["junk"], ["hbuf"])
            transpose_tile(c, hbuf, "hbuf", hT, lambda g: ("hT", g), 0)

            def epi(cb, ps, pk, s=s):
                rope_epi_ops(c, ps, pk, 4, cs[s], ("cs", s), hbuf[:, cb * 256:(cb + 1) * 256], "hbuf", tmp, "tmp")
            proj(c, hT, lambda kc: [("hT", kc // 4)], 1, w_q, D, wb, "wb", epi)
            for hg in range(8):
                ps, pk = c.psum()
                for j in range(4):
                    h = hg * 4 + j
                    c.emit("pe", lambda e, ps=ps, j=j, h=h: e.transpose(ps[0:64, j * 128:(j + 1) * 128], hbuf[:, h * 64:(h + 1) * 64], c.ident[:]),
                           reads=["hbuf", "const"], writes=pk)
                c.emit("act", lambda e, ps=ps, hg=hg: e.copy(out=qT[:, hg * 4:(hg + 1) * 4, :],
                                                             in_=ps[0:64, :].rearrange("p (a b) -> p a b", a=4)),
                       reads=pk, writes=[("qT", hg)])
            for g in range(8):
                es_ = g % 2
                for kb in range(kb0, 2):
                    ps, pk = c.psum()
                    for j in range(4):
                        c.emit("pe", lambda e, ps=ps, j=j, g=g, kb=kb, s=s: e.matmul(
                            ps[:, j * 128:(j + 1) * 128], lhsT=kTs[s][:, g, kb * 128:(kb + 1) * 128], rhs=qT[:, 4 * g + j, :],
                            start=True, stop=True), reads=[("kTs", s), ("qT", g)], writes=pk)
                    c.emit("act", lambda e, ps=ps, kb=kb, es_=es_: e.activation(out=E[es_][kb][:], in_=ps[:, :], func=AF.Exp, scale=0.125),
                           reads=pk, writes=[("E", es_, kb)])
                    m = mC if kb == 1 else mP
                    c.emit("pool", lambda e, kb=kb, es_=es_, m=m: e.tensor_tensor(out=E[es_][kb][:], in0=E[es_][kb][:], in1=m[:], op=ALU.mult),
                           reads=[("E", es_, kb), "mC", "mP"], writes=[("E", es_, kb)])
                po, pok = c.psum()
                for j in range(4):
                    for kb in range(kb0, 2):
                        c.emit("pe", lambda e, po=po, j=j, kb=kb, g=g, s=s, es_=es_, kb0=kb0: e.matmul(
                            po[:, j * 65:(j + 1) * 65], lhsT=E[es_][kb][:, j * 128:(j + 1) * 128], rhs=v1[s][:, kb, g, :],
                            start=(kb == kb0), stop=(kb == 1)), reads=[("E", es_, kb), ("v1", s)], writes=pok)
                po3 = po[:, 0:260].rearrange("p (h d) -> p h d", d=65)
                c.emit("dve", lambda e, po3=po3, g=g: e.tensor_tensor(out=dn[:, 0:4], in0=po3[:, :, 64], in1=sk[:, 4 * g:4 * g + 4], op=ALU.add),
                       reads=pok + ["sk"], writes=["dn"])
                c.emit("dve", lambda e: e.reciprocal(out=dn[:, 4:8], in_=dn[:, 0:4]), reads=["dn"], writes=["dn"])
                c.emit("dve", lambda e, po3=po3, g=g: e.tensor_tensor(
                    out=attn[:, g * 256:(g + 1) * 256].rearrange("p (h d) -> p h d", d=64), in0=po3[:, :, 0:64],
                    in1=dn[:, 4:8].unsqueeze(2).to_broadcast([128, 4, 64]), op=ALU.mult), reads=pok + ["dn"], writes=["junk"])
            if dbg is not None:
                c.dma("sync", dbg[rs, :], attn[:], reads=["junk"], writes=[("dbg", t)])
            transpose_tile(c, attn, "junk", hT, lambda g: ("hT", g), 0)

            def epi2(cb, ps, pk):
                c.emit("dve", lambda e, cb=cb, ps=ps: e.tensor_tensor(out=xt[:, cb * 256:(cb + 1) * 256], in0=ps[:, 0:256],
                                                                       in1=xt[:, cb * 256:(cb + 1) * 256], op=ALU.add),
                       reads=pk + ["xt"], writes=["xt"])
            proj(c, hT, lambda kc: [("hT", kc // 4)], 1, w_o, D, wb, "wb", epi2)
            c.dma("sync", x_d[rs, :], xt[:], reads=["xt"], writes=[("xd", t)])
        c.flush()


def test_attn_program(S):
    nc = bass.Bass("TRN2", target_bir_lowering=False)
    x_in = nc.dram_tensor("x", [S, D], F32, kind="ExternalInput").ap()
    pos = nc.dram_tensor("pos", [S], I32, kind="ExternalInput").ap()
    gk = nc.dram_tensor("gk", [1, D], F32, kind="ExternalInput").ap()
    g1 = nc.dram_tensor("g1", [1, D], F32, kind="ExternalInput").ap()
    wkv = nc.dram_tensor("wkv", [D, 1024], F32, kind="ExternalInput").ap()
    wq = nc.dram_tensor("wq", [D, D], F32, kind="ExternalInput").ap()
    wo = nc.dram_tensor("wo", [D, D], F32, kind="ExternalInput").ap()
    sinks = nc.dram_tensor("sinks", [1, NH], F32, kind="ExternalInput").ap()
    xo = nc.dram_tensor("xo", [S, D], F32, kind="ExternalOutput").ap()
    cos_d = nc.dram_tensor("cos_d", [S, 32], F32, kind="ExternalOutput").ap()
    sin_d = nc.dram_tensor("sin_d", [S, 32], F32, kind="ExternalOutput").ap()
    kT_d = nc.dram_tensor("kT_d", [8, 64, S], F32, kind="ExternalOutput").ap()
    v_d = nc.dram_tensor("v_d", [S, 512], F32, kind="ExternalOutput").ap()
    with ExitStack() as es:
        c = Ctx(nc, es, S)
        make_consts(c)
        make_masks(c)
        copy_rows(c, xo, x_in, S)
        rope_tables_phase(c, pos, cos_d, sin_d)
        kv_phase(c, xo, gk, wkv, cos_d, sin_d, kT_d, v_d)
        dbg = nc.dram_tensor("dbg", [S, D], F32, kind="ExternalOutput").ap()
        attn_phase(c, xo, g1, wq, wo, sinks, cos_d, sin_d, kT_d, v_d, dbg=dbg)
    return nc


def ple_phase(c, x_d, g_row, w_gate, w_up, p_d, wdt=F32):
    with ExitStack() as pes:
        xt = c.sb("xt", [128, D], es=pes)
        gam = c.sb("gam", [128, D], es=pes)
        junk = c.sb("junk", [128, D], es=pes)
        hbuf = c.sb("hbuf", [128, D], es=pes)
        stat = c.sb("stat", [128, 8], es=pes)
        hT = c.sb("hT", [128, KC, 128], wdt, es=pes)
        wb = [c.sb("wb", [128, KC, 256], wdt, es=pes) for _ in range(2)]
        wup = c.sb("wup", [128, 2, D], es=pes)
        pt = c.sb("pt", [128, 256], es=pes)
        pT = c.sb("pT", [128, 2, 128], es=pes)
        sg = [c.sb("sg", [128, 256], es=pes) for _ in range(2)]
        c.dma("sync", gam[:], g_row.partition_broadcast(128), writes=["gam"])
        c.dma("sync", wup[:], w_up.rearrange("(ch p) n -> p ch n", p=128), writes=["wup"])
        for t in range(c.S // 128):
            rs = slice(t * 128, (t + 1) * 128)
            c.dma("sync", xt[:], x_d[rs, :], writes=["xt"])
            c.dma("pool", pt[:], p_d[rs, :], writes=["pt"])
            norm_tile(c, xt, gam, junk[:], hbuf[:], stat, ["junk"], ["hbuf"])
            transpose_tile(c, hbuf, "hbuf", hT, lambda g: ("hT", g), 0)
            transpose_tile(c, pt, "pt", pT, lambda g: "pT", 0, nchunks=2)

            def epi(cb, ps, pk):
                s = c.wslot("sg")
                c.emit("act", lambda e, s=s, ps=ps: e.activation(out=sg[s][:], in_=ps[:, 0:256], func=AF.Sigmoid),
                       reads=pk, writes=[("sg", s)])
                pu, puk = c.psum()
                for ch in range(2):
                    c.emit("pe", lambda e, pu=pu, ch=ch, cb=cb: e.matmul(pu[:, 0:256], lhsT=pT[:, ch, :],
                                                                         rhs=wup[:, ch, cb * 256:(cb + 1) * 256],
                                                                         start=(ch == 0), stop=(ch == 1)),
                           reads=["pT", "wup"], writes=puk)
                c.emit("dve", lambda e, s=s, pu=pu: e.tensor_tensor(out=sg[s][:], in0=sg[s][:], in1=pu[:, 0:256], op=ALU.mult),
                       reads=puk + [("sg", s)], writes=[("sg", s)])
                c.emit("dve", lambda e, s=s, cb=cb: e.tensor_tensor(out=xt[:, cb * 256:(cb + 1) * 256], in0=sg[s][:],
                                                                    in1=xt[:, cb * 256:(cb + 1) * 256], op=ALU.add),
                       reads=[("sg", s), "xt"], writes=["xt"])
            proj(c, hT, lambda kc: [("hT", kc // 4)], 1, w_gate, D, wb, "wb", epi)
            c.dma("sync", x_d[rs, :], xt[:], reads=["xt"], writes=[("xd", t)])
        c.flush()


def proj4(c, xT, xkeyf, W, N, wbufs, wname, epi, kcn=KC, qi=[0]):
    CW = 256
    Wv = W.rearrange("(kc p) n -> p kc n", p=128)
    for cb in range(N // CW):
        s = c.wslot(wname)
        qi[0] += 1
        c.dma("sync" if qi[0] % 2 else "pool", wbufs[s][:, 0:kcn, 0:CW], Wv[:, :, cb * CW:(cb + 1) * CW],
              writes=[(wname, s)])
        for half in range(2):
            ps, pk = c.psum()
            for t2 in range(2):
                tt = half * 2 + t2
                for kc in range(kcn):
                    c.emit("pe", lambda e, ps=ps, t2=t2, tt=tt, kc=kc, s=s: e.matmul(
                        ps[:, t2 * CW:(t2 + 1) * CW], lhsT=xT[:, kc, tt * 128:(tt + 1) * 128], rhs=wbufs[s][:, kc, 0:CW],
                        start=(kc == 0), stop=(kc == kcn - 1)),
                        reads=[(wname, s)] + xkeyf(kc), writes=pk)
            epi(cb, half, ps, pk)


def ple_phase4(c, x_d, g_row, w_gate, w_up, p_d, wdt=F32):
    TT = 4
    with ExitStack() as pes:
        xt = [c.sb("xt", [128, D], es=pes) for _ in range(TT)]
        gam = c.sb("gam", [128, D], es=pes)
        junk = c.sb("junk", [128, D], es=pes)
        hbuf = c.sb("hbuf", [128, D], es=pes)
        stat = c.sb("stat", [128, 8], es=pes)
        hT = c.sb("hT", [128, KC, TT * 128], wdt, es=pes)
        wb = [c.sb("wb", [128, KC, 256], wdt, es=pes) for _ in range(2)]
        wup = c.sb("wup", [128, 2, D], es=pes)
        pt = [c.sb("pt", [128, 256], es=pes) for _ in range(2)]
        pT = c.sb("pT", [128, 2, TT * 128], es=pes)
        sg = [c.sb("sg", [128, 512], es=pes) for _ in range(2)]
        c.dma("sync", gam[:], g_row.partition_broadcast(128), writes=["gam"])
        c.dma("sync", wup[:], w_up.rearrange("(ch p) n -> p ch n", p=128), writes=["wup"])
        for tb in range(c.S // (TT * 128)):
            for tt in range(TT):
                r0 = (tb * TT + tt) * 128
                c.dma("sync", xt[tt][:], x_d[r0:r0 + 128, :], writes=[("xt", tt)])
                ps_ = tt % 2
                c.dma("pool", pt[ps_][:], p_d[r0:r0 + 128, :], writes=[("pt", ps_)])
                c.emit("act", lambda e, tt=tt: e.activation(out=junk[:], in_=xt[tt][:], func=AF.Square, accum_out=stat[:, 0:1]),
                       reads=[("xt", tt)], writes=["junk", "stat"])
                c.emit("dve", lambda e: e.tensor_scalar(out=stat[:, 1:2], in0=stat[:, 0:1], scalar1=1.0 / D, scalar2=RMS_EPS,
                                                        op0=ALU.mult, op1=ALU.add), reads=["stat"], writes=["stat"])
                c.emit("act", lambda e: e.activation(out=stat[:, 3:4], in_=stat[:, 1:2], func=AF.Sqrt), reads=["stat"], writes=["stat"])
                c.emit("dve", lambda e: e.reciprocal(out=stat[:, 2:3], in_=stat[:, 3:4]), reads=["stat"], writes=["stat"])
                c.emit("dve", lambda e, tt=tt: e.scalar_tensor_tensor(out=hbuf[:], in0=xt[tt][:], scalar=stat[:, 2:3], in1=gam[:],
                                                                      op0=ALU.mult, op1=ALU.mult),
                       reads=[("xt", tt), "stat", "gam"], writes=["hbuf"])
                transpose_tile(c, hbuf, "hbuf", hT, lambda g: ("hT", g), tt * 128)
                transpose_tile(c, pt[ps_], ("pt", ps_), pT, lambda g: "pT", tt * 128, nchunks=2)

            def epi(cb, half, ps, pk):
                s = c.wslot("sg")
                c.emit("act", lambda e, s=s, ps=ps: e.activation(out=sg[s][:], in_=ps[:, :], func=AF.Sigmoid),
                       reads=pk, writes=[("sg", s)])
                pu, puk = c.psum()
                for t2 in range(2):
                    tt = half * 2 + t2
                    for ch in range(2):
                        c.emit("pe", lambda e, pu=pu, ch=ch, cb=cb, t2=t2, tt=tt: e.matmul(
                            pu[:, t2 * 256:(t2 + 1) * 256], lhsT=pT[:, ch, tt * 128:(tt + 1) * 128],
                            rhs=wup[:, ch, cb * 256:(cb + 1) * 256], start=(ch == 0), stop=(ch == 1)),
                            reads=["pT", "wup"], writes=puk)
                c.emit("dve", lambda e, s=s, pu=pu: e.tensor_tensor(out=sg[s][:], in0=sg[s][:], in1=pu[:, :], op=ALU.mult),
                       reads=puk + [("sg", s)], writes=[("sg", s)])
                for t2 in range(2):
                    tt = half * 2 + t2
                    c.emit("pool", lambda e, s=s, cb=cb, t2=t2, tt=tt: e.tensor_tensor(
                        out=xt[tt][:, cb * 256:(cb + 1) * 256], in0=sg[s][:, t2 * 256:(t2 + 1) * 256],
                        in1=xt[tt][:, cb * 256:(cb + 1) * 256], op=ALU.add),
                        reads=[("sg", s), ("xt", tt)], writes=[("xt", tt)])
            proj4(c, hT, lambda kc: [("hT", kc // 4)], w_gate, D, wb, "wb", epi)
            for tt in range(TT):
                r0 = (tb * TT + tt) * 128
                c.dma("sync", x_d[r0:r0 + 128, :], xt[tt][:], reads=[("xt", tt)], writes=[("xd", r0 // 128)])
        c.flush()


def final_norm_phase(c, x_d, g_row, y_d):
    with ExitStack() as pes:
        xt = c.sb("xt", [128, D], es=pes)
        gam = c.sb("gam", [128, D], es=pes)
        junk = c.sb("junk", [128, D], es=pes)
        hb = [c.sb("hbuf", [128, D], es=pes) for _ in range(2)]
        stat = c.sb("stat", [128, 8], es=pes)
        c.dma("sync", gam[:], g_row.partition_broadcast(128), writes=["gam"])
        for t in range(c.S // 128):
            rs = slice(t * 128, (t + 1) * 128)
            s = t % 2
            c.dma("sync", xt[:], x_d[rs, :], writes=["xt"])
            norm_tile(c, xt, gam, junk[:], hb[s][:], stat, ["junk"], [("hb", s)])
            c.dma("pool", y_d[rs, :], hb[s][:], reads=[("hb", s)], writes=[("yd", t)])
        c.flush()


def convert_ffn_weights_phase(c, wg, wu, wd, wg_b, wu_b, wd_b, ff):
    HC = ff // 128
    CH = 2816 if ff % 2816 == 0 else ff
    with ExitStack() as pes:
        fb = [c.sb("cvf", [128, 2816], es=pes) for _ in range(3)]
        bb = [c.sb("cvb", [128, 2816], BF16, es=pes) for _ in range(3)]
        it = 0
        engs = ["act", "dve", "pool"]

        def cast(s, n, it):
            e_ = engs[it % 3]
            if e_ == "act":
                c.emit("act", lambda e: e.copy(out=bb[s][:, 0:n], in_=fb[s][:, 0:n]), reads=[("cvf", s)], writes=[("cvb", s)])
            else:
                c.emit(e_, lambda e: e.tensor_copy(out=bb[s][:, 0:n], in_=fb[s][:, 0:n]), reads=[("cvf", s)], writes=[("cvb", s)])
        for (w, wb_) in ((wg, wg_b), (wu, wu_b)):
            for kc in range(KC):
                for c0 in range(0, ff, CH):
                    s = it % 3
                    c.dma("sync" if it % 2 == 0 else "act", fb[s][:, 0:CH], w[kc * 128:(kc + 1) * 128, c0:c0 + CH], writes=[("cvf", s)])
                    cast(s, CH, it)
                    h0, nh = c0 // 128, CH // 128
                    c.dma("pool", wb_[h0:h0 + nh, :, kc, :].rearrange("h p c -> p h c"),
                          bb[s][:, 0:CH].rearrange("p (h c) -> p h c", c=128), reads=[("cvb", s)], writes=[("cvo", it)])
                    it += 1
        for hc in range(HC):
            s = it % 3
            c.dma("sync" if it % 2 == 0 else "act", fb[s][:, 0:D], wd[hc * 128:(hc + 1) * 128, :], writes=[("cvf", s)])
            cast(s, D, it)
            c.dma("pool", wd_b[hc // 4, :, :, hc % 4, :].rearrange("db p c -> p db c"),
                  bb[s][:, 0:D].rearrange("p (db c) -> p db c", c=512), reads=[("cvb", s)], writes=[("cvo", it)])
            it += 1
        c.flush()


def run_ffn_bf16(c, x_d, gam_row, wg_b, wu_b, wd_b, ff):
    TB, NTT = 512, 4
    HC = ff // 128
    HG = HC // 4
    with ExitStack() as pes:
        xt = [c.sb("xt", [128, D], es=pes) for _ in range(NTT)]
        gam = c.sb("gam", [128, D], es=pes)
        junk = c.sb("junk", [128, D], es=pes)
        hbuf = c.sb("hbuf", [128, D], es=pes)
        stat = c.sb("stat", [128, 8], es=pes)
        hT = c.sb("hT", [128, KC, TB], BF16, es=pes)
        act = c.sb("act", [128, HC, TB], BF16, es=pes)
        wgs = [c.sb("wgs", [128, KC, 128], BF16, es=pes) for _ in range(3)]
        wus = [c.sb("wus", [128, KC, 128], BF16, es=pes) for _ in range(3)]
        wds = [c.sb("wds", [128, 4, 512], BF16, es=pes) for _ in range(3)]
        sg = [c.sb("sg", [128, TB], es=pes) for _ in range(2)]
        c.dma("sync", gam[:], gam_row.partition_broadcast(128), writes=["gam"])
        B = dict(junk=junk, stat=stat, hbuf=hbuf)
        wi = 0
        di = 0
        for tb in range(c.S // TB):
            for tt in range(NTT):
                r0 = tb * TB + tt * 128
                c.dma("sync", xt[tt][:], x_d[r0:r0 + 128, :], writes=[("xt", tt)])
                norm_to_hT(c, xt[tt][:], ("xt", tt), gam[:], "gam", hT, lambda g: ("hT", g), tt * 128, B)
            for hc in range(HC):
                s = wi % 3
                wi += 1
                c.dma("sync", wgs[s][:], wg_b[hc], writes=[("wg", s)])
                c.dma("act", wus[s][:], wu_b[hc], writes=[("wu", s)])
                psg, pgk = c.psum()
                psu, puk = c.psum()
                for kc in range(KC):
                    c.emit("pe", lambda e, psg=psg, s=s, kc=kc: e.matmul(psg[:, :], lhsT=wgs[s][:, kc, :], rhs=hT[:, kc, :],
                                                                          start=(kc == 0), stop=(kc == KC - 1)),
                           reads=[("wg", s), ("hT", kc // 4)], writes=pgk)
                for kc in range(KC):
                    c.emit("pe", lambda e, psu=psu, s=s, kc=kc: e.matmul(psu[:, :], lhsT=wus[s][:, kc, :], rhs=hT[:, kc, :],
                                                                          start=(kc == 0), stop=(kc == KC - 1)),
                           reads=[("wu", s), ("hT", kc // 4)], writes=puk)
                ss = hc % 2
                c.emit("act", lambda e, psg=psg, ss=ss: e.activation(out=sg[ss][:], in_=psg[:, :], func=AF.Silu),
                       reads=pgk, writes=[("sg", ss)])
                c.emit("dve", lambda e, psu=psu, ss=ss, hc=hc: e.tensor_tensor(out=act[:, hc, :], in0=sg[ss][:], in1=psu[:, :],
                                                                               op=ALU.mult),
                       reads=puk + [("sg", ss)], writes=[("act", hc)])
            for db in range(D // 512):
                pds = [c.psum() for _ in range(NTT)]
                for hg in range(HG):
                    s = di % 3
                    di += 1
                    c.dma("sync" if di % 2 else "pool", wds[s][:], wd_b[hg, db], writes=[("wd", s)])
                    for tt in range(NTT):
                        pd, pdk = pds[tt]
                        for j in range(4):
                            hc = hg * 4 + j
                            c.emit("pe", lambda e, pd=pd, s=s, j=j, hc=hc, tt=tt: e.matmul(
                                pd[:, :], lhsT=act[:, hc, tt * 128:(tt + 1) * 128], rhs=wds[s][:, j, :],
                                start=(hc == 0), stop=(hc == HC - 1)),
                                reads=[("act", hc), ("wd", s)], writes=pdk)
                for tt in range(NTT):
                    pd, pdk = pds[tt]
                    c.emit("dve", lambda e, pd=pd, tt=tt, db=db: e.scalar_tensor_tensor(
                        out=xt[tt][:, db * 512:(db + 1) * 512], in0=pd[:, :], scalar=0.5,
                        in1=xt[tt][:, db * 512:(db + 1) * 512], op0=ALU.mult, op1=ALU.add),
                        reads=pdk + [("xt", tt)], writes=[("xt", tt)])
            for tt in range(NTT):
                r0 = tb * TB + tt * 128
                c.dma("pool", x_d[r0:r0 + 128, :], xt[tt][:], reads=[("xt", tt)], writes=[("xd", r0 // 128)])
        c.flush()


def convert_mats_phase(c, pairs):
    with ExitStack() as pes:
        fb = [c.sb("cvf", [128, 2048], es=pes) for _ in range(3)]
        bb = [c.sb("cvb", [128, 2048], BF16, es=pes) for _ in range(3)]
        engs = ["act", "dve", "pool"]
        it = 0
        for (w, wb_) in pairs:
            K_, N_ = w.shape
            for kc in range(K_ // 128):
                for c0 in range(0, N_, 2048):
                    n = min(2048, N_ - c0)
                    s = it % 3
                    c.dma("sync" if it % 2 == 0 else "act", fb[s][:, 0:n], w[kc * 128:(kc + 1) * 128, c0:c0 + n], writes=[("cvf", s)])
                    e_ = engs[it % 3]
                    if e_ == "act":
                        c.emit("act", lambda e, s=s, n=n: e.copy(out=bb[s][:, 0:n], in_=fb[s][:, 0:n]), reads=[("cvf", s)], writes=[("cvb", s)])
                    else:
                        c.emit(e_, lambda e, s=s, n=n: e.tensor_copy(out=bb[s][:, 0:n], in_=fb[s][:, 0:n]), reads=[("cvf", s)], writes=[("cvb", s)])
                    c.dma("pool", wb_[kc * 128:(kc + 1) * 128, c0:c0 + n], bb[s][:, 0:n], reads=[("cvb", s)], writes=[("cvo", it)])
                    it += 1
        c.flush()


FF = 5632
DEPTH = 4
FULL_SHAPES = dict(
    norm_g=(4, 4, D), ffn_w_gate=(4, 2, D, FF), ffn_w_up=(4, 2, D, FF), ffn_w_down=(4, 2, FF, D),
    ple_w_up=(4, 256, D), ple_w_gate=(4, D, D), kv_norm_g=(1, D), w_kv=(D, 1024), attn_w_q=(2, D, D),
    attn_w_o=(2, D, D), attn_sinks=(2, NH), final_norm_g=(1, D))
FULL_SHAPES.update(RWKV_SHAPES)


def build_full(S, ff=FF, layers=(0, 1, 2, 3)):
    nc = bass.Bass("TRN2", target_bir_lowering=False)
    shapes = dict(FULL_SHAPES)
    shapes["ffn_w_gate"] = (4, 2, D, ff)
    shapes["ffn_w_up"] = (4, 2, D, ff)
    shapes["ffn_w_down"] = (4, 2, ff, D)
    I = {k: nc.dram_tensor(k, list(v), F32, kind="ExternalInput").ap() for k, v in shapes.items()}
    x_in = nc.dram_tensor("x", [S, D], F32, kind="ExternalInput").ap()
    p_in = nc.dram_tensor("p", [DEPTH, S, 256], F32, kind="ExternalInput").ap()
    pos = nc.dram_tensor("positions", [S], I32, kind="ExternalInput").ap()
    y = nc.dram_tensor("y", [S, D], F32, kind="ExternalOutput").ap()
    scr = lambda n, shp=None: nc.dram_tensor("scr_" + n, list(shp or [S, D]), F32).ap()
    xs = scr("x")
    O = {k: scr(k) for k in ["r", "k", "v", "sw", "a", "g", "sv", "At", "Rt", "Bh", "Kh", "Bt", "Kt", "bonus", "y"]}
    O["el"] = scr("el", [S // 128, D])
    V0, V1 = scr("V0"), scr("V1")
    cos_d, sin_d = scr("cos", [S, 32]), scr("sin", [S, 32])
    kT_d, v_d = scr("kT", [8, 64, S]), scr("vkv", [S, 512])
    HC = ff // 128
    WB = {}
    for i in layers:
        for hf in range(2):
            WB[(i, hf)] = (nc.dram_tensor("wgb_%d_%d" % (i, hf), [HC, 128, KC, 128], BF16).ap(),
                           nc.dram_tensor("wub_%d_%d" % (i, hf), [HC, 128, KC, 128], BF16).ap(),
                           nc.dram_tensor("wdb_%d_%d" % (i, hf), [HC // 4, 4, 128, 4, 512], BF16).ap())
    with ExitStack() as es:
        c = Ctx(nc, es, S)
        make_consts(c)
        make_masks(c)
        copy_rows(c, xs, x_in, S)
        rope_tables_phase(c, pos, cos_d, sin_d)
        bf = lambda n, shp: nc.dram_tensor("bf_" + n, list(shp), BF16).ap()
        MB = {}
        pairs = []
        for i in layers:
            MB[("ple", i)] = bf("ple%d" % i, [D, D])
            pairs.append((I["ple_w_gate"][i], MB[("ple", i)]))
            if i < 2:
                MB[("wo", i)] = bf("rwo%d" % i, [D, D])
                pairs.append((I["rwkv_w_o"][i], MB[("wo", i)]))
                for q_, nm in enumerate(["wr_b", "wk_b", "wv_b"]):
                    MB[(nm, i)] = bf("%s%d" % (nm, i), [D, D])
                    pairs.append((I["rwkv_w_rkv"][i, q_], MB[(nm, i)]))
            else:
                MB[("aq", i)] = bf("aq%d" % i, [D, D])
                MB[("ao", i)] = bf("ao%d" % i, [D, D])
                pairs.append((I["attn_w_q"][i - 2], MB[("aq", i)]))
                pairs.append((I["attn_w_o"][i - 2], MB[("ao", i)]))
        MB["kv"] = bf("wkv", [D, 1024])
        pairs.append((I["w_kv"], MB["kv"]))
        convert_mats_phase(c, pairs)
        for i in layers:
            for hf in range(2):
                convert_ffn_weights_phase(c, I["ffn_w_gate"][i, hf], I["ffn_w_up"][i, hf], I["ffn_w_down"][i, hf],
                                          WB[(i, hf)][0], WB[(i, hf)][1], WB[(i, hf)][2], ff)
        for i in layers:
            if i == 2:
                kv_phase(c, xs, I["kv_norm_g"], MB["kv"], cos_d, sin_d, kT_d, v_d, wdt=BF16)
            run_ffn_bf16(c, xs, I["norm_g"][i, 0:1, :], WB[(i, 0)][0], WB[(i, 0)][1], WB[(i, 0)][2], ff)
            if i < 2:
                P = rwkv_param_aps(I, i, i == 1)
                P["g"] = I["norm_g"][i, 1:2, :]
                for nm in ["wr_b", "wk_b", "wv_b"]:
                    P[nm] = MB[(nm, i)]
                Oi = dict(O)
                Oi["V"] = V0 if i == 0 else V1
                Oi["vf"] = V0
                rwkv_proj_phase(c, xs, P, Oi, i == 1)
                rwkv_prep_phase(c, P, Oi, i == 1)
                rwkv_scan_phase(c, Oi)
                P["wo"] = MB[("wo", i)]
                if S % 512 == 0:
                    rwkv_post_phase4(c, xs, P, Oi, wdt=BF16)
                else:
                    rwkv_post_phase(c, xs, P, Oi, wdt=BF16)
            else:
                j = i - 2
                attn_phase(c, xs, I["norm_g"][i, 1:2, :], MB[("aq", i)], MB[("ao", i)], I["attn_sinks"][j:j + 1, :],
                           cos_d, sin_d, kT_d, v_d, wdt=BF16)
            run_ffn_bf16(c, xs, I["norm_g"][i, 2:3, :], WB[(i, 1)][0], WB[(i, 1)][1], WB[(i, 1)][2], ff)
            if S % 512 == 0:
                ple_phase4(c, xs, I["norm_g"][i, 3:4, :], MB[("ple", i)], I["ple_w_up"][i], p_in[i], wdt=BF16)
            else:
                ple_phase(c, xs, I["norm_g"][i, 3:4, :], MB[("ple", i)], I["ple_w_up"][i], p_in[i], wdt=BF16)
        final_norm_phase(c, xs, I["final_norm_g"], y)
    return nc


N_CORES = 4
_NC_CACHE = {}


def kernel(**inputs):
    x = np.ascontiguousarray(inputs["x"], dtype=np.float32)
    B, S, _ = x.shape
    if S not in _NC_CACHE:
        _NC_CACHE[S] = build_full(S)
    nc = _NC_CACHE[S]
    shared = {}
    for k, shp in FULL_SHAPES.items():
        shared[k] = np.ascontiguousarray(np.asarray(inputs[k], dtype=np.float32).reshape(shp))
    p = np.asarray(inputs["p"], dtype=np.float32)
    pos = np.asarray(inputs["positions"], dtype=np.int32)
    in_maps = []
    for b in range(B):
        m = dict(shared)
        m["x"] = x[b]
        m["p"] = np.ascontiguousarray(p[:, b])
        m["positions"] = np.ascontiguousarray(pos[b])
        in_maps.append(m)
    res = run_bass_kernel_spmd(nc, in_maps, core_ids=list(range(B)))
    return np.stack([np.asarray(r["y"], dtype=np.float32) for r in res.results], axis=0)
```
